# Optimizing a Trainium2 kernel written in Bass

```python
import jax, jax.numpy as jnp
from jax import lax
import numpy as np

D_MODEL = 1024
BATCH = 8
SEQ = 8192
DEPTH = 2

GRID_W = 64
QBLK = 128
ROPE_THETA = 10000.0
HEAD_DIM = 64
A_HEADS = 8
A_KV = 2
B_HEADS = 8
B_KV = 2
WINDOW = 128
AB_IN = (A_HEADS + 2 * A_KV + B_HEADS + 2 * B_KV) * HEAD_DIM
AB_MIX = (A_HEADS + B_HEADS) * HEAD_DIM
MLA_HEADS = 16
Q_LORA = 256
KV_LORA = 128
QK_NOPE = 64
QK_ROPE = 32
V_DIM = 64
MLA_DOWN = Q_LORA + KV_LORA + QK_ROPE
N_EXPERTS = 16
EC_FACTOR = 2
D_EXPERT = 1024
N_EVEN = (DEPTH + 1) // 2
N_ODD = DEPTH // 2
ALPHA = (2.0 * DEPTH) ** 0.25
BETA = (8.0 * DEPTH) ** -0.25
NEG_INF = -1e30

kernel_name = "hybrid_axialgqa_swa_mla_ecmoe_deepnorm"


def rms_norm(x, g, eps=1e-6):
    xf = x.astype(jnp.float32)
    y = xf * lax.rsqrt(jnp.mean(xf * xf, axis=-1, keepdims=True) + eps)
    return (y * g.astype(jnp.float32)).astype(x.dtype)


def layer_norm(x, g, b, eps=1e-5):
    xf = x.astype(jnp.float32)
    mu = jnp.mean(xf, axis=-1, keepdims=True)
    var = jnp.mean(jnp.square(xf - mu), axis=-1, keepdims=True)
    y = (xf - mu) * lax.rsqrt(var + eps)
    return (y * g.astype(jnp.float32) + b.astype(jnp.float32)).astype(x.dtype)


def rope_angles(pos, dim):
    freqs = ROPE_THETA ** (-(jnp.arange(0, dim, 2, dtype=jnp.float32) / dim))
    return pos[:, None] * freqs[None, :]


def apply_rope(x, ang):
    cos = jnp.cos(ang)[None, :, None, :]
    sin = jnp.sin(ang)[None, :, None, :]
    xf = x.astype(jnp.float32)
    x1, x2 = jnp.split(xf, 2, axis=-1)
    return jnp.concatenate([x1 * cos - x2 * sin, x2 * cos + x1 * sin], axis=-1).astype(x.dtype)


def apply_axial_rope(x, ang_row, ang_col):
    half = x.shape[-1] // 2
    return jnp.concatenate([apply_rope(x[..., :half], ang_row),
                            apply_rope(x[..., half:], ang_col)], axis=-1)


def dense_attention(q, k, v, scale):
    B, S, H, dq = q.shape
    KV = k.shape[2]
    G = H // KV
    dv = v.shape[-1]
    nb = S // QBLK
    qb = q.reshape(B, nb, QBLK, KV, G, dq).transpose(1, 0, 2, 3, 4, 5)

    def one_block(qblk):
        s = jnp.einsum('bqkgd,bskd->bkgqs', qblk, k, preferred_element_type=jnp.float32) * scale
        p = jax.nn.softmax(s, axis=-1).astype(v.dtype)
        return jnp.einsum('bkgqs,bskd->bqkgd', p, v)

    o = lax.map(one_block, qb)
    return o.transpose(1, 0, 2, 3, 4, 5).reshape(B, S, H * dv)


def window_attention(q, k, v, sink, scale):
    B, S, H, d = q.shape
    KV = k.shape[2]
    G = H // KV
    dv = v.shape[-1]
    nb = S // QBLK
    span = QBLK + 2 * WINDOW
    pad = ((0, 0), (WINDOW, WINDOW), (0, 0), (0, 0))
    kp = jnp.pad(k, pad)
    vp = jnp.pad(v, pad)
    qb = q.reshape(B, nb, QBLK, KV, G, d).transpose(1, 0, 2, 3, 4, 5)
    offs = jnp.arange(span) - WINDOW
    qa = jnp.arange(QBLK)
    rel_ok = jnp.abs(offs[None, :] - qa[:, None]) <= WINDOW
    sink_l = sink.astype(jnp.float32).reshape(KV, G)[None, :, :, None, None]

    def one_block(args):
        qblk, i = args
        start = i * QBLK
        kw = lax.dynamic_slice_in_dim(kp, start, span, axis=1)
        vw = lax.dynamic_slice_in_dim(vp, start, span, axis=1)
        kpos = start + offs
        valid = rel_ok & ((kpos >= 0) & (kpos < S))[None, :]
        s = jnp.einsum('bqkgd,bskd->bkgqs', qblk, kw, preferred_element_type=jnp.float32) * scale
        s = jnp.where(valid[None, None, None], s, NEG_INF)
        s = jnp.concatenate([s, jnp.broadcast_to(sink_l, s.shape[:-1] + (1,))], axis=-1)
        p = jax.nn.softmax(s, axis=-1)[..., :-1].astype(vw.dtype)
        return jnp.einsum('bkgqs,bskd->bqkgd', p, vw)

    o = lax.map(one_block, (qb, jnp.arange(nb)))
    return o.transpose(1, 0, 2, 3, 4, 5).reshape(B, S, H * dv)


def mix_ab(x, w_in, q_norm, k_norm, sink, w_out):
    B, S, _ = x.shape
    proj = x @ w_in
    sizes = [A_HEADS * HEAD_DIM, A_KV * HEAD_DIM, A_KV * HEAD_DIM,
             B_HEADS * HEAD_DIM, B_KV * HEAD_DIM, B_KV * HEAD_DIM]
    cuts = list(np.cumsum(sizes)[:-1])
    qa, ka, va, qb, kb, vb = jnp.split(proj, cuts, axis=-1)
    rows = S // GRID_W
    row_pos = jnp.repeat(jnp.arange(rows, dtype=jnp.float32), GRID_W)
    col_pos = jnp.tile(jnp.arange(GRID_W, dtype=jnp.float32), rows)
    seq_pos = jnp.arange(S, dtype=jnp.float32)
    ang_row = rope_angles(row_pos, HEAD_DIM // 2)
    ang_col = rope_angles(col_pos, HEAD_DIM // 2)
    ang_seq = rope_angles(seq_pos, HEAD_DIM)
    qa = apply_axial_rope(rms_norm(qa.reshape(B, S, A_HEADS, HEAD_DIM), q_norm), ang_row, ang_col)
    ka = apply_axial_rope(rms_norm(ka.reshape(B, S, A_KV, HEAD_DIM), k_norm), ang_row, ang_col)
    va = va.reshape(B, S, A_KV, HEAD_DIM)
    oa = dense_attention(qa, ka, va, HEAD_DIM ** -0.5)
    qb = apply_rope(qb.reshape(B, S, B_HEADS, HEAD_DIM), ang_seq)
    kb = apply_rope(kb.reshape(B, S, B_KV, HEAD_DIM), ang_seq)
    vb = vb.reshape(B, S, B_KV, HEAD_DIM)
    ob = window_attention(qb, kb, vb, sink, HEAD_DIM ** -0.5)
    return jnp.concatenate([oa, ob], axis=-1) @ w_out


def mix_mla(x, w_down, q_norm, kv_norm, w_uq, w_ukv, w_out):
    B, S, _ = x.shape
    proj = x @ w_down
    c_q, c_kv, k_r = jnp.split(proj, [Q_LORA, Q_LORA + KV_LORA], axis=-1)
    q = (rms_norm(c_q, q_norm) @ w_uq).reshape(B, S, MLA_HEADS, QK_NOPE + QK_ROPE)
    kv = (rms_norm(c_kv, kv_norm) @ w_ukv).reshape(B, S, MLA_HEADS, QK_NOPE + V_DIM)
    q_nope, q_rope = jnp.split(q, [QK_NOPE], axis=-1)
    k_nope, v = jnp.split(kv, [QK_NOPE], axis=-1)
    ang = rope_angles(jnp.arange(S, dtype=jnp.float32), QK_ROPE)
    q_rope = apply_rope(q_rope, ang)
    k_r = apply_rope(k_r[:, :, None, :], ang)
    q = jnp.concatenate([q_nope, q_rope], axis=-1)
    k = jnp.concatenate([k_nope, jnp.broadcast_to(k_r, (B, S, MLA_HEADS, QK_ROPE))], axis=-1)
    o = dense_attention(q, k, v, (QK_NOPE + QK_ROPE) ** -0.5)
    return o @ w_out


def ec_moe(x, w_router, w_gate, w_up, w_down):
    B, S, D = x.shape
    cap = EC_FACTOR * S // N_EXPERTS
    logits = jnp.einsum('bsd,de->bse', x, w_router, preferred_element_type=jnp.float32)
    aff = jax.nn.softmax(logits, axis=-1)
    g, idx = lax.top_k(aff.transpose(0, 2, 1), cap)
    xg = jax.vmap(lambda xb, ib: xb[ib])(x, idx)
    h = jax.nn.silu(jnp.einsum('becd,edf->becf', xg, w_gate)) * jnp.einsum('becd,edf->becf', xg, w_up)
    y = jnp.einsum('becf,efd->becd', h, w_down) * g[..., None].astype(x.dtype)
    return jax.vmap(lambda yb, ib: jnp.zeros((S, D), x.dtype).at[ib.reshape(-1)].add(yb.reshape(-1, D)))(y, idx)


def setup_inputs(seed: int = 0) -> dict:
    key = jax.random.key(seed)
    ks = iter(jax.random.split(key, 32))

    def nrm(shape, scale):
        return jax.random.normal(next(ks), shape, jnp.float32) * scale

    D = D_MODEL
    return {
        "x": nrm((BATCH, SEQ, D), 1.0),
        "ab_w_in": nrm((N_EVEN, D, AB_IN), D ** -0.5),
        "ab_q_norm": 1.0 + nrm((N_EVEN, HEAD_DIM), 0.02),
        "ab_k_norm": 1.0 + nrm((N_EVEN, HEAD_DIM), 0.02),
        "ab_sink": nrm((N_EVEN, B_HEADS), 1.0),
        "ab_w_out": nrm((N_EVEN, AB_MIX, D), BETA * AB_MIX ** -0.5),
        "mla_w_down": nrm((N_ODD, D, MLA_DOWN), D ** -0.5),
        "mla_q_norm": 1.0 + nrm((N_ODD, Q_LORA), 0.02),
        "mla_kv_norm": 1.0 + nrm((N_ODD, KV_LORA), 0.02),
        "mla_w_uq": nrm((N_ODD, Q_LORA, MLA_HEADS * (QK_NOPE + QK_ROPE)), Q_LORA ** -0.5),
        "mla_w_ukv": nrm((N_ODD, KV_LORA, MLA_HEADS * (QK_NOPE + V_DIM)), KV_LORA ** -0.5),
        "mla_w_out": nrm((N_ODD, MLA_HEADS * V_DIM, D), BETA * (MLA_HEADS * V_DIM) ** -0.5),
        "ln_mix_g": 1.0 + nrm((DEPTH, D), 0.02),
        "ln_mix_b": nrm((DEPTH, D), 0.02),
        "moe_router": nrm((DEPTH, D, N_EXPERTS), D ** -0.5),
        "moe_w_gate": nrm((DEPTH, N_EXPERTS, D, D_EXPERT), D ** -0.5),
        "moe_w_up": nrm((DEPTH, N_EXPERTS, D, D_EXPERT), D ** -0.5),
        "moe_w_down": nrm((DEPTH, N_EXPERTS, D_EXPERT, D), BETA * D_EXPERT ** -0.5),
        "ln_ffn_g": 1.0 + nrm((DEPTH, D), 0.02),
        "ln_ffn_b": nrm((DEPTH, D), 0.02),
    }


def reference(x, ab_w_in, ab_q_norm, ab_k_norm, ab_sink, ab_w_out,
              mla_w_down, mla_q_norm, mla_kv_norm, mla_w_uq, mla_w_ukv, mla_w_out,
              ln_mix_g, ln_mix_b, moe_router, moe_w_gate, moe_w_up, moe_w_down,
              ln_ffn_g, ln_ffn_b):
    for l in range(DEPTH):
        i = l // 2
        if l % 2 == 0:
            h = mix_ab(x, ab_w_in[i], ab_q_norm[i], ab_k_norm[i], ab_sink[i], ab_w_out[i])
        else:
            h = mix_mla(x, mla_w_down[i], mla_q_norm[i], mla_kv_norm[i],
                        mla_w_uq[i], mla_w_ukv[i], mla_w_out[i])
        x = layer_norm(ALPHA * x + h, ln_mix_g[l], ln_mix_b[l])
        f = ec_moe(x, moe_router[l], moe_w_gate[l], moe_w_up[l], moe_w_down[l])
        x = layer_norm(ALPHA * x + f, ln_ffn_g[l], ln_ffn_b[l])
    return x
```

```python
import contextlib
import numpy as np
import concourse.bass as bass
import concourse.mybir as mybir
from concourse.bass_utils import run_bass_kernel_spmd

F32 = mybir.dt.float32
BF16 = mybir.dt.bfloat16
I32 = mybir.dt.int32
ALU = mybir.AluOpType
AF = mybir.ActivationFunctionType
AX = mybir.AxisListType

S = 8192
D = 1024
NT = S // 128
NE = 16
CAP = 1024
DEPTH = 2
ALPHA = (2.0 * DEPTH) ** 0.25
XW = D + NE

SAME_ENGINE_SYNC = True


class Buf:
    __slots__ = ("w", "r", "rd")

    def __init__(self):
        self.w = None
        self.r = {}
        self.rd = []


class Op:
    __slots__ = ("eng", "fn", "deps", "sig", "sem", "val", "dma", "inc")

    def __init__(self, eng, fn, dma):
        self.eng = eng
        self.fn = fn
        self.dma = dma
        self.deps = []
        self.sig = dma
        self.sem = None
        self.val = 0
        self.inc = 16 if dma else 1


ENGS = ("sp", "pe", "act", "dve", "pool")


class Phase:
    KD = {"sp": 8, "pool": 6, "act": 4}

    def __init__(self, nc, name):
        self.nc = nc
        self.name = name
        self.ops = {e: [] for e in ENGS}
        self.bufs = {}
        self.dma_ops = {q: [] for q in self.KD}

    def b(self, *key):
        bb = self.bufs.get(key)
        if bb is None:
            bb = self.bufs[key] = Buf()
        return bb

    def add(self, eng, fn, reads=(), writes=(), dma=False):
        op = Op(eng, fn, dma)
        deps = []
        for b in reads:
            if b.w is not None:
                deps.append(b.w)
        for b in writes:
            if b.w is not None:
                deps.append(b.w)
            deps.extend(b.r.values())
            deps.extend(b.rd)
        if dma:
            lst = self.dma_ops[eng]
            k = self.KD[eng]
            if len(lst) >= k:
                deps.append(lst[len(lst) - k])
            lst.append(op)
        seen = set()
        for d in deps:
            if d is op or id(d) in seen:
                continue
            seen.add(id(d))
            if (not d.dma) and (not dma) and d.eng == eng:
                if eng == "pe" or not SAME_ENGINE_SYNC:
                    continue
            op.deps.append(d)
            d.sig = True
        for b in reads:
            if dma:
                b.rd.append(op)
            else:
                b.r[eng] = op
        for b in writes:
            b.w = op
            b.r = {}
            b.rd = []
        self.ops[eng].append(op)
        return op

    def dma(self, q, out, in_, reads=(), writes=(), **kw):
        return self.add(q, lambda e: e.dma_start(out=out, in_=in_, **kw), reads, writes, dma=True)

    def emit(self):
        nc = self.nc
        allsems = []

        def newsem(nm):
            h = nc.alloc_semaphore(name=nm)
            allsems.append(h)
            return h

        if True:
            csem = {}
            for e in ("pe", "act", "dve", "pool"):
                if any(o.sig and not o.dma for o in self.ops[e]):
                    csem[e] = newsem(f"{self.name}_c_{e}")
            dsem = {}
            for q, k in self.KD.items():
                if self.dma_ops[q]:
                    dsem[q] = [newsem(f"{self.name}_d_{q}{i}") for i in range(min(k, len(self.dma_ops[q])))]
            for e in ENGS:
                cnt = 0
                for o in self.ops[e]:
                    if o.dma:
                        continue
                    if o.sig:
                        cnt += 1
                        o.sem = csem[e]
                        o.val = cnt
            final = {q: {} for q in self.KD}
            for q, lst in self.dma_ops.items():
                k = self.KD[q]
                for n, o in enumerate(lst):
                    o.sem = dsem[q][n % k]
                    o.val = 16 * (n // k + 1)
                    final[q][n % k] = (o.sem, o.val)
            ops = self.ops

            def run(e_name, eng):
                waited = {}
                for o in ops[e_name]:
                    need = {}
                    for d in o.deps:
                        key = id(d.sem)
                        if waited.get(key, 0) >= d.val:
                            continue
                        if key not in need or need[key][1] < d.val:
                            need[key] = (d.sem, d.val)
                    for key, (sem, val) in need.items():
                        eng.wait_ge(sem, val)
                        waited[key] = val
                    ins = o.fn(eng)
                    if o.sig:
                        ins.then_inc(o.sem, o.inc)
                if e_name in final:
                    for (sem, val) in final[e_name].values():
                        if waited.get(id(sem), 0) < val:
                            eng.wait_ge(sem, val)

            with nc.Block() as block:
                if ops["sp"]:
                    @block.sync
                    def _(e):
                        run("sp", e)
                if ops["pe"]:
                    @block.tensor
                    def _(e):
                        run("pe", e)
                if ops["act"]:
                    @block.scalar
                    def _(e):
                        run("act", e)
                if ops["dve"]:
                    @block.vector
                    def _(e):
                        run("dve", e)
                if ops["pool"]:
                    @block.gpsimd
                    def _(e):
                        run("pool", e)
            nc.clear_and_free_semaphores(allsems)
            nc.all_engine_barrier()


def _rope_tab(pos, dim):
    freqs = (10000.0 ** (-(np.arange(0, dim, 2, dtype=np.float32) / np.float32(dim)))).astype(np.float32)
    ang = pos.astype(np.float32)[:, None] * freqs[None, :]
    return np.cos(ang).astype(np.float32), np.sin(ang).astype(np.float32)


def make_consts():
    t = np.arange(S)
    cr, sr = _rope_tab((t // 64).astype(np.float32), 32)
    cc, sc = _rope_tab((t % 64).astype(np.float32), 32)
    cs, ss = _rope_tab(t.astype(np.float32), 64)
    cm, sm = _rope_tab(t.astype(np.float32), 32)
    cosA = np.concatenate([cr, cr, cc, cc], 1)
    sinA = np.concatenate([-sr, sr, -sc, sc], 1)
    cosB = np.concatenate([cs, cs], 1)
    sinB = np.concatenate([-ss, ss], 1)
    cosC = np.concatenate([cm, cm], 1)
    sinC = np.concatenate([-sm, sm], 1)
    tab = np.concatenate([cosA, sinA, cosB, sinB, cosC, sinC], 1).astype(np.float32)
    ident = np.eye(128, dtype=np.float32)
    p = np.arange(128)
    tri = (p[:, None] < p[None, :]).astype(np.float32)
    mprev = (p[None, :] <= p[:, None]).astype(np.float32)
    mnext = (p[:, None] <= p[None, :]).astype(np.float32)
    svec = (np.arange(8)[None, :] * 128 + p[:, None]).astype(np.float32)
    keep = np.ones((128, NE, 64), np.float32)
    keep[:, :, 0] = 0.0
    vsink = np.zeros((128, 128), np.float32)
    vsink[0:2, 64:128] = 1.0
    cst = np.concatenate([ident, tri, mprev, mnext, svec, keep.reshape(128, -1), vsink], 1).astype(np.float32)
    return tab, cst


def make_wmask():
    kl = np.arange(128)[:, None]
    q = np.arange(512)[None, :]
    ms = []
    for r in range(6):
        ms.append((np.abs(q - ((r - 1) * 128 + kl)) <= 128).astype(np.float32))
    return np.concatenate(ms, 1)


C_IDENT, C_TRI, C_MPREV, C_MNEXT, C_SVEC, C_KEEP = 0, 128, 256, 384, 512, 520
C_VSINK = 520 + NE * 64
C_W = C_VSINK + 128


class Prog:
    def __init__(self, dbg=()):
        self.nc = bass.Bass("TRN2", target_bir_lowering=False)
        self.dbg = set(dbg)
        self.dram = {}

    def din(self, name, shape, dt=F32):
        t = self.nc.dram_tensor(name, list(shape), dt, kind="ExternalInput")
        self.dram[name] = t
        return t

    def dtmp(self, name, shape, dt=F32, out=False):
        kind = "ExternalOutput" if (out or name in self.dbg) else "Internal"
        t = self.nc.dram_tensor(name, list(shape), dt, kind=kind)
        self.dram[name] = t
        return t


def sb(st, nc, name, shape, dt):
    return st.enter_context(nc.sbuf_tensor(name, list(shape), dt))


def ps(st, nc, name, shape, dt):
    return st.enter_context(nc.psum_tensor(name, list(shape), dt))


def load_consts(ph, nc, st, P):
    cst = sb(st, nc, ph.name + "_cst", [128, C_W], F32)
    ph.dma("sp", cst[:, :], P.dram["cst"][:, :], writes=[ph.b("cst")])
    idb = sb(st, nc, ph.name + "_idb", [128, 128], BF16)
    ph.add("dve", lambda e: e.tensor_copy(idb[:, :], cst[:, C_IDENT:C_IDENT + 128]),
           reads=[ph.b("cst")], writes=[ph.b("idb")])
    return cst, idb


def emit_xT(ph, xs, xs_buf, xT, xT_buf, pT, pT_bufs, cst, ncol=8, evac=("act", "dve")):
    ident = cst[:, C_IDENT:C_IDENT + 128]
    ng = (ncol + 3) // 4
    for g in range(ng):
        pt = pT[g % len(pT)]
        pb = pT_bufs[g % len(pT)]
        n = min(4, ncol - g * 4)
        for c in range(n):
            cc = g * 4 + c
            ph.add("pe", (lambda e, pt=pt, c=c, cc=cc: e.transpose(pt[:, c * 128:(c + 1) * 128],
                                                                   xs[:, cc * 128:(cc + 1) * 128], ident)),
                   reads=[xs_buf, ph.b("cst")], writes=[pb])
        eng = evac[g % len(evac)]
        if eng == "act":
            ph.add("act", (lambda e, pt=pt, g=g, n=n: e.copy(
                xT[:, g * 4:g * 4 + n, :], pt[:, 0:n * 128].rearrange("p (c t) -> p c t", c=n))),
                reads=[pb], writes=[xT_buf])
        else:
            ph.add("dve", (lambda e, pt=pt, g=g, n=n: e.tensor_copy(
                xT[:, g * 4:g * 4 + n, :], pt[:, 0:n * 128].rearrange("p (c t) -> p c t", c=n))),
                reads=[pb], writes=[xT_buf])


def load_weight_bf16(ph, nc, st, name, w_ap, kc, n, stage, stage_bufs, cast_engs=("dve", "pool")):
    wb = sb(st, nc, name, [128, kc, n], BF16)
    wv = w_ap.rearrange("(c p) n -> p c n", p=128)
    cnt = 0
    for c in range(kc):
        for n0 in range(0, n, stage[0].shape[1]):
            n1 = min(n, n0 + stage[0].shape[1])
            s = cnt % len(stage)
            ph.dma("sp", stage[s][:, 0:n1 - n0], wv[:, c, n0:n1], writes=[stage_bufs[s]])
            eng = cast_engs[cnt % len(cast_engs)]
            ph.add(eng, (lambda e, s=s, c=c, n0=n0, n1=n1: e.tensor_copy(wb[:, c, n0:n1], stage[s][:, 0:n1 - n0])),
                   reads=[stage_bufs[s]], writes=[ph.b(name)])
            cnt += 1
    return wb


def emit_rope(ph, eng, out, src, cos, sin, nh, hd, tmp1, tmp2, rbufs, wbufs, blocks):
    hb = hd // blocks // 2
    v = lambda a: a.rearrange("p (h b t f) -> p h b t f", h=nh, b=blocks, t=2, f=hb)
    tb = lambda a: a.rearrange("p (b t f) -> p b t f", b=blocks, t=2, f=hb)
    cosb = cos.unsqueeze(1).to_broadcast([128, nh, hd]).rearrange("p h d -> p (h d)") if False else None
    s3 = src.rearrange("p (h d) -> p h d", h=nh)
    t13 = tmp1.rearrange("p (h d) -> p h d", h=nh)
    cos3 = cos.unsqueeze(1).to_broadcast([128, nh, hd])
    ph.add(eng, lambda e: e.tensor_tensor(t13, s3, cos3, ALU.mult), reads=rbufs, writes=[wbufs[0]])
    sv, t2v = v(src), v(tmp2)
    sn = tb(sin)
    for t in range(2):
        sin_b = sn[:, :, t, :].unsqueeze(1).to_broadcast([128, nh, blocks, hb])
        ph.add(eng, (lambda e, t=t, sin_b=sin_b: e.tensor_tensor(t2v[:, :, :, t, :], sv[:, :, :, 1 - t, :], sin_b, ALU.mult)),
               reads=rbufs, writes=[wbufs[1]])
    ph.add(eng, lambda e: e.tensor_tensor(out, tmp1, tmp2, ALU.add), reads=[wbufs[0], wbufs[1]], writes=[wbufs[2]])


def rope3(ph, eng, out3, src3, cos, sin, nh, hd, blocks, t1, t2, rbufs, tbufs, wbuf):
    hb = hd // blocks // 2
    t13 = t1.rearrange("p (h d) -> p h d", h=nh)
    t23 = t2.rearrange("p (h d) -> p h d", h=nh)
    cos3 = cos.unsqueeze(1).to_broadcast([128, nh, hd])
    ph.add(eng, lambda e: e.tensor_tensor(t13, src3, cos3, ALU.mult), reads=rbufs, writes=[tbufs[0]])
    for b in range(blocks):
        for t in range(2):
            o0 = b * 2 * hb + t * hb
            s0 = b * 2 * hb + (1 - t) * hb
            sin_b = sin[:, o0:o0 + hb].unsqueeze(1).to_broadcast([128, nh, hb])
            ph.add(eng, (lambda e, o0=o0, s0=s0, sin_b=sin_b: e.tensor_tensor(
                t23[:, :, o0:o0 + hb], src3[:, :, s0:s0 + hb], sin_b, ALU.mult)),
                reads=rbufs, writes=[tbufs[1]])
    ph.add(eng, lambda e: e.tensor_tensor(out3, t13, t23, ALU.add), reads=[tbufs[0], tbufs[1]], writes=[wbuf])


def lowp(nc, f):
    def g(e):
        with nc.allow_low_precision(reason="exact small integer counts"):
            return f(e)
    return g


def bc(ap1d):
    return ap1d.partition_broadcast(128)


def phase_proj0(P, x_ap):
    nc = P.nc
    dr = P.dram
    with contextlib.ExitStack() as st:
        ph = Phase(nc, "p1")
        cst, idb = load_consts(ph, nc, st, P)
        stage = [sb(st, nc, f"p1_stg{i}", [128, 1536], F32) for i in range(2)]
        wb = load_weight_bf16(ph, nc, st, "p1_wb", dr["w_in"][:, :], 8, 1536, stage,
                              [ph.b("stg", i) for i in range(2)], cast_engs=("dve", "pool"))
        gq = sb(st, nc, "p1_gq", [128, 64], F32)
        gk = sb(st, nc, "p1_gk", [128, 64], F32)
        ph.dma("sp", gq[:, :], bc(dr["ab_q_norm"][0, :]), writes=[ph.b("gq")])
        ph.dma("sp", gk[:, :], bc(dr["ab_k_norm"][0, :]), writes=[ph.b("gk")])
        NSL = 2
        xs = [sb(st, nc, f"p1_xs{i}", [128, 1024], F32) for i in range(NSL)]
        tb = [sb(st, nc, f"p1_tb{i}", [128, 256], F32) for i in range(NSL)]
        xT = [sb(st, nc, f"p1_xT{i}", [128, 8, 128], BF16) for i in range(NSL)]
        pr = [sb(st, nc, f"p1_pr{i}", [128, 1536], F32) for i in range(NSL)]
        sq = sb(st, nc, "p1_sq", [128, 640], F32)
        ss = sb(st, nc, "p1_ss", [128, 10], F32)
        sd = sb(st, nc, "p1_sd", [128, 10], F32)
        rs = sb(st, nc, "p1_rs", [128, 10], F32)
        t0 = sb(st, nc, "p1_t0", [128, 640], F32)
        ta1 = sb(st, nc, "p1_ta1", [128, 640], F32)
        ta2 = sb(st, nc, "p1_ta2", [128, 640], F32)
        tb1 = sb(st, nc, "p1_rb1", [128, 640], F32)
        tb2 = sb(st, nc, "p1_rb2", [128, 640], F32)
        oa = [sb(st, nc, f"p1_oa{i}", [128, 640], BF16) for i in range(NSL)]
        ob = [sb(st, nc, f"p1_ob{i}", [128, 640], BF16) for i in range(NSL)]
        stA = [sb(st, nc, f"p1_stA{i}", [128, 5, 128], BF16) for i in range(NSL)]
        stB = [sb(st, nc, f"p1_stB{i}", [128, 5, 128], BF16) for i in range(NSL)]
        vst = [sb(st, nc, f"p1_vst{i}", [128, 4, 128], BF16) for i in range(NSL)]
        pT = [ps(st, nc, f"p1_pT{i}", [128, 512], F32) for i in range(2)]
        pp = [ps(st, nc, f"p1_pp{i}", [128, 512], F32) for i in range(3)]
        ptA = ps(st, nc, "p1_ptA", [128, 1024], BF16)
        ptB = ps(st, nc, "p1_ptB", [128, 1024], BF16)
        for s in range(NSL):
            ph.add("pool", (lambda e, s=s: e.memset(vst[s][:, :, :], 1.0)), writes=[ph.b("vst", s)])
        eps = sb(st, nc, "p1_eps", [128, 1], F32)
        ph.add("pool", lambda e: e.memset(eps[:, :], 1e-6), writes=[ph.b("eps")])
        qTA, qTB, kTA, kTB, vall = dr["qT_A"], dr["qT_B"], dr["kT_A"], dr["kT_B"], dr["v_all"]
        def stageA(i):
            s = i % NSL
            tok = slice(i * 128, (i + 1) * 128)
            B = lambda n: ph.b(n, s)
            ph.dma("sp", xs[s][:, :], x_ap[tok, :], writes=[B("xs")])
            ph.dma("sp", tb[s][:, :], dr["tab"][tok, 0:256], writes=[B("tb")])
            emit_xT(ph, xs[s], B("xs"), xT[s], B("xT"), pT, [ph.b("pT", 0), ph.b("pT", 1)], cst)
            for n in range(3):
                for c in range(8):
                    ph.add("pe", (lambda e, n=n, c=c, s=s: e.matmul(pp[n][:, :], xT[s][:, c, :], wb[:, c, n * 512:(n + 1) * 512],
                                                                     start=(c == 0), stop=(c == 7))),
                           reads=[B("xT"), ph.b("p1_wb")], writes=[ph.b("pp", n)])
                ph.add("act", (lambda e, n=n, s=s: e.copy(pr[s][:, n * 512:(n + 1) * 512], pp[n][:, :])),
                       reads=[ph.b("pp", n)], writes=[B("pr")])

        def stageB(i):
            s = i % NSL
            tok = slice(i * 128, (i + 1) * 128)
            B = lambda n: ph.b(n, s)
            ph.add("act", (lambda e, s=s: e.activation(sq[:, :], pr[s][:, 0:640], AF.Square)), reads=[B("pr")], writes=[ph.b("sq")])
            ph.add("dve", lambda e: e.tensor_reduce(ss[:, :], sq[:, :].rearrange("p (h d) -> p h d", h=10), AX.X, ALU.add),
                   reads=[ph.b("sq")], writes=[ph.b("ss")])
            ph.add("act", lambda e: e.activation(sd[:, :], ss[:, :], AF.Sqrt, bias=eps[:, :], scale=1.0 / 64),
                   reads=[ph.b("ss"), ph.b("eps")], writes=[ph.b("sd")])
            ph.add("dve", lambda e: e.reciprocal(rs[:, :], sd[:, :]), reads=[ph.b("sd")], writes=[ph.b("rs")])
            ph.add("dve", (lambda e, s=s: e.tensor_tensor(t0[:, :].rearrange("p (h d) -> p h d", h=10),
                                                          pr[s][:, 0:640].rearrange("p (h d) -> p h d", h=10),
                                                          rs[:, :].unsqueeze(2).to_broadcast([128, 10, 64]), ALU.mult)),
                   reads=[B("pr"), ph.b("rs")], writes=[ph.b("t0")])
            ph.add("dve", lambda e: e.tensor_tensor(t0[:, 0:512].rearrange("p (h d) -> p h d", h=8),
                                                    t0[:, 0:512].rearrange("p (h d) -> p h d", h=8),
                                                    gq[:, :].unsqueeze(1).to_broadcast([128, 8, 64]), ALU.mult),
                   reads=[ph.b("gq")], writes=[ph.b("t0")])
            ph.add("dve", lambda e: e.tensor_tensor(t0[:, 512:640].rearrange("p (h d) -> p h d", h=2),
                                                    t0[:, 512:640].rearrange("p (h d) -> p h d", h=2),
                                                    gk[:, :].unsqueeze(1).to_broadcast([128, 2, 64]), ALU.mult),
                   reads=[ph.b("gk")], writes=[ph.b("t0")])
            rope3(ph, "dve", oa[s][:, :].rearrange("p (h d) -> p h d", h=10), t0[:, :].rearrange("p (h d) -> p h d", h=10),
                  tb[s][:, 0:64], tb[s][:, 64:128], 10, 64, 2, ta1[:, :], ta2[:, :],
                  [ph.b("t0"), B("tb")], [ph.b("ta1"), ph.b("ta2")], B("oa"))
            rope3(ph, "pool", ob[s][:, :].rearrange("p (h d) -> p h d", h=10), pr[s][:, 640:1280].rearrange("p (h d) -> p h d", h=10),
                  tb[s][:, 128:192], tb[s][:, 192:256], 10, 64, 1, tb1[:, :], tb2[:, :],
                  [B("pr"), B("tb")], [ph.b("tb1"), ph.b("tb2")], B("ob"))
            ph.add("act", (lambda e, s=s: e.copy(vst[s][:, :, 0:64], pr[s][:, 1280:1536].rearrange("p (h d) -> p h d", h=4))),
                   reads=[B("pr")], writes=[B("vst")])
            ph.dma("sp", vall[:, :, i, :].rearrange("k p f -> p k f"), vst[s][:, :, :], reads=[B("vst")])
            for (src, sbuf_, ptt, ptn, stt, stn, qT, kT) in ((oa, "oa", ptA, "ptA", stA, "stA", qTA, kTA),
                                                             (ob, "ob", ptB, "ptB", stB, "stB", qTB, kTB)):
                for c in range(5):
                    ph.add("pe", (lambda e, src=src, ptt=ptt, c=c, s=s: e.transpose(ptt[:, c * 128:(c + 1) * 128],
                                                                                    src[s][:, c * 128:(c + 1) * 128], idb[:, :])),
                           reads=[B(sbuf_), ph.b("idb")], writes=[ph.b(ptn)])
                ph.add("act", (lambda e, ptt=ptt, stt=stt, s=s: e.copy(stt[s][:, :, :], ptt[:, 0:640].rearrange("p (c t) -> p c t", c=5))),
                       reads=[ph.b(ptn)], writes=[B(stn)])
                ph.dma("sp", qT[:, :, tok].rearrange("j p t -> p j t"), stt[s][:, 0:4, :], reads=[B(stn)])
                for kv in range(2):
                    for dup in range(2):
                        ph.dma("sp", kT[kv, dup * 64:(dup + 1) * 64, tok], stt[s][kv * 64:(kv + 1) * 64, 4, :], reads=[B(stn)])

        for i in range(NT + 1):
            if i < NT:
                stageA(i)
            if i >= 1:
                stageB(i - 1)
        ph.emit()


def phase_dense_attn(P, name, jobs, scale, win=False):
    nc = P.nc
    dr = P.dram
    with contextlib.ExitStack() as st:
        ph = Phase(nc, name)
        if win:
            cst, idb = load_consts(ph, nc, st, P)
            wm = sb(st, nc, f"{name}_wm", [128, 3072], BF16)
            wst = sb(st, nc, f"{name}_wst", [128, 3072], F32)
            ph.dma("sp", wst[:, :], dr["wmask"][:, :], writes=[ph.b("wst")])
            ph.add("dve", lambda e: e.tensor_copy(wm[:, :], wst[:, :]), reads=[ph.b("wst")], writes=[ph.b("wm")])
            vsk = sb(st, nc, f"{name}_vsk", [128, 128], BF16)
            ph.add("dve", lambda e: e.tensor_copy(vsk[:, :], cst[:, C_VSINK:C_VSINK + 128]), reads=[ph.b("cst")], writes=[ph.b("vsk")])
            sk = sb(st, nc, f"{name}_sk", [128, 8], F32)
            es = sb(st, nc, f"{name}_es", [128, 8], F32)
            ehb = sb(st, nc, f"{name}_ehb", [128, 8], BF16)
            ehf = sb(st, nc, f"{name}_ehf", [128, 8], F32)
            elo = sb(st, nc, f"{name}_elo", [128, 8], F32)
            ssel = sb(st, nc, f"{name}_ssel", [128, 8], F32)
            onesb = sb(st, nc, f"{name}_onesb", [128, 512], BF16)
            psink = [sb(st, nc, f"{name}_psink{i}", [128, 512], BF16) for i in range(2)]
            ph.dma("sp", sk[:, :], bc(dr["ab_sink"][0, :]), writes=[ph.b("sk")])
            ph.add("act", lambda e: e.activation(es[:, :], sk[:, :], AF.Exp), reads=[ph.b("sk")], writes=[ph.b("es")])
            ph.add("dve", lowp(nc, lambda e: e.tensor_copy(ehb[:, :], es[:, :])), reads=[ph.b("es")], writes=[ph.b("ehb")])
            ph.add("dve", lambda e: e.tensor_copy(ehf[:, :], ehb[:, :]), reads=[ph.b("ehb")], writes=[ph.b("ehf")])
            ph.add("dve", lambda e: e.tensor_tensor(elo[:, :], es[:, :], ehf[:, :], ALU.subtract), reads=[ph.b("es"), ph.b("ehf")], writes=[ph.b("elo")])
            ph.add("dve", lambda e: e.tensor_scalar(ehf[:, :], ehf[:, :], cst[:, C_IDENT:C_IDENT + 1], None, ALU.mult), reads=[ph.b("cst")], writes=[ph.b("ehf")])
            ph.add("dve", lambda e: e.scalar_tensor_tensor(ssel[:, :], elo[:, :], cst[:, C_IDENT + 1:C_IDENT + 2], ehf[:, :], ALU.mult, ALU.add),
                   reads=[ph.b("elo"), ph.b("ehf"), ph.b("cst")], writes=[ph.b("ssel")])
            ph.add("dve", lambda e: e.memset(onesb[:, :], 1.0), writes=[ph.b("onesb")])
        NSL = 2
        qs = [sb(st, nc, f"{name}_q{i}", [128, S], BF16) for i in range(NSL)]
        pad = (jobs[0]["K"] == 64)
        nk = 2 if pad else 1
        ks = [[sb(st, nc, f"{name}_k{i}_{hf}", [128, S], BF16) for hf in range(nk)] for i in range(NSL)]
        vs = [sb(st, nc, f"{name}_v{i}", [128, NT, 128], BF16) for i in range(NSL)]
        NP_ = 4
        pTs = [sb(st, nc, f"{name}_pT{i}", [128, 512], BF16) for i in range(NP_)]
        rec = sb(st, nc, f"{name}_rec", [64, 512], F32)
        onb = [sb(st, nc, f"{name}_on{i}", [64, 512], BF16) for i in range(2)]
        NS_ = 3
        sT = [ps(st, nc, f"{name}_sT{i}", [128, 512], F32) for i in range(NS_)]
        oacc = [ps(st, nc, f"{name}_oa{i}", [128, 512], F32) for i in range(2)]
        LA = 2
        if not pad:
            for i in range(NSL):
                ph.add("pool", (lambda e, i=i: e.memset(qs[i][64:128, :], 0.0)), writes=[ph.b("q", i)])
                ph.add("pool", (lambda e, i=i: e.memset(ks[i][0][64:128, :], 0.0)), writes=[ph.b("k", i)])
        else:
            for i in range(NSL):
                ph.add("pool", (lambda e, i=i: e.memset(ks[i][0][64:128, :], 0.0)), writes=[ph.b("k", i)])
                ph.add("pool", (lambda e, i=i: e.memset(ks[i][1][0:64, :], 0.0)), writes=[ph.b("k", i)])
        qk_prev = kk_prev = vv_prev = None
        slot = {"q": -1, "k": -1, "v": -1}
        cur = {}
        on_cnt = 0
        for jb in jobs:
            for key, tiles, src in (("q", qs, jb["qsrc"]), ("k", ks, jb["ksrc"]), ("v", vs, jb["vsrc"])):
                if cur.get(key) != src[0]:
                    slot[key] = (slot[key] + 1) % NSL
                    sl = slot[key]
                    if key == "v":
                        ph.dma("sp", tiles[sl][:, :, :].rearrange("p c f -> p (c f)"), src[1].rearrange("p c f -> p (c f)"), writes=[ph.b(key, sl)])
                    elif key == "k" and pad:
                        ph.dma("sp", tiles[sl][0][0:64, :], src[1][0:64, :], writes=[ph.b(key, sl)])
                        ph.dma("sp", tiles[sl][1][64:128, :], src[1][64:128, :], writes=[ph.b(key, sl)])
                    elif key == "k":
                        rows = src[2]
                        ph.dma("sp", tiles[sl][0][0:rows, :], src[1], writes=[ph.b(key, sl)])
                    else:
                        rows = src[2]
                        ph.dma("sp", tiles[sl][0:rows, :], src[1], writes=[ph.b(key, sl)])
                    cur[key] = src[0]
            q_t, k_t, v_t = qs[slot["q"]], ks[slot["k"]][(jb["pb"] // 64) if pad else 0], vs[slot["v"]]
            qB, kB, vB = ph.b("q", slot["q"]), ph.b("k", slot["k"]), ph.b("v", slot["v"])
            if "dbg_q9" in dr and name == "p9" and jb is jobs[0]:
                ph.dma("sp", dr["dbg_q9"][:, :], q_t[:, :], reads=[qB])
                ph.dma("sp", dr["dbg_k9"][:, :], k_t[:, :], reads=[kB])
            pb, K = jb["pb"], jb["K"]
            if win:
                steps = [(g, c) for g in range(16) for c in range(4 * g - 1, 4 * g + 5) if 0 <= c < NT]
                hh = jb["h"]
                psl = hh % 2
                ph.add("dve", (lambda e, psl=psl, hh=hh: e.tensor_scalar(psink[psl][:, :], onesb[:, :], ssel[:, hh:hh + 1], None, ALU.mult)),
                       reads=[ph.b("onesb"), ph.b("ssel")], writes=[ph.b("psink", psl)])
            else:
                steps = [(g, c) for g in range(16) for c in range(NT)]
            nsteps = len(steps)

            def qk(n, q_t=q_t, k_t=k_t, qB=qB, kB=kB, pb=pb, K=K):
                g, c = steps[n]
                b = n % NS_
                ph.add("pe", (lambda e: e.matmul(sT[b][:, :], k_t[:, c * 128:(c + 1) * 128],
                                                 q_t[:, g * 512:(g + 1) * 512], start=True, stop=True)),
                       reads=[qB, kB], writes=[ph.b("sT", b)])

            for n in range(min(LA, nsteps)):
                qk(n)
            for n in range(nsteps):
                g, c = steps[n]
                first = (n == 0) or steps[n - 1][0] != g
                last = (n == nsteps - 1) or steps[n + 1][0] != g
                if n + LA < nsteps:
                    qk(n + LA)
                b = n % NS_
                pslot = n % NP_
                ph.add("act", (lambda e, b=b, pslot=pslot: e.activation(pTs[pslot][:, :], sT[b][:, :], AF.Exp, scale=scale)),
                       reads=[ph.b("sT", b)], writes=[ph.b("pT", pslot)])
                if win:
                    r_ = c - (4 * g - 1)
                    ph.add("dve", (lambda e, pslot=pslot, r_=r_: e.tensor_tensor(pTs[pslot][:, :], pTs[pslot][:, :], wm[:, r_ * 512:(r_ + 1) * 512], ALU.mult)),
                           reads=[ph.b("wm")], writes=[ph.b("pT", pslot)])
                ob_ = g % 2
                ph.add("pe", (lambda e, ob_=ob_, c=c, pslot=pslot, v_t=v_t, first=first, last=last: e.matmul(
                    oacc[ob_][:, :], v_t[:, c, :], pTs[pslot][:, :], start=first, stop=(last and not win))),
                    reads=[vB, ph.b("pT", pslot)], writes=[ph.b("oacc", ob_)])
                if last and win:
                    ph.add("pe", (lambda e, ob_=ob_, psl=psl: e.matmul(oacc[ob_][:, :], vsk[:, :], psink[psl][:, :], start=False, stop=True)),
                           reads=[ph.b("vsk"), ph.b("psink", psl)], writes=[ph.b("oacc", ob_)])
                if last:
                    os_ = on_cnt % 2
                    on_cnt += 1
                    ph.add("dve", (lambda e, ob_=ob_: e.reciprocal(rec[:, :], oacc[ob_][64:128, :])),
                           reads=[ph.b("oacc", ob_)], writes=[ph.b("rec")])
                    ph.add("dve", (lambda e, ob_=ob_, os_=os_: e.tensor_tensor(onb[os_][:, :], oacc[ob_][0:64, :], rec[:, :], ALU.mult)),
                           reads=[ph.b("oacc", ob_), ph.b("rec")], writes=[ph.b("on", os_)])
                    ph.dma("sp", jb["out"][:, g * 512:(g + 1) * 512], onb[os_][:, :], reads=[ph.b("on", os_)])
                    if "dbg_rec" in dr and name == "p9" and jb is jobs[0] and g == 0:
                        ph.dma("sp", dr["dbg_rec"][:, :], rec[:, :], reads=[ph.b("rec")])
                        dnum = sb(st, nc, "p9_dnum", [128, 512], F32)
                        ph.add("dve", (lambda e, ob_=ob_: e.tensor_copy(dnum[:, :], oacc[ob_][:, :])), reads=[ph.b("oacc", ob_)], writes=[ph.b("dnum")])
                        ph.dma("sp", dr["dbg_num"][:, :], dnum[:, :], reads=[ph.b("dnum")])
        ph.emit()


def phase_win_attn(P):
    nc = P.nc
    dr = P.dram
    name = "p3"
    scale = 64 ** -0.5
    with contextlib.ExitStack() as st:
        ph = Phase(nc, name)
        cst, idb = load_consts(ph, nc, st, P)
        mk = sb(st, nc, "p3_mk", [128, 2, 128], BF16)
        ph.add("dve", lambda e: e.tensor_copy(mk[:, 0, :], cst[:, C_MPREV:C_MPREV + 128]), reads=[ph.b("cst")], writes=[ph.b("mk")])
        ph.add("dve", lambda e: e.tensor_copy(mk[:, 1, :], cst[:, C_MNEXT:C_MNEXT + 128]), reads=[ph.b("cst")], writes=[ph.b("mk")])
        sk = sb(st, nc, "p3_sk", [128, 8], F32)
        es = sb(st, nc, "p3_es", [128, 8], F32)
        ph.dma("sp", sk[:, :], bc(dr["ab_sink"][0, :]), writes=[ph.b("sk")])
        ph.add("act", lambda e: e.activation(es[:, :], sk[:, :], AF.Exp), reads=[ph.b("sk")], writes=[ph.b("es")])
        qs = [sb(st, nc, f"p3_q{i}", [128, S], BF16) for i in range(2)]
        ks = [sb(st, nc, f"p3_k{i}", [128, S], BF16) for i in range(2)]
        vs = [sb(st, nc, f"p3_v{i}", [128, NT, 128], BF16) for i in range(2)]
        pTs = [sb(st, nc, f"p3_pT{i}", [128, 384], BF16) for i in range(4)]
        den = sb(st, nc, "p3_den", [64, 512], F32)
        rec = sb(st, nc, "p3_rec", [64, 512], F32)
        onb = [sb(st, nc, f"p3_on{i}", [64, 512], BF16) for i in range(2)]
        sT = [ps(st, nc, f"p3_sT{i}", [128, 512], F32) for i in range(3)]
        oacc = [ps(st, nc, f"p3_oa{i}", [128, 512], F32) for i in range(2)]
        cnt = 0
        gcnt = 0
        for h in range(8):
            j, half, kv = h // 2, h % 2, h // 4
            pb = half * 64
            if half == 0:
                ph.dma("sp", qs[j % 2][:, :], dr["qT_B"][j, :, :], writes=[ph.b("q", j % 2)])
            if h % 4 == 0:
                ph.dma("sp", ks[kv][:, :], dr["kT_B"][kv, :, :], writes=[ph.b("k", kv)])
                ph.dma("sp", vs[kv][:, :, :].rearrange("p c f -> p (c f)"), dr["v_all"][2 + kv, :, :, :].rearrange("p c f -> p (c f)"), writes=[ph.b("v", kv)])
                if "dbgv0" in dr and kv == 0:
                    ph.dma("sp", dr["dbgv0"][:, :, :].rearrange("p c f -> p (c f)"), vs[0][:, :, :].rearrange("p c f -> p (c f)"), reads=[ph.b("v", 0)])
            q_t, k_t, v_t = qs[j % 2], ks[kv], vs[kv]
            qB, kB, vB = ph.b("q", j % 2), ph.b("k", kv), ph.b("v", kv)
            for g in range(16):
                ob_ = gcnt % 2
                gcnt += 1
                for ti in range(4):
                    i = g * 4 + ti
                    b = cnt % 3
                    pslot = cnt % 4
                    cnt += 1
                    chunks = [c for c in (i - 1, i, i + 1) if 0 <= c < NT]
                    for c in chunks:
                        sl = c - i + 1
                        ph.add("pe", (lambda e, b=b, sl=sl, c=c, i=i: e.matmul(sT[b][:, sl * 128:(sl + 1) * 128],
                                                                                k_t[pb:pb + 64, c * 128:(c + 1) * 128],
                                                                                q_t[pb:pb + 64, i * 128:(i + 1) * 128], start=True, stop=True)),
                               reads=[qB, kB], writes=[ph.b("sT", b)])
                    lo_, hi_ = (chunks[0] - i + 1) * 128, (chunks[-1] - i + 2) * 128
                    ph.add("act", (lambda e, b=b, pslot=pslot, lo_=lo_, hi_=hi_: e.activation(pTs[pslot][:, lo_:hi_], sT[b][:, lo_:hi_], AF.Exp, scale=scale)),
                           reads=[ph.b("sT", b)], writes=[ph.b("pT", pslot)])
                    for c in chunks:
                        sl = c - i + 1
                        if sl != 1:
                            m = 0 if sl == 0 else 1
                            ph.add("dve", (lambda e, pslot=pslot, sl=sl, m=m: e.tensor_tensor(pTs[pslot][:, sl * 128:(sl + 1) * 128],
                                                                                                pTs[pslot][:, sl * 128:(sl + 1) * 128], mk[:, m, :], ALU.mult)),
                                   reads=[ph.b("mk")], writes=[ph.b("pT", pslot)])
                    for ci, c in enumerate(chunks):
                        sl = c - i + 1
                        ph.add("pe", (lambda e, ob_=ob_, ti=ti, c=c, sl=sl, pslot=pslot, ci=ci, nch=len(chunks): e.matmul(
                            oacc[ob_][:, ti * 128:(ti + 1) * 128], v_t[:, c, :], pTs[pslot][:, sl * 128:(sl + 1) * 128],
                            start=(ci == 0), stop=(ci == nch - 1))),
                            reads=[vB, ph.b("pT", pslot)], writes=[ph.b("oacc", ob_)])
                os_ = gcnt % 2
                ph.add("dve", (lambda e, ob_=ob_, h=h: e.tensor_scalar(den[:, :], oacc[ob_][64:128, :], es[64:128, h:h + 1], None, ALU.add)),
                       reads=[ph.b("oacc", ob_), ph.b("es")], writes=[ph.b("den")])
                ph.add("dve", lambda e: e.reciprocal(rec[:, :], den[:, :]), reads=[ph.b("den")], writes=[ph.b("rec")])
                ph.add("dve", (lambda e, ob_=ob_, os_=os_: e.tensor_tensor(onb[os_][:, :], oacc[ob_][0:64, :], rec[:, :], ALU.mult)),
                       reads=[ph.b("oacc", ob_), ph.b("rec")], writes=[ph.b("on", os_)])
                ph.dma("sp", dr["oT"][512 + h * 64:512 + (h + 1) * 64, g * 512:(g + 1) * 512], onb[os_][:, :], reads=[ph.b("on", os_)])
            if "dbgv1" in dr and h == 3:
                ph.dma("sp", dr["dbgv1"][:, :, :].rearrange("p c f -> p (c f)"), vs[0][:, :, :].rearrange("p c f -> p (c f)"), reads=[ph.b("v", 0)])
        ph.emit()


def emit_ln(ph, nc, name, r, rB, out_ap, outB, g_b, b_b, junk, small, eps_t, eng2="pool"):
    sm, nm, ssq, sd, rstd = (small[:, k:k + 1] for k in range(5))
    ph.add("dve", lambda e: e.tensor_reduce(sm, r, AX.X, ALU.add), reads=[rB], writes=[ph.b(name, "sm")])
    ph.add("dve", lambda e: e.tensor_scalar(nm, sm, -1.0 / D, None, ALU.mult), reads=[ph.b(name, "sm")], writes=[ph.b(name, "nm")])
    ph.add("act", lambda e: e.activation(junk, r, AF.Square, bias=nm, scale=1.0, accum_out=ssq),
           reads=[rB, ph.b(name, "nm")], writes=[ph.b(name, "ssq"), ph.b(name, "junk")])
    ph.add("act", lambda e: e.activation(sd, ssq, AF.Sqrt, bias=eps_t, scale=1.0 / D), reads=[ph.b(name, "ssq")], writes=[ph.b(name, "sd")])
    ph.add("dve", lambda e: e.reciprocal(rstd, sd), reads=[ph.b(name, "sd")], writes=[ph.b(name, "rstd")])
    ph.add("dve", lambda e: e.tensor_scalar(r, r, nm, rstd, ALU.add, ALU.mult), reads=[ph.b(name, "nm"), ph.b(name, "rstd")], writes=[rB])
    ph.add(eng2, lambda e: e.tensor_tensor(r, r, g_b, ALU.mult), reads=[ph.b("lng")], writes=[rB])
    ph.add(eng2, lambda e: e.tensor_tensor(out_ap, r, b_b, ALU.add), reads=[rB, ph.b("lnb")], writes=[outB])


def phase_outproj(P, name, w_out, xin_rows, lng, lnb, w_router, xext, acc, affd):
    nc = P.nc
    dr = P.dram
    with contextlib.ExitStack() as st:
        ph = Phase(nc, name)
        cst, idb = load_consts(ph, nc, st, P)
        stage = [sb(st, nc, f"{name}_stg{i}", [128, 1024], F32) for i in range(2)]
        wo = load_weight_bf16(ph, nc, st, f"{name}_wo", w_out, 8, 1024, stage, [ph.b("stg", i) for i in range(2)])
        wr = sb(st, nc, f"{name}_wr", [128, 8, NE], F32)
        for hh in range(2):
            ph.dma("sp", wr[:, hh * 4:(hh + 1) * 4, :], w_router.rearrange("(c p) n -> p c n", p=128)[:, hh * 4:(hh + 1) * 4, :], writes=[ph.b("wr")])
        g_b = sb(st, nc, f"{name}_g", [128, D], F32)
        b_b = sb(st, nc, f"{name}_b", [128, D], F32)
        ph.dma("sp", g_b[:, :], bc(lng), writes=[ph.b("lng")])
        ph.dma("sp", b_b[:, :], bc(lnb), writes=[ph.b("lnb")])
        eps_t = sb(st, nc, f"{name}_eps", [128, 1], F32)
        ph.add("pool", lambda e: e.memset(eps_t[:, :], 1e-5), writes=[ph.b("eps")])
        oTs = [sb(st, nc, f"{name}_oT{i}", [128, 8, 512], BF16) for i in range(2)]
        xs = [sb(st, nc, f"{name}_xs{i}", [128, D], F32) for i in range(2)]
        r = [sb(st, nc, f"{name}_r{i}", [128, D], F32) for i in range(2)]
        xo = [sb(st, nc, f"{name}_xo{i}", [128, XW], F32) for i in range(2)]
        ax = [sb(st, nc, f"{name}_ax{i}", [128, D], F32) for i in range(2)]
        junk = sb(st, nc, f"{name}_junk", [128, D], BF16)
        small = sb(st, nc, f"{name}_small", [128, 8], F32)
        xnT = sb(st, nc, f"{name}_xnT", [128, 8, 128], F32)
        lg = sb(st, nc, f"{name}_lg", [128, 4], F32)
        ex = sb(st, nc, f"{name}_ex", [128, NE], F32)
        py = [ps(st, nc, f"{name}_py{i}", [128, 512], F32) for i in range(2)]
        pT = [ps(st, nc, f"{name}_pT{i}", [128, 512], F32) for i in range(2)]
        pl = ps(st, nc, f"{name}_pl", [128, 512], F32)
        oTv = dr["oT"][:, :].rearrange("(c p) t -> p c t", p=128)
        def stageA(i):
            s = i % 2
            g4, t4 = divmod(i, 4)
            gs = g4 % 2
            tok = slice(i * 128, (i + 1) * 128)
            B = lambda n: ph.b(n, s)
            if t4 == 0:
                for hh in range(2):
                    ph.dma("sp", oTs[gs][:, hh * 4:(hh + 1) * 4, :], oTv[:, hh * 4:(hh + 1) * 4, g4 * 512:(g4 + 1) * 512], writes=[ph.b("oTs", gs)])
            ph.dma("sp", xs[s][:, :], xin_rows(tok), writes=[B("xs")])
            for n in range(2):
                for c in range(8):
                    ph.add("pe", (lambda e, n=n, c=c, gs=gs, t4=t4: e.matmul(py[n][:, :], oTs[gs][:, c, t4 * 128:(t4 + 1) * 128],
                                                                             wo[:, c, n * 512:(n + 1) * 512], start=(c == 0), stop=(c == 7))),
                           reads=[ph.b("oTs", gs), ph.b(f"{name}_wo")], writes=[ph.b("py", n)])
                ph.add("dve", (lambda e, n=n, s=s: e.scalar_tensor_tensor(r[s][:, n * 512:(n + 1) * 512], xs[s][:, n * 512:(n + 1) * 512], ALPHA,
                                                                          py[n][:, :], ALU.mult, ALU.add)),
                       reads=[B("xs"), ph.b("py", n)], writes=[B("r")])

        def stageB(i):
            s = i % 2
            tok = slice(i * 128, (i + 1) * 128)
            B = lambda n: ph.b(n, s)
            emit_ln(ph, nc, name, r[s][:, :], B("r"), xo[s][:, 0:D], B("xo"), g_b[:, :], b_b[:, :], junk[:, :], small[:, :], eps_t[:, :])
            ph.add("act", (lambda e, s=s: e.mul(ax[s][:, :], xo[s][:, 0:D], ALPHA)), reads=[B("xo")], writes=[B("ax")])
            ph.dma("sp", acc[tok, :], ax[s][:, :], reads=[B("ax")])
            for g in range(2):
                for c in range(4):
                    cc = g * 4 + c
                    ph.add("pe", (lambda e, g=g, c=c, cc=cc, s=s: e.transpose(pT[g][:, c * 128:(c + 1) * 128], xo[s][:, cc * 128:(cc + 1) * 128],
                                                                              cst[:, C_IDENT:C_IDENT + 128])),
                           reads=[B("xo"), ph.b("cst")], writes=[ph.b("pT", g)])
                ph.add("act", (lambda e, g=g: e.copy(xnT[:, g * 4:(g + 1) * 4, :], pT[g][:, :].rearrange("p (c t) -> p c t", c=4))),
                       reads=[ph.b("pT", g)], writes=[ph.b("xnT")])
            for c in range(8):
                ph.add("pe", (lambda e, c=c: e.matmul(pl[:, 0:NE], xnT[:, c, :], wr[:, c, :], start=(c == 0), stop=(c == 7))),
                       reads=[ph.b("xnT"), ph.b("wr")], writes=[ph.b("pl")])
            ph.add("dve", lambda e: e.tensor_reduce(lg[:, 0:1], pl[:, 0:NE], AX.X, ALU.max), reads=[ph.b("pl")], writes=[ph.b("lg0")])
            ph.add("dve", lambda e: e.tensor_scalar(lg[:, 1:2], lg[:, 0:1], -1.0, None, ALU.mult), reads=[ph.b("lg0")], writes=[ph.b("lg1")])
            ph.add("act", lambda e: e.activation(ex[:, :], pl[:, 0:NE], AF.Exp, bias=lg[:, 1:2], scale=1.0, accum_out=lg[:, 2:3]),
                   reads=[ph.b("pl"), ph.b("lg1")], writes=[ph.b("ex"), ph.b("lg2")])
            ph.add("dve", lambda e: e.reciprocal(lg[:, 3:4], lg[:, 2:3]), reads=[ph.b("lg2")], writes=[ph.b("lg3")])
            ph.add("dve", (lambda e, s=s: e.tensor_scalar(xo[s][:, D:XW], ex[:, :], lg[:, 3:4], None, ALU.mult)),
                   reads=[ph.b("ex"), ph.b("lg3")], writes=[B("xo")])
            ph.dma("sp", xext[tok, :], xo[s][:, :], reads=[B("xo")])
            ph.dma("sp", affd[tok, :], xo[s][:, D:XW], reads=[B("xo")])

        for i in range(NT + 1):
            if i < NT:
                stageA(i)
            if i >= 1:
                stageB(i - 1)
        ph.emit()


def phase_route(P, name, xext, idx):
    nc = P.nc
    dr = P.dram
    with contextlib.ExitStack() as st:
        ph = Phase(nc, name)
        cst, idb = load_consts(ph, nc, st, P)
        aff = sb(st, nc, f"{name}_aff", [128, 64, NE], F32)
        cmp_ = sb(st, nc, f"{name}_cmp", [128, 64, NE], F32)
        mT = sb(st, nc, f"{name}_mT", [128, NE, 64], F32)
        Cl = sb(st, nc, f"{name}_Cl", [128, NE, 64], F32)
        Cg = sb(st, nc, f"{name}_Cg", [128, NE, 64], F32)
        lo = sb(st, nc, f"{name}_lo", [128, NE], F32)
        mid = sb(st, nc, f"{name}_mid", [128, NE], F32)
        sel = sb(st, nc, f"{name}_sel", [128, NE], F32)
        cntp = sb(st, nc, f"{name}_cntp", [128, NE], BF16)
        ones = sb(st, nc, f"{name}_ones", [128, 128], BF16)
        trib = sb(st, nc, f"{name}_trib", [128, 128], BF16)
        idxf = sb(st, nc, f"{name}_idxf", [128, NE, 8], F32)
        junk = sb(st, nc, f"{name}_junk", [128, S], BF16)
        crow = [sb(st, nc, f"{name}_crow{i}", [128, S], F32) for i in range(2)]
        tot = ps(st, nc, f"{name}_tot", [128, 512], F32)
        off = ps(st, nc, f"{name}_off", [128, 512], F32)
        ph.dma("sp", aff[:, :, :].rearrange("p i e -> p (i e)"), dr["affd"][:, :].rearrange("(p i) w -> p (i w)", p=128), writes=[ph.b("aff")])
        ph.add("dve", lambda e: e.memset(ones[:, :], 1.0), writes=[ph.b("ones")])
        ph.add("dve", lambda e: e.tensor_copy(trib[:, :], cst[:, C_TRI:C_TRI + 128]), reads=[ph.b("cst")], writes=[ph.b("trib")])
        ph.add("dve", lambda e: e.memset(lo[:, :], 0.0), writes=[ph.b("lo")])
        for k in range(30):
            w = 0.5 ** (k + 1)
            ph.add("dve", (lambda e, w=w: e.tensor_scalar(mid[:, :], lo[:, :], w, None, ALU.add)), reads=[ph.b("lo")], writes=[ph.b("mid")])
            ph.add("dve", lambda e: e.tensor_tensor(cmp_[:, :, :], aff[:, :, :], mid[:, :].unsqueeze(1).to_broadcast([128, 64, NE]), ALU.is_ge),
                   reads=[ph.b("aff"), ph.b("mid")], writes=[ph.b("cmp")])
            ph.add("dve", lowp(nc, lambda e: e.tensor_reduce(cntp[:, :], cmp_[:, :, :].rearrange("p i e -> p e i"), AX.X, ALU.add)),
                   reads=[ph.b("cmp")], writes=[ph.b("cntp")])
            ph.add("pe", lambda e: e.matmul(tot[:, 0:NE], ones[:, :], cntp[:, :], start=True, stop=True),
                   reads=[ph.b("ones"), ph.b("cntp")], writes=[ph.b("tot")])
            ph.add("dve", lambda e: e.tensor_scalar(sel[:, :], tot[:, 0:NE], CAP - 0.5, None, ALU.is_ge), reads=[ph.b("tot")], writes=[ph.b("sel")])
            ph.add("dve", (lambda e, w=w: e.scalar_tensor_tensor(lo[:, :], sel[:, :], w, lo[:, :], ALU.mult, ALU.add)),
                   reads=[ph.b("sel")], writes=[ph.b("lo")])
        ph.add("dve", lambda e: e.tensor_tensor(mT[:, :, :].rearrange("p e i -> p i e"), aff[:, :, :],
                                                lo[:, :].unsqueeze(1).to_broadcast([128, 64, NE]), ALU.is_ge),
               reads=[ph.b("aff"), ph.b("lo")], writes=[ph.b("mT")])
        ph.add("dve", lambda e: e.tensor_tensor_scan(Cl[:, :, :].rearrange("p e i -> p (e i)"), cst[:, C_KEEP:C_KEEP + NE * 64],
                                                     mT[:, :, :].rearrange("p e i -> p (e i)"), 0.0, ALU.mult, ALU.add),
               reads=[ph.b("mT"), ph.b("cst")], writes=[ph.b("Cl")])
        ph.add("dve", lambda e: e.tensor_copy(cntp[:, :], Cl[:, :, 63]), reads=[ph.b("Cl")], writes=[ph.b("cntp")])
        ph.add("pe", lambda e: e.matmul(off[:, 0:NE], trib[:, :], cntp[:, :], start=True, stop=True),
               reads=[ph.b("trib"), ph.b("cntp")], writes=[ph.b("off")])
        ph.add("dve", lambda e: e.tensor_copy(sel[:, :], off[:, 0:NE]), reads=[ph.b("off")], writes=[ph.b("sel")])
        ph.add("dve", lambda e: e.tensor_tensor(Cg[:, :, :], Cl[:, :, :], sel[:, :].unsqueeze(2).to_broadcast([128, NE, 64]), ALU.add),
               reads=[ph.b("Cl"), ph.b("sel")], writes=[ph.b("Cg")])
        cd = dr[f"{name}_Cd"]
        for e4 in range(4):
            ph.dma("sp", cd[e4 * 4:(e4 + 1) * 4, :].rearrange("e (p i) -> p e i", p=128), Cg[:, e4 * 4:(e4 + 1) * 4, :], reads=[ph.b("Cg")], writes=[ph.b("Cd", e4)])
        junk2 = sb(st, nc, f"{name}_junk2", [128, S], BF16)
        svh = sb(st, nc, f"{name}_svh", [128, 8], F32)
        ph.add("dve", lambda e: e.tensor_scalar(svh[:, :], cst[:, C_SVEC:C_SVEC + 8], 0.5, None, ALU.add), reads=[ph.b("cst")], writes=[ph.b("svh")])
        for e_ in range(NE):
            s = e_ % 2
            ph.dma("sp", crow[s][:, :], bc(cd[e_, :]), reads=[ph.b("Cd", e_ // 4)], writes=[ph.b("crow", s)])
            for j in range(8):
                if j % 2 == 0:
                    ph.add("dve", (lambda e, s=s, e_=e_, j=j: e.tensor_scalar(junk[:, :], crow[s][:, :], cst[:, C_SVEC + j:C_SVEC + j + 1], None,
                                                                               ALU.is_le, ALU.add, accum_out=idxf[:, e_, j:j + 1])),
                           reads=[ph.b("crow", s), ph.b("cst")], writes=[ph.b("idxf", "dve"), ph.b("junk")])
                else:
                    ph.add("act", (lambda e, s=s, e_=e_, j=j: e.activation(junk2[:, :], crow[s][:, :], AF.Sign, bias=svh[:, j:j + 1], scale=-1.0,
                                                                            accum_out=idxf[:, e_, j:j + 1])),
                           reads=[ph.b("crow", s), ph.b("svh")], writes=[ph.b("idxf", "act"), ph.b("junk2")])
        i4 = idxf[:, :, :].rearrange("p e (j t) -> p e j t", t=2)[:, :, :, 1]
        ph.add("dve", lambda e: e.tensor_scalar(i4, i4, 0.5, float(S // 2), ALU.mult, ALU.add), reads=[ph.b("idxf", "act")], writes=[ph.b("idxf", "act")])
        ph.add("dve", lambda e: e.tensor_copy(idx[:, :, :], idxf[:, :, :]), reads=[ph.b("idxf", "act"), ph.b("idxf", "dve")], writes=[ph.b("idx")])
        if "dbg_idx" in dr and name == "p5":
            ph.dma("sp", dr["dbg_idx"][:, :], idx[:, :, :].rearrange("p e j -> p (e j)"), reads=[ph.b("idx")])
            ph.dma("sp", dr["dbg_lo"][:, :], lo[:, :], reads=[ph.b("lo")])
        ph.emit()


def phase_experts(P, name, l, xext, acc, idx):
    nc = P.nc
    dr = P.dram
    with contextlib.ExitStack() as st:
        ph = Phase(nc, name)
        cst, idb = load_consts(ph, nc, st, P)
        NST = 3
        stage = [sb(st, nc, f"{name}_stg{i}", [128, 2048], F32) for i in range(NST)]
        wts = {k: [sb(st, nc, f"{name}_{k}{i}", [128, 8, 1024], BF16) for i in range(2)] for k in ("wg", "wu", "wd")}
        xg = [sb(st, nc, f"{name}_xg{i}", [128, XW], F32) for i in range(2)]
        xgT2 = [sb(st, nc, f"{name}_xgT{i}", [128, 8, 1024], BF16) for i in range(2)]
        hT = sb(st, nc, f"{name}_hT", [128, 8, 1024], BF16)
        gt = sb(st, nc, f"{name}_gt", [128, 2, 8], F32)
        sg = [sb(st, nc, f"{name}_sg{i}", [128, 512], F32) for i in range(2)]
        yo = [sb(st, nc, f"{name}_yo{i}", [128, D], F32) for i in range(2)]
        pT = [ps(st, nc, f"{name}_pT{i}", [128, 512], F32) for i in range(2)]
        pg = [ps(st, nc, f"{name}_pg{i}", [128, 512], F32) for i in range(2)]
        pu = [ps(st, nc, f"{name}_pu{i}", [128, 512], F32) for i in range(2)]
        py = [ps(st, nc, f"{name}_py{i}", [128, 512], F32) for i in range(2)]
        srcs = {"wg": dr["moe_w_gate"], "wu": dr["moe_w_up"], "wd": dr["moe_w_down"]}
        stg_cnt = [0]
        cast_engs = ("pool", "dve", "pool", "act")

        def load_w(e_):
            sl = e_ % 2
            for k in ("wg", "wu", "wd"):
                wv = srcs[k][l, e_, :, :].rearrange("(c p) n -> p c n", p=128)
                for c2 in range(4):
                    s = stg_cnt[0] % NST
                    eng = cast_engs[stg_cnt[0] % len(cast_engs)]
                    stg_cnt[0] += 1
                    ph.dma("sp", stage[s][:, :].rearrange("p (c n) -> p c n", c=2), wv[:, 2 * c2:2 * c2 + 2, :], writes=[ph.b("stg", s)])
                    dst = wts[k][sl][:, 2 * c2:2 * c2 + 2, :]
                    srcv = stage[s][:, :].rearrange("p (c n) -> p c n", c=2)
                    if eng == "act":
                        ph.add("act", (lambda e, dst=dst, srcv=srcv: e.copy(dst, srcv)), reads=[ph.b("stg", s)], writes=[ph.b(k, sl)])
                    else:
                        ph.add(eng, (lambda e, dst=dst, srcv=srcv: e.tensor_copy(dst, srcv)), reads=[ph.b("stg", s)], writes=[ph.b(k, sl)])

        load_w(0)
        prev_sc = []

        def emit_gather(e_, j):
            s = (e_ * 8 + j) % 2
            ph.add("pool", (lambda e, s=s, e_=e_, j=j: e.indirect_dma_start(
                out=xg[s][:, :], out_offset=None, in_=xext[:, :],
                in_offset=bass.IndirectOffsetOnAxis(ap=idx[:, e_, j:j + 1], axis=0))),
                reads=[ph.b("idx")], writes=[ph.b("xg", s)], dma=True)

        def emit_trans(e_, j):
            s = (e_ * 8 + j) % 2
            xs_ = e_ % 2
            emit_xT(ph, xg[s], ph.b("xg", s), xgT2[xs_][:, :, j * 128:(j + 1) * 128], ph.b("xgT", xs_), pT, [ph.b("pT", 0), ph.b("pT", 1)], cst)
            ph.add("act", (lambda e, s=s, e_=e_, j=j, xs_=xs_: e.copy(gt[:, xs_, j:j + 1], xg[s][:, D + e_:D + e_ + 1])),
                   reads=[ph.b("xg", s)], writes=[ph.b("gt", xs_)])

        emit_gather(0, 0)
        for j in range(8):
            if j + 1 < 8:
                emit_gather(0, j + 1)
            emit_trans(0, j)
        for e_ in range(NE):
            sl = e_ % 2
            wg, wu, wd = wts["wg"][sl], wts["wu"][sl], wts["wd"][sl]
            gsl = e_ % 2
            xgT = xgT2[sl]
            xgB = ph.b("xgT", sl)
            if e_ + 1 < NE:
                load_w(e_ + 1)
                emit_gather(e_ + 1, 0)
            n = 0
            for fc in range(8):
                if e_ + 1 < NE and fc + 1 < 8:
                    emit_gather(e_ + 1, fc + 1)
                for half in range(2):
                    b = n % 2
                    n += 1
                    for (wt, pt_, nm, kn) in ((wg, pg, "pg", "wg"), (wu, pu, "pu", "wu")):
                        for c in range(8):
                            ph.add("pe", (lambda e, wt=wt, pt_=pt_, b=b, c=c, fc=fc, half=half, xgT=xgT: e.matmul(
                                pt_[b][:, :], wt[:, c, fc * 128:(fc + 1) * 128], xgT[:, c, half * 512:(half + 1) * 512],
                                start=(c == 0), stop=(c == 7))),
                                reads=[ph.b(kn, sl), xgB], writes=[ph.b(nm, b)])
                    ph.add("act", (lambda e, b=b: e.activation(sg[b][:, :], pg[b][:, :], AF.Silu)), reads=[ph.b("pg", b)], writes=[ph.b("sg", b)])
                    ph.add("dve", (lambda e, b=b, fc=fc, half=half: e.tensor_tensor(hT[:, fc, half * 512:(half + 1) * 512], sg[b][:, :], pu[b][:, :], ALU.mult)),
                           reads=[ph.b("sg", b), ph.b("pu", b)], writes=[ph.b("hT")])
                if e_ + 1 < NE:
                    emit_trans(e_ + 1, fc)
            new_sc = []
            for j in range(8):
                ys = (e_ * 8 + j) % 2
                for half in range(2):
                    for fc in range(8):
                        ph.add("pe", (lambda e, half=half, fc=fc, j=j, wd=wd: e.matmul(py[half][:, :], hT[:, fc, j * 128:(j + 1) * 128],
                                                                                wd[:, fc, half * 512:(half + 1) * 512], start=(fc == 0), stop=(fc == 7))),
                               reads=[ph.b("hT"), ph.b("wd", sl)], writes=[ph.b("py", half)])
                    eng = "act" if half == 0 else "dve"
                    if eng == "act":
                        ph.add("act", (lambda e, ys=ys, half=half, gsl=gsl, j=j: e.mul(yo[ys][:, half * 512:(half + 1) * 512], py[half][:, :], gt[:, gsl, j:j + 1])),
                               reads=[ph.b("py", half), ph.b("gt", gsl)], writes=[ph.b("yo", ys)])
                    else:
                        ph.add("dve", (lambda e, ys=ys, half=half, gsl=gsl, j=j: e.tensor_scalar(yo[ys][:, half * 512:(half + 1) * 512], py[half][:, :],
                                                                                                  gt[:, gsl, j:j + 1], None, ALU.mult)),
                               reads=[ph.b("py", half), ph.b("gt", gsl)], writes=[ph.b("yo", ys)])
                if "dbg_xg" in dr and name == "p6" and e_ == 0 and j == 0:
                    ph.dma("sp", dr["dbg_yo"][:, :], yo[ys][:, :], reads=[ph.b("yo", ys)])
                    ph.dma("sp", dr["dbg_hT"][:, :], hT[:, :, :].rearrange("p c t -> p (c t)"), reads=[ph.b("hT")])
                op = ph.add("pool", (lambda e, ys=ys, e_=e_, j=j: e.indirect_dma_start(
                    out=acc[:, :], out_offset=bass.IndirectOffsetOnAxis(ap=idx[:, e_, j:j + 1], axis=0),
                    in_=yo[ys][:, :], in_offset=None, compute_op=ALU.add)),
                    reads=[ph.b("idx"), ph.b("yo", ys)], writes=[], dma=True)
                for d in prev_sc:
                    if d not in op.deps:
                        op.deps.append(d)
                new_sc.append(op)
            prev_sc = new_sc
        ph.emit()


def phase_ln(P, name, acc, lng, lnb, out_rows):
    nc = P.nc
    with contextlib.ExitStack() as st:
        ph = Phase(nc, name)
        g_b = sb(st, nc, f"{name}_g", [128, D], F32)
        b_b = sb(st, nc, f"{name}_b", [128, D], F32)
        ph.dma("sp", g_b[:, :], bc(lng), writes=[ph.b("lng")])
        ph.dma("sp", b_b[:, :], bc(lnb), writes=[ph.b("lnb")])
        eps_t = sb(st, nc, f"{name}_eps", [128, 1], F32)
        ph.add("pool", lambda e: e.memset(eps_t[:, :], 1e-5), writes=[ph.b("eps")])
        r = [sb(st, nc, f"{name}_r{i}", [128, D], F32) for i in range(3)]
        o = [sb(st, nc, f"{name}_o{i}", [128, D], F32) for i in range(3)]
        junk = sb(st, nc, f"{name}_junk", [128, D], BF16)
        small = sb(st, nc, f"{name}_small", [128, 8], F32)
        for i in range(NT + 1):
            if i < NT:
                s = i % 3
                tok = slice(i * 128, (i + 1) * 128)
                ph.dma("sp", r[s][:, :], acc[tok, :], writes=[ph.b("r", s)])
            if i >= 1:
                s = (i - 1) % 3
                tok = slice((i - 1) * 128, i * 128)
                emit_ln(ph, nc, name, r[s][:, :], ph.b("r", s), o[s][:, :], ph.b("o", s), g_b[:, :], b_b[:, :], junk[:, :], small[:, :], eps_t[:, :])
                ph.dma("sp", out_rows(tok), o[s][:, :], reads=[ph.b("o", s)])
        ph.emit()


def phase_proj_mla(P, xin_rows):
    nc = P.nc
    dr = P.dram
    name = "p8"
    with contextlib.ExitStack() as st:
        ph = Phase(nc, name)
        cst, idb = load_consts(ph, nc, st, P)
        stage = [sb(st, nc, f"p8_stg{i}", [128, 2048], F32) for i in range(2)]
        sB = [ph.b("stg", i) for i in range(2)]
        wdn = load_weight_bf16(ph, nc, st, "p8_wdn", dr["mla_w_down"][0, :, :], 8, 416, stage, sB)
        wuq = load_weight_bf16(ph, nc, st, "p8_wuq", dr["mla_w_uq"][0, :, :], 2, 1536, stage, sB)
        wukv = load_weight_bf16(ph, nc, st, "p8_wukv", dr["mla_w_ukv"][0, :, :], 1, 2048, stage, sB)
        gq = sb(st, nc, "p8_gq", [128, 256], F32)
        gk = sb(st, nc, "p8_gk", [128, 128], F32)
        ph.dma("sp", gq[:, :], bc(dr["mla_q_norm"][0, :]), writes=[ph.b("gq")])
        ph.dma("sp", gk[:, :], bc(dr["mla_kv_norm"][0, :]), writes=[ph.b("gk")])
        eps = sb(st, nc, "p8_eps", [128, 1], F32)
        ph.add("pool", lambda e: e.memset(eps[:, :], 1e-6), writes=[ph.b("eps")])
        xs = [sb(st, nc, f"p8_xs{i}", [128, D], F32) for i in range(2)]
        tb = [sb(st, nc, f"p8_tb{i}", [128, 64], F32) for i in range(2)]
        xT = [sb(st, nc, f"p8_xT{i}", [128, 8, 128], BF16) for i in range(2)]
        pr = sb(st, nc, "p8_pr", [128, 416], F32)
        junk = sb(st, nc, "p8_junk", [128, 256], BF16)
        sm = sb(st, nc, "p8_sm", [128, 8], F32)
        cn = sb(st, nc, "p8_cn", [128, 384], BF16)
        cnT = sb(st, nc, "p8_cnT", [128, 3, 128], BF16)
        qsb = sb(st, nc, "p8_qs", [128, 1536], F32)
        kvs = sb(st, nc, "p8_kvs", [128, 2048], F32)
        qb = sb(st, nc, "p8_qb", [128, 16, 96], BF16)
        kb = sb(st, nc, "p8_kb", [128, 16, 96], BF16)
        kr = sb(st, nc, "p8_kr", [128, 32], BF16)
        t1 = sb(st, nc, "p8_t1", [128, 512], F32)
        t2 = sb(st, nc, "p8_t2", [128, 512], F32)
        t3 = sb(st, nc, "p8_t3", [128, 32], F32)
        t4 = sb(st, nc, "p8_t4", [128, 32], F32)
        stq = [sb(st, nc, f"p8_stq{i}", [128, 16, 128], BF16) for i in range(2)]
        stk = [sb(st, nc, f"p8_stk{i}", [128, 16, 128], BF16) for i in range(2)]
        vst = [sb(st, nc, f"p8_vst{i}", [128, 16, 128], BF16) for i in range(2)]
        pT = [ps(st, nc, f"p8_pT{i}", [128, 512], F32) for i in range(2)]
        pp = ps(st, nc, "p8_pp", [128, 512], F32)
        pc = ps(st, nc, "p8_pc", [128, 1024], BF16)
        pb_ = [ps(st, nc, f"p8_pb{i}", [128, 512], F32) for i in range(4)]
        for s in range(2):
            ph.add("pool", (lambda e, s=s: e.memset(vst[s][:, :, :], 1.0)), writes=[ph.b("vst", s)])
        for i in range(NT):
            s = i % 2
            tok = slice(i * 128, (i + 1) * 128)
            B = lambda n: ph.b(n, s)
            ph.dma("sp", xs[s][:, :], xin_rows(tok), writes=[B("xs")])
            ph.dma("sp", tb[s][:, :], dr["tab"][tok, 256:320], writes=[B("tb")])
            emit_xT(ph, xs[s], B("xs"), xT[s], B("xT"), pT, [ph.b("pT", 0), ph.b("pT", 1)], cst)
            for c in range(8):
                ph.add("pe", (lambda e, c=c, s=s: e.matmul(pp[:, 0:416], xT[s][:, c, :], wdn[:, c, :], start=(c == 0), stop=(c == 7))),
                       reads=[B("xT"), ph.b("p8_wdn")], writes=[ph.b("pp")])
            ph.add("act", lambda e: e.copy(pr[:, :], pp[:, 0:416]), reads=[ph.b("pp")], writes=[ph.b("pr")])
            ph.add("act", lambda e: e.activation(junk[:, 0:256], pr[:, 0:256], AF.Square, accum_out=sm[:, 0:1]), reads=[ph.b("pr")], writes=[ph.b("sm0"), ph.b("junk")])
            ph.add("act", lambda e: e.activation(junk[:, 0:128], pr[:, 256:384], AF.Square, accum_out=sm[:, 1:2]), reads=[ph.b("pr")], writes=[ph.b("sm1"), ph.b("junk")])
            ph.add("act", lambda e: e.activation(sm[:, 2:3], sm[:, 0:1], AF.Sqrt, bias=eps[:, :], scale=1.0 / 256), reads=[ph.b("sm0"), ph.b("eps")], writes=[ph.b("sm2")])
            ph.add("act", lambda e: e.activation(sm[:, 3:4], sm[:, 1:2], AF.Sqrt, bias=eps[:, :], scale=1.0 / 128), reads=[ph.b("sm1"), ph.b("eps")], writes=[ph.b("sm3")])
            ph.add("dve", lambda e: e.reciprocal(sm[:, 4:6], sm[:, 2:4]), reads=[ph.b("sm2"), ph.b("sm3")], writes=[ph.b("sm4")])
            ph.add("dve", lambda e: e.scalar_tensor_tensor(cn[:, 0:256], pr[:, 0:256], sm[:, 4:5], gq[:, :], ALU.mult, ALU.mult),
                   reads=[ph.b("pr"), ph.b("sm4"), ph.b("gq")], writes=[ph.b("cn")])
            ph.add("dve", lambda e: e.scalar_tensor_tensor(cn[:, 256:384], pr[:, 256:384], sm[:, 5:6], gk[:, :], ALU.mult, ALU.mult),
                   reads=[ph.b("pr"), ph.b("sm4"), ph.b("gk")], writes=[ph.b("cn")])
            for c in range(3):
                ph.add("pe", (lambda e, c=c: e.transpose(pc[:, c * 128:(c + 1) * 128], cn[:, c * 128:(c + 1) * 128], idb[:, :])),
                       reads=[ph.b("cn"), ph.b("idb")], writes=[ph.b("pc")])
            ph.add("act", lambda e: e.copy(cnT[:, :, :], pc[:, 0:384].rearrange("p (c t) -> p c t", c=3)), reads=[ph.b("pc")], writes=[ph.b("cnT")])
            for n in range(3):
                for c in range(2):
                    ph.add("pe", (lambda e, n=n, c=c: e.matmul(pb_[n][:, :], cnT[:, c, :], wuq[:, c, n * 512:(n + 1) * 512], start=(c == 0), stop=(c == 1))),
                           reads=[ph.b("cnT"), ph.b("p8_wuq")], writes=[ph.b("pb", n)])
                ph.add("act", (lambda e, n=n: e.copy(qsb[:, n * 512:(n + 1) * 512], pb_[n][:, :])), reads=[ph.b("pb", n)], writes=[ph.b("qs")])
            q3 = qsb[:, :].rearrange("p (h d) -> p h d", h=16)
            ph.add("pool", lambda e: e.tensor_copy(qb[:, :, 0:64], q3[:, :, 0:64]), reads=[ph.b("qs")], writes=[ph.b("qb")])
            rope3(ph, "dve", qb[:, :, 64:96], q3[:, :, 64:96], tb[s][:, 0:32], tb[s][:, 32:64], 16, 32, 1, t1[:, :], t2[:, :],
                  [ph.b("qs"), B("tb")], [ph.b("t1"), ph.b("t2")], ph.b("qb"))
            for n in range(4):
                bn = (3 + n) % 4
                ph.add("pe", (lambda e, n=n, bn=bn: e.matmul(pb_[bn][:, :], cnT[:, 2, :], wukv[:, 0, n * 512:(n + 1) * 512], start=True, stop=True)),
                       reads=[ph.b("cnT"), ph.b("p8_wukv")], writes=[ph.b("pb", bn)])
                ph.add("act", (lambda e, n=n, bn=bn: e.copy(kvs[:, n * 512:(n + 1) * 512], pb_[bn][:, :])), reads=[ph.b("pb", bn)], writes=[ph.b("kvs")])
            kv3 = kvs[:, :].rearrange("p (h d) -> p h d", h=16)
            ph.add("pool", lambda e: e.tensor_copy(kb[:, :, 0:64], kv3[:, :, 0:64]), reads=[ph.b("kvs")], writes=[ph.b("kb")])
            ph.add("pool", (lambda e, s=s: e.tensor_copy(vst[s][:, :, 0:64], kv3[:, :, 64:128])), reads=[ph.b("kvs")], writes=[B("vst")])
            for h4 in range(4):
                ph.dma("sp", dr["v_C"][h4 * 4:(h4 + 1) * 4, :, i, :].rearrange("k p f -> p k f"), vst[s][:, h4 * 4:(h4 + 1) * 4, :], reads=[B("vst")])
            rope3(ph, "dve", kr[:, :].unsqueeze(1), pr[:, 384:416].unsqueeze(1), tb[s][:, 0:32], tb[s][:, 32:64], 1, 32, 1, t3[:, :], t4[:, :],
                  [ph.b("pr"), B("tb")], [ph.b("t3"), ph.b("t4")], ph.b("kr"))
            ph.add("dve", lambda e: e.tensor_copy(kb[:, :, 64:96], kr[:, :].unsqueeze(1).to_broadcast([128, 16, 32])),
                   reads=[ph.b("kr")], writes=[ph.b("kb")])
            for (src, sn, stt, stn, dst) in ((qb, "qb", stq, "stq", dr["qT_C"]), (kb, "kb", stk, "stk", dr["kT_C"])):
                for hh in range(2):
                    bnk = [pb_[0], pb_[1]] if sn == "qb" else [pb_[2], pb_[3]]
                    bnn = [0, 1] if sn == "qb" else [2, 3]
                    ptile = bnk[hh]
                    pv = ptile[:, :].bitcast(BF16)
                    for h8 in range(8):
                        h = hh * 8 + h8
                        ph.add("pe", (lambda e, pv=pv, h8=h8, h=h, src=src: e.transpose(pv[0:96, h8 * 128:(h8 + 1) * 128], src[:, h, :], idb[:, :])),
                               reads=[ph.b(sn), ph.b("idb")], writes=[ph.b("pb", bnn[hh])])
                    ph.add("act", (lambda e, pv=pv, hh=hh, stt=stt, s=s: e.copy(stt[s][0:96, hh * 8:(hh + 1) * 8, :],
                                                                                 pv[0:96, :].rearrange("p (c t) -> p c t", c=8))),
                           reads=[ph.b("pb", bnn[hh])], writes=[B(stn)])
                for h4 in range(4):
                    ph.dma("sp", dst[h4 * 4:(h4 + 1) * 4, :, tok].rearrange("h p t -> p h t"), stt[s][0:96, h4 * 4:(h4 + 1) * 4, :], reads=[B(stn)])
        ph.emit()


PHASES = ["p1", "p2", "p3", "p4", "p5", "p6", "p7", "p8", "p9", "p10", "p11", "p12", "p13"]


def build(upto="p13", dbg=()):
    P = Prog(dbg)
    nc = P.nc
    x = P.din("x", [S, D])
    P.din("w_in", [D, 1536])
    P.din("ab_q_norm", [1, 64])
    P.din("ab_k_norm", [1, 64])
    P.din("ab_sink", [1, 8])
    P.din("ab_w_out", [D, D])
    P.din("mla_w_down", [1, D, 416])
    P.din("mla_q_norm", [1, 256])
    P.din("mla_kv_norm", [1, 128])
    P.din("mla_w_uq", [1, 256, 1536])
    P.din("mla_w_ukv", [1, 128, 2048])
    P.din("mla_w_out", [D, D])
    for n in ("ln_mix_g", "ln_mix_b", "ln_ffn_g", "ln_ffn_b"):
        P.din(n, [2, D])
    P.din("moe_router", [2, D, NE])
    for n in ("moe_w_gate", "moe_w_up", "moe_w_down"):
        P.din(n, [2, NE, D, D])
    P.din("tab", [S, 320])
    P.din("cst", [128, C_W])
    P.din("wmask", [128, 3072])
    P.dtmp("qT_A", [4, 128, S], BF16)
    P.dtmp("qT_B", [4, 128, S], BF16)
    P.dtmp("kT_A", [2, 128, S], BF16)
    P.dtmp("kT_B", [2, 128, S], BF16)
    P.dtmp("v_all", [4, 128, NT, 128], BF16)
    P.dtmp("oT", [D, S], BF16)
    P.dtmp("x1ext", [S, XW])
    P.dtmp("acc", [S, D])
    P.dtmp("affd", [S, NE])
    P.dtmp("p5_Cd", [NE, S])
    P.dtmp("p11_Cd", [NE, S])
    P.dtmp("x2ext", [S, XW])
    P.dtmp("x3ext", [S, XW])
    P.dtmp("acc2", [S, D])
    P.dtmp("qT_C", [16, 96, S], BF16)
    P.dtmp("kT_C", [16, 96, S], BF16)
    P.dtmp("v_C", [16, 128, NT, 128], BF16)
    out = P.dtmp("out", [S, D], out=True)
    if "dbg_idx" in dbg:
        P.dtmp("dbg_idx", [128, 128], I32)
        P.dtmp("dbg_lo", [128, NE])
    if "dbg_xg" in dbg:
        P.dtmp("dbg_xg", [128, XW])
        P.dtmp("dbg_yo", [128, D])
        P.dtmp("dbg_hT", [128, 8192], BF16)
    if "dbg_q9" in dbg:
        P.dtmp("dbg_q9", [128, S], BF16)
        P.dtmp("dbg_k9", [128, S], BF16)
    if "dbg_rec" in dbg:
        P.dtmp("dbg_rec", [64, 512])
        P.dtmp("dbg_num", [128, 512])
    if "dbgv0" in dbg:
        P.dtmp("dbgv0", [128, NT, 128], BF16)
        P.dtmp("dbgv1", [128, NT, 128], BF16)
    dr = P.dram
    last = PHASES.index(upto)
    on = lambda p: PHASES.index(p) <= last
    with contextlib.ExitStack() as st:
        idx = sb(st, nc, "idx_all", [128, NE, 8], I32)
        if on("p1"):
            phase_proj0(P, x)
        if on("p2"):
            jobs = []
            for h in range(8):
                j, half, kv = h // 2, h % 2, h // 4
                jobs.append(dict(qsrc=(("qA", j), dr["qT_A"][j, :, :], 128), ksrc=(("kA", kv), dr["kT_A"][kv, :, :], 128),
                                 vsrc=(("vA", kv), dr["v_all"][kv, :, :, :]), pb=half * 64, K=64, out=dr["oT"][h * 64:(h + 1) * 64, :]))
            phase_dense_attn(P, "p2", jobs, 64 ** -0.5)
        if on("p3"):
            jobs = []
            for h in range(8):
                j, half, kv = h // 2, h % 2, h // 4
                jobs.append(dict(qsrc=(("qB", j), dr["qT_B"][j, :, :], 128), ksrc=(("kB", kv), dr["kT_B"][kv, :, :], 128),
                                 vsrc=(("vB", kv), dr["v_all"][2 + kv, :, :, :]), pb=half * 64, K=64, h=h,
                                 out=dr["oT"][512 + h * 64:512 + (h + 1) * 64, :]))
            phase_dense_attn(P, "p3", jobs, 64 ** -0.5, win=True)
        if on("p4"):
            phase_outproj(P, "p4", dr["ab_w_out"][:, :], lambda tok: x[tok, :], dr["ln_mix_g"][0, :], dr["ln_mix_b"][0, :],
                          dr["moe_router"][0, :, :], dr["x1ext"], dr["acc"], dr["affd"])
        if on("p5"):
            phase_route(P, "p5", dr["x1ext"], idx)
        if on("p6"):
            phase_experts(P, "p6", 0, dr["x1ext"], dr["acc"], idx)
        if on("p7"):
            phase_ln(P, "p7", dr["acc"], dr["ln_ffn_g"][0, :], dr["ln_ffn_b"][0, :], lambda tok: dr["x2ext"][tok, 0:D])
        if on("p8"):
            phase_proj_mla(P, lambda tok: dr["x2ext"][tok, 0:D])
        if on("p9"):
            jobs = []
            for h in range(16):
                jobs.append(dict(qsrc=(("qC", h), dr["qT_C"][h, :, :], 96), ksrc=(("kC", h), dr["kT_C"][h, :, :], 96),
                                 vsrc=(("vC", h), dr["v_C"][h, :, :, :]), pb=0, K=96, out=dr["oT"][h * 64:(h + 1) * 64, :]))
            phase_dense_attn(P, "p9", jobs, 96 ** -0.5)
        if on("p10"):
            phase_outproj(P, "p10", dr["mla_w_out"][:, :], lambda tok: dr["x2ext"][tok, 0:D], dr["ln_mix_g"][1, :], dr["ln_mix_b"][1, :],
                          dr["moe_router"][1, :, :], dr["x3ext"], dr["acc2"], dr["affd"])
        if on("p11"):
            phase_route(P, "p11", dr["x3ext"], idx)
        if on("p12"):
            phase_experts(P, "p12", 1, dr["x3ext"], dr["acc2"], idx)
        if on("p13"):
            phase_ln(P, "p13", dr["acc2"], dr["ln_ffn_g"][1, :], dr["ln_ffn_b"][1, :], lambda tok: out[tok, :])
    return P


_PERM = np.concatenate([np.arange(0, 512), np.arange(512, 640), np.arange(768, 1280), np.arange(1280, 1408),
                        np.arange(640, 768), np.arange(1408, 1536)])


def make_in_maps(inputs, ncores=8):
    tab, cst = make_consts()
    f = lambda a: np.ascontiguousarray(np.asarray(a, dtype=np.float32))
    shared = {
        "w_in": f(np.asarray(inputs["ab_w_in"])[0][:, _PERM]),
        "ab_q_norm": f(inputs["ab_q_norm"]), "ab_k_norm": f(inputs["ab_k_norm"]), "ab_sink": f(inputs["ab_sink"]),
        "ab_w_out": f(np.asarray(inputs["ab_w_out"])[0]),
        "mla_w_down": f(inputs["mla_w_down"]), "mla_q_norm": f(inputs["mla_q_norm"]), "mla_kv_norm": f(inputs["mla_kv_norm"]),
        "mla_w_uq": f(inputs["mla_w_uq"]), "mla_w_ukv": f(inputs["mla_w_ukv"]), "mla_w_out": f(np.asarray(inputs["mla_w_out"])[0]),
        "ln_mix_g": f(inputs["ln_mix_g"]), "ln_mix_b": f(inputs["ln_mix_b"]), "ln_ffn_g": f(inputs["ln_ffn_g"]), "ln_ffn_b": f(inputs["ln_ffn_b"]),
        "moe_router": f(inputs["moe_router"]), "moe_w_gate": f(inputs["moe_w_gate"]), "moe_w_up": f(inputs["moe_w_up"]),
        "moe_w_down": f(inputs["moe_w_down"]), "tab": tab, "cst": cst, "wmask": make_wmask(),
    }
    xs = np.asarray(inputs["x"], dtype=np.float32)
    maps = []
    for c in range(ncores):
        m = dict(shared)
        m["x"] = np.ascontiguousarray(xs[c])
        maps.append(m)
    return maps


def kernel(**inputs):
    P = build()
    maps = make_in_maps(inputs, 8)
    res = run_bass_kernel_spmd(P.nc, maps, core_ids=list(range(8)))
    return np.stack([np.asarray(r["out"], dtype=np.float32) for r in res.results], axis=0)
```

```python
import contextlib
import numpy as np
import concourse.bass as bass
import concourse.mybir as mybir
from concourse.bass_utils import run_bass_kernel_spmd

F32 = mybir.dt.float32
BF16 = mybir.dt.bfloat16
I32 = mybir.dt.int32
ALU = mybir.AluOpType
AF = mybir.ActivationFunctionType
AX = mybir.AxisListType

S = 8192
D = 1024
NT = S // 128
NE = 16
CAP = 1024
DEPTH = 2
ALPHA = (2.0 * DEPTH) ** 0.25
XW = D + NE

SAME_ENGINE_SYNC = True


class Buf:
    __slots__ = ("w", "r", "rd")

    def __init__(self):
        self.w = None
        self.r = {}
        self.rd = []


class Op:
    __slots__ = ("eng", "fn", "deps", "sig", "sem", "val", "dma", "inc")

    def __init__(self, eng, fn, dma):
        self.eng = eng
        self.fn = fn
        self.dma = dma
        self.deps = []
        self.sig = dma
        self.sem = None
        self.val = 0
        self.inc = 16 if dma else 1


ENGS = ("sp", "pe", "act", "dve", "pool")


class Phase:
    KD = {"sp": 8, "pool": 6, "act": 4}

    def __init__(self, nc, name):
        self.nc = nc
        self.name = name
        self.ops = {e: [] for e in ENGS}
        self.bufs = {}
        self.dma_ops = {q: [] for q in self.KD}

    def b(self, *key):
        bb = self.bufs.get(key)
        if bb is None:
            bb = self.bufs[key] = Buf()
        return bb

    def add(self, eng, fn, reads=(), writes=(), dma=False):
        op = Op(eng, fn, dma)
        deps = []
        for b in reads:
            if b.w is not None:
                deps.append(b.w)
        for b in writes:
            if b.w is not None:
                deps.append(b.w)
            deps.extend(b.r.values())
            deps.extend(b.rd)
        if dma:
            lst = self.dma_ops[eng]
            k = self.KD[eng]
            if len(lst) >= k:
                deps.append(lst[len(lst) - k])
            lst.append(op)
        seen = set()
        for d in deps:
            if d is op or id(d) in seen:
                continue
            seen.add(id(d))
            if (not d.dma) and (not dma) and d.eng == eng:
                if eng == "pe" or not SAME_ENGINE_SYNC:
                    continue
            op.deps.append(d)
            d.sig = True
        for b in reads:
            if dma:
                b.rd.append(op)
            else:
                b.r[eng] = op
        for b in writes:
            b.w = op
            b.r = {}
            b.rd = []
        self.ops[eng].append(op)
        return op

    def dma(self, q, out, in_, reads=(), writes=(), **kw):
        return self.add(q, lambda e: e.dma_start(out=out, in_=in_, **kw), reads, writes, dma=True)

    def emit(self):
        nc = self.nc
        allsems = []

        def newsem(nm):
            h = nc.alloc_semaphore(name=nm)
            allsems.append(h)
            return h

        if True:
            csem = {}
            for e in ("pe", "act", "dve", "pool"):
                if any(o.sig and not o.dma for o in self.ops[e]):
                    csem[e] = newsem(f"{self.name}_c_{e}")
            dsem = {}
            for q, k in self.KD.items():
                if self.dma_ops[q]:
                    dsem[q] = [newsem(f"{self.name}_d_{q}{i}") for i in range(min(k, len(self.dma_ops[q])))]
            for e in ENGS:
                cnt = 0
                for o in self.ops[e]:
                    if o.dma:
                        continue
                    if o.sig:
                        cnt += 1
                        o.sem = csem[e]
                        o.val = cnt
            final = {q: {} for q in self.KD}
            for q, lst in self.dma_ops.items():
                k = self.KD[q]
                for n, o in enumerate(lst):
                    o.sem = dsem[q][n % k]
                    o.val = 16 * (n // k + 1)
                    final[q][n % k] = (o.sem, o.val)
            ops = self.ops

            def run(e_name, eng):
                waited = {}
                for o in ops[e_name]:
                    need = {}
                    for d in o.deps:
                        key = id(d.sem)
                        if waited.get(key, 0) >= d.val:
                            continue
                        if key not in need or need[key][1] < d.val:
                            need[key] = (d.sem, d.val)
                    for key, (sem, val) in need.items():
                        eng.wait_ge(sem, val)
                        waited[key] = val
                    ins = o.fn(eng)
                    if o.sig:
                        ins.then_inc(o.sem, o.inc)
                if e_name in final:
                    for (sem, val) in final[e_name].values():
                        if waited.get(id(sem), 0) < val:
                            eng.wait_ge(sem, val)

            with nc.Block() as block:
                if ops["sp"]:
                    @block.sync
                    def _(e):
                        run("sp", e)
                if ops["pe"]:
                    @block.tensor
                    def _(e):
                        run("pe", e)
                if ops["act"]:
                    @block.scalar
                    def _(e):
                        run("act", e)
                if ops["dve"]:
                    @block.vector
                    def _(e):
                        run("dve", e)
                if ops["pool"]:
                    @block.gpsimd
                    def _(e):
                        run("pool", e)
            nc.clear_and_free_semaphores(allsems)
            nc.all_engine_barrier()


def _rope_tab(pos, dim):
    freqs = (10000.0 ** (-(np.arange(0, dim, 2, dtype=np.float32) / np.float32(dim)))).astype(np.float32)
    ang = pos.astype(np.float32)[:, None] * freqs[None, :]
    return np.cos(ang).astype(np.float32), np.sin(ang).astype(np.float32)


def make_consts():
    t = np.arange(S)
    cr, sr = _rope_tab((t // 64).astype(np.float32), 32)
    cc, sc = _rope_tab((t % 64).astype(np.float32), 32)
    cs, ss = _rope_tab(t.astype(np.float32), 64)
    cm, sm = _rope_tab(t.astype(np.float32), 32)
    cosA = np.concatenate([cr, cr, cc, cc], 1)
    sinA = np.concatenate([-sr, sr, -sc, sc], 1)
    cosB = np.concatenate([cs, cs], 1)
    sinB = np.concatenate([-ss, ss], 1)
    cosC = np.concatenate([cm, cm], 1)
    sinC = np.concatenate([-sm, sm], 1)
    tab = np.concatenate([cosA, sinA, cosB, sinB, cosC, sinC], 1).astype(np.float32)
    ident = np.eye(128, dtype=np.float32)
    p = np.arange(128)
    tri = (p[:, None] < p[None, :]).astype(np.float32)
    mprev = (p[None, :] <= p[:, None]).astype(np.float32)
    mnext = (p[:, None] <= p[None, :]).astype(np.float32)
    svec = (np.arange(8)[None, :] * 128 + p[:, None]).astype(np.float32)
    keep = np.ones((128, NE, 64), np.float32)
    keep[:, :, 0] = 0.0
    vsink = np.zeros((128, 128), np.float32)
    vsink[0:2, 64:128] = 1.0
    cst = np.concatenate([ident, tri, mprev, mnext, svec, keep.reshape(128, -1), vsink], 1).astype(np.float32)
    return tab, cst


def make_wmask():
    kl = np.arange(128)[:, None]
    q = np.arange(512)[None, :]
    ms = []
    for r in range(6):
        ms.append((np.abs(q - ((r - 1) * 128 + kl)) <= 128).astype(np.float32))
    return np.concatenate(ms, 1)


C_IDENT, C_TRI, C_MPREV, C_MNEXT, C_SVEC, C_KEEP = 0, 128, 256, 384, 512, 520
C_VSINK = 520 + NE * 64
C_W = C_VSINK + 128


class Prog:
    def __init__(self, dbg=()):
        self.nc = bass.Bass("TRN2", target_bir_lowering=False)
        self.dbg = set(dbg)
        self.dram = {}

    def din(self, name, shape, dt=F32):
        t = self.nc.dram_tensor(name, list(shape), dt, kind="ExternalInput")
        self.dram[name] = t
        return t

    def dtmp(self, name, shape, dt=F32, out=False):
        kind = "ExternalOutput" if (out or name in self.dbg) else "Internal"
        t = self.nc.dram_tensor(name, list(shape), dt, kind=kind)
        self.dram[name] = t
        return t


def sb(st, nc, name, shape, dt):
    return st.enter_context(nc.sbuf_tensor(name, list(shape), dt))


def ps(st, nc, name, shape, dt):
    return st.enter_context(nc.psum_tensor(name, list(shape), dt))


def load_consts(ph, nc, st, P):
    cst = sb(st, nc, ph.name + "_cst", [128, C_W], F32)
    ph.dma("sp", cst[:, :], P.dram["cst"][:, :], writes=[ph.b("cst")])
    idb = sb(st, nc, ph.name + "_idb", [128, 128], BF16)
    ph.add("dve", lambda e: e.tensor_copy(idb[:, :], cst[:, C_IDENT:C_IDENT + 128]),
           reads=[ph.b("cst")], writes=[ph.b("idb")])
    return cst, idb


def emit_xT(ph, xs, xs_buf, xT, xT_buf, pT, pT_bufs, cst, ncol=8, evac=("act", "dve")):
    ident = cst[:, C_IDENT:C_IDENT + 128]
    ng = (ncol + 3) // 4
    for g in range(ng):
        pt = pT[g % len(pT)]
        pb = pT_bufs[g % len(pT)]
        n = min(4, ncol - g * 4)
        for c in range(n):
            cc = g * 4 + c
            ph.add("pe", (lambda e, pt=pt, c=c, cc=cc: e.transpose(pt[:, c * 128:(c + 1) * 128],
                                                                   xs[:, cc * 128:(cc + 1) * 128], ident)),
                   reads=[xs_buf, ph.b("cst")], writes=[pb])
        eng = evac[g % len(evac)]
        if eng == "act":
            ph.add("act", (lambda e, pt=pt, g=g, n=n: e.copy(
                xT[:, g * 4:g * 4 + n, :], pt[:, 0:n * 128].rearrange("p (c t) -> p c t", c=n))),
                reads=[pb], writes=[xT_buf])
        else:
            ph.add("dve", (lambda e, pt=pt, g=g, n=n: e.tensor_copy(
                xT[:, g * 4:g * 4 + n, :], pt[:, 0:n * 128].rearrange("p (c t) -> p c t", c=n))),
                reads=[pb], writes=[xT_buf])


def load_weight_bf16(ph, nc, st, name, w_ap, kc, n, stage, stage_bufs, cast_engs=("dve", "pool")):
    wb = sb(st, nc, name, [128, kc, n], BF16)
    wv = w_ap.rearrange("(c p) n -> p c n", p=128)
    cnt = 0
    for c in range(kc):
        for n0 in range(0, n, stage[0].shape[1]):
            n1 = min(n, n0 + stage[0].shape[1])
            s = cnt % len(stage)
            ph.dma("sp", stage[s][:, 0:n1 - n0], wv[:, c, n0:n1], writes=[stage_bufs[s]])
            eng = cast_engs[cnt % len(cast_engs)]
            ph.add(eng, (lambda e, s=s, c=c, n0=n0, n1=n1: e.tensor_copy(wb[:, c, n0:n1], stage[s][:, 0:n1 - n0])),
                   reads=[stage_bufs[s]], writes=[ph.b(name)])
            cnt += 1
    return wb


def emit_rope(ph, eng, out, src, cos, sin, nh, hd, tmp1, tmp2, rbufs, wbufs, blocks):
    hb = hd // blocks // 2
    v = lambda a: a.rearrange("p (h b t f) -> p h b t f", h=nh, b=blocks, t=2, f=hb)
    tb = lambda a: a.rearrange("p (b t f) -> p b t f", b=blocks, t=2, f=hb)
    cosb = cos.unsqueeze(1).to_broadcast([128, nh, hd]).rearrange("p h d -> p (h d)") if False else None
    s3 = src.rearrange("p (h d) -> p h d", h=nh)
    t13 = tmp1.rearrange("p (h d) -> p h d", h=nh)
    cos3 = cos.unsqueeze(1).to_broadcast([128, nh, hd])
    ph.add(eng, lambda e: e.tensor_tensor(t13, s3, cos3, ALU.mult), reads=rbufs, writes=[wbufs[0]])
    sv, t2v = v(src), v(tmp2)
    sn = tb(sin)
    for t in range(2):
        sin_b = sn[:, :, t, :].unsqueeze(1).to_broadcast([128, nh, blocks, hb])
        ph.add(eng, (lambda e, t=t, sin_b=sin_b: e.tensor_tensor(t2v[:, :, :, t, :], sv[:, :, :, 1 - t, :], sin_b, ALU.mult)),
               reads=rbufs, writes=[wbufs[1]])
    ph.add(eng, lambda e: e.tensor_tensor(out, tmp1, tmp2, ALU.add), reads=[wbufs[0], wbufs[1]], writes=[wbufs[2]])


def rope3(ph, eng, out3, src3, cos, sin, nh, hd, blocks, t1, t2, rbufs, tbufs, wbuf):
    hb = hd // blocks // 2
    t13 = t1.rearrange("p (h d) -> p h d", h=nh)
    t23 = t2.rearrange("p (h d) -> p h d", h=nh)
    cos3 = cos.unsqueeze(1).to_broadcast([128, nh, hd])
    ph.add(eng, lambda e: e.tensor_tensor(t13, src3, cos3, ALU.mult), reads=rbufs, writes=[tbufs[0]])
    for b in range(blocks):
        for t in range(2):
            o0 = b * 2 * hb + t * hb
            s0 = b * 2 * hb + (1 - t) * hb
            sin_b = sin[:, o0:o0 + hb].unsqueeze(1).to_broadcast([128, nh, hb])
            ph.add(eng, (lambda e, o0=o0, s0=s0, sin_b=sin_b: e.tensor_tensor(
                t23[:, :, o0:o0 + hb], src3[:, :, s0:s0 + hb], sin_b, ALU.mult)),
                reads=rbufs, writes=[tbufs[1]])
    ph.add(eng, lambda e: e.tensor_tensor(out3, t13, t23, ALU.add), reads=[tbufs[0], tbufs[1]], writes=[wbuf])


def lowp(nc, f):
    def g(e):
        with nc.allow_low_precision(reason="exact small integer counts"):
            return f(e)
    return g


def bc(ap1d):
    return ap1d.partition_broadcast(128)


def phase_proj0(P, x_ap):
    nc = P.nc
    dr = P.dram
    with contextlib.ExitStack() as st:
        ph = Phase(nc, "p1")
        cst, idb = load_consts(ph, nc, st, P)
        stage = [sb(st, nc, f"p1_stg{i}", [128, 1536], F32) for i in range(2)]
        wb = load_weight_bf16(ph, nc, st, "p1_wb", dr["w_in"][:, :], 8, 1536, stage,
                              [ph.b("stg", i) for i in range(2)], cast_engs=("dve", "pool"))
        gq = sb(st, nc, "p1_gq", [128, 64], F32)
        gk = sb(st, nc, "p1_gk", [128, 64], F32)
        ph.dma("sp", gq[:, :], bc(dr["ab_q_norm"][0, :]), writes=[ph.b("gq")])
        ph.dma("sp", gk[:, :], bc(dr["ab_k_norm"][0, :]), writes=[ph.b("gk")])
        NSL = 2
        xs = [sb(st, nc, f"p1_xs{i}", [128, 1024], F32) for i in range(NSL)]
        tb = [sb(st, nc, f"p1_tb{i}", [128, 256], F32) for i in range(NSL)]
        xT = [sb(st, nc, f"p1_xT{i}", [128, 8, 128], BF16) for i in range(NSL)]
        pr = [sb(st, nc, f"p1_pr{i}", [128, 1536], F32) for i in range(NSL)]
        sq = sb(st, nc, "p1_sq", [128, 640], F32)
        ss = sb(st, nc, "p1_ss", [128, 10], F32)
        sd = sb(st, nc, "p1_sd", [128, 10], F32)
        rs = sb(st, nc, "p1_rs", [128, 10], F32)
        t0 = sb(st, nc, "p1_t0", [128, 640], F32)
        ta1 = sb(st, nc, "p1_ta1", [128, 640], F32)
        ta2 = sb(st, nc, "p1_ta2", [128, 640], F32)
        tb1 = sb(st, nc, "p1_rb1", [128, 640], F32)
        tb2 = sb(st, nc, "p1_rb2", [128, 640], F32)
        oa = [sb(st, nc, f"p1_oa{i}", [128, 640], BF16) for i in range(NSL)]
        ob = [sb(st, nc, f"p1_ob{i}", [128, 640], BF16) for i in range(NSL)]
        stA = [sb(st, nc, f"p1_stA{i}", [128, 5, 128], BF16) for i in range(NSL)]
        stB = [sb(st, nc, f"p1_stB{i}", [128, 5, 128], BF16) for i in range(NSL)]
        vst = [sb(st, nc, f"p1_vst{i}", [128, 4, 128], BF16) for i in range(NSL)]
        pT = [ps(st, nc, f"p1_pT{i}", [128, 512], F32) for i in range(2)]
        pp = [ps(st, nc, f"p1_pp{i}", [128, 512], F32) for i in range(3)]
        ptA = ps(st, nc, "p1_ptA", [128, 1024], BF16)
        ptB = ps(st, nc, "p1_ptB", [128, 1024], BF16)
        for s in range(NSL):
            ph.add("pool", (lambda e, s=s: e.memset(vst[s][:, :, :], 1.0)), writes=[ph.b("vst", s)])
        eps = sb(st, nc, "p1_eps", [128, 1], F32)
        ph.add("pool", lambda e: e.memset(eps[:, :], 1e-6), writes=[ph.b("eps")])
        qTA, qTB, kTA, kTB, vall = dr["qT_A"], dr["qT_B"], dr["kT_A"], dr["kT_B"], dr["v_all"]
        def stageA(i):
            s = i % NSL
            tok = slice(i * 128, (i + 1) * 128)
            B = lambda n: ph.b(n, s)
            ph.dma("sp", xs[s][:, :], x_ap[tok, :], writes=[B("xs")])
            ph.dma("sp", tb[s][:, :], dr["tab"][tok, 0:256], writes=[B("tb")])
            emit_xT(ph, xs[s], B("xs"), xT[s], B("xT"), pT, [ph.b("pT", 0), ph.b("pT", 1)], cst)
            for n in range(3):
                for c in range(8):
                    ph.add("pe", (lambda e, n=n, c=c, s=s: e.matmul(pp[n][:, :], xT[s][:, c, :], wb[:, c, n * 512:(n + 1) * 512],
                                                                     start=(c == 0), stop=(c == 7))),
                           reads=[B("xT"), ph.b("p1_wb")], writes=[ph.b("pp", n)])
                ph.add("act", (lambda e, n=n, s=s: e.copy(pr[s][:, n * 512:(n + 1) * 512], pp[n][:, :])),
                       reads=[ph.b("pp", n)], writes=[B("pr")])

        def stageB(i):
            s = i % NSL
            tok = slice(i * 128, (i + 1) * 128)
            B = lambda n: ph.b(n, s)
            ph.add("act", (lambda e, s=s: e.activation(sq[:, :], pr[s][:, 0:640], AF.Square)), reads=[B("pr")], writes=[ph.b("sq")])
            ph.add("dve", lambda e: e.tensor_reduce(ss[:, :], sq[:, :].rearrange("p (h d) -> p h d", h=10), AX.X, ALU.add),
                   reads=[ph.b("sq")], writes=[ph.b("ss")])
            ph.add("act", lambda e: e.activation(sd[:, :], ss[:, :], AF.Sqrt, bias=eps[:, :], scale=1.0 / 64),
                   reads=[ph.b("ss"), ph.b("eps")], writes=[ph.b("sd")])
            ph.add("dve", lambda e: e.reciprocal(rs[:, :], sd[:, :]), reads=[ph.b("sd")], writes=[ph.b("rs")])
            ph.add("dve", (lambda e, s=s: e.tensor_tensor(t0[:, :].rearrange("p (h d) -> p h d", h=10),
                                                          pr[s][:, 0:640].rearrange("p (h d) -> p h d", h=10),
                                                          rs[:, :].unsqueeze(2).to_broadcast([128, 10, 64]), ALU.mult)),
                   reads=[B("pr"), ph.b("rs")], writes=[ph.b("t0")])
            ph.add("dve", lambda e: e.tensor_tensor(t0[:, 0:512].rearrange("p (h d) -> p h d", h=8),
                                                    t0[:, 0:512].rearrange("p (h d) -> p h d", h=8),
                                                    gq[:, :].unsqueeze(1).to_broadcast([128, 8, 64]), ALU.mult),
                   reads=[ph.b("gq")], writes=[ph.b("t0")])
            ph.add("dve", lambda e: e.tensor_tensor(t0[:, 512:640].rearrange("p (h d) -> p h d", h=2),
                                                    t0[:, 512:640].rearrange("p (h d) -> p h d", h=2),
                                                    gk[:, :].unsqueeze(1).to_broadcast([128, 2, 64]), ALU.mult),
                   reads=[ph.b("gk")], writes=[ph.b("t0")])
            rope3(ph, "dve", oa[s][:, :].rearrange("p (h d) -> p h d", h=10), t0[:, :].rearrange("p (h d) -> p h d", h=10),
                  tb[s][:, 0:64], tb[s][:, 64:128], 10, 64, 2, ta1[:, :], ta2[:, :],
                  [ph.b("t0"), B("tb")], [ph.b("ta1"), ph.b("ta2")], B("oa"))
            rope3(ph, "pool", ob[s][:, :].rearrange("p (h d) -> p h d", h=10), pr[s][:, 640:1280].rearrange("p (h d) -> p h d", h=10),
                  tb[s][:, 128:192], tb[s][:, 192:256], 10, 64, 1, tb1[:, :], tb2[:, :],
                  [B("pr"), B("tb")], [ph.b("tb1"), ph.b("tb2")], B("ob"))
            ph.add("act", (lambda e, s=s: e.copy(vst[s][:, :, 0:64], pr[s][:, 1280:1536].rearrange("p (h d) -> p h d", h=4))),
                   reads=[B("pr")], writes=[B("vst")])
            ph.dma("sp", vall[:, :, i, :].rearrange("k p f -> p k f"), vst[s][:, :, :], reads=[B("vst")])
            for (src, sbuf_, ptt, ptn, stt, stn, qT, kT) in ((oa, "oa", ptA, "ptA", stA, "stA", qTA, kTA),
                                                             (ob, "ob", ptB, "ptB", stB, "stB", qTB, kTB)):
                for c in range(5):
                    ph.add("pe", (lambda e, src=src, ptt=ptt, c=c, s=s: e.transpose(ptt[:, c * 128:(c + 1) * 128],
                                                                                    src[s][:, c * 128:(c + 1) * 128], idb[:, :])),
                           reads=[B(sbuf_), ph.b("idb")], writes=[ph.b(ptn)])
                ph.add("act", (lambda e, ptt=ptt, stt=stt, s=s: e.copy(stt[s][:, :, :], ptt[:, 0:640].rearrange("p (c t) -> p c t", c=5))),
                       reads=[ph.b(ptn)], writes=[B(stn)])
                ph.dma("sp", qT[:, :, tok].rearrange("j p t -> p j t"), stt[s][:, 0:4, :], reads=[B(stn)])
                for kv in range(2):
                    for dup in range(2):
                        ph.dma("sp", kT[kv, dup * 64:(dup + 1) * 64, tok], stt[s][kv * 64:(kv + 1) * 64, 4, :], reads=[B(stn)])

        for i in range(NT + 1):
            if i < NT:
                stageA(i)
            if i >= 1:
                stageB(i - 1)
        ph.emit()


def phase_dense_attn(P, name, jobs, scale, win=False):
    nc = P.nc
    dr = P.dram
    with contextlib.ExitStack() as st:
        ph = Phase(nc, name)
        if win:
            cst, idb = load_consts(ph, nc, st, P)
            wm = sb(st, nc, f"{name}_wm", [128, 3072], BF16)
            wst = sb(st, nc, f"{name}_wst", [128, 3072], F32)
            ph.dma("sp", wst[:, :], dr["wmask"][:, :], writes=[ph.b("wst")])
            ph.add("dve", lambda e: e.tensor_copy(wm[:, :], wst[:, :]), reads=[ph.b("wst")], writes=[ph.b("wm")])
            vsk = sb(st, nc, f"{name}_vsk", [128, 128], BF16)
            ph.add("dve", lambda e: e.tensor_copy(vsk[:, :], cst[:, C_VSINK:C_VSINK + 128]), reads=[ph.b("cst")], writes=[ph.b("vsk")])
            sk = sb(st, nc, f"{name}_sk", [128, 8], F32)
            es = sb(st, nc, f"{name}_es", [128, 8], F32)
            ehb = sb(st, nc, f"{name}_ehb", [128, 8], BF16)
            ehf = sb(st, nc, f"{name}_ehf", [128, 8], F32)
            elo = sb(st, nc, f"{name}_elo", [128, 8], F32)
            ssel = sb(st, nc, f"{name}_ssel", [128, 8], F32)
            onesb = sb(st, nc, f"{name}_onesb", [128, 512], BF16)
            psink = [sb(st, nc, f"{name}_psink{i}", [128, 512], BF16) for i in range(2)]
            ph.dma("sp", sk[:, :], bc(dr["ab_sink"][0, :]), writes=[ph.b("sk")])
            ph.add("act", lambda e: e.activation(es[:, :], sk[:, :], AF.Exp), reads=[ph.b("sk")], writes=[ph.b("es")])
            ph.add("dve", lowp(nc, lambda e: e.tensor_copy(ehb[:, :], es[:, :])), reads=[ph.b("es")], writes=[ph.b("ehb")])
            ph.add("dve", lambda e: e.tensor_copy(ehf[:, :], ehb[:, :]), reads=[ph.b("ehb")], writes=[ph.b("ehf")])
            ph.add("dve", lambda e: e.tensor_tensor(elo[:, :], es[:, :], ehf[:, :], ALU.subtract), reads=[ph.b("es"), ph.b("ehf")], writes=[ph.b("elo")])
            ph.add("dve", lambda e: e.tensor_scalar(ehf[:, :], ehf[:, :], cst[:, C_IDENT:C_IDENT + 1], None, ALU.mult), reads=[ph.b("cst")], writes=[ph.b("ehf")])
            ph.add("dve", lambda e: e.scalar_tensor_tensor(ssel[:, :], elo[:, :], cst[:, C_IDENT + 1:C_IDENT + 2], ehf[:, :], ALU.mult, ALU.add),
                   reads=[ph.b("elo"), ph.b("ehf"), ph.b("cst")], writes=[ph.b("ssel")])
            ph.add("dve", lambda e: e.memset(onesb[:, :], 1.0), writes=[ph.b("onesb")])
        NSL = 2
        qs = [sb(st, nc, f"{name}_q{i}", [128, S], BF16) for i in range(NSL)]
        pad = (jobs[0]["K"] == 64)
        nk = 2 if pad else 1
        ks = [[sb(st, nc, f"{name}_k{i}_{hf}", [128, S], BF16) for hf in range(nk)] for i in range(NSL)]
        vs = [sb(st, nc, f"{name}_v{i}", [128, NT, 128], BF16) for i in range(NSL)]
        NP_ = 4
        pTs = [sb(st, nc, f"{name}_pT{i}", [128, 512], BF16) for i in range(NP_)]
        rec = sb(st, nc, f"{name}_rec", [64, 512], F32)
        onb = [sb(st, nc, f"{name}_on{i}", [64, 512], BF16) for i in range(2)]
        NS_ = 3
        sT = [ps(st, nc, f"{name}_sT{i}", [128, 512], F32) for i in range(NS_)]
        oacc = [ps(st, nc, f"{name}_oa{i}", [128, 512], F32) for i in range(2)]
        LA = 2
        if not pad:
            for i in range(NSL):
                ph.add("pool", (lambda e, i=i: e.memset(qs[i][64:128, :], 0.0)), writes=[ph.b("q", i)])
                ph.add("pool", (lambda e, i=i: e.memset(ks[i][0][64:128, :], 0.0)), writes=[ph.b("k", i)])
        else:
            for i in range(NSL):
                ph.add("pool", (lambda e, i=i: e.memset(ks[i][0][64:128, :], 0.0)), writes=[ph.b("k", i)])
                ph.add("pool", (lambda e, i=i: e.memset(ks[i][1][0:64, :], 0.0)), writes=[ph.b("k", i)])
        qk_prev = kk_prev = vv_prev = None
        slot = {"q": -1, "k": -1, "v": -1}
        cur = {}
        on_cnt = 0
        jslots = []

        def issue_loads(jb):
            for key, tiles, src in (("q", qs, jb["qsrc"]), ("k", ks, jb["ksrc"]), ("v", vs, jb["vsrc"])):
                if cur.get(key) != src[0]:
                    slot[key] = (slot[key] + 1) % NSL
                    sl = slot[key]
                    if key == "v":
                        ph.dma("sp", tiles[sl][:, :, :].rearrange("p c f -> p (c f)"), src[1].rearrange("p c f -> p (c f)"), writes=[ph.b(key, sl)])
                    elif key == "k" and pad:
                        ph.dma("sp", tiles[sl][0][0:64, :], src[1][0:64, :], writes=[ph.b(key, sl)])
                        ph.dma("sp", tiles[sl][1][64:128, :], src[1][64:128, :], writes=[ph.b(key, sl)])
                    elif key == "k":
                        rows = src[2]
                        ph.dma("sp", tiles[sl][0][0:rows, :], src[1], writes=[ph.b(key, sl)])
                    else:
                        rows = src[2]
                        ph.dma("sp", tiles[sl][0:rows, :], src[1], writes=[ph.b(key, sl)])
                    cur[key] = src[0]
            jslots.append(dict(slot))

        issue_loads(jobs[0])
        for ji, jb in enumerate(jobs):
            if ji + 1 < len(jobs):
                issue_loads(jobs[ji + 1])
            jsl = jslots[ji]
            q_t, k_t, v_t = qs[jsl["q"]], ks[jsl["k"]][(jb["pb"] // 64) if pad else 0], vs[jsl["v"]]
            qB, kB, vB = ph.b("q", jsl["q"]), ph.b("k", jsl["k"]), ph.b("v", jsl["v"])
            if "dbg_q9" in dr and name == "p9" and jb is jobs[0]:
                ph.dma("sp", dr["dbg_q9"][:, :], q_t[:, :], reads=[qB])
                ph.dma("sp", dr["dbg_k9"][:, :], k_t[:, :], reads=[kB])
            pb, K = jb["pb"], jb["K"]
            if win:
                steps = [(g, c) for g in range(16) for c in range(4 * g - 1, 4 * g + 5) if 0 <= c < NT]
                hh = jb["h"]
                psl = hh % 2
                ph.add("dve", (lambda e, psl=psl, hh=hh: e.tensor_scalar(psink[psl][:, :], onesb[:, :], ssel[:, hh:hh + 1], None, ALU.mult)),
                       reads=[ph.b("onesb"), ph.b("ssel")], writes=[ph.b("psink", psl)])
            else:
                steps = [(g, c) for g in range(16) for c in range(NT)]
            nsteps = len(steps)

            def qk(n, q_t=q_t, k_t=k_t, qB=qB, kB=kB, pb=pb, K=K):
                g, c = steps[n]
                b = n % NS_
                ph.add("pe", (lambda e: e.matmul(sT[b][:, :], k_t[:, c * 128:(c + 1) * 128],
                                                 q_t[:, g * 512:(g + 1) * 512], start=True, stop=True)),
                       reads=[qB, kB], writes=[ph.b("sT", b)])

            for n in range(min(LA, nsteps)):
                qk(n)
            for n in range(nsteps):
                g, c = steps[n]
                first = (n == 0) or steps[n - 1][0] != g
                last = (n == nsteps - 1) or steps[n + 1][0] != g
                if n + LA < nsteps:
                    qk(n + LA)
                b = n % NS_
                pslot = n % NP_
                ph.add("act", (lambda e, b=b, pslot=pslot: e.activation(pTs[pslot][:, :], sT[b][:, :], AF.Exp, scale=scale)),
                       reads=[ph.b("sT", b)], writes=[ph.b("pT", pslot)])
                if win:
                    r_ = c - (4 * g - 1)
                    ph.add("dve", (lambda e, pslot=pslot, r_=r_: e.tensor_tensor(pTs[pslot][:, :], pTs[pslot][:, :], wm[:, r_ * 512:(r_ + 1) * 512], ALU.mult)),
                           reads=[ph.b("wm")], writes=[ph.b("pT", pslot)])
                ob_ = g % 2
                ph.add("pe", (lambda e, ob_=ob_, c=c, pslot=pslot, v_t=v_t, first=first, last=last: e.matmul(
                    oacc[ob_][:, :], v_t[:, c, :], pTs[pslot][:, :], start=first, stop=(last and not win))),
                    reads=[vB, ph.b("pT", pslot)], writes=[ph.b("oacc", ob_)])
                if last and win:
                    ph.add("pe", (lambda e, ob_=ob_, psl=psl: e.matmul(oacc[ob_][:, :], vsk[:, :], psink[psl][:, :], start=False, stop=True)),
                           reads=[ph.b("vsk"), ph.b("psink", psl)], writes=[ph.b("oacc", ob_)])
                if last:
                    os_ = on_cnt % 2
                    on_cnt += 1
                    ph.add("dve", (lambda e, ob_=ob_: e.reciprocal(rec[:, :], oacc[ob_][64:128, :])),
                           reads=[ph.b("oacc", ob_)], writes=[ph.b("rec")])
                    ph.add("dve", (lambda e, ob_=ob_, os_=os_: e.tensor_tensor(onb[os_][:, :], oacc[ob_][0:64, :], rec[:, :], ALU.mult)),
                           reads=[ph.b("oacc", ob_), ph.b("rec")], writes=[ph.b("on", os_)])
                    ph.dma("sp", jb["out"][:, g * 512:(g + 1) * 512], onb[os_][:, :], reads=[ph.b("on", os_)])
                    if "dbg_rec" in dr and name == "p9" and jb is jobs[0] and g == 0:
                        ph.dma("sp", dr["dbg_rec"][:, :], rec[:, :], reads=[ph.b("rec")])
                        dnum = sb(st, nc, "p9_dnum", [128, 512], F32)
                        ph.add("dve", (lambda e, ob_=ob_: e.tensor_copy(dnum[:, :], oacc[ob_][:, :])), reads=[ph.b("oacc", ob_)], writes=[ph.b("dnum")])
                        ph.dma("sp", dr["dbg_num"][:, :], dnum[:, :], reads=[ph.b("dnum")])
        ph.emit()


def phase_win_attn(P):
    nc = P.nc
    dr = P.dram
    name = "p3"
    scale = 64 ** -0.5
    with contextlib.ExitStack() as st:
        ph = Phase(nc, name)
        cst, idb = load_consts(ph, nc, st, P)
        mk = sb(st, nc, "p3_mk", [128, 2, 128], BF16)
        ph.add("dve", lambda e: e.tensor_copy(mk[:, 0, :], cst[:, C_MPREV:C_MPREV + 128]), reads=[ph.b("cst")], writes=[ph.b("mk")])
        ph.add("dve", lambda e: e.tensor_copy(mk[:, 1, :], cst[:, C_MNEXT:C_MNEXT + 128]), reads=[ph.b("cst")], writes=[ph.b("mk")])
        sk = sb(st, nc, "p3_sk", [128, 8], F32)
        es = sb(st, nc, "p3_es", [128, 8], F32)
        ph.dma("sp", sk[:, :], bc(dr["ab_sink"][0, :]), writes=[ph.b("sk")])
        ph.add("act", lambda e: e.activation(es[:, :], sk[:, :], AF.Exp), reads=[ph.b("sk")], writes=[ph.b("es")])
        qs = [sb(st, nc, f"p3_q{i}", [128, S], BF16) for i in range(2)]
        ks = [sb(st, nc, f"p3_k{i}", [128, S], BF16) for i in range(2)]
        vs = [sb(st, nc, f"p3_v{i}", [128, NT, 128], BF16) for i in range(2)]
        pTs = [sb(st, nc, f"p3_pT{i}", [128, 384], BF16) for i in range(4)]
        den = sb(st, nc, "p3_den", [64, 512], F32)
        rec = sb(st, nc, "p3_rec", [64, 512], F32)
        onb = [sb(st, nc, f"p3_on{i}", [64, 512], BF16) for i in range(2)]
        sT = [ps(st, nc, f"p3_sT{i}", [128, 512], F32) for i in range(3)]
        oacc = [ps(st, nc, f"p3_oa{i}", [128, 512], F32) for i in range(2)]
        cnt = 0
        gcnt = 0
        for h in range(8):
            j, half, kv = h // 2, h % 2, h // 4
            pb = half * 64
            if half == 0:
                ph.dma("sp", qs[j % 2][:, :], dr["qT_B"][j, :, :], writes=[ph.b("q", j % 2)])
            if h % 4 == 0:
                ph.dma("sp", ks[kv][:, :], dr["kT_B"][kv, :, :], writes=[ph.b("k", kv)])
                ph.dma("sp", vs[kv][:, :, :].rearrange("p c f -> p (c f)"), dr["v_all"][2 + kv, :, :, :].rearrange("p c f -> p (c f)"), writes=[ph.b("v", kv)])
                if "dbgv0" in dr and kv == 0:
                    ph.dma("sp", dr["dbgv0"][:, :, :].rearrange("p c f -> p (c f)"), vs[0][:, :, :].rearrange("p c f -> p (c f)"), reads=[ph.b("v", 0)])
            q_t, k_t, v_t = qs[j % 2], ks[kv], vs[kv]
            qB, kB, vB = ph.b("q", j % 2), ph.b("k", kv), ph.b("v", kv)
            for g in range(16):
                ob_ = gcnt % 2
                gcnt += 1
                for ti in range(4):
                    i = g * 4 + ti
                    b = cnt % 3
                    pslot = cnt % 4
                    cnt += 1
                    chunks = [c for c in (i - 1, i, i + 1) if 0 <= c < NT]
                    for c in chunks:
                        sl = c - i + 1
                        ph.add("pe", (lambda e, b=b, sl=sl, c=c, i=i: e.matmul(sT[b][:, sl * 128:(sl + 1) * 128],
                                                                                k_t[pb:pb + 64, c * 128:(c + 1) * 128],
                                                                                q_t[pb:pb + 64, i * 128:(i + 1) * 128], start=True, stop=True)),
                               reads=[qB, kB], writes=[ph.b("sT", b)])
                    lo_, hi_ = (chunks[0] - i + 1) * 128, (chunks[-1] - i + 2) * 128
                    ph.add("act", (lambda e, b=b, pslot=pslot, lo_=lo_, hi_=hi_: e.activation(pTs[pslot][:, lo_:hi_], sT[b][:, lo_:hi_], AF.Exp, scale=scale)),
                           reads=[ph.b("sT", b)], writes=[ph.b("pT", pslot)])
                    for c in chunks:
                        sl = c - i + 1
                        if sl != 1:
                            m = 0 if sl == 0 else 1
                            ph.add("dve", (lambda e, pslot=pslot, sl=sl, m=m: e.tensor_tensor(pTs[pslot][:, sl * 128:(sl + 1) * 128],
                                                                                                pTs[pslot][:, sl * 128:(sl + 1) * 128], mk[:, m, :], ALU.mult)),
                                   reads=[ph.b("mk")], writes=[ph.b("pT", pslot)])
                    for ci, c in enumerate(chunks):
                        sl = c - i + 1
                        ph.add("pe", (lambda e, ob_=ob_, ti=ti, c=c, sl=sl, pslot=pslot, ci=ci, nch=len(chunks): e.matmul(
                            oacc[ob_][:, ti * 128:(ti + 1) * 128], v_t[:, c, :], pTs[pslot][:, sl * 128:(sl + 1) * 128],
                            start=(ci == 0), stop=(ci == nch - 1))),
                            reads=[vB, ph.b("pT", pslot)], writes=[ph.b("oacc", ob_)])
                os_ = gcnt % 2
                ph.add("dve", (lambda e, ob_=ob_, h=h: e.tensor_scalar(den[:, :], oacc[ob_][64:128, :], es[64:128, h:h + 1], None, ALU.add)),
                       reads=[ph.b("oacc", ob_), ph.b("es")], writes=[ph.b("den")])
                ph.add("dve", lambda e: e.reciprocal(rec[:, :], den[:, :]), reads=[ph.b("den")], writes=[ph.b("rec")])
                ph.add("dve", (lambda e, ob_=ob_, os_=os_: e.tensor_tensor(onb[os_][:, :], oacc[ob_][0:64, :], rec[:, :], ALU.mult)),
                       reads=[ph.b("oacc", ob_), ph.b("rec")], writes=[ph.b("on", os_)])
                ph.dma("sp", dr["oT"][512 + h * 64:512 + (h + 1) * 64, g * 512:(g + 1) * 512], onb[os_][:, :], reads=[ph.b("on", os_)])
            if "dbgv1" in dr and h == 3:
                ph.dma("sp", dr["dbgv1"][:, :, :].rearrange("p c f -> p (c f)"), vs[0][:, :, :].rearrange("p c f -> p (c f)"), reads=[ph.b("v", 0)])
        ph.emit()


def emit_ln(ph, nc, name, r, rB, out_ap, outB, g_b, b_b, junk, small, eps_t, eng2="pool"):
    sm, nm, ssq, sd, rstd = (small[:, k:k + 1] for k in range(5))
    ph.add("dve", lambda e: e.tensor_reduce(sm, r, AX.X, ALU.add), reads=[rB], writes=[ph.b(name, "sm")])
    ph.add("dve", lambda e: e.tensor_scalar(nm, sm, -1.0 / D, None, ALU.mult), reads=[ph.b(name, "sm")], writes=[ph.b(name, "nm")])
    ph.add("act", lambda e: e.activation(junk, r, AF.Square, bias=nm, scale=1.0, accum_out=ssq),
           reads=[rB, ph.b(name, "nm")], writes=[ph.b(name, "ssq"), ph.b(name, "junk")])
    ph.add("act", lambda e: e.activation(sd, ssq, AF.Sqrt, bias=eps_t, scale=1.0 / D), reads=[ph.b(name, "ssq")], writes=[ph.b(name, "sd")])
    ph.add("dve", lambda e: e.reciprocal(rstd, sd), reads=[ph.b(name, "sd")], writes=[ph.b(name, "rstd")])
    ph.add("dve", lambda e: e.tensor_scalar(r, r, nm, rstd, ALU.add, ALU.mult), reads=[ph.b(name, "nm"), ph.b(name, "rstd")], writes=[rB])
    ph.add(eng2, lambda e: e.tensor_tensor(r, r, g_b, ALU.mult), reads=[ph.b("lng")], writes=[rB])
    ph.add(eng2, lambda e: e.tensor_tensor(out_ap, r, b_b, ALU.add), reads=[rB, ph.b("lnb")], writes=[outB])


def phase_outproj(P, name, w_out, xin_rows, lng, lnb, w_router, xext, acc, affd):
    nc = P.nc
    dr = P.dram
    with contextlib.ExitStack() as st:
        ph = Phase(nc, name)
        cst, idb = load_consts(ph, nc, st, P)
        stage = [sb(st, nc, f"{name}_stg{i}", [128, 1024], F32) for i in range(2)]
        wo = load_weight_bf16(ph, nc, st, f"{name}_wo", w_out, 8, 1024, stage, [ph.b("stg", i) for i in range(2)])
        wr = sb(st, nc, f"{name}_wr", [128, 8, NE], F32)
        for hh in range(2):
            ph.dma("sp", wr[:, hh * 4:(hh + 1) * 4, :], w_router.rearrange("(c p) n -> p c n", p=128)[:, hh * 4:(hh + 1) * 4, :], writes=[ph.b("wr")])
        g_b = sb(st, nc, f"{name}_g", [128, D], F32)
        b_b = sb(st, nc, f"{name}_b", [128, D], F32)
        ph.dma("sp", g_b[:, :], bc(lng), writes=[ph.b("lng")])
        ph.dma("sp", b_b[:, :], bc(lnb), writes=[ph.b("lnb")])
        eps_t = sb(st, nc, f"{name}_eps", [128, 1], F32)
        ph.add("pool", lambda e: e.memset(eps_t[:, :], 1e-5), writes=[ph.b("eps")])
        oTs = [sb(st, nc, f"{name}_oT{i}", [128, 8, 512], BF16) for i in range(2)]
        xs = [sb(st, nc, f"{name}_xs{i}", [128, D], F32) for i in range(4)]
        r = [sb(st, nc, f"{name}_r{i}", [128, D], F32) for i in range(4)]
        xo = [sb(st, nc, f"{name}_xo{i}", [128, XW], F32) for i in range(2)]
        ax = [sb(st, nc, f"{name}_ax{i}", [128, D], F32) for i in range(2)]
        junk2 = [sb(st, nc, f"{name}_junk{i}", [128, D], BF16) for i in range(2)]
        small2 = [sb(st, nc, f"{name}_small{i}", [128, 8], F32) for i in range(2)]
        xnT2 = [sb(st, nc, f"{name}_xnT{i}", [128, 8, 128], F32) for i in range(2)]
        lg2 = [sb(st, nc, f"{name}_lg{i}", [128, 4], F32) for i in range(2)]
        ex2 = [sb(st, nc, f"{name}_ex{i}", [128, NE], F32) for i in range(2)]
        py = [ps(st, nc, f"{name}_py{i}", [128, 512], F32) for i in range(2)]
        pT2 = [ps(st, nc, f"{name}_pT{i}", [128, 512], F32) for i in range(2)]
        pl2 = [ps(st, nc, f"{name}_pl{i}", [128, 512], F32) for i in range(2)]
        oTv = dr["oT"][:, :].rearrange("(c p) t -> p c t", p=128)
        def stageA(i):
            s = i % 4
            g4, t4 = divmod(i, 4)
            gs = g4 % 2
            tok = slice(i * 128, (i + 1) * 128)
            B = lambda n: ph.b(n, s)
            if t4 == 0:
                for hh in range(2):
                    ph.dma("sp", oTs[gs][:, hh * 4:(hh + 1) * 4, :], oTv[:, hh * 4:(hh + 1) * 4, g4 * 512:(g4 + 1) * 512], writes=[ph.b("oTs", gs)])
            ph.dma("sp", xs[s][:, :], xin_rows(tok), writes=[B("xs")])
            for n in range(2):
                for c in range(8):
                    ph.add("pe", (lambda e, n=n, c=c, gs=gs, t4=t4: e.matmul(py[n][:, :], oTs[gs][:, c, t4 * 128:(t4 + 1) * 128],
                                                                             wo[:, c, n * 512:(n + 1) * 512], start=(c == 0), stop=(c == 7))),
                           reads=[ph.b("oTs", gs), ph.b(f"{name}_wo")], writes=[ph.b("py", n)])
                ph.add("dve", (lambda e, n=n, s=s: e.scalar_tensor_tensor(r[s][:, n * 512:(n + 1) * 512], xs[s][:, n * 512:(n + 1) * 512], ALPHA,
                                                                          py[n][:, :], ALU.mult, ALU.add)),
                       reads=[B("xs"), ph.b("py", n)], writes=[B("r")])

        def stageB(i):
            s = i % 2
            tok = slice(i * 128, (i + 1) * 128)
            B = lambda n: ph.b(n, s)
            junk, small, xnT, lg, ex, pTs_, pl = junk2[s], small2[s], xnT2[s], lg2[s], ex2[s], pT2[s], pl2[s]
            rr = r[i % 4][:, :]
            rB = ph.b("r", i % 4)
            sm, nm, ssq, sd, rstd = (small[:, k:k + 1] for k in range(5))
            ph.add("dve", lambda e: e.tensor_reduce(sm, rr, AX.X, ALU.add), reads=[rB], writes=[B("sm")]); yield
            ph.add("dve", lambda e: e.tensor_scalar(nm, sm, -1.0 / D, None, ALU.mult), reads=[B("sm")], writes=[B("nm")]); yield
            ph.add("act", lambda e: e.activation(junk[:, :], rr, AF.Square, bias=nm, scale=1.0, accum_out=ssq),
                   reads=[rB, B("nm")], writes=[B("ssq"), B("junk")]); yield
            ph.add("act", lambda e: e.activation(sd, ssq, AF.Sqrt, bias=eps_t[:, :], scale=1.0 / D), reads=[B("ssq")], writes=[B("sd")]); yield
            ph.add("dve", lambda e: e.reciprocal(rstd, sd), reads=[B("sd")], writes=[B("rstd")]); yield
            ph.add("dve", lambda e: e.tensor_scalar(rr, rr, nm, rstd, ALU.add, ALU.mult), reads=[B("nm"), B("rstd")], writes=[rB]); yield
            ph.add("pool", lambda e: e.tensor_tensor(rr, rr, g_b[:, :], ALU.mult), reads=[ph.b("lng")], writes=[rB]); yield
            ph.add("pool", lambda e: e.tensor_tensor(xo[s][:, 0:D], rr, b_b[:, :], ALU.add), reads=[rB, ph.b("lnb")], writes=[B("xo")]); yield
            ph.add("act", lambda e: e.mul(ax[s][:, :], xo[s][:, 0:D], ALPHA), reads=[B("xo")], writes=[B("ax")]); yield
            ph.dma("sp", acc[tok, :], ax[s][:, :], reads=[B("ax")]); yield
            for g in range(2):
                for c in range(4):
                    cc = g * 4 + c
                    ph.add("pe", (lambda e, c=c, cc=cc: e.transpose(pTs_[:, c * 128:(c + 1) * 128], xo[s][:, cc * 128:(cc + 1) * 128],
                                                                    cst[:, C_IDENT:C_IDENT + 128])),
                           reads=[B("xo"), ph.b("cst")], writes=[B("pT")])
                yield
                ph.add("act", (lambda e, g=g: e.copy(xnT[:, g * 4:(g + 1) * 4, :], pTs_[:, :].rearrange("p (c t) -> p c t", c=4))),
                       reads=[B("pT")], writes=[B("xnT")]); yield
            for c in range(8):
                ph.add("pe", (lambda e, c=c: e.matmul(pl[:, 0:NE], xnT[:, c, :], wr[:, c, :], start=(c == 0), stop=(c == 7))),
                       reads=[B("xnT"), ph.b("wr")], writes=[B("pl")])
            yield
            ph.add("dve", lambda e: e.tensor_reduce(lg[:, 0:1], pl[:, 0:NE], AX.X, ALU.max), reads=[B("pl")], writes=[B("lg0")]); yield
            ph.add("dve", lambda e: e.tensor_scalar(lg[:, 1:2], lg[:, 0:1], -1.0, None, ALU.mult), reads=[B("lg0")], writes=[B("lg1")]); yield
            ph.add("act", lambda e: e.activation(ex[:, :], pl[:, 0:NE], AF.Exp, bias=lg[:, 1:2], scale=1.0, accum_out=lg[:, 2:3]),
                   reads=[B("pl"), B("lg1")], writes=[B("ex"), B("lg2")]); yield
            ph.add("dve", lambda e: e.reciprocal(lg[:, 3:4], lg[:, 2:3]), reads=[B("lg2")], writes=[B("lg3")]); yield
            ph.add("dve", lambda e: e.tensor_scalar(xo[s][:, D:XW], ex[:, :], lg[:, 3:4], None, ALU.mult),
                   reads=[B("ex"), B("lg3")], writes=[B("xo")]); yield
            ph.dma("sp", xext[tok, :], xo[s][:, :], reads=[B("xo")]); yield
            ph.dma("sp", affd[tok, :], xo[s][:, D:XW], reads=[B("xo")]); yield

        def drive(gens):
            gens = list(gens)
            while gens:
                for g_ in list(gens):
                    try:
                        next(g_)
                    except StopIteration:
                        gens.remove(g_)

        stageA(0)
        stageA(1)
        for i in range(0, NT, 2):
            if i + 2 < NT:
                stageA(i + 2)
                stageA(i + 3)
            drive([stageB(i), stageB(i + 1)])
        ph.emit()


def phase_route(P, name, xext, idx):
    nc = P.nc
    dr = P.dram
    with contextlib.ExitStack() as st:
        ph = Phase(nc, name)
        cst, idb = load_consts(ph, nc, st, P)
        aff = sb(st, nc, f"{name}_aff", [128, 64, NE], F32)
        cmp_ = sb(st, nc, f"{name}_cmp", [128, 64, NE], F32)
        mT = sb(st, nc, f"{name}_mT", [128, NE, 64], F32)
        Cl = sb(st, nc, f"{name}_Cl", [128, NE, 64], F32)
        Cg = sb(st, nc, f"{name}_Cg", [128, NE, 64], F32)
        lo = sb(st, nc, f"{name}_lo", [128, NE], F32)
        mid = sb(st, nc, f"{name}_mid", [128, NE], F32)
        sel = sb(st, nc, f"{name}_sel", [128, NE], F32)
        cntp = sb(st, nc, f"{name}_cntp", [128, NE], BF16)
        ones = sb(st, nc, f"{name}_ones", [128, 128], BF16)
        trib = sb(st, nc, f"{name}_trib", [128, 128], BF16)
        idxf = sb(st, nc, f"{name}_idxf", [128, NE, 8], F32)
        junk = sb(st, nc, f"{name}_junk", [128, S], BF16)
        crow = [sb(st, nc, f"{name}_crow{i}", [128, S], F32) for i in range(2)]
        tot = ps(st, nc, f"{name}_tot", [128, 512], F32)
        off = ps(st, nc, f"{name}_off", [128, 512], F32)
        ph.dma("sp", aff[:, :, :].rearrange("p i e -> p (i e)"), dr["affd"][:, :].rearrange("(p i) w -> p (i w)", p=128), writes=[ph.b("aff")])
        ph.add("dve", lambda e: e.memset(ones[:, :], 1.0), writes=[ph.b("ones")])
        ph.add("dve", lambda e: e.tensor_copy(trib[:, :], cst[:, C_TRI:C_TRI + 128]), reads=[ph.b("cst")], writes=[ph.b("trib")])
        ph.add("dve", lambda e: e.memset(lo[:, :], 0.0), writes=[ph.b("lo")])
        for k in range(30):
            w = 0.5 ** (k + 1)
            ph.add("dve", (lambda e, w=w: e.tensor_scalar(mid[:, :], lo[:, :], w, None, ALU.add)), reads=[ph.b("lo")], writes=[ph.b("mid")])
            ph.add("dve", lambda e: e.tensor_tensor(cmp_[:, :, :], aff[:, :, :], mid[:, :].unsqueeze(1).to_broadcast([128, 64, NE]), ALU.is_ge),
                   reads=[ph.b("aff"), ph.b("mid")], writes=[ph.b("cmp")])
            ph.add("dve", lowp(nc, lambda e: e.tensor_reduce(cntp[:, :], cmp_[:, :, :].rearrange("p i e -> p e i"), AX.X, ALU.add)),
                   reads=[ph.b("cmp")], writes=[ph.b("cntp")])
            ph.add("pe", lambda e: e.matmul(tot[:, 0:NE], ones[:, :], cntp[:, :], start=True, stop=True),
                   reads=[ph.b("ones"), ph.b("cntp")], writes=[ph.b("tot")])
            ph.add("dve", lambda e: e.tensor_scalar(sel[:, :], tot[:, 0:NE], CAP - 0.5, None, ALU.is_ge), reads=[ph.b("tot")], writes=[ph.b("sel")])
            ph.add("dve", (lambda e, w=w: e.scalar_tensor_tensor(lo[:, :], sel[:, :], w, lo[:, :], ALU.mult, ALU.add)),
                   reads=[ph.b("sel")], writes=[ph.b("lo")])
        ph.add("dve", lambda e: e.tensor_tensor(mT[:, :, :].rearrange("p e i -> p i e"), aff[:, :, :],
                                                lo[:, :].unsqueeze(1).to_broadcast([128, 64, NE]), ALU.is_ge),
               reads=[ph.b("aff"), ph.b("lo")], writes=[ph.b("mT")])
        ph.add("dve", lambda e: e.tensor_tensor_scan(Cl[:, :, :].rearrange("p e i -> p (e i)"), cst[:, C_KEEP:C_KEEP + NE * 64],
                                                     mT[:, :, :].rearrange("p e i -> p (e i)"), 0.0, ALU.mult, ALU.add),
               reads=[ph.b("mT"), ph.b("cst")], writes=[ph.b("Cl")])
        ph.add("dve", lambda e: e.tensor_copy(cntp[:, :], Cl[:, :, 63]), reads=[ph.b("Cl")], writes=[ph.b("cntp")])
        ph.add("pe", lambda e: e.matmul(off[:, 0:NE], trib[:, :], cntp[:, :], start=True, stop=True),
               reads=[ph.b("trib"), ph.b("cntp")], writes=[ph.b("off")])
        ph.add("dve", lambda e: e.tensor_copy(sel[:, :], off[:, 0:NE]), reads=[ph.b("off")], writes=[ph.b("sel")])
        ph.add("dve", lambda e: e.tensor_tensor(Cg[:, :, :], Cl[:, :, :], sel[:, :].unsqueeze(2).to_broadcast([128, NE, 64]), ALU.add),
               reads=[ph.b("Cl"), ph.b("sel")], writes=[ph.b("Cg")])
        cd = dr[f"{name}_Cd"]
        for e4 in range(4):
            ph.dma("sp", cd[e4 * 4:(e4 + 1) * 4, :].rearrange("e (p i) -> p e i", p=128), Cg[:, e4 * 4:(e4 + 1) * 4, :], reads=[ph.b("Cg")], writes=[ph.b("Cd", e4)])
        junk2 = sb(st, nc, f"{name}_junk2", [128, S], BF16)
        svh = sb(st, nc, f"{name}_svh", [128, 8], F32)
        ph.add("dve", lambda e: e.tensor_scalar(svh[:, :], cst[:, C_SVEC:C_SVEC + 8], 0.5, None, ALU.add), reads=[ph.b("cst")], writes=[ph.b("svh")])
        for e_ in range(NE):
            s = e_ % 2
            ph.dma("sp", crow[s][:, :], bc(cd[e_, :]), reads=[ph.b("Cd", e_ // 4)], writes=[ph.b("crow", s)])
            for j in range(8):
                if j % 2 == 0:
                    ph.add("dve", (lambda e, s=s, e_=e_, j=j: e.tensor_scalar(junk[:, :], crow[s][:, :], cst[:, C_SVEC + j:C_SVEC + j + 1], None,
                                                                               ALU.is_le, ALU.add, accum_out=idxf[:, e_, j:j + 1])),
                           reads=[ph.b("crow", s), ph.b("cst")], writes=[ph.b("idxf", "dve"), ph.b("junk")])
                else:
                    ph.add("act", (lambda e, s=s, e_=e_, j=j: e.activation(junk2[:, :], crow[s][:, :], AF.Sign, bias=svh[:, j:j + 1], scale=-1.0,
                                                                            accum_out=idxf[:, e_, j:j + 1])),
                           reads=[ph.b("crow", s), ph.b("svh")], writes=[ph.b("idxf", "act"), ph.b("junk2")])
        i4 = idxf[:, :, :].rearrange("p e (j t) -> p e j t", t=2)[:, :, :, 1]
        ph.add("dve", lambda e: e.tensor_scalar(i4, i4, 0.5, float(S // 2), ALU.mult, ALU.add), reads=[ph.b("idxf", "act")], writes=[ph.b("idxf", "act")])
        ph.add("dve", lambda e: e.tensor_copy(idx[:, :, :], idxf[:, :, :]), reads=[ph.b("idxf", "act"), ph.b("idxf", "dve")], writes=[ph.b("idx")])
        if "dbg_idx" in dr and name == "p5":
            ph.dma("sp", dr["dbg_idx"][:, :], idx[:, :, :].rearrange("p e j -> p (e j)"), reads=[ph.b("idx")])
            ph.dma("sp", dr["dbg_lo"][:, :], lo[:, :], reads=[ph.b("lo")])
        ph.emit()


def phase_experts(P, name, l, xext, acc, idx):
    nc = P.nc
    dr = P.dram
    with contextlib.ExitStack() as st:
        ph = Phase(nc, name)
        cst, idb = load_consts(ph, nc, st, P)
        NST = 3
        stage = [sb(st, nc, f"{name}_stg{i}", [128, 2048], F32) for i in range(NST)]
        wts = {k: [sb(st, nc, f"{name}_{k}{i}", [128, 8, 1024], BF16) for i in range(2)] for k in ("wg", "wu", "wd")}
        xg = [sb(st, nc, f"{name}_xg{i}", [128, XW], F32) for i in range(2)]
        xgT2 = [sb(st, nc, f"{name}_xgT{i}", [128, 8, 1024], BF16) for i in range(2)]
        hT = sb(st, nc, f"{name}_hT", [128, 8, 1024], BF16)
        gt = sb(st, nc, f"{name}_gt", [128, 2, 8], F32)
        sg = [sb(st, nc, f"{name}_sg{i}", [128, 512], F32) for i in range(2)]
        yo = [sb(st, nc, f"{name}_yo{i}", [128, D], F32) for i in range(2)]
        pT = [ps(st, nc, f"{name}_pT{i}", [128, 512], F32) for i in range(2)]
        pg = [ps(st, nc, f"{name}_pg{i}", [128, 512], F32) for i in range(2)]
        pu = [ps(st, nc, f"{name}_pu{i}", [128, 512], F32) for i in range(2)]
        py = [ps(st, nc, f"{name}_py{i}", [128, 512], F32) for i in range(2)]
        srcs = {"wg": dr["moe_w_gate"], "wu": dr["moe_w_up"], "wd": dr["moe_w_down"]}
        stg_cnt = [0]
        cast_engs = ("pool", "dve", "pool", "act")

        def load_w(e_):
            sl = e_ % 2
            for k in ("wg", "wu", "wd"):
                wv = srcs[k][l, e_, :, :].rearrange("(c p) n -> p c n", p=128)
                for c2 in range(4):
                    s = stg_cnt[0] % NST
                    eng = cast_engs[stg_cnt[0] % len(cast_engs)]
                    stg_cnt[0] += 1
                    ph.dma("sp", stage[s][:, :].rearrange("p (c n) -> p c n", c=2), wv[:, 2 * c2:2 * c2 + 2, :], writes=[ph.b("stg", s)])
                    dst = wts[k][sl][:, 2 * c2:2 * c2 + 2, :]
                    srcv = stage[s][:, :].rearrange("p (c n) -> p c n", c=2)
                    if eng == "act":
                        ph.add("act", (lambda e, dst=dst, srcv=srcv: e.copy(dst, srcv)), reads=[ph.b("stg", s)], writes=[ph.b(k, sl)])
                    else:
                        ph.add(eng, (lambda e, dst=dst, srcv=srcv: e.tensor_copy(dst, srcv)), reads=[ph.b("stg", s)], writes=[ph.b(k, sl)])

        load_w(0)
        prev_sc = []

        def emit_gather(e_, j):
            s = (e_ * 8 + j) % 2
            ph.add("pool", (lambda e, s=s, e_=e_, j=j: e.indirect_dma_start(
                out=xg[s][:, :], out_offset=None, in_=xext[:, :],
                in_offset=bass.IndirectOffsetOnAxis(ap=idx[:, e_, j:j + 1], axis=0))),
                reads=[ph.b("idx")], writes=[ph.b("xg", s)], dma=True)

        def emit_trans(e_, j):
            s = (e_ * 8 + j) % 2
            xs_ = e_ % 2
            emit_xT(ph, xg[s], ph.b("xg", s), xgT2[xs_][:, :, j * 128:(j + 1) * 128], ph.b("xgT", xs_), pT, [ph.b("pT", 0), ph.b("pT", 1)], cst)
            ph.add("act", (lambda e, s=s, e_=e_, j=j, xs_=xs_: e.copy(gt[:, xs_, j:j + 1], xg[s][:, D + e_:D + e_ + 1])),
                   reads=[ph.b("xg", s)], writes=[ph.b("gt", xs_)])

        emit_gather(0, 0)
        for j in range(8):
            if j + 1 < 8:
                emit_gather(0, j + 1)
            emit_trans(0, j)
        for e_ in range(NE):
            sl = e_ % 2
            wg, wu, wd = wts["wg"][sl], wts["wu"][sl], wts["wd"][sl]
            gsl = e_ % 2
            xgT = xgT2[sl]
            xgB = ph.b("xgT", sl)
            if e_ + 1 < NE:
                load_w(e_ + 1)
                emit_gather(e_ + 1, 0)
            n = 0
            for fc in range(8):
                if e_ + 1 < NE and fc + 1 < 8:
                    emit_gather(e_ + 1, fc + 1)
                for half in range(2):
                    b = n % 2
                    n += 1
                    for (wt, pt_, nm, kn) in ((wg, pg, "pg", "wg"), (wu, pu, "pu", "wu")):
                        for c in range(8):
                            ph.add("pe", (lambda e, wt=wt, pt_=pt_, b=b, c=c, fc=fc, half=half, xgT=xgT: e.matmul(
                                pt_[b][:, :], wt[:, c, fc * 128:(fc + 1) * 128], xgT[:, c, half * 512:(half + 1) * 512],
                                start=(c == 0), stop=(c == 7))),
                                reads=[ph.b(kn, sl), xgB], writes=[ph.b(nm, b)])
                    ph.add("act", (lambda e, b=b: e.activation(sg[b][:, :], pg[b][:, :], AF.Silu)), reads=[ph.b("pg", b)], writes=[ph.b("sg", b)])
                    ph.add("dve", (lambda e, b=b, fc=fc, half=half: e.tensor_tensor(hT[:, fc, half * 512:(half + 1) * 512], sg[b][:, :], pu[b][:, :], ALU.mult)),
                           reads=[ph.b("sg", b), ph.b("pu", b)], writes=[ph.b("hT")])
                if e_ + 1 < NE:
                    emit_trans(e_ + 1, fc)
            new_sc = []
            for j in range(8):
                ys = (e_ * 8 + j) % 2
                for half in range(2):
                    for fc in range(8):
                        ph.add("pe", (lambda e, half=half, fc=fc, j=j, wd=wd: e.matmul(py[half][:, :], hT[:, fc, j * 128:(j + 1) * 128],
                                                                                wd[:, fc, half * 512:(half + 1) * 512], start=(fc == 0), stop=(fc == 7))),
                               reads=[ph.b("hT"), ph.b("wd", sl)], writes=[ph.b("py", half)])
                    eng = "act" if half == 0 else "dve"
                    if eng == "act":
                        ph.add("act", (lambda e, ys=ys, half=half, gsl=gsl, j=j: e.mul(yo[ys][:, half * 512:(half + 1) * 512], py[half][:, :], gt[:, gsl, j:j + 1])),
                               reads=[ph.b("py", half), ph.b("gt", gsl)], writes=[ph.b("yo", ys)])
                    else:
                        ph.add("dve", (lambda e, ys=ys, half=half, gsl=gsl, j=j: e.tensor_scalar(yo[ys][:, half * 512:(half + 1) * 512], py[half][:, :],
                                                                                                  gt[:, gsl, j:j + 1], None, ALU.mult)),
                               reads=[ph.b("py", half), ph.b("gt", gsl)], writes=[ph.b("yo", ys)])
                if "dbg_xg" in dr and name == "p6" and e_ == 0 and j == 0:
                    ph.dma("sp", dr["dbg_yo"][:, :], yo[ys][:, :], reads=[ph.b("yo", ys)])
                    ph.dma("sp", dr["dbg_hT"][:, :], hT[:, :, :].rearrange("p c t -> p (c t)"), reads=[ph.b("hT")])
                op = ph.add("pool", (lambda e, ys=ys, e_=e_, j=j: e.indirect_dma_start(
                    out=acc[:, :], out_offset=bass.IndirectOffsetOnAxis(ap=idx[:, e_, j:j + 1], axis=0),
                    in_=yo[ys][:, :], in_offset=None, compute_op=ALU.add)),
                    reads=[ph.b("idx"), ph.b("yo", ys)], writes=[], dma=True)
                for d in prev_sc:
                    if d not in op.deps:
                        op.deps.append(d)
                new_sc.append(op)
            prev_sc = new_sc
        ph.emit()


def phase_ln(P, name, acc, lng, lnb, out_rows):
    nc = P.nc
    with contextlib.ExitStack() as st:
        ph = Phase(nc, name)
        g_b = sb(st, nc, f"{name}_g", [128, D], F32)
        b_b = sb(st, nc, f"{name}_b", [128, D], F32)
        ph.dma("sp", g_b[:, :], bc(lng), writes=[ph.b("lng")])
        ph.dma("sp", b_b[:, :], bc(lnb), writes=[ph.b("lnb")])
        eps_t = sb(st, nc, f"{name}_eps", [128, 1], F32)
        ph.add("pool", lambda e: e.memset(eps_t[:, :], 1e-5), writes=[ph.b("eps")])
        r = [sb(st, nc, f"{name}_r{i}", [128, D], F32) for i in range(3)]
        o = [sb(st, nc, f"{name}_o{i}", [128, D], F32) for i in range(3)]
        junk = sb(st, nc, f"{name}_junk", [128, D], BF16)
        small = sb(st, nc, f"{name}_small", [128, 8], F32)
        for i in range(NT + 1):
            if i < NT:
                s = i % 3
                tok = slice(i * 128, (i + 1) * 128)
                ph.dma("sp", r[s][:, :], acc[tok, :], writes=[ph.b("r", s)])
            if i >= 1:
                s = (i - 1) % 3
                tok = slice((i - 1) * 128, i * 128)
                emit_ln(ph, nc, name, r[s][:, :], ph.b("r", s), o[s][:, :], ph.b("o", s), g_b[:, :], b_b[:, :], junk[:, :], small[:, :], eps_t[:, :])
                ph.dma("sp", out_rows(tok), o[s][:, :], reads=[ph.b("o", s)])
        ph.emit()


def phase_proj_mla(P, xin_rows):
    nc = P.nc
    dr = P.dram
    name = "p8"
    with contextlib.ExitStack() as st:
        ph = Phase(nc, name)
        cst, idb = load_consts(ph, nc, st, P)
        stage = [sb(st, nc, f"p8_stg{i}", [128, 2048], F32) for i in range(2)]
        sB = [ph.b("stg", i) for i in range(2)]
        wdn = load_weight_bf16(ph, nc, st, "p8_wdn", dr["mla_w_down"][0, :, :], 8, 416, stage, sB)
        wuq = load_weight_bf16(ph, nc, st, "p8_wuq", dr["mla_w_uq"][0, :, :], 2, 1536, stage, sB)
        wukv = load_weight_bf16(ph, nc, st, "p8_wukv", dr["mla_w_ukv"][0, :, :], 1, 2048, stage, sB)
        gq = sb(st, nc, "p8_gq", [128, 256], F32)
        gk = sb(st, nc, "p8_gk", [128, 128], F32)
        ph.dma("sp", gq[:, :], bc(dr["mla_q_norm"][0, :]), writes=[ph.b("gq")])
        ph.dma("sp", gk[:, :], bc(dr["mla_kv_norm"][0, :]), writes=[ph.b("gk")])
        eps = sb(st, nc, "p8_eps", [128, 1], F32)
        ph.add("pool", lambda e: e.memset(eps[:, :], 1e-6), writes=[ph.b("eps")])
        xs = [sb(st, nc, f"p8_xs{i}", [128, D], F32) for i in range(2)]
        tb = [sb(st, nc, f"p8_tb{i}", [128, 64], F32) for i in range(2)]
        xT = [sb(st, nc, f"p8_xT{i}", [128, 8, 128], BF16) for i in range(2)]
        pr = sb(st, nc, "p8_pr", [128, 416], F32)
        junk = sb(st, nc, "p8_junk", [128, 256], BF16)
        sm = sb(st, nc, "p8_sm", [128, 8], F32)
        cn = sb(st, nc, "p8_cn", [128, 384], BF16)
        cnT = sb(st, nc, "p8_cnT", [128, 3, 128], BF16)
        qsb = sb(st, nc, "p8_qs", [128, 1536], F32)
        kvs = sb(st, nc, "p8_kvs", [128, 2048], F32)
        qb = sb(st, nc, "p8_qb", [128, 16, 96], BF16)
        kb = sb(st, nc, "p8_kb", [128, 16, 96], BF16)
        kr = sb(st, nc, "p8_kr", [128, 32], BF16)
        t1 = sb(st, nc, "p8_t1", [128, 512], F32)
        t2 = sb(st, nc, "p8_t2", [128, 512], F32)
        t3 = sb(st, nc, "p8_t3", [128, 32], F32)
        t4 = sb(st, nc, "p8_t4", [128, 32], F32)
        stq = [sb(st, nc, f"p8_stq{i}", [128, 16, 128], BF16) for i in range(2)]
        stk = [sb(st, nc, f"p8_stk{i}", [128, 16, 128], BF16) for i in range(2)]
        vst = [sb(st, nc, f"p8_vst{i}", [128, 16, 128], BF16) for i in range(2)]
        pT = [ps(st, nc, f"p8_pT{i}", [128, 512], F32) for i in range(2)]
        pp = ps(st, nc, "p8_pp", [128, 512], F32)
        pc = ps(st, nc, "p8_pc", [128, 1024], BF16)
        pb_ = [ps(st, nc, f"p8_pb{i}", [128, 512], F32) for i in range(4)]
        for s in range(2):
            ph.add("pool", (lambda e, s=s: e.memset(vst[s][:, :, :], 1.0)), writes=[ph.b("vst", s)])
        for i in range(NT):
            s = i % 2
            tok = slice(i * 128, (i + 1) * 128)
            B = lambda n: ph.b(n, s)
            ph.dma("sp", xs[s][:, :], xin_rows(tok), writes=[B("xs")])
            ph.dma("sp", tb[s][:, :], dr["tab"][tok, 256:320], writes=[B("tb")])
            emit_xT(ph, xs[s], B("xs"), xT[s], B("xT"), pT, [ph.b("pT", 0), ph.b("pT", 1)], cst)
            for c in range(8):
                ph.add("pe", (lambda e, c=c, s=s: e.matmul(pp[:, 0:416], xT[s][:, c, :], wdn[:, c, :], start=(c == 0), stop=(c == 7))),
                       reads=[B("xT"), ph.b("p8_wdn")], writes=[ph.b("pp")])
            ph.add("act", lambda e: e.copy(pr[:, :], pp[:, 0:416]), reads=[ph.b("pp")], writes=[ph.b("pr")])
            ph.add("act", lambda e: e.activation(junk[:, 0:256], pr[:, 0:256], AF.Square, accum_out=sm[:, 0:1]), reads=[ph.b("pr")], writes=[ph.b("sm0"), ph.b("junk")])
            ph.add("act", lambda e: e.activation(junk[:, 0:128], pr[:, 256:384], AF.Square, accum_out=sm[:, 1:2]), reads=[ph.b("pr")], writes=[ph.b("sm1"), ph.b("junk")])
            ph.add("act", lambda e: e.activation(sm[:, 2:3], sm[:, 0:1], AF.Sqrt, bias=eps[:, :], scale=1.0 / 256), reads=[ph.b("sm0"), ph.b("eps")], writes=[ph.b("sm2")])
            ph.add("act", lambda e: e.activation(sm[:, 3:4], sm[:, 1:2], AF.Sqrt, bias=eps[:, :], scale=1.0 / 128), reads=[ph.b("sm1"), ph.b("eps")], writes=[ph.b("sm3")])
            ph.add("dve", lambda e: e.reciprocal(sm[:, 4:6], sm[:, 2:4]), reads=[ph.b("sm2"), ph.b("sm3")], writes=[ph.b("sm4")])
            ph.add("dve", lambda e: e.scalar_tensor_tensor(cn[:, 0:256], pr[:, 0:256], sm[:, 4:5], gq[:, :], ALU.mult, ALU.mult),
                   reads=[ph.b("pr"), ph.b("sm4"), ph.b("gq")], writes=[ph.b("cn")])
            ph.add("dve", lambda e: e.scalar_tensor_tensor(cn[:, 256:384], pr[:, 256:384], sm[:, 5:6], gk[:, :], ALU.mult, ALU.mult),
                   reads=[ph.b("pr"), ph.b("sm4"), ph.b("gk")], writes=[ph.b("cn")])
            for c in range(3):
                ph.add("pe", (lambda e, c=c: e.transpose(pc[:, c * 128:(c + 1) * 128], cn[:, c * 128:(c + 1) * 128], idb[:, :])),
                       reads=[ph.b("cn"), ph.b("idb")], writes=[ph.b("pc")])
            ph.add("act", lambda e: e.copy(cnT[:, :, :], pc[:, 0:384].rearrange("p (c t) -> p c t", c=3)), reads=[ph.b("pc")], writes=[ph.b("cnT")])
            for n in range(3):
                for c in range(2):
                    ph.add("pe", (lambda e, n=n, c=c: e.matmul(pb_[n][:, :], cnT[:, c, :], wuq[:, c, n * 512:(n + 1) * 512], start=(c == 0), stop=(c == 1))),
                           reads=[ph.b("cnT"), ph.b("p8_wuq")], writes=[ph.b("pb", n)])
                ph.add("act", (lambda e, n=n: e.copy(qsb[:, n * 512:(n + 1) * 512], pb_[n][:, :])), reads=[ph.b("pb", n)], writes=[ph.b("qs")])
            q3 = qsb[:, :].rearrange("p (h d) -> p h d", h=16)
            ph.add("pool", lambda e: e.tensor_copy(qb[:, :, 0:64], q3[:, :, 0:64]), reads=[ph.b("qs")], writes=[ph.b("qb")])
            rope3(ph, "dve", qb[:, :, 64:96], q3[:, :, 64:96], tb[s][:, 0:32], tb[s][:, 32:64], 16, 32, 1, t1[:, :], t2[:, :],
                  [ph.b("qs"), B("tb")], [ph.b("t1"), ph.b("t2")], ph.b("qb"))
            for n in range(4):
                bn = (3 + n) % 4
                ph.add("pe", (lambda e, n=n, bn=bn: e.matmul(pb_[bn][:, :], cnT[:, 2, :], wukv[:, 0, n * 512:(n + 1) * 512], start=True, stop=True)),
                       reads=[ph.b("cnT"), ph.b("p8_wukv")], writes=[ph.b("pb", bn)])
                ph.add("act", (lambda e, n=n, bn=bn: e.copy(kvs[:, n * 512:(n + 1) * 512], pb_[bn][:, :])), reads=[ph.b("pb", bn)], writes=[ph.b("kvs")])
            kv3 = kvs[:, :].rearrange("p (h d) -> p h d", h=16)
            ph.add("pool", lambda e: e.tensor_copy(kb[:, :, 0:64], kv3[:, :, 0:64]), reads=[ph.b("kvs")], writes=[ph.b("kb")])
            ph.add("pool", (lambda e, s=s: e.tensor_copy(vst[s][:, :, 0:64], kv3[:, :, 64:128])), reads=[ph.b("kvs")], writes=[B("vst")])
            for h4 in range(4):
                ph.dma("sp", dr["v_C"][h4 * 4:(h4 + 1) * 4, :, i, :].rearrange("k p f -> p k f"), vst[s][:, h4 * 4:(h4 + 1) * 4, :], reads=[B("vst")])
            rope3(ph, "dve", kr[:, :].unsqueeze(1), pr[:, 384:416].unsqueeze(1), tb[s][:, 0:32], tb[s][:, 32:64], 1, 32, 1, t3[:, :], t4[:, :],
                  [ph.b("pr"), B("tb")], [ph.b("t3"), ph.b("t4")], ph.b("kr"))
            ph.add("dve", lambda e: e.tensor_copy(kb[:, :, 64:96], kr[:, :].unsqueeze(1).to_broadcast([128, 16, 32])),
                   reads=[ph.b("kr")], writes=[ph.b("kb")])
            for (src, sn, stt, stn, dst) in ((qb, "qb", stq, "stq", dr["qT_C"]), (kb, "kb", stk, "stk", dr["kT_C"])):
                for hh in range(2):
                    bnk = [pb_[0], pb_[1]] if sn == "qb" else [pb_[2], pb_[3]]
                    bnn = [0, 1] if sn == "qb" else [2, 3]
                    ptile = bnk[hh]
                    pv = ptile[:, :].bitcast(BF16)
                    for h8 in range(8):
                        h = hh * 8 + h8
                        ph.add("pe", (lambda e, pv=pv, h8=h8, h=h, src=src: e.transpose(pv[0:96, h8 * 128:(h8 + 1) * 128], src[:, h, :], idb[:, :])),
                               reads=[ph.b(sn), ph.b("idb")], writes=[ph.b("pb", bnn[hh])])
                    ph.add("act", (lambda e, pv=pv, hh=hh, stt=stt, s=s: e.copy(stt[s][0:96, hh * 8:(hh + 1) * 8, :],
                                                                                 pv[0:96, :].rearrange("p (c t) -> p c t", c=8))),
                           reads=[ph.b("pb", bnn[hh])], writes=[B(stn)])
                for h4 in range(4):
                    ph.dma("sp", dst[h4 * 4:(h4 + 1) * 4, :, tok].rearrange("h p t -> p h t"), stt[s][0:96, h4 * 4:(h4 + 1) * 4, :], reads=[B(stn)])
        ph.emit()


PHASES = ["p1", "p2", "p3", "p4", "p5", "p6", "p7", "p8", "p9", "p10", "p11", "p12", "p13"]


def build(upto="p13", dbg=()):
    P = Prog(dbg)
    nc = P.nc
    x = P.din("x", [S, D])
    P.din("w_in", [D, 1536])
    P.din("ab_q_norm", [1, 64])
    P.din("ab_k_norm", [1, 64])
    P.din("ab_sink", [1, 8])
    P.din("ab_w_out", [D, D])
    P.din("mla_w_down", [1, D, 416])
    P.din("mla_q_norm", [1, 256])
    P.din("mla_kv_norm", [1, 128])
    P.din("mla_w_uq", [1, 256, 1536])
    P.din("mla_w_ukv", [1, 128, 2048])
    P.din("mla_w_out", [D, D])
    for n in ("ln_mix_g", "ln_mix_b", "ln_ffn_g", "ln_ffn_b"):
        P.din(n, [2, D])
    P.din("moe_router", [2, D, NE])
    for n in ("moe_w_gate", "moe_w_up", "moe_w_down"):
        P.din(n, [2, NE, D, D])
    P.din("tab", [S, 320])
    P.din("cst", [128, C_W])
    P.din("wmask", [128, 3072])
    P.dtmp("qT_A", [4, 128, S], BF16)
    P.dtmp("qT_B", [4, 128, S], BF16)
    P.dtmp("kT_A", [2, 128, S], BF16)
    P.dtmp("kT_B", [2, 128, S], BF16)
    P.dtmp("v_all", [4, 128, NT, 128], BF16)
    P.dtmp("oT", [D, S], BF16)
    P.dtmp("x1ext", [S, XW])
    P.dtmp("acc", [S, D])
    P.dtmp("affd", [S, NE])
    P.dtmp("p5_Cd", [NE, S])
    P.dtmp("p11_Cd", [NE, S])
    P.dtmp("x2ext", [S, XW])
    P.dtmp("x3ext", [S, XW])
    P.dtmp("acc2", [S, D])
    P.dtmp("qT_C", [16, 96, S], BF16)
    P.dtmp("kT_C", [16, 96, S], BF16)
    P.dtmp("v_C", [16, 128, NT, 128], BF16)
    out = P.dtmp("out", [S, D], out=True)
    if "dbg_idx" in dbg:
        P.dtmp("dbg_idx", [128, 128], I32)
        P.dtmp("dbg_lo", [128, NE])
    if "dbg_xg" in dbg:
        P.dtmp("dbg_xg", [128, XW])
        P.dtmp("dbg_yo", [128, D])
        P.dtmp("dbg_hT", [128, 8192], BF16)
    if "dbg_q9" in dbg:
        P.dtmp("dbg_q9", [128, S], BF16)
        P.dtmp("dbg_k9", [128, S], BF16)
    if "dbg_rec" in dbg:
        P.dtmp("dbg_rec", [64, 512])
        P.dtmp("dbg_num", [128, 512])
    if "dbgv0" in dbg:
        P.dtmp("dbgv0", [128, NT, 128], BF16)
        P.dtmp("dbgv1", [128, NT, 128], BF16)
    dr = P.dram
    last = PHASES.index(upto)
    on = lambda p: PHASES.index(p) <= last
    with contextlib.ExitStack() as st:
        idx = sb(st, nc, "idx_all", [128, NE, 8], I32)
        if on("p1"):
            phase_proj0(P, x)
        if on("p2"):
            jobs = []
            for h in range(8):
                j, half, kv = h // 2, h % 2, h // 4
                jobs.append(dict(qsrc=(("qA", j), dr["qT_A"][j, :, :], 128), ksrc=(("kA", kv), dr["kT_A"][kv, :, :], 128),
                                 vsrc=(("vA", kv), dr["v_all"][kv, :, :, :]), pb=half * 64, K=64, out=dr["oT"][h * 64:(h + 1) * 64, :]))
            phase_dense_attn(P, "p2", jobs, 64 ** -0.5)
        if on("p3"):
            jobs = []
            for h in range(8):
                j, half, kv = h // 2, h % 2, h // 4
                jobs.append(dict(qsrc=(("qB", j), dr["qT_B"][j, :, :], 128), ksrc=(("kB", kv), dr["kT_B"][kv, :, :], 128),
                                 vsrc=(("vB", kv), dr["v_all"][2 + kv, :, :, :]), pb=half * 64, K=64, h=h,
                                 out=dr["oT"][512 + h * 64:512 + (h + 1) * 64, :]))
            phase_dense_attn(P, "p3", jobs, 64 ** -0.5, win=True)
        if on("p4"):
            phase_outproj(P, "p4", dr["ab_w_out"][:, :], lambda tok: x[tok, :], dr["ln_mix_g"][0, :], dr["ln_mix_b"][0, :],
                          dr["moe_router"][0, :, :], dr["x1ext"], dr["acc"], dr["affd"])
        if on("p5"):
            phase_route(P, "p5", dr["x1ext"], idx)
        if on("p6"):
            phase_experts(P, "p6", 0, dr["x1ext"], dr["acc"], idx)
        if on("p7"):
            phase_ln(P, "p7", dr["acc"], dr["ln_ffn_g"][0, :], dr["ln_ffn_b"][0, :], lambda tok: dr["x2ext"][tok, 0:D])
        if on("p8"):
            phase_proj_mla(P, lambda tok: dr["x2ext"][tok, 0:D])
        if on("p9"):
            jobs = []
            for h in range(16):
                jobs.append(dict(qsrc=(("qC", h), dr["qT_C"][h, :, :], 96), ksrc=(("kC", h), dr["kT_C"][h, :, :], 96),
                                 vsrc=(("vC", h), dr["v_C"][h, :, :, :]), pb=0, K=96, out=dr["oT"][h * 64:(h + 1) * 64, :]))
            phase_dense_attn(P, "p9", jobs, 96 ** -0.5)
        if on("p10"):
            phase_outproj(P, "p10", dr["mla_w_out"][:, :], lambda tok: dr["x2ext"][tok, 0:D], dr["ln_mix_g"][1, :], dr["ln_mix_b"][1, :],
                          dr["moe_router"][1, :, :], dr["x3ext"], dr["acc2"], dr["affd"])
        if on("p11"):
            phase_route(P, "p11", dr["x3ext"], idx)
        if on("p12"):
            phase_experts(P, "p12", 1, dr["x3ext"], dr["acc2"], idx)
        if on("p13"):
            phase_ln(P, "p13", dr["acc2"], dr["ln_ffn_g"][1, :], dr["ln_ffn_b"][1, :], lambda tok: out[tok, :])
    return P


_PERM = np.concatenate([np.arange(0, 512), np.arange(512, 640), np.arange(768, 1280), np.arange(1280, 1408),
                        np.arange(640, 768), np.arange(1408, 1536)])


def make_in_maps(inputs, ncores=8):
    tab, cst = make_consts()
    f = lambda a: np.ascontiguousarray(np.asarray(a, dtype=np.float32))
    shared = {
        "w_in": f(np.asarray(inputs["ab_w_in"])[0][:, _PERM]),
        "ab_q_norm": f(inputs["ab_q_norm"]), "ab_k_norm": f(inputs["ab_k_norm"]), "ab_sink": f(inputs["ab_sink"]),
        "ab_w_out": f(np.asarray(inputs["ab_w_out"])[0]),
        "mla_w_down": f(inputs["mla_w_down"]), "mla_q_norm": f(inputs["mla_q_norm"]), "mla_kv_norm": f(inputs["mla_kv_norm"]),
        "mla_w_uq": f(inputs["mla_w_uq"]), "mla_w_ukv": f(inputs["mla_w_ukv"]), "mla_w_out": f(np.asarray(inputs["mla_w_out"])[0]),
        "ln_mix_g": f(inputs["ln_mix_g"]), "ln_mix_b": f(inputs["ln_mix_b"]), "ln_ffn_g": f(inputs["ln_ffn_g"]), "ln_ffn_b": f(inputs["ln_ffn_b"]),
        "moe_router": f(inputs["moe_router"]), "moe_w_gate": f(inputs["moe_w_gate"]), "moe_w_up": f(inputs["moe_w_up"]),
        "moe_w_down": f(inputs["moe_w_down"]), "tab": tab, "cst": cst, "wmask": make_wmask(),
    }
    xs = np.asarray(inputs["x"], dtype=np.float32)
    maps = []
    for c in range(ncores):
        m = dict(shared)
        m["x"] = np.ascontiguousarray(xs[c])
        maps.append(m)
    return maps


def kernel(**inputs):
    P = build()
    maps = make_in_maps(inputs, 8)
    res = run_bass_kernel_spmd(P.nc, maps, core_ids=list(range(8)))
    return np.stack([np.asarray(r["out"], dtype=np.float32) for r in res.results], axis=0)
```

```python
import contextlib
import numpy as np
import concourse.bass as bass
import concourse.mybir as mybir
from concourse.bass_utils import run_bass_kernel_spmd

F32 = mybir.dt.float32
BF16 = mybir.dt.bfloat16
I32 = mybir.dt.int32
ALU = mybir.AluOpType
AF = mybir.ActivationFunctionType
AX = mybir.AxisListType

S = 8192
D = 1024
NT = S // 128
NE = 16
CAP = 1024
DEPTH = 2
ALPHA = (2.0 * DEPTH) ** 0.25
XW = D + NE

SAME_ENGINE_SYNC = True


class Buf:
    __slots__ = ("w", "r", "rd")

    def __init__(self):
        self.w = None
        self.r = {}
        self.rd = []


class Op:
    __slots__ = ("eng", "fn", "deps", "sig", "sem", "val", "dma", "inc")

    def __init__(self, eng, fn, dma):
        self.eng = eng
        self.fn = fn
        self.dma = dma
        self.deps = []
        self.sig = dma
        self.sem = None
        self.val = 0
        self.inc = 16 if dma else 1


ENGS = ("sp", "pe", "act", "dve", "pool")


class Phase:
    KD = {"sp": 8, "pool": 6, "act": 4}

    def __init__(self, nc, name):
        self.nc = nc
        self.name = name
        self.ops = {e: [] for e in ENGS}
        self.bufs = {}
        self.dma_ops = {q: [] for q in self.KD}

    def b(self, *key):
        bb = self.bufs.get(key)
        if bb is None:
            bb = self.bufs[key] = Buf()
        return bb

    def add(self, eng, fn, reads=(), writes=(), dma=False):
        op = Op(eng, fn, dma)
        deps = []
        for b in reads:
            if b.w is not None:
                deps.append(b.w)
        for b in writes:
            if b.w is not None:
                deps.append(b.w)
            deps.extend(b.r.values())
            deps.extend(b.rd)
        if dma:
            lst = self.dma_ops[eng]
            k = self.KD[eng]
            if len(lst) >= k:
                deps.append(lst[len(lst) - k])
            lst.append(op)
        seen = set()
        for d in deps:
            if d is op or id(d) in seen:
                continue
            seen.add(id(d))
            if (not d.dma) and (not dma) and d.eng == eng:
                if eng == "pe" or not SAME_ENGINE_SYNC:
                    continue
            op.deps.append(d)
            d.sig = True
        for b in reads:
            if dma:
                b.rd.append(op)
            else:
                b.r[eng] = op
        for b in writes:
            b.w = op
            b.r = {}
            b.rd = []
        self.ops[eng].append(op)
        return op

    def dma(self, q, out, in_, reads=(), writes=(), **kw):
        return self.add(q, lambda e: e.dma_start(out=out, in_=in_, **kw), reads, writes, dma=True)

    def emit(self):
        nc = self.nc
        allsems = []

        def newsem(nm):
            h = nc.alloc_semaphore(name=nm)
            allsems.append(h)
            return h

        if True:
            csem = {}
            for e in ("pe", "act", "dve", "pool"):
                if any(o.sig and not o.dma for o in self.ops[e]):
                    csem[e] = newsem(f"{self.name}_c_{e}")
            dsem = {}
            for q, k in self.KD.items():
                if self.dma_ops[q]:
                    dsem[q] = [newsem(f"{self.name}_d_{q}{i}") for i in range(min(k, len(self.dma_ops[q])))]
            for e in ENGS:
                cnt = 0
                for o in self.ops[e]:
                    if o.dma:
                        continue
                    if o.sig:
                        cnt += 1
                        o.sem = csem[e]
                        o.val = cnt
            final = {q: {} for q in self.KD}
            for q, lst in self.dma_ops.items():
                k = self.KD[q]
                for n, o in enumerate(lst):
                    o.sem = dsem[q][n % k]
                    o.val = 16 * (n // k + 1)
                    final[q][n % k] = (o.sem, o.val)
            ops = self.ops

            def run(e_name, eng):
                waited = {}
                for o in ops[e_name]:
                    need = {}
                    for d in o.deps:
                        key = id(d.sem)
                        if waited.get(key, 0) >= d.val:
                            continue
                        if key not in need or need[key][1] < d.val:
                            need[key] = (d.sem, d.val)
                    for key, (sem, val) in need.items():
                        eng.wait_ge(sem, val)
                        waited[key] = val
                    ins = o.fn(eng)
                    if o.sig:
                        ins.then_inc(o.sem, o.inc)
                if e_name in final:
                    for (sem, val) in final[e_name].values():
                        if waited.get(id(sem), 0) < val:
                            eng.wait_ge(sem, val)

            with nc.Block() as block:
                if ops["sp"]:
                    @block.sync
                    def _(e):
                        run("sp", e)
                if ops["pe"]:
                    @block.tensor
                    def _(e):
                        run("pe", e)
                if ops["act"]:
                    @block.scalar
                    def _(e):
                        run("act", e)
                if ops["dve"]:
                    @block.vector
                    def _(e):
                        run("dve", e)
                if ops["pool"]:
                    @block.gpsimd
                    def _(e):
                        run("pool", e)
            nc.clear_and_free_semaphores(allsems)
            nc.all_engine_barrier()


def _rope_tab(pos, dim):
    freqs = (10000.0 ** (-(np.arange(0, dim, 2, dtype=np.float32) / np.float32(dim)))).astype(np.float32)
    ang = pos.astype(np.float32)[:, None] * freqs[None, :]
    return np.cos(ang).astype(np.float32), np.sin(ang).astype(np.float32)


def make_consts():
    t = np.arange(S)
    cr, sr = _rope_tab((t // 64).astype(np.float32), 32)
    cc, sc = _rope_tab((t % 64).astype(np.float32), 32)
    cs, ss = _rope_tab(t.astype(np.float32), 64)
    cm, sm = _rope_tab(t.astype(np.float32), 32)
    cosA = np.concatenate([cr, cr, cc, cc], 1)
    sinA = np.concatenate([-sr, sr, -sc, sc], 1)
    cosB = np.concatenate([cs, cs], 1)
    sinB = np.concatenate([-ss, ss], 1)
    cosC = np.concatenate([cm, cm], 1)
    sinC = np.concatenate([-sm, sm], 1)
    tab = np.concatenate([cosA, sinA, cosB, sinB, cosC, sinC], 1).astype(np.float32)
    ident = np.eye(128, dtype=np.float32)
    p = np.arange(128)
    tri = (p[:, None] < p[None, :]).astype(np.float32)
    mprev = (p[None, :] <= p[:, None]).astype(np.float32)
    mnext = (p[:, None] <= p[None, :]).astype(np.float32)
    svec = (np.arange(8)[None, :] * 128 + p[:, None]).astype(np.float32)
    keep = np.ones((128, NE, 64), np.float32)
    keep[:, :, 0] = 0.0
    vsink = np.zeros((128, 128), np.float32)
    vsink[0:2, 64:128] = 1.0
    cst = np.concatenate([ident, tri, mprev, mnext, svec, keep.reshape(128, -1), vsink], 1).astype(np.float32)
    return tab, cst


def make_wmask():
    kl = np.arange(128)[:, None]
    q = np.arange(512)[None, :]
    ms = []
    for r in range(6):
        ms.append((np.abs(q - ((r - 1) * 128 + kl)) <= 128).astype(np.float32))
    return np.concatenate(ms, 1)


C_IDENT, C_TRI, C_MPREV, C_MNEXT, C_SVEC, C_KEEP = 0, 128, 256, 384, 512, 520
C_VSINK = 520 + NE * 64
C_W = C_VSINK + 128


class Prog:
    def __init__(self, dbg=()):
        self.nc = bass.Bass("TRN2", target_bir_lowering=False)
        self.dbg = set(dbg)
        self.dram = {}

    def din(self, name, shape, dt=F32):
        t = self.nc.dram_tensor(name, list(shape), dt, kind="ExternalInput")
        self.dram[name] = t
        return t

    def dtmp(self, name, shape, dt=F32, out=False):
        kind = "ExternalOutput" if (out or name in self.dbg) else "Internal"
        t = self.nc.dram_tensor(name, list(shape), dt, kind=kind)
        self.dram[name] = t
        return t


def sb(st, nc, name, shape, dt):
    return st.enter_context(nc.sbuf_tensor(name, list(shape), dt))


def ps(st, nc, name, shape, dt):
    return st.enter_context(nc.psum_tensor(name, list(shape), dt))


def load_consts(ph, nc, st, P):
    cst = sb(st, nc, ph.name + "_cst", [128, C_W], F32)
    ph.dma("sp", cst[:, :], P.dram["cst"][:, :], writes=[ph.b("cst")])
    idb = sb(st, nc, ph.name + "_idb", [128, 128], BF16)
    ph.add("dve", lambda e: e.tensor_copy(idb[:, :], cst[:, C_IDENT:C_IDENT + 128]),
           reads=[ph.b("cst")], writes=[ph.b("idb")])
    return cst, idb


def emit_xT(ph, xs, xs_buf, xT, xT_buf, pT, pT_bufs, cst, ncol=8, evac=("act", "dve")):
    ident = cst[:, C_IDENT:C_IDENT + 128]
    ng = (ncol + 3) // 4
    for g in range(ng):
        pt = pT[g % len(pT)]
        pb = pT_bufs[g % len(pT)]
        n = min(4, ncol - g * 4)
        for c in range(n):
            cc = g * 4 + c
            ph.add("pe", (lambda e, pt=pt, c=c, cc=cc: e.transpose(pt[:, c * 128:(c + 1) * 128],
                                                                   xs[:, cc * 128:(cc + 1) * 128], ident)),
                   reads=[xs_buf, ph.b("cst")], writes=[pb])
        eng = evac[g % len(evac)]
        if eng == "act":
            ph.add("act", (lambda e, pt=pt, g=g, n=n: e.copy(
                xT[:, g * 4:g * 4 + n, :], pt[:, 0:n * 128].rearrange("p (c t) -> p c t", c=n))),
                reads=[pb], writes=[xT_buf])
        else:
            ph.add("dve", (lambda e, pt=pt, g=g, n=n: e.tensor_copy(
                xT[:, g * 4:g * 4 + n, :], pt[:, 0:n * 128].rearrange("p (c t) -> p c t", c=n))),
                reads=[pb], writes=[xT_buf])


def load_weight_bf16(ph, nc, st, name, w_ap, kc, n, stage, stage_bufs, cast_engs=("dve", "pool")):
    wb = sb(st, nc, name, [128, kc, n], BF16)
    wv = w_ap.rearrange("(c p) n -> p c n", p=128)
    cnt = 0
    for c in range(kc):
        for n0 in range(0, n, stage[0].shape[1]):
            n1 = min(n, n0 + stage[0].shape[1])
            s = cnt % len(stage)
            ph.dma("sp", stage[s][:, 0:n1 - n0], wv[:, c, n0:n1], writes=[stage_bufs[s]])
            eng = cast_engs[cnt % len(cast_engs)]
            ph.add(eng, (lambda e, s=s, c=c, n0=n0, n1=n1: e.tensor_copy(wb[:, c, n0:n1], stage[s][:, 0:n1 - n0])),
                   reads=[stage_bufs[s]], writes=[ph.b(name)])
            cnt += 1
    return wb


def emit_rope(ph, eng, out, src, cos, sin, nh, hd, tmp1, tmp2, rbufs, wbufs, blocks):
    hb = hd // blocks // 2
    v = lambda a: a.rearrange("p (h b t f) -> p h b t f", h=nh, b=blocks, t=2, f=hb)
    tb = lambda a: a.rearrange("p (b t f) -> p b t f", b=blocks, t=2, f=hb)
    cosb = cos.unsqueeze(1).to_broadcast([128, nh, hd]).rearrange("p h d -> p (h d)") if False else None
    s3 = src.rearrange("p (h d) -> p h d", h=nh)
    t13 = tmp1.rearrange("p (h d) -> p h d", h=nh)
    cos3 = cos.unsqueeze(1).to_broadcast([128, nh, hd])
    ph.add(eng, lambda e: e.tensor_tensor(t13, s3, cos3, ALU.mult), reads=rbufs, writes=[wbufs[0]])
    sv, t2v = v(src), v(tmp2)
    sn = tb(sin)
    for t in range(2):
        sin_b = sn[:, :, t, :].unsqueeze(1).to_broadcast([128, nh, blocks, hb])
        ph.add(eng, (lambda e, t=t, sin_b=sin_b: e.tensor_tensor(t2v[:, :, :, t, :], sv[:, :, :, 1 - t, :], sin_b, ALU.mult)),
               reads=rbufs, writes=[wbufs[1]])
    ph.add(eng, lambda e: e.tensor_tensor(out, tmp1, tmp2, ALU.add), reads=[wbufs[0], wbufs[1]], writes=[wbufs[2]])


def rope3(ph, eng, out3, src3, cos, sin, nh, hd, blocks, t1, t2, rbufs, tbufs, wbuf):
    hb = hd // blocks // 2
    t13 = t1.rearrange("p (h d) -> p h d", h=nh)
    t23 = t2.rearrange("p (h d) -> p h d", h=nh)
    cos3 = cos.unsqueeze(1).to_broadcast([128, nh, hd])
    ph.add(eng, lambda e: e.tensor_tensor(t13, src3, cos3, ALU.mult), reads=rbufs, writes=[tbufs[0]])
    for b in range(blocks):
        for t in range(2):
            o0 = b * 2 * hb + t * hb
            s0 = b * 2 * hb + (1 - t) * hb
            sin_b = sin[:, o0:o0 + hb].unsqueeze(1).to_broadcast([128, nh, hb])
            ph.add(eng, (lambda e, o0=o0, s0=s0, sin_b=sin_b: e.tensor_tensor(
                t23[:, :, o0:o0 + hb], src3[:, :, s0:s0 + hb], sin_b, ALU.mult)),
                reads=rbufs, writes=[tbufs[1]])
    ph.add(eng, lambda e: e.tensor_tensor(out3, t13, t23, ALU.add), reads=[tbufs[0], tbufs[1]], writes=[wbuf])


def lowp(nc, f):
    def g(e):
        with nc.allow_low_precision(reason="exact small integer counts"):
            return f(e)
    return g


def bc(ap1d):
    return ap1d.partition_broadcast(128)


def phase_proj0(P, x_ap):
    nc = P.nc
    dr = P.dram
    with contextlib.ExitStack() as st:
        ph = Phase(nc, "p1")
        cst, idb = load_consts(ph, nc, st, P)
        stage = [sb(st, nc, f"p1_stg{i}", [128, 1536], F32) for i in range(2)]
        wb = load_weight_bf16(ph, nc, st, "p1_wb", dr["w_in"][:, :], 8, 1536, stage,
                              [ph.b("stg", i) for i in range(2)], cast_engs=("dve", "pool"))
        gq = sb(st, nc, "p1_gq", [128, 64], F32)
        gk = sb(st, nc, "p1_gk", [128, 64], F32)
        ph.dma("sp", gq[:, :], bc(dr["ab_q_norm"][0, :]), writes=[ph.b("gq")])
        ph.dma("sp", gk[:, :], bc(dr["ab_k_norm"][0, :]), writes=[ph.b("gk")])
        NSL = 2
        xs = [sb(st, nc, f"p1_xs{i}", [128, 1024], F32) for i in range(NSL)]
        tb = [sb(st, nc, f"p1_tb{i}", [128, 256], F32) for i in range(NSL)]
        xT = [sb(st, nc, f"p1_xT{i}", [128, 8, 128], BF16) for i in range(NSL)]
        pr = [sb(st, nc, f"p1_pr{i}", [128, 1536], F32) for i in range(NSL)]
        sq = sb(st, nc, "p1_sq", [128, 640], F32)
        ss = sb(st, nc, "p1_ss", [128, 10], F32)
        sd = sb(st, nc, "p1_sd", [128, 10], F32)
        rs = sb(st, nc, "p1_rs", [128, 10], F32)
        t0 = sb(st, nc, "p1_t0", [128, 640], F32)
        ta1 = sb(st, nc, "p1_ta1", [128, 640], F32)
        ta2 = sb(st, nc, "p1_ta2", [128, 640], F32)
        tb1 = sb(st, nc, "p1_rb1", [128, 640], F32)
        tb2 = sb(st, nc, "p1_rb2", [128, 640], F32)
        oa = [sb(st, nc, f"p1_oa{i}", [128, 640], BF16) for i in range(NSL)]
        ob = [sb(st, nc, f"p1_ob{i}", [128, 640], BF16) for i in range(NSL)]
        stA = [sb(st, nc, f"p1_stA{i}", [128, 5, 128], BF16) for i in range(NSL)]
        stB = [sb(st, nc, f"p1_stB{i}", [128, 5, 128], BF16) for i in range(NSL)]
        vst = [sb(st, nc, f"p1_vst{i}", [128, 4, 128], BF16) for i in range(NSL)]
        pT = [ps(st, nc, f"p1_pT{i}", [128, 512], F32) for i in range(2)]
        pp = [ps(st, nc, f"p1_pp{i}", [128, 512], F32) for i in range(3)]
        ptA = ps(st, nc, "p1_ptA", [128, 1024], BF16)
        ptB = ps(st, nc, "p1_ptB", [128, 1024], BF16)
        for s in range(NSL):
            ph.add("pool", (lambda e, s=s: e.memset(vst[s][:, :, :], 1.0)), writes=[ph.b("vst", s)])
        eps = sb(st, nc, "p1_eps", [128, 1], F32)
        ph.add("pool", lambda e: e.memset(eps[:, :], 1e-6), writes=[ph.b("eps")])
        qTA, qTB, kTA, kTB, vall = dr["qT_A"], dr["qT_B"], dr["kT_A"], dr["kT_B"], dr["v_all"]
        def stageA(i):
            s = i % NSL
            tok = slice(i * 128, (i + 1) * 128)
            B = lambda n: ph.b(n, s)
            ph.dma("sp", xs[s][:, :], x_ap[tok, :], writes=[B("xs")])
            ph.dma("sp", tb[s][:, :], dr["tab"][tok, 0:256], writes=[B("tb")])
            emit_xT(ph, xs[s], B("xs"), xT[s], B("xT"), pT, [ph.b("pT", 0), ph.b("pT", 1)], cst)
            for n in range(3):
                for c in range(8):
                    ph.add("pe", (lambda e, n=n, c=c, s=s: e.matmul(pp[n][:, :], xT[s][:, c, :], wb[:, c, n * 512:(n + 1) * 512],
                                                                     start=(c == 0), stop=(c == 7))),
                           reads=[B("xT"), ph.b("p1_wb")], writes=[ph.b("pp", n)])
                ph.add("act", (lambda e, n=n, s=s: e.copy(pr[s][:, n * 512:(n + 1) * 512], pp[n][:, :])),
                       reads=[ph.b("pp", n)], writes=[B("pr")])

        def stageB(i):
            s = i % NSL
            tok = slice(i * 128, (i + 1) * 128)
            B = lambda n: ph.b(n, s)
            ph.add("act", (lambda e, s=s: e.activation(sq[:, :], pr[s][:, 0:640], AF.Square)), reads=[B("pr")], writes=[ph.b("sq")])
            ph.add("dve", lambda e: e.tensor_reduce(ss[:, :], sq[:, :].rearrange("p (h d) -> p h d", h=10), AX.X, ALU.add),
                   reads=[ph.b("sq")], writes=[ph.b("ss")])
            ph.add("act", lambda e: e.activation(sd[:, :], ss[:, :], AF.Sqrt, bias=eps[:, :], scale=1.0 / 64),
                   reads=[ph.b("ss"), ph.b("eps")], writes=[ph.b("sd")])
            ph.add("dve", lambda e: e.reciprocal(rs[:, :], sd[:, :]), reads=[ph.b("sd")], writes=[ph.b("rs")])
            ph.add("dve", (lambda e, s=s: e.tensor_tensor(t0[:, :].rearrange("p (h d) -> p h d", h=10),
                                                          pr[s][:, 0:640].rearrange("p (h d) -> p h d", h=10),
                                                          rs[:, :].unsqueeze(2).to_broadcast([128, 10, 64]), ALU.mult)),
                   reads=[B("pr"), ph.b("rs")], writes=[ph.b("t0")])
            ph.add("dve", lambda e: e.tensor_tensor(t0[:, 0:512].rearrange("p (h d) -> p h d", h=8),
                                                    t0[:, 0:512].rearrange("p (h d) -> p h d", h=8),
                                                    gq[:, :].unsqueeze(1).to_broadcast([128, 8, 64]), ALU.mult),
                   reads=[ph.b("gq")], writes=[ph.b("t0")])
            ph.add("dve", lambda e: e.tensor_tensor(t0[:, 512:640].rearrange("p (h d) -> p h d", h=2),
                                                    t0[:, 512:640].rearrange("p (h d) -> p h d", h=2),
                                                    gk[:, :].unsqueeze(1).to_broadcast([128, 2, 64]), ALU.mult),
                   reads=[ph.b("gk")], writes=[ph.b("t0")])
            rope3(ph, "dve", oa[s][:, :].rearrange("p (h d) -> p h d", h=10), t0[:, :].rearrange("p (h d) -> p h d", h=10),
                  tb[s][:, 0:64], tb[s][:, 64:128], 10, 64, 2, ta1[:, :], ta2[:, :],
                  [ph.b("t0"), B("tb")], [ph.b("ta1"), ph.b("ta2")], B("oa"))
            rope3(ph, "pool", ob[s][:, :].rearrange("p (h d) -> p h d", h=10), pr[s][:, 640:1280].rearrange("p (h d) -> p h d", h=10),
                  tb[s][:, 128:192], tb[s][:, 192:256], 10, 64, 1, tb1[:, :], tb2[:, :],
                  [B("pr"), B("tb")], [ph.b("tb1"), ph.b("tb2")], B("ob"))
            ph.add("act", (lambda e, s=s: e.copy(vst[s][:, :, 0:64], pr[s][:, 1280:1536].rearrange("p (h d) -> p h d", h=4))),
                   reads=[B("pr")], writes=[B("vst")])
            ph.dma("sp", vall[:, :, i, :].rearrange("k p f -> p k f"), vst[s][:, :, :], reads=[B("vst")])
            for (src, sbuf_, ptt, ptn, stt, stn, qT, kT) in ((oa, "oa", ptA, "ptA", stA, "stA", qTA, kTA),
                                                             (ob, "ob", ptB, "ptB", stB, "stB", qTB, kTB)):
                for c in range(5):
                    ph.add("pe", (lambda e, src=src, ptt=ptt, c=c, s=s: e.transpose(ptt[:, c * 128:(c + 1) * 128],
                                                                                    src[s][:, c * 128:(c + 1) * 128], idb[:, :])),
                           reads=[B(sbuf_), ph.b("idb")], writes=[ph.b(ptn)])
                ph.add("act", (lambda e, ptt=ptt, stt=stt, s=s: e.copy(stt[s][:, :, :], ptt[:, 0:640].rearrange("p (c t) -> p c t", c=5))),
                       reads=[ph.b(ptn)], writes=[B(stn)])
                ph.dma("sp", qT[:, :, tok].rearrange("j p t -> p j t"), stt[s][:, 0:4, :], reads=[B(stn)])
                for kv in range(2):
                    for dup in range(2):
                        ph.dma("sp", kT[kv, dup * 64:(dup + 1) * 64, tok], stt[s][kv * 64:(kv + 1) * 64, 4, :], reads=[B(stn)])

        for i in range(NT + 1):
            if i < NT:
                stageA(i)
            if i >= 1:
                stageB(i - 1)
        ph.emit()


def phase_dense_attn(P, name, jobs, scale, win=False):
    nc = P.nc
    dr = P.dram
    with contextlib.ExitStack() as st:
        ph = Phase(nc, name)
        if win:
            cst, idb = load_consts(ph, nc, st, P)
            wm = sb(st, nc, f"{name}_wm", [128, 3072], BF16)
            wst = sb(st, nc, f"{name}_wst", [128, 3072], F32)
            ph.dma("sp", wst[:, :], dr["wmask"][:, :], writes=[ph.b("wst")])
            ph.add("dve", lambda e: e.tensor_copy(wm[:, :], wst[:, :]), reads=[ph.b("wst")], writes=[ph.b("wm")])
            vsk = sb(st, nc, f"{name}_vsk", [128, 128], BF16)
            ph.add("dve", lambda e: e.tensor_copy(vsk[:, :], cst[:, C_VSINK:C_VSINK + 128]), reads=[ph.b("cst")], writes=[ph.b("vsk")])
            sk = sb(st, nc, f"{name}_sk", [128, 8], F32)
            es = sb(st, nc, f"{name}_es", [128, 8], F32)
            ehb = sb(st, nc, f"{name}_ehb", [128, 8], BF16)
            ehf = sb(st, nc, f"{name}_ehf", [128, 8], F32)
            elo = sb(st, nc, f"{name}_elo", [128, 8], F32)
            ssel = sb(st, nc, f"{name}_ssel", [128, 8], F32)
            onesb = sb(st, nc, f"{name}_onesb", [128, 512], BF16)
            psink = [sb(st, nc, f"{name}_psink{i}", [128, 512], BF16) for i in range(2)]
            ph.dma("sp", sk[:, :], bc(dr["ab_sink"][0, :]), writes=[ph.b("sk")])
            ph.add("act", lambda e: e.activation(es[:, :], sk[:, :], AF.Exp), reads=[ph.b("sk")], writes=[ph.b("es")])
            ph.add("dve", lowp(nc, lambda e: e.tensor_copy(ehb[:, :], es[:, :])), reads=[ph.b("es")], writes=[ph.b("ehb")])
            ph.add("dve", lambda e: e.tensor_copy(ehf[:, :], ehb[:, :]), reads=[ph.b("ehb")], writes=[ph.b("ehf")])
            ph.add("dve", lambda e: e.tensor_tensor(elo[:, :], es[:, :], ehf[:, :], ALU.subtract), reads=[ph.b("es"), ph.b("ehf")], writes=[ph.b("elo")])
            ph.add("dve", lambda e: e.tensor_scalar(ehf[:, :], ehf[:, :], cst[:, C_IDENT:C_IDENT + 1], None, ALU.mult), reads=[ph.b("cst")], writes=[ph.b("ehf")])
            ph.add("dve", lambda e: e.scalar_tensor_tensor(ssel[:, :], elo[:, :], cst[:, C_IDENT + 1:C_IDENT + 2], ehf[:, :], ALU.mult, ALU.add),
                   reads=[ph.b("elo"), ph.b("ehf"), ph.b("cst")], writes=[ph.b("ssel")])
            ph.add("dve", lambda e: e.memset(onesb[:, :], 1.0), writes=[ph.b("onesb")])
        NSL = 2
        qs = [sb(st, nc, f"{name}_q{i}", [128, S], BF16) for i in range(NSL)]
        pad = (jobs[0]["K"] == 64)
        nk = 2 if pad else 1
        ks = [[sb(st, nc, f"{name}_k{i}_{hf}", [128, S], BF16) for hf in range(nk)] for i in range(NSL)]
        vs = [sb(st, nc, f"{name}_v{i}", [128, NT, 128], BF16) for i in range(NSL)]
        NP_ = 6 if win else 4
        pTs = [sb(st, nc, f"{name}_pT{i}", [128, 512], BF16) for i in range(NP_)]
        rec = sb(st, nc, f"{name}_rec", [64, 512], F32)
        onb = [sb(st, nc, f"{name}_on{i}", [64, 512], BF16) for i in range(2)]
        NS_ = 5 if win else 3
        sT = [ps(st, nc, f"{name}_sT{i}", [128, 512], F32) for i in range(NS_)]
        oacc = [ps(st, nc, f"{name}_oa{i}", [128, 512], F32) for i in range(2)]
        LA = 4 if win else 2
        if not pad:
            for i in range(NSL):
                ph.add("pool", (lambda e, i=i: e.memset(qs[i][64:128, :], 0.0)), writes=[ph.b("q", i)])
                ph.add("pool", (lambda e, i=i: e.memset(ks[i][0][64:128, :], 0.0)), writes=[ph.b("k", i)])
        else:
            for i in range(NSL):
                ph.add("pool", (lambda e, i=i: e.memset(ks[i][0][64:128, :], 0.0)), writes=[ph.b("k", i)])
                ph.add("pool", (lambda e, i=i: e.memset(ks[i][1][0:64, :], 0.0)), writes=[ph.b("k", i)])
        qk_prev = kk_prev = vv_prev = None
        slot = {"q": -1, "k": -1, "v": -1}
        cur = {}
        on_cnt = 0
        jslots = []

        def issue_loads(jb):
            for key, tiles, src in (("q", qs, jb["qsrc"]), ("k", ks, jb["ksrc"]), ("v", vs, jb["vsrc"])):
                if cur.get(key) != src[0]:
                    slot[key] = (slot[key] + 1) % NSL
                    sl = slot[key]
                    if key == "v":
                        ph.dma("sp", tiles[sl][:, :, :].rearrange("p c f -> p (c f)"), src[1].rearrange("p c f -> p (c f)"), writes=[ph.b(key, sl)])
                    elif key == "k" and pad:
                        ph.dma("sp", tiles[sl][0][0:64, :], src[1][0:64, :], writes=[ph.b(key, sl)])
                        ph.dma("sp", tiles[sl][1][64:128, :], src[1][64:128, :], writes=[ph.b(key, sl)])
                    elif key == "k":
                        rows = src[2]
                        ph.dma("sp", tiles[sl][0][0:rows, :], src[1], writes=[ph.b(key, sl)])
                    else:
                        rows = src[2]
                        ph.dma("sp", tiles[sl][0:rows, :], src[1], writes=[ph.b(key, sl)])
                    cur[key] = src[0]
            jslots.append(dict(slot))

        issue_loads(jobs[0])
        for ji, jb in enumerate(jobs):
            if ji + 1 < len(jobs):
                issue_loads(jobs[ji + 1])
            jsl = jslots[ji]
            q_t, k_t, v_t = qs[jsl["q"]], ks[jsl["k"]][(jb["pb"] // 64) if pad else 0], vs[jsl["v"]]
            qB, kB, vB = ph.b("q", jsl["q"]), ph.b("k", jsl["k"]), ph.b("v", jsl["v"])
            if "dbg_q9" in dr and name == "p9" and jb is jobs[0]:
                ph.dma("sp", dr["dbg_q9"][:, :], q_t[:, :], reads=[qB])
                ph.dma("sp", dr["dbg_k9"][:, :], k_t[:, :], reads=[kB])
            pb, K = jb["pb"], jb["K"]
            if win:
                steps = [(g, c) for g in range(16) for c in range(4 * g - 1, 4 * g + 5) if 0 <= c < NT]
                hh = jb["h"]
                psl = hh % 2
                ph.add("dve", (lambda e, psl=psl, hh=hh: e.tensor_scalar(psink[psl][:, :], onesb[:, :], ssel[:, hh:hh + 1], None, ALU.mult)),
                       reads=[ph.b("onesb"), ph.b("ssel")], writes=[ph.b("psink", psl)])
            else:
                steps = [(g, c) for g in range(16) for c in range(NT)]
            nsteps = len(steps)

            def qk(n, q_t=q_t, k_t=k_t, qB=qB, kB=kB, pb=pb, K=K):
                g, c = steps[n]
                b = n % NS_
                ph.add("pe", (lambda e: e.matmul(sT[b][:, :], k_t[:, c * 128:(c + 1) * 128],
                                                 q_t[:, g * 512:(g + 1) * 512], start=True, stop=True)),
                       reads=[qB, kB], writes=[ph.b("sT", b)])

            for n in range(min(LA, nsteps)):
                qk(n)
            for n in range(nsteps):
                g, c = steps[n]
                first = (n == 0) or steps[n - 1][0] != g
                last = (n == nsteps - 1) or steps[n + 1][0] != g
                if n + LA < nsteps:
                    qk(n + LA)
                b = n % NS_
                pslot = n % NP_
                ph.add("act", (lambda e, b=b, pslot=pslot: e.activation(pTs[pslot][:, :], sT[b][:, :], AF.Exp, scale=scale)),
                       reads=[ph.b("sT", b)], writes=[ph.b("pT", pslot)])
                if win:
                    r_ = c - (4 * g - 1)
                    ph.add("dve", (lambda e, pslot=pslot, r_=r_: e.tensor_tensor(pTs[pslot][:, :], pTs[pslot][:, :], wm[:, r_ * 512:(r_ + 1) * 512], ALU.mult)),
                           reads=[ph.b("wm")], writes=[ph.b("pT", pslot)])
                ob_ = g % 2
                ph.add("pe", (lambda e, ob_=ob_, c=c, pslot=pslot, v_t=v_t, first=first, last=last: e.matmul(
                    oacc[ob_][:, :], v_t[:, c, :], pTs[pslot][:, :], start=first, stop=(last and not win))),
                    reads=[vB, ph.b("pT", pslot)], writes=[ph.b("oacc", ob_)])
                if last and win:
                    ph.add("pe", (lambda e, ob_=ob_, psl=psl: e.matmul(oacc[ob_][:, :], vsk[:, :], psink[psl][:, :], start=False, stop=True)),
                           reads=[ph.b("vsk"), ph.b("psink", psl)], writes=[ph.b("oacc", ob_)])
                if last:
                    os_ = on_cnt % 2
                    on_cnt += 1
                    ph.add("dve", (lambda e, ob_=ob_: e.reciprocal(rec[:, :], oacc[ob_][64:128, :])),
                           reads=[ph.b("oacc", ob_)], writes=[ph.b("rec")])
                    ph.add("dve", (lambda e, ob_=ob_, os_=os_: e.tensor_tensor(onb[os_][:, :], oacc[ob_][0:64, :], rec[:, :], ALU.mult)),
                           reads=[ph.b("oacc", ob_), ph.b("rec")], writes=[ph.b("on", os_)])
                    ph.dma("sp", jb["out"][:, g * 512:(g + 1) * 512], onb[os_][:, :], reads=[ph.b("on", os_)])
                    if "dbg_rec" in dr and name == "p9" and jb is jobs[0] and g == 0:
                        ph.dma("sp", dr["dbg_rec"][:, :], rec[:, :], reads=[ph.b("rec")])
                        dnum = sb(st, nc, "p9_dnum", [128, 512], F32)
                        ph.add("dve", (lambda e, ob_=ob_: e.tensor_copy(dnum[:, :], oacc[ob_][:, :])), reads=[ph.b("oacc", ob_)], writes=[ph.b("dnum")])
                        ph.dma("sp", dr["dbg_num"][:, :], dnum[:, :], reads=[ph.b("dnum")])
        ph.emit()


def phase_win_attn(P):
    nc = P.nc
    dr = P.dram
    name = "p3"
    scale = 64 ** -0.5
    with contextlib.ExitStack() as st:
        ph = Phase(nc, name)
        cst, idb = load_consts(ph, nc, st, P)
        mk = sb(st, nc, "p3_mk", [128, 2, 128], BF16)
        ph.add("dve", lambda e: e.tensor_copy(mk[:, 0, :], cst[:, C_MPREV:C_MPREV + 128]), reads=[ph.b("cst")], writes=[ph.b("mk")])
        ph.add("dve", lambda e: e.tensor_copy(mk[:, 1, :], cst[:, C_MNEXT:C_MNEXT + 128]), reads=[ph.b("cst")], writes=[ph.b("mk")])
        sk = sb(st, nc, "p3_sk", [128, 8], F32)
        es = sb(st, nc, "p3_es", [128, 8], F32)
        ph.dma("sp", sk[:, :], bc(dr["ab_sink"][0, :]), writes=[ph.b("sk")])
        ph.add("act", lambda e: e.activation(es[:, :], sk[:, :], AF.Exp), reads=[ph.b("sk")], writes=[ph.b("es")])
        qs = [sb(st, nc, f"p3_q{i}", [128, S], BF16) for i in range(2)]
        ks = [sb(st, nc, f"p3_k{i}", [128, S], BF16) for i in range(2)]
        vs = [sb(st, nc, f"p3_v{i}", [128, NT, 128], BF16) for i in range(2)]
        pTs = [sb(st, nc, f"p3_pT{i}", [128, 384], BF16) for i in range(4)]
        den = sb(st, nc, "p3_den", [64, 512], F32)
        rec = sb(st, nc, "p3_rec", [64, 512], F32)
        onb = [sb(st, nc, f"p3_on{i}", [64, 512], BF16) for i in range(2)]
        sT = [ps(st, nc, f"p3_sT{i}", [128, 512], F32) for i in range(3)]
        oacc = [ps(st, nc, f"p3_oa{i}", [128, 512], F32) for i in range(2)]
        cnt = 0
        gcnt = 0
        for h in range(8):
            j, half, kv = h // 2, h % 2, h // 4
            pb = half * 64
            if half == 0:
                ph.dma("sp", qs[j % 2][:, :], dr["qT_B"][j, :, :], writes=[ph.b("q", j % 2)])
            if h % 4 == 0:
                ph.dma("sp", ks[kv][:, :], dr["kT_B"][kv, :, :], writes=[ph.b("k", kv)])
                ph.dma("sp", vs[kv][:, :, :].rearrange("p c f -> p (c f)"), dr["v_all"][2 + kv, :, :, :].rearrange("p c f -> p (c f)"), writes=[ph.b("v", kv)])
                if "dbgv0" in dr and kv == 0:
                    ph.dma("sp", dr["dbgv0"][:, :, :].rearrange("p c f -> p (c f)"), vs[0][:, :, :].rearrange("p c f -> p (c f)"), reads=[ph.b("v", 0)])
            q_t, k_t, v_t = qs[j % 2], ks[kv], vs[kv]
            qB, kB, vB = ph.b("q", j % 2), ph.b("k", kv), ph.b("v", kv)
            for g in range(16):
                ob_ = gcnt % 2
                gcnt += 1
                for ti in range(4):
                    i = g * 4 + ti
                    b = cnt % 3
                    pslot = cnt % 4
                    cnt += 1
                    chunks = [c for c in (i - 1, i, i + 1) if 0 <= c < NT]
                    for c in chunks:
                        sl = c - i + 1
                        ph.add("pe", (lambda e, b=b, sl=sl, c=c, i=i: e.matmul(sT[b][:, sl * 128:(sl + 1) * 128],
                                                                                k_t[pb:pb + 64, c * 128:(c + 1) * 128],
                                                                                q_t[pb:pb + 64, i * 128:(i + 1) * 128], start=True, stop=True)),
                               reads=[qB, kB], writes=[ph.b("sT", b)])
                    lo_, hi_ = (chunks[0] - i + 1) * 128, (chunks[-1] - i + 2) * 128
                    ph.add("act", (lambda e, b=b, pslot=pslot, lo_=lo_, hi_=hi_: e.activation(pTs[pslot][:, lo_:hi_], sT[b][:, lo_:hi_], AF.Exp, scale=scale)),
                           reads=[ph.b("sT", b)], writes=[ph.b("pT", pslot)])
                    for c in chunks:
                        sl = c - i + 1
                        if sl != 1:
                            m = 0 if sl == 0 else 1
                            ph.add("dve", (lambda e, pslot=pslot, sl=sl, m=m: e.tensor_tensor(pTs[pslot][:, sl * 128:(sl + 1) * 128],
                                                                                                pTs[pslot][:, sl * 128:(sl + 1) * 128], mk[:, m, :], ALU.mult)),
                                   reads=[ph.b("mk")], writes=[ph.b("pT", pslot)])
                    for ci, c in enumerate(chunks):
                        sl = c - i + 1
                        ph.add("pe", (lambda e, ob_=ob_, ti=ti, c=c, sl=sl, pslot=pslot, ci=ci, nch=len(chunks): e.matmul(
                            oacc[ob_][:, ti * 128:(ti + 1) * 128], v_t[:, c, :], pTs[pslot][:, sl * 128:(sl + 1) * 128],
                            start=(ci == 0), stop=(ci == nch - 1))),
                            reads=[vB, ph.b("pT", pslot)], writes=[ph.b("oacc", ob_)])
                os_ = gcnt % 2
                ph.add("dve", (lambda e, ob_=ob_, h=h: e.tensor_scalar(den[:, :], oacc[ob_][64:128, :], es[64:128, h:h + 1], None, ALU.add)),
                       reads=[ph.b("oacc", ob_), ph.b("es")], writes=[ph.b("den")])
                ph.add("dve", lambda e: e.reciprocal(rec[:, :], den[:, :]), reads=[ph.b("den")], writes=[ph.b("rec")])
                ph.add("dve", (lambda e, ob_=ob_, os_=os_: e.tensor_tensor(onb[os_][:, :], oacc[ob_][0:64, :], rec[:, :], ALU.mult)),
                       reads=[ph.b("oacc", ob_), ph.b("rec")], writes=[ph.b("on", os_)])
                ph.dma("sp", dr["oT"][512 + h * 64:512 + (h + 1) * 64, g * 512:(g + 1) * 512], onb[os_][:, :], reads=[ph.b("on", os_)])
            if "dbgv1" in dr and h == 3:
                ph.dma("sp", dr["dbgv1"][:, :, :].rearrange("p c f -> p (c f)"), vs[0][:, :, :].rearrange("p c f -> p (c f)"), reads=[ph.b("v", 0)])
        ph.emit()


def emit_ln(ph, nc, name, r, rB, out_ap, outB, g_b, b_b, junk, small, eps_t, eng2="pool"):
    sm, nm, ssq, sd, rstd = (small[:, k:k + 1] for k in range(5))
    ph.add("dve", lambda e: e.tensor_reduce(sm, r, AX.X, ALU.add), reads=[rB], writes=[ph.b(name, "sm")])
    ph.add("dve", lambda e: e.tensor_scalar(nm, sm, -1.0 / D, None, ALU.mult), reads=[ph.b(name, "sm")], writes=[ph.b(name, "nm")])
    ph.add("act", lambda e: e.activation(junk, r, AF.Square, bias=nm, scale=1.0, accum_out=ssq),
           reads=[rB, ph.b(name, "nm")], writes=[ph.b(name, "ssq"), ph.b(name, "junk")])
    ph.add("act", lambda e: e.activation(sd, ssq, AF.Sqrt, bias=eps_t, scale=1.0 / D), reads=[ph.b(name, "ssq")], writes=[ph.b(name, "sd")])
    ph.add("dve", lambda e: e.reciprocal(rstd, sd), reads=[ph.b(name, "sd")], writes=[ph.b(name, "rstd")])
    ph.add("dve", lambda e: e.tensor_scalar(r, r, nm, rstd, ALU.add, ALU.mult), reads=[ph.b(name, "nm"), ph.b(name, "rstd")], writes=[rB])
    ph.add(eng2, lambda e: e.tensor_tensor(r, r, g_b, ALU.mult), reads=[ph.b("lng")], writes=[rB])
    ph.add(eng2, lambda e: e.tensor_tensor(out_ap, r, b_b, ALU.add), reads=[rB, ph.b("lnb")], writes=[outB])


def phase_outproj(P, name, w_out, xin_rows, lng, lnb, w_router, xext, acc, affd):
    nc = P.nc
    dr = P.dram
    with contextlib.ExitStack() as st:
        ph = Phase(nc, name)
        cst, idb = load_consts(ph, nc, st, P)
        stage = [sb(st, nc, f"{name}_stg{i}", [128, 1024], F32) for i in range(2)]
        wo = load_weight_bf16(ph, nc, st, f"{name}_wo", w_out, 8, 1024, stage, [ph.b("stg", i) for i in range(2)])
        wr = sb(st, nc, f"{name}_wr", [128, 8, NE], F32)
        for hh in range(2):
            ph.dma("sp", wr[:, hh * 4:(hh + 1) * 4, :], w_router.rearrange("(c p) n -> p c n", p=128)[:, hh * 4:(hh + 1) * 4, :], writes=[ph.b("wr")])
        g_b = sb(st, nc, f"{name}_g", [128, D], F32)
        b_b = sb(st, nc, f"{name}_b", [128, D], F32)
        ph.dma("sp", g_b[:, :], bc(lng), writes=[ph.b("lng")])
        ph.dma("sp", b_b[:, :], bc(lnb), writes=[ph.b("lnb")])
        eps_t = sb(st, nc, f"{name}_eps", [128, 1], F32)
        ph.add("pool", lambda e: e.memset(eps_t[:, :], 1e-5), writes=[ph.b("eps")])
        oTs = [sb(st, nc, f"{name}_oT{i}", [128, 8, 512], BF16) for i in range(2)]
        xs = [sb(st, nc, f"{name}_xs{i}", [128, D], F32) for i in range(4)]
        r = [sb(st, nc, f"{name}_r{i}", [128, D], F32) for i in range(4)]
        xo = [sb(st, nc, f"{name}_xo{i}", [128, XW], F32) for i in range(2)]
        ax = [sb(st, nc, f"{name}_ax{i}", [128, D], F32) for i in range(2)]
        junk2 = [sb(st, nc, f"{name}_junk{i}", [128, D], BF16) for i in range(2)]
        small2 = [sb(st, nc, f"{name}_small{i}", [128, 8], F32) for i in range(2)]
        xnT2 = [sb(st, nc, f"{name}_xnT{i}", [128, 8, 128], F32) for i in range(2)]
        lg2 = [sb(st, nc, f"{name}_lg{i}", [128, 4], F32) for i in range(2)]
        ex2 = [sb(st, nc, f"{name}_ex{i}", [128, NE], F32) for i in range(2)]
        py = [ps(st, nc, f"{name}_py{i}", [128, 512], F32) for i in range(2)]
        pT2 = [ps(st, nc, f"{name}_pT{i}", [128, 512], F32) for i in range(2)]
        pl2 = [ps(st, nc, f"{name}_pl{i}", [128, 512], F32) for i in range(2)]
        oTv = dr["oT"][:, :].rearrange("(c p) t -> p c t", p=128)
        def stageA(i):
            s = i % 4
            g4, t4 = divmod(i, 4)
            gs = g4 % 2
            tok = slice(i * 128, (i + 1) * 128)
            B = lambda n: ph.b(n, s)
            if t4 == 0:
                for hh in range(2):
                    ph.dma("sp", oTs[gs][:, hh * 4:(hh + 1) * 4, :], oTv[:, hh * 4:(hh + 1) * 4, g4 * 512:(g4 + 1) * 512], writes=[ph.b("oTs", gs)])
            ph.dma("sp", xs[s][:, :], xin_rows(tok), writes=[B("xs")])
            for n in range(2):
                for c in range(8):
                    ph.add("pe", (lambda e, n=n, c=c, gs=gs, t4=t4: e.matmul(py[n][:, :], oTs[gs][:, c, t4 * 128:(t4 + 1) * 128],
                                                                             wo[:, c, n * 512:(n + 1) * 512], start=(c == 0), stop=(c == 7))),
                           reads=[ph.b("oTs", gs), ph.b(f"{name}_wo")], writes=[ph.b("py", n)])
                ph.add("dve", (lambda e, n=n, s=s: e.scalar_tensor_tensor(r[s][:, n * 512:(n + 1) * 512], xs[s][:, n * 512:(n + 1) * 512], ALPHA,
                                                                          py[n][:, :], ALU.mult, ALU.add)),
                       reads=[B("xs"), ph.b("py", n)], writes=[B("r")])

        def stageB(i):
            s = i % 2
            tok = slice(i * 128, (i + 1) * 128)
            B = lambda n: ph.b(n, s)
            junk, small, xnT, lg, ex, pTs_, pl = junk2[s], small2[s], xnT2[s], lg2[s], ex2[s], pT2[s], pl2[s]
            rr = r[i % 4][:, :]
            rB = ph.b("r", i % 4)
            sm, nm, ssq, sd, rstd = (small[:, k:k + 1] for k in range(5))
            ph.add("dve", lambda e: e.tensor_reduce(sm, rr, AX.X, ALU.add), reads=[rB], writes=[B("sm")]); yield
            ph.add("dve", lambda e: e.tensor_scalar(nm, sm, -1.0 / D, None, ALU.mult), reads=[B("sm")], writes=[B("nm")]); yield
            ph.add("act", lambda e: e.activation(junk[:, :], rr, AF.Square, bias=nm, scale=1.0, accum_out=ssq),
                   reads=[rB, B("nm")], writes=[B("ssq"), B("junk")]); yield
            ph.add("act", lambda e: e.activation(sd, ssq, AF.Sqrt, bias=eps_t[:, :], scale=1.0 / D), reads=[B("ssq")], writes=[B("sd")]); yield
            ph.add("dve", lambda e: e.reciprocal(rstd, sd), reads=[B("sd")], writes=[B("rstd")]); yield
            ph.add("dve", lambda e: e.tensor_scalar(rr, rr, nm, rstd, ALU.add, ALU.mult), reads=[B("nm"), B("rstd")], writes=[rB]); yield
            ph.add("pool", lambda e: e.tensor_tensor(rr, rr, g_b[:, :], ALU.mult), reads=[ph.b("lng")], writes=[rB]); yield
            ph.add("pool", lambda e: e.tensor_tensor(xo[s][:, 0:D], rr, b_b[:, :], ALU.add), reads=[rB, ph.b("lnb")], writes=[B("xo")]); yield
            ph.add("act", lambda e: e.mul(ax[s][:, :], xo[s][:, 0:D], ALPHA), reads=[B("xo")], writes=[B("ax")]); yield
            ph.dma("sp", acc[tok, :], ax[s][:, :], reads=[B("ax")]); yield
            for g in range(2):
                for c in range(4):
                    cc = g * 4 + c
                    ph.add("pe", (lambda e, c=c, cc=cc: e.transpose(pTs_[:, c * 128:(c + 1) * 128], xo[s][:, cc * 128:(cc + 1) * 128],
                                                                    cst[:, C_IDENT:C_IDENT + 128])),
                           reads=[B("xo"), ph.b("cst")], writes=[B("pT")])
                yield
                ph.add("act", (lambda e, g=g: e.copy(xnT[:, g * 4:(g + 1) * 4, :], pTs_[:, :].rearrange("p (c t) -> p c t", c=4))),
                       reads=[B("pT")], writes=[B("xnT")]); yield
            for c in range(8):
                ph.add("pe", (lambda e, c=c: e.matmul(pl[:, 0:NE], xnT[:, c, :], wr[:, c, :], start=(c == 0), stop=(c == 7))),
                       reads=[B("xnT"), ph.b("wr")], writes=[B("pl")])
            yield
            ph.add("dve", lambda e: e.tensor_reduce(lg[:, 0:1], pl[:, 0:NE], AX.X, ALU.max), reads=[B("pl")], writes=[B("lg0")]); yield
            ph.add("dve", lambda e: e.tensor_scalar(lg[:, 1:2], lg[:, 0:1], -1.0, None, ALU.mult), reads=[B("lg0")], writes=[B("lg1")]); yield
            ph.add("act", lambda e: e.activation(ex[:, :], pl[:, 0:NE], AF.Exp, bias=lg[:, 1:2], scale=1.0, accum_out=lg[:, 2:3]),
                   reads=[B("pl"), B("lg1")], writes=[B("ex"), B("lg2")]); yield
            ph.add("dve", lambda e: e.reciprocal(lg[:, 3:4], lg[:, 2:3]), reads=[B("lg2")], writes=[B("lg3")]); yield
            ph.add("dve", lambda e: e.tensor_scalar(xo[s][:, D:XW], ex[:, :], lg[:, 3:4], None, ALU.mult),
                   reads=[B("ex"), B("lg3")], writes=[B("xo")]); yield
            ph.dma("sp", xext[tok, :], xo[s][:, :], reads=[B("xo")]); yield
            ph.dma("sp", affd[tok, :], xo[s][:, D:XW], reads=[B("xo")]); yield

        def drive(gens):
            gens = list(gens)
            while gens:
                for g_ in list(gens):
                    try:
                        next(g_)
                    except StopIteration:
                        gens.remove(g_)

        stageA(0)
        stageA(1)
        for i in range(0, NT, 2):
            if i + 2 < NT:
                stageA(i + 2)
                stageA(i + 3)
            drive([stageB(i), stageB(i + 1)])
        ph.emit()


def phase_route(P, name, xext, idx):
    nc = P.nc
    dr = P.dram
    with contextlib.ExitStack() as st:
        ph = Phase(nc, name)
        cst, idb = load_consts(ph, nc, st, P)
        aff = sb(st, nc, f"{name}_aff", [128, 64, NE], F32)
        cmp_ = sb(st, nc, f"{name}_cmp", [128, 64, NE], F32)
        mT = sb(st, nc, f"{name}_mT", [128, NE, 64], F32)
        Cl = sb(st, nc, f"{name}_Cl", [128, NE, 64], F32)
        Cg = sb(st, nc, f"{name}_Cg", [128, NE, 64], F32)
        lo = sb(st, nc, f"{name}_lo", [128, NE], F32)
        mid = sb(st, nc, f"{name}_mid", [128, NE], F32)
        sel = sb(st, nc, f"{name}_sel", [128, NE], F32)
        cntp = sb(st, nc, f"{name}_cntp", [128, NE], BF16)
        ones = sb(st, nc, f"{name}_ones", [128, 128], BF16)
        trib = sb(st, nc, f"{name}_trib", [128, 128], BF16)
        idxf = sb(st, nc, f"{name}_idxf", [128, NE, 8], F32)
        junk = sb(st, nc, f"{name}_junk", [128, S], BF16)
        crow = [sb(st, nc, f"{name}_crow{i}", [128, S], F32) for i in range(2)]
        tot = ps(st, nc, f"{name}_tot", [128, 512], F32)
        off = ps(st, nc, f"{name}_off", [128, 512], F32)
        ph.dma("sp", aff[:, :, :].rearrange("p i e -> p (i e)"), dr["affd"][:, :].rearrange("(p i) w -> p (i w)", p=128), writes=[ph.b("aff")])
        ph.add("dve", lambda e: e.memset(ones[:, :], 1.0), writes=[ph.b("ones")])
        ph.add("dve", lambda e: e.tensor_copy(trib[:, :], cst[:, C_TRI:C_TRI + 128]), reads=[ph.b("cst")], writes=[ph.b("trib")])
        ph.add("dve", lambda e: e.memset(lo[:, :], 0.0), writes=[ph.b("lo")])
        for k in range(30):
            w = 0.5 ** (k + 1)
            ph.add("dve", (lambda e, w=w: e.tensor_scalar(mid[:, :], lo[:, :], w, None, ALU.add)), reads=[ph.b("lo")], writes=[ph.b("mid")])
            ph.add("dve", lambda e: e.tensor_tensor(cmp_[:, :, :], aff[:, :, :], mid[:, :].unsqueeze(1).to_broadcast([128, 64, NE]), ALU.is_ge),
                   reads=[ph.b("aff"), ph.b("mid")], writes=[ph.b("cmp")])
            ph.add("dve", lowp(nc, lambda e: e.tensor_reduce(cntp[:, :], cmp_[:, :, :].rearrange("p i e -> p e i"), AX.X, ALU.add)),
                   reads=[ph.b("cmp")], writes=[ph.b("cntp")])
            ph.add("pe", lambda e: e.matmul(tot[:, 0:NE], ones[:, :], cntp[:, :], start=True, stop=True),
                   reads=[ph.b("ones"), ph.b("cntp")], writes=[ph.b("tot")])
            ph.add("dve", lambda e: e.tensor_scalar(sel[:, :], tot[:, 0:NE], CAP - 0.5, None, ALU.is_ge), reads=[ph.b("tot")], writes=[ph.b("sel")])
            ph.add("dve", (lambda e, w=w: e.scalar_tensor_tensor(lo[:, :], sel[:, :], w, lo[:, :], ALU.mult, ALU.add)),
                   reads=[ph.b("sel")], writes=[ph.b("lo")])
        ph.add("dve", lambda e: e.tensor_tensor(mT[:, :, :].rearrange("p e i -> p i e"), aff[:, :, :],
                                                lo[:, :].unsqueeze(1).to_broadcast([128, 64, NE]), ALU.is_ge),
               reads=[ph.b("aff"), ph.b("lo")], writes=[ph.b("mT")])
        ph.add("dve", lambda e: e.tensor_tensor_scan(Cl[:, :, :].rearrange("p e i -> p (e i)"), cst[:, C_KEEP:C_KEEP + NE * 64],
                                                     mT[:, :, :].rearrange("p e i -> p (e i)"), 0.0, ALU.mult, ALU.add),
               reads=[ph.b("mT"), ph.b("cst")], writes=[ph.b("Cl")])
        ph.add("dve", lambda e: e.tensor_copy(cntp[:, :], Cl[:, :, 63]), reads=[ph.b("Cl")], writes=[ph.b("cntp")])
        ph.add("pe", lambda e: e.matmul(off[:, 0:NE], trib[:, :], cntp[:, :], start=True, stop=True),
               reads=[ph.b("trib"), ph.b("cntp")], writes=[ph.b("off")])
        ph.add("dve", lambda e: e.tensor_copy(sel[:, :], off[:, 0:NE]), reads=[ph.b("off")], writes=[ph.b("sel")])
        ph.add("dve", lambda e: e.tensor_tensor(Cg[:, :, :], Cl[:, :, :], sel[:, :].unsqueeze(2).to_broadcast([128, NE, 64]), ALU.add),
               reads=[ph.b("Cl"), ph.b("sel")], writes=[ph.b("Cg")])
        cd = dr[f"{name}_Cd"]
        for e4 in range(4):
            ph.dma("sp", cd[e4 * 4:(e4 + 1) * 4, :].rearrange("e (p i) -> p e i", p=128), Cg[:, e4 * 4:(e4 + 1) * 4, :], reads=[ph.b("Cg")], writes=[ph.b("Cd", e4)])
        junk2 = sb(st, nc, f"{name}_junk2", [128, S], BF16)
        svh = sb(st, nc, f"{name}_svh", [128, 8], F32)
        ph.add("dve", lambda e: e.tensor_scalar(svh[:, :], cst[:, C_SVEC:C_SVEC + 8], 0.5, None, ALU.add), reads=[ph.b("cst")], writes=[ph.b("svh")])
        for e_ in range(NE):
            s = e_ % 2
            ph.dma("sp", crow[s][:, :], bc(cd[e_, :]), reads=[ph.b("Cd", e_ // 4)], writes=[ph.b("crow", s)])
            for j in range(8):
                if j % 2 == 0:
                    ph.add("dve", (lambda e, s=s, e_=e_, j=j: e.tensor_scalar(junk[:, :], crow[s][:, :], cst[:, C_SVEC + j:C_SVEC + j + 1], None,
                                                                               ALU.is_le, ALU.add, accum_out=idxf[:, e_, j:j + 1])),
                           reads=[ph.b("crow", s), ph.b("cst")], writes=[ph.b("idxf", "dve"), ph.b("junk")])
                else:
                    ph.add("act", (lambda e, s=s, e_=e_, j=j: e.activation(junk2[:, :], crow[s][:, :], AF.Sign, bias=svh[:, j:j + 1], scale=-1.0,
                                                                            accum_out=idxf[:, e_, j:j + 1])),
                           reads=[ph.b("crow", s), ph.b("svh")], writes=[ph.b("idxf", "act"), ph.b("junk2")])
        i4 = idxf[:, :, :].rearrange("p e (j t) -> p e j t", t=2)[:, :, :, 1]
        ph.add("dve", lambda e: e.tensor_scalar(i4, i4, 0.5, float(S // 2), ALU.mult, ALU.add), reads=[ph.b("idxf", "act")], writes=[ph.b("idxf", "act")])
        ph.add("dve", lambda e: e.tensor_copy(idx[:, :, :], idxf[:, :, :]), reads=[ph.b("idxf", "act"), ph.b("idxf", "dve")], writes=[ph.b("idx")])
        if "dbg_idx" in dr and name == "p5":
            ph.dma("sp", dr["dbg_idx"][:, :], idx[:, :, :].rearrange("p e j -> p (e j)"), reads=[ph.b("idx")])
            ph.dma("sp", dr["dbg_lo"][:, :], lo[:, :], reads=[ph.b("lo")])
        ph.emit()


def phase_experts(P, name, l, xext, acc, idx):
    nc = P.nc
    dr = P.dram
    with contextlib.ExitStack() as st:
        ph = Phase(nc, name)
        cst, idb = load_consts(ph, nc, st, P)
        NST = 3
        stage = [sb(st, nc, f"{name}_stg{i}", [128, 2048], F32) for i in range(NST)]
        wts = {k: [sb(st, nc, f"{name}_{k}{i}", [128, 8, 1024], BF16) for i in range(2)] for k in ("wg", "wu", "wd")}
        xg = [sb(st, nc, f"{name}_xg{i}", [128, XW], F32) for i in range(2)]
        xgT2 = [sb(st, nc, f"{name}_xgT{i}", [128, 8, 1024], BF16) for i in range(2)]
        hT = sb(st, nc, f"{name}_hT", [128, 8, 1024], BF16)
        gt = sb(st, nc, f"{name}_gt", [128, 2, 8], F32)
        sg = [sb(st, nc, f"{name}_sg{i}", [128, 512], F32) for i in range(2)]
        yo = [sb(st, nc, f"{name}_yo{i}", [128, D], F32) for i in range(4)]
        pT = [ps(st, nc, f"{name}_pT{i}", [128, 512], F32) for i in range(2)]
        pg = [ps(st, nc, f"{name}_pg{i}", [128, 512], F32) for i in range(2)]
        pu = [ps(st, nc, f"{name}_pu{i}", [128, 512], F32) for i in range(2)]
        py = [ps(st, nc, f"{name}_py{i}", [128, 512], F32) for i in range(2)]
        srcs = {"wg": dr["moe_w_gate"], "wu": dr["moe_w_up"], "wd": dr["moe_w_down"]}
        stg_cnt = [0]
        cast_engs = ("pool", "dve", "pool", "act")

        def load_w(e_):
            sl = e_ % 2
            for k in ("wg", "wu", "wd"):
                wv = srcs[k][l, e_, :, :].rearrange("(c p) n -> p c n", p=128)
                for c2 in range(4):
                    s = stg_cnt[0] % NST
                    eng = cast_engs[stg_cnt[0] % len(cast_engs)]
                    stg_cnt[0] += 1
                    ph.dma("sp", stage[s][:, :].rearrange("p (c n) -> p c n", c=2), wv[:, 2 * c2:2 * c2 + 2, :], writes=[ph.b("stg", s)])
                    dst = wts[k][sl][:, 2 * c2:2 * c2 + 2, :]
                    srcv = stage[s][:, :].rearrange("p (c n) -> p c n", c=2)
                    if eng == "act":
                        ph.add("act", (lambda e, dst=dst, srcv=srcv: e.copy(dst, srcv)), reads=[ph.b("stg", s)], writes=[ph.b(k, sl)])
                    else:
                        ph.add(eng, (lambda e, dst=dst, srcv=srcv: e.tensor_copy(dst, srcv)), reads=[ph.b("stg", s)], writes=[ph.b(k, sl)])

        load_w(0)
        prev_sc = []

        def emit_gather(e_, j):
            s = (e_ * 8 + j) % 2
            ph.add("pool", (lambda e, s=s, e_=e_, j=j: e.indirect_dma_start(
                out=xg[s][:, :], out_offset=None, in_=xext[:, :],
                in_offset=bass.IndirectOffsetOnAxis(ap=idx[:, e_, j:j + 1], axis=0))),
                reads=[ph.b("idx")], writes=[ph.b("xg", s)], dma=True)

        def emit_trans(e_, j):
            s = (e_ * 8 + j) % 2
            xs_ = e_ % 2
            emit_xT(ph, xg[s], ph.b("xg", s), xgT2[xs_][:, :, j * 128:(j + 1) * 128], ph.b("xgT", xs_), pT, [ph.b("pT", 0), ph.b("pT", 1)], cst)
            ph.add("act", (lambda e, s=s, e_=e_, j=j, xs_=xs_: e.copy(gt[:, xs_, j:j + 1], xg[s][:, D + e_:D + e_ + 1])),
                   reads=[ph.b("xg", s)], writes=[ph.b("gt", xs_)])

        emit_gather(0, 0)
        for j in range(8):
            if j + 1 < 8:
                emit_gather(0, j + 1)
            emit_trans(0, j)
        for e_ in range(NE):
            sl = e_ % 2
            wg, wu, wd = wts["wg"][sl], wts["wu"][sl], wts["wd"][sl]
            gsl = e_ % 2
            xgT = xgT2[sl]
            xgB = ph.b("xgT", sl)
            if e_ + 1 < NE:
                load_w(e_ + 1)
                emit_gather(e_ + 1, 0)
            n = 0
            for fc in range(8):
                if e_ + 1 < NE and fc + 1 < 8:
                    emit_gather(e_ + 1, fc + 1)
                for half in range(2):
                    b = n % 2
                    n += 1
                    for (wt, pt_, nm, kn) in ((wg, pg, "pg", "wg"), (wu, pu, "pu", "wu")):
                        for c in range(8):
                            ph.add("pe", (lambda e, wt=wt, pt_=pt_, b=b, c=c, fc=fc, half=half, xgT=xgT: e.matmul(
                                pt_[b][:, :], wt[:, c, fc * 128:(fc + 1) * 128], xgT[:, c, half * 512:(half + 1) * 512],
                                start=(c == 0), stop=(c == 7))),
                                reads=[ph.b(kn, sl), xgB], writes=[ph.b(nm, b)])
                    ph.add("act", (lambda e, b=b: e.activation(sg[b][:, :], pg[b][:, :], AF.Silu)), reads=[ph.b("pg", b)], writes=[ph.b("sg", b)])
                    ph.add("dve", (lambda e, b=b, fc=fc, half=half: e.tensor_tensor(hT[:, fc, half * 512:(half + 1) * 512], sg[b][:, :], pu[b][:, :], ALU.mult)),
                           reads=[ph.b("sg", b), ph.b("pu", b)], writes=[ph.b("hT")])
                if e_ + 1 < NE:
                    emit_trans(e_ + 1, fc)
            new_sc = []
            for j in range(8):
                ys = (e_ * 8 + j) % 4
                for half in range(2):
                    for fc in range(8):
                        ph.add("pe", (lambda e, half=half, fc=fc, j=j, wd=wd: e.matmul(py[half][:, :], hT[:, fc, j * 128:(j + 1) * 128],
                                                                                wd[:, fc, half * 512:(half + 1) * 512], start=(fc == 0), stop=(fc == 7))),
                               reads=[ph.b("hT"), ph.b("wd", sl)], writes=[ph.b("py", half)])
                    eng = "act" if half == 0 else "dve"
                    if eng == "act":
                        ph.add("act", (lambda e, ys=ys, half=half, gsl=gsl, j=j: e.mul(yo[ys][:, half * 512:(half + 1) * 512], py[half][:, :], gt[:, gsl, j:j + 1])),
                               reads=[ph.b("py", half), ph.b("gt", gsl)], writes=[ph.b("yo", ys)])
                    else:
                        ph.add("dve", (lambda e, ys=ys, half=half, gsl=gsl, j=j: e.tensor_scalar(yo[ys][:, half * 512:(half + 1) * 512], py[half][:, :],
                                                                                                  gt[:, gsl, j:j + 1], None, ALU.mult)),
                               reads=[ph.b("py", half), ph.b("gt", gsl)], writes=[ph.b("yo", ys)])
                if "dbg_xg" in dr and name == "p6" and e_ == 0 and j == 0:
                    ph.dma("sp", dr["dbg_yo"][:, :], yo[ys][:, :], reads=[ph.b("yo", ys)])
                    ph.dma("sp", dr["dbg_hT"][:, :], hT[:, :, :].rearrange("p c t -> p (c t)"), reads=[ph.b("hT")])
                op = ph.add("pool", (lambda e, ys=ys, e_=e_, j=j: e.indirect_dma_start(
                    out=acc[:, :], out_offset=bass.IndirectOffsetOnAxis(ap=idx[:, e_, j:j + 1], axis=0),
                    in_=yo[ys][:, :], in_offset=None, compute_op=ALU.add)),
                    reads=[ph.b("idx"), ph.b("yo", ys)], writes=[], dma=True)
                for d in prev_sc:
                    if d not in op.deps:
                        op.deps.append(d)
                new_sc.append(op)
            prev_sc = new_sc
        ph.emit()


def phase_ln(P, name, acc, lng, lnb, out_rows):
    nc = P.nc
    with contextlib.ExitStack() as st:
        ph = Phase(nc, name)
        g_b = sb(st, nc, f"{name}_g", [128, D], F32)
        b_b = sb(st, nc, f"{name}_b", [128, D], F32)
        ph.dma("sp", g_b[:, :], bc(lng), writes=[ph.b("lng")])
        ph.dma("sp", b_b[:, :], bc(lnb), writes=[ph.b("lnb")])
        eps_t = sb(st, nc, f"{name}_eps", [128, 1], F32)
        ph.add("pool", lambda e: e.memset(eps_t[:, :], 1e-5), writes=[ph.b("eps")])
        r = [sb(st, nc, f"{name}_r{i}", [128, D], F32) for i in range(3)]
        o = [sb(st, nc, f"{name}_o{i}", [128, D], F32) for i in range(3)]
        junk = sb(st, nc, f"{name}_junk", [128, D], BF16)
        small = sb(st, nc, f"{name}_small", [128, 8], F32)
        for i in range(NT + 1):
            if i < NT:
                s = i % 3
                tok = slice(i * 128, (i + 1) * 128)
                ph.dma("sp", r[s][:, :], acc[tok, :], writes=[ph.b("r", s)])
            if i >= 1:
                s = (i - 1) % 3
                tok = slice((i - 1) * 128, i * 128)
                emit_ln(ph, nc, name, r[s][:, :], ph.b("r", s), o[s][:, :], ph.b("o", s), g_b[:, :], b_b[:, :], junk[:, :], small[:, :], eps_t[:, :])
                ph.dma("sp", out_rows(tok), o[s][:, :], reads=[ph.b("o", s)])
        ph.emit()


def phase_proj_mla(P, xin_rows):
    nc = P.nc
    dr = P.dram
    name = "p8"
    with contextlib.ExitStack() as st:
        ph = Phase(nc, name)
        cst, idb = load_consts(ph, nc, st, P)
        stage = [sb(st, nc, f"p8_stg{i}", [128, 2048], F32) for i in range(2)]
        sB = [ph.b("stg", i) for i in range(2)]
        wdn = load_weight_bf16(ph, nc, st, "p8_wdn", dr["mla_w_down"][0, :, :], 8, 416, stage, sB)
        wuq = load_weight_bf16(ph, nc, st, "p8_wuq", dr["mla_w_uq"][0, :, :], 2, 1536, stage, sB)
        wukv = load_weight_bf16(ph, nc, st, "p8_wukv", dr["mla_w_ukv"][0, :, :], 1, 2048, stage, sB)
        gq = sb(st, nc, "p8_gq", [128, 256], F32)
        gk = sb(st, nc, "p8_gk", [128, 128], F32)
        ph.dma("sp", gq[:, :], bc(dr["mla_q_norm"][0, :]), writes=[ph.b("gq")])
        ph.dma("sp", gk[:, :], bc(dr["mla_kv_norm"][0, :]), writes=[ph.b("gk")])
        eps = sb(st, nc, "p8_eps", [128, 1], F32)
        ph.add("pool", lambda e: e.memset(eps[:, :], 1e-6), writes=[ph.b("eps")])
        xs = [sb(st, nc, f"p8_xs{i}", [128, D], F32) for i in range(2)]
        tb = [sb(st, nc, f"p8_tb{i}", [128, 64], F32) for i in range(2)]
        xT = [sb(st, nc, f"p8_xT{i}", [128, 8, 128], BF16) for i in range(2)]
        pr2 = [sb(st, nc, f"p8_pr{i}", [128, 416], F32) for i in range(2)]
        junk = sb(st, nc, "p8_junk", [128, 256], BF16)
        sm = sb(st, nc, "p8_sm", [128, 8], F32)
        cn = sb(st, nc, "p8_cn", [128, 384], BF16)
        cnT = sb(st, nc, "p8_cnT", [128, 3, 128], BF16)
        qsb = sb(st, nc, "p8_qs", [128, 1536], F32)
        kvs = sb(st, nc, "p8_kvs", [128, 2048], F32)
        qb = sb(st, nc, "p8_qb", [128, 16, 96], BF16)
        kb = sb(st, nc, "p8_kb", [128, 16, 96], BF16)
        kr = sb(st, nc, "p8_kr", [128, 32], BF16)
        t1 = sb(st, nc, "p8_t1", [128, 512], F32)
        t2 = sb(st, nc, "p8_t2", [128, 512], F32)
        t3 = sb(st, nc, "p8_t3", [128, 32], F32)
        t4 = sb(st, nc, "p8_t4", [128, 32], F32)
        stq = [sb(st, nc, f"p8_stq{i}", [128, 16, 128], BF16) for i in range(2)]
        stk = [sb(st, nc, f"p8_stk{i}", [128, 16, 128], BF16) for i in range(2)]
        vst = [sb(st, nc, f"p8_vst{i}", [128, 16, 128], BF16) for i in range(2)]
        pT = [ps(st, nc, f"p8_pT{i}", [128, 512], F32) for i in range(2)]
        pp = ps(st, nc, "p8_pp", [128, 512], F32)
        pc = ps(st, nc, "p8_pc", [128, 1024], BF16)
        pb_ = [ps(st, nc, f"p8_pb{i}", [128, 512], F32) for i in range(4)]
        for s in range(2):
            ph.add("pool", (lambda e, s=s: e.memset(vst[s][:, :, :], 1.0)), writes=[ph.b("vst", s)])
        def stageA(i):
            s = i % 2
            tok = slice(i * 128, (i + 1) * 128)
            B = lambda n: ph.b(n, s)
            ph.dma("sp", xs[s][:, :], xin_rows(tok), writes=[B("xs")])
            ph.dma("sp", tb[s][:, :], dr["tab"][tok, 256:320], writes=[B("tb")])
            emit_xT(ph, xs[s], B("xs"), xT[s], B("xT"), pT, [ph.b("pT", 0), ph.b("pT", 1)], cst)
            for c in range(8):
                ph.add("pe", (lambda e, c=c, s=s: e.matmul(pp[:, 0:416], xT[s][:, c, :], wdn[:, c, :], start=(c == 0), stop=(c == 7))),
                       reads=[B("xT"), ph.b("p8_wdn")], writes=[ph.b("pp")])
            ph.add("act", (lambda e, s=s: e.copy(pr2[s][:, :], pp[:, 0:416])), reads=[ph.b("pp")], writes=[ph.b("pr", s)])

        def stageB(i):
            s = i % 2
            tok = slice(i * 128, (i + 1) * 128)
            B = lambda n: ph.b(n, s)
            pr = pr2[s]
            prB = ph.b("pr", s)
            ph.add("act", lambda e: e.activation(junk[:, 0:256], pr[:, 0:256], AF.Square, accum_out=sm[:, 0:1]), reads=[prB], writes=[ph.b("sm0"), ph.b("junk")])
            ph.add("act", lambda e: e.activation(junk[:, 0:128], pr[:, 256:384], AF.Square, accum_out=sm[:, 1:2]), reads=[prB], writes=[ph.b("sm1"), ph.b("junk")])
            ph.add("act", lambda e: e.activation(sm[:, 2:3], sm[:, 0:1], AF.Sqrt, bias=eps[:, :], scale=1.0 / 256), reads=[ph.b("sm0"), ph.b("eps")], writes=[ph.b("sm2")])
            ph.add("act", lambda e: e.activation(sm[:, 3:4], sm[:, 1:2], AF.Sqrt, bias=eps[:, :], scale=1.0 / 128), reads=[ph.b("sm1"), ph.b("eps")], writes=[ph.b("sm3")])
            ph.add("dve", lambda e: e.reciprocal(sm[:, 4:6], sm[:, 2:4]), reads=[ph.b("sm2"), ph.b("sm3")], writes=[ph.b("sm4")])
            ph.add("dve", lambda e: e.scalar_tensor_tensor(cn[:, 0:256], pr[:, 0:256], sm[:, 4:5], gq[:, :], ALU.mult, ALU.mult),
                   reads=[prB, ph.b("sm4"), ph.b("gq")], writes=[ph.b("cn")])
            ph.add("dve", lambda e: e.scalar_tensor_tensor(cn[:, 256:384], pr[:, 256:384], sm[:, 5:6], gk[:, :], ALU.mult, ALU.mult),
                   reads=[prB, ph.b("sm4"), ph.b("gk")], writes=[ph.b("cn")])
            for c in range(3):
                ph.add("pe", (lambda e, c=c: e.transpose(pc[:, c * 128:(c + 1) * 128], cn[:, c * 128:(c + 1) * 128], idb[:, :])),
                       reads=[ph.b("cn"), ph.b("idb")], writes=[ph.b("pc")])
            ph.add("act", lambda e: e.copy(cnT[:, :, :], pc[:, 0:384].rearrange("p (c t) -> p c t", c=3)), reads=[ph.b("pc")], writes=[ph.b("cnT")])
            for n in range(3):
                for c in range(2):
                    ph.add("pe", (lambda e, n=n, c=c: e.matmul(pb_[n][:, :], cnT[:, c, :], wuq[:, c, n * 512:(n + 1) * 512], start=(c == 0), stop=(c == 1))),
                           reads=[ph.b("cnT"), ph.b("p8_wuq")], writes=[ph.b("pb", n)])
                ph.add("act", (lambda e, n=n: e.copy(qsb[:, n * 512:(n + 1) * 512], pb_[n][:, :])), reads=[ph.b("pb", n)], writes=[ph.b("qs")])
            q3 = qsb[:, :].rearrange("p (h d) -> p h d", h=16)
            ph.add("pool", lambda e: e.tensor_copy(qb[:, :, 0:64], q3[:, :, 0:64]), reads=[ph.b("qs")], writes=[ph.b("qb")])
            rope3(ph, "dve", qb[:, :, 64:96], q3[:, :, 64:96], tb[s][:, 0:32], tb[s][:, 32:64], 16, 32, 1, t1[:, :], t2[:, :],
                  [ph.b("qs"), B("tb")], [ph.b("t1"), ph.b("t2")], ph.b("qb"))
            for n in range(4):
                bn = (3 + n) % 4
                ph.add("pe", (lambda e, n=n, bn=bn: e.matmul(pb_[bn][:, :], cnT[:, 2, :], wukv[:, 0, n * 512:(n + 1) * 512], start=True, stop=True)),
                       reads=[ph.b("cnT"), ph.b("p8_wukv")], writes=[ph.b("pb", bn)])
                ph.add("act", (lambda e, n=n, bn=bn: e.copy(kvs[:, n * 512:(n + 1) * 512], pb_[bn][:, :])), reads=[ph.b("pb", bn)], writes=[ph.b("kvs")])
            kv3 = kvs[:, :].rearrange("p (h d) -> p h d", h=16)
            ph.add("pool", lambda e: e.tensor_copy(kb[:, :, 0:64], kv3[:, :, 0:64]), reads=[ph.b("kvs")], writes=[ph.b("kb")])
            ph.add("pool", (lambda e, s=s: e.tensor_copy(vst[s][:, :, 0:64], kv3[:, :, 64:128])), reads=[ph.b("kvs")], writes=[B("vst")])
            for h4 in range(4):
                ph.dma("sp", dr["v_C"][h4 * 4:(h4 + 1) * 4, :, i, :].rearrange("k p f -> p k f"), vst[s][:, h4 * 4:(h4 + 1) * 4, :], reads=[B("vst")])
            rope3(ph, "dve", kr[:, :].unsqueeze(1), pr[:, 384:416].unsqueeze(1), tb[s][:, 0:32], tb[s][:, 32:64], 1, 32, 1, t3[:, :], t4[:, :],
                  [prB, B("tb")], [ph.b("t3"), ph.b("t4")], ph.b("kr"))
            ph.add("dve", lambda e: e.tensor_copy(kb[:, :, 64:96], kr[:, :].unsqueeze(1).to_broadcast([128, 16, 32])),
                   reads=[ph.b("kr")], writes=[ph.b("kb")])
            for (src, sn, stt, stn, dst) in ((qb, "qb", stq, "stq", dr["qT_C"]), (kb, "kb", stk, "stk", dr["kT_C"])):
                for hh in range(2):
                    bnk = [pb_[0], pb_[1]] if sn == "qb" else [pb_[2], pb_[3]]
                    bnn = [0, 1] if sn == "qb" else [2, 3]
                    ptile = bnk[hh]
                    pv = ptile[:, :].bitcast(BF16)
                    for h8 in range(8):
                        h = hh * 8 + h8
                        ph.add("pe", (lambda e, pv=pv, h8=h8, h=h, src=src: e.transpose(pv[0:96, h8 * 128:(h8 + 1) * 128], src[:, h, :], idb[:, :])),
                               reads=[ph.b(sn), ph.b("idb")], writes=[ph.b("pb", bnn[hh])])
                    ph.add("act", (lambda e, pv=pv, hh=hh, stt=stt, s=s: e.copy(stt[s][0:96, hh * 8:(hh + 1) * 8, :],
                                                                                 pv[0:96, :].rearrange("p (c t) -> p c t", c=8))),
                           reads=[ph.b("pb", bnn[hh])], writes=[B(stn)])
                for h4 in range(4):
                    ph.dma("sp", dst[h4 * 4:(h4 + 1) * 4, :, tok].rearrange("h p t -> p h t"), stt[s][0:96, h4 * 4:(h4 + 1) * 4, :], reads=[B(stn)])
        for i in range(NT + 1):
            if i < NT:
                stageA(i)
            if i >= 1:
                stageB(i - 1)
        ph.emit()


PHASES = ["p1", "p2", "p3", "p4", "p5", "p6", "p7", "p8", "p9", "p10", "p11", "p12", "p13"]


def build(upto="p13", dbg=()):
    P = Prog(dbg)
    nc = P.nc
    x = P.din("x", [S, D])
    P.din("w_in", [D, 1536])
    P.din("ab_q_norm", [1, 64])
    P.din("ab_k_norm", [1, 64])
    P.din("ab_sink", [1, 8])
    P.din("ab_w_out", [D, D])
    P.din("mla_w_down", [1, D, 416])
    P.din("mla_q_norm", [1, 256])
    P.din("mla_kv_norm", [1, 128])
    P.din("mla_w_uq", [1, 256, 1536])
    P.din("mla_w_ukv", [1, 128, 2048])
    P.din("mla_w_out", [D, D])
    for n in ("ln_mix_g", "ln_mix_b", "ln_ffn_g", "ln_ffn_b"):
        P.din(n, [2, D])
    P.din("moe_router", [2, D, NE])
    for n in ("moe_w_gate", "moe_w_up", "moe_w_down"):
        P.din(n, [2, NE, D, D])
    P.din("tab", [S, 320])
    P.din("cst", [128, C_W])
    P.din("wmask", [128, 3072])
    P.dtmp("qT_A", [4, 128, S], BF16)
    P.dtmp("qT_B", [4, 128, S], BF16)
    P.dtmp("kT_A", [2, 128, S], BF16)
    P.dtmp("kT_B", [2, 128, S], BF16)
    P.dtmp("v_all", [4, 128, NT, 128], BF16)
    P.dtmp("oT", [D, S], BF16)
    P.dtmp("x1ext", [S, XW])
    P.dtmp("acc", [S, D])
    P.dtmp("affd", [S, NE])
    P.dtmp("p5_Cd", [NE, S])
    P.dtmp("p11_Cd", [NE, S])
    P.dtmp("x2ext", [S, XW])
    P.dtmp("x3ext", [S, XW])
    P.dtmp("acc2", [S, D])
    P.dtmp("qT_C", [16, 96, S], BF16)
    P.dtmp("kT_C", [16, 96, S], BF16)
    P.dtmp("v_C", [16, 128, NT, 128], BF16)
    out = P.dtmp("out", [S, D], out=True)
    if "dbg_idx" in dbg:
        P.dtmp("dbg_idx", [128, 128], I32)
        P.dtmp("dbg_lo", [128, NE])
    if "dbg_xg" in dbg:
        P.dtmp("dbg_xg", [128, XW])
        P.dtmp("dbg_yo", [128, D])
        P.dtmp("dbg_hT", [128, 8192], BF16)
    if "dbg_q9" in dbg:
        P.dtmp("dbg_q9", [128, S], BF16)
        P.dtmp("dbg_k9", [128, S], BF16)
    if "dbg_rec" in dbg:
        P.dtmp("dbg_rec", [64, 512])
        P.dtmp("dbg_num", [128, 512])
    if "dbgv0" in dbg:
        P.dtmp("dbgv0", [128, NT, 128], BF16)
        P.dtmp("dbgv1", [128, NT, 128], BF16)
    dr = P.dram
    last = PHASES.index(upto)
    on = lambda p: PHASES.index(p) <= last
    with contextlib.ExitStack() as st:
        idx = sb(st, nc, "idx_all", [128, NE, 8], I32)
        if on("p1"):
            phase_proj0(P, x)
        if on("p2"):
            jobs = []
            for h in range(8):
                j, half, kv = h // 2, h % 2, h // 4
                jobs.append(dict(qsrc=(("qA", j), dr["qT_A"][j, :, :], 128), ksrc=(("kA", kv), dr["kT_A"][kv, :, :], 128),
                                 vsrc=(("vA", kv), dr["v_all"][kv, :, :, :]), pb=half * 64, K=64, out=dr["oT"][h * 64:(h + 1) * 64, :]))
            phase_dense_attn(P, "p2", jobs, 64 ** -0.5)
        if on("p3"):
            jobs = []
            for h in range(8):
                j, half, kv = h // 2, h % 2, h // 4
                jobs.append(dict(qsrc=(("qB", j), dr["qT_B"][j, :, :], 128), ksrc=(("kB", kv), dr["kT_B"][kv, :, :], 128),
                                 vsrc=(("vB", kv), dr["v_all"][2 + kv, :, :, :]), pb=half * 64, K=64, h=h,
                                 out=dr["oT"][512 + h * 64:512 + (h + 1) * 64, :]))
            phase_dense_attn(P, "p3", jobs, 64 ** -0.5, win=True)
        if on("p4"):
            phase_outproj(P, "p4", dr["ab_w_out"][:, :], lambda tok: x[tok, :], dr["ln_mix_g"][0, :], dr["ln_mix_b"][0, :],
                          dr["moe_router"][0, :, :], dr["x1ext"], dr["acc"], dr["affd"])
        if on("p5"):
            phase_route(P, "p5", dr["x1ext"], idx)
        if on("p6"):
            phase_experts(P, "p6", 0, dr["x1ext"], dr["acc"], idx)
        if on("p7"):
            phase_ln(P, "p7", dr["acc"], dr["ln_ffn_g"][0, :], dr["ln_ffn_b"][0, :], lambda tok: dr["x2ext"][tok, 0:D])
        if on("p8"):
            phase_proj_mla(P, lambda tok: dr["x2ext"][tok, 0:D])
        if on("p9"):
            jobs = []
            for h in range(16):
                jobs.append(dict(qsrc=(("qC", h), dr["qT_C"][h, :, :], 96), ksrc=(("kC", h), dr["kT_C"][h, :, :], 96),
                                 vsrc=(("vC", h), dr["v_C"][h, :, :, :]), pb=0, K=96, out=dr["oT"][h * 64:(h + 1) * 64, :]))
            phase_dense_attn(P, "p9", jobs, 96 ** -0.5)
        if on("p10"):
            phase_outproj(P, "p10", dr["mla_w_out"][:, :], lambda tok: dr["x2ext"][tok, 0:D], dr["ln_mix_g"][1, :], dr["ln_mix_b"][1, :],
                          dr["moe_router"][1, :, :], dr["x3ext"], dr["acc2"], dr["affd"])
        if on("p11"):
            phase_route(P, "p11", dr["x3ext"], idx)
        if on("p12"):
            phase_experts(P, "p12", 1, dr["x3ext"], dr["acc2"], idx)
        if on("p13"):
            phase_ln(P, "p13", dr["acc2"], dr["ln_ffn_g"][1, :], dr["ln_ffn_b"][1, :], lambda tok: out[tok, :])
    return P


_PERM = np.concatenate([np.arange(0, 512), np.arange(512, 640), np.arange(768, 1280), np.arange(1280, 1408),
                        np.arange(640, 768), np.arange(1408, 1536)])


def make_in_maps(inputs, ncores=8):
    tab, cst = make_consts()
    f = lambda a: np.ascontiguousarray(np.asarray(a, dtype=np.float32))
    shared = {
        "w_in": f(np.asarray(inputs["ab_w_in"])[0][:, _PERM]),
        "ab_q_norm": f(inputs["ab_q_norm"]), "ab_k_norm": f(inputs["ab_k_norm"]), "ab_sink": f(inputs["ab_sink"]),
        "ab_w_out": f(np.asarray(inputs["ab_w_out"])[0]),
        "mla_w_down": f(inputs["mla_w_down"]), "mla_q_norm": f(inputs["mla_q_norm"]), "mla_kv_norm": f(inputs["mla_kv_norm"]),
        "mla_w_uq": f(inputs["mla_w_uq"]), "mla_w_ukv": f(inputs["mla_w_ukv"]), "mla_w_out": f(np.asarray(inputs["mla_w_out"])[0]),
        "ln_mix_g": f(inputs["ln_mix_g"]), "ln_mix_b": f(inputs["ln_mix_b"]), "ln_ffn_g": f(inputs["ln_ffn_g"]), "ln_ffn_b": f(inputs["ln_ffn_b"]),
        "moe_router": f(inputs["moe_router"]), "moe_w_gate": f(inputs["moe_w_gate"]), "moe_w_up": f(inputs["moe_w_up"]),
        "moe_w_down": f(inputs["moe_w_down"]), "tab": tab, "cst": cst, "wmask": make_wmask(),
    }
    xs = np.asarray(inputs["x"], dtype=np.float32)
    maps = []
    for c in range(ncores):
        m = dict(shared)
        m["x"] = np.ascontiguousarray(xs[c])
        maps.append(m)
    return maps


def kernel(**inputs):
    P = build()
    maps = make_in_maps(inputs, 8)
    res = run_bass_kernel_spmd(P.nc, maps, core_ids=list(range(8)))
    return np.stack([np.asarray(r["out"], dtype=np.float32) for r in res.results], axis=0)
```

```python
import contextlib
import numpy as np
import concourse.bass as bass
import concourse.mybir as mybir
from concourse.bass_utils import run_bass_kernel_spmd

F32 = mybir.dt.float32
BF16 = mybir.dt.bfloat16
I32 = mybir.dt.int32
ALU = mybir.AluOpType
AF = mybir.ActivationFunctionType
AX = mybir.AxisListType

S = 8192
D = 1024
NT = S // 128
NE = 16
CAP = 1024
DEPTH = 2
ALPHA = (2.0 * DEPTH) ** 0.25
XW = D + NE

SAME_ENGINE_SYNC = True


class Buf:
    __slots__ = ("w", "r", "rd")

    def __init__(self):
        self.w = None
        self.r = {}
        self.rd = []


class Op:
    __slots__ = ("eng", "fn", "deps", "sig", "sem", "val", "dma", "inc")

    def __init__(self, eng, fn, dma):
        self.eng = eng
        self.fn = fn
        self.dma = dma
        self.deps = []
        self.sig = dma
        self.sem = None
        self.val = 0
        self.inc = 16 if dma else 1


ENGS = ("sp", "pe", "act", "dve", "pool")


class Phase:
    KD = {"sp": 8, "pool": 6, "act": 4}

    def __init__(self, nc, name):
        self.nc = nc
        self.name = name
        self.ops = {e: [] for e in ENGS}
        self.bufs = {}
        self.dma_ops = {q: [] for q in self.KD}

    def b(self, *key):
        bb = self.bufs.get(key)
        if bb is None:
            bb = self.bufs[key] = Buf()
        return bb

    def add(self, eng, fn, reads=(), writes=(), dma=False):
        op = Op(eng, fn, dma)
        deps = []
        for b in reads:
            if b.w is not None:
                deps.append(b.w)
        for b in writes:
            if b.w is not None:
                deps.append(b.w)
            deps.extend(b.r.values())
            deps.extend(b.rd)
        if dma:
            lst = self.dma_ops[eng]
            k = self.KD[eng]
            if len(lst) >= k:
                deps.append(lst[len(lst) - k])
            lst.append(op)
        seen = set()
        for d in deps:
            if d is op or id(d) in seen:
                continue
            seen.add(id(d))
            if (not d.dma) and (not dma) and d.eng == eng:
                if eng == "pe" or not SAME_ENGINE_SYNC:
                    continue
            op.deps.append(d)
            d.sig = True
        for b in reads:
            if dma:
                b.rd.append(op)
            else:
                b.r[eng] = op
        for b in writes:
            b.w = op
            b.r = {}
            b.rd = []
        self.ops[eng].append(op)
        return op

    def dma(self, q, out, in_, reads=(), writes=(), **kw):
        return self.add(q, lambda e: e.dma_start(out=out, in_=in_, **kw), reads, writes, dma=True)

    def emit(self):
        nc = self.nc
        allsems = []

        def newsem(nm):
            h = nc.alloc_semaphore(name=nm)
            allsems.append(h)
            return h

        if True:
            csem = {}
            for e in ("pe", "act", "dve", "pool"):
                if any(o.sig and not o.dma for o in self.ops[e]):
                    csem[e] = newsem(f"{self.name}_c_{e}")
            dsem = {}
            for q, k in self.KD.items():
                if self.dma_ops[q]:
                    dsem[q] = [newsem(f"{self.name}_d_{q}{i}") for i in range(min(k, len(self.dma_ops[q])))]
            for e in ENGS:
                cnt = 0
                for o in self.ops[e]:
                    if o.dma:
                        continue
                    if o.sig:
                        cnt += 1
                        o.sem = csem[e]
                        o.val = cnt
            final = {q: {} for q in self.KD}
            for q, lst in self.dma_ops.items():
                k = self.KD[q]
                for n, o in enumerate(lst):
                    o.sem = dsem[q][n % k]
                    o.val = 16 * (n // k + 1)
                    final[q][n % k] = (o.sem, o.val)
            ops = self.ops

            def run(e_name, eng):
                waited = {}
                for o in ops[e_name]:
                    need = {}
                    for d in o.deps:
                        key = id(d.sem)
                        if waited.get(key, 0) >= d.val:
                            continue
                        if key not in need or need[key][1] < d.val:
                            need[key] = (d.sem, d.val)
                    for key, (sem, val) in need.items():
                        eng.wait_ge(sem, val)
                        waited[key] = val
                    ins = o.fn(eng)
                    if o.sig:
                        ins.then_inc(o.sem, o.inc)
                if e_name in final:
                    for (sem, val) in final[e_name].values():
                        if waited.get(id(sem), 0) < val:
                            eng.wait_ge(sem, val)

            with nc.Block() as block:
                if ops["sp"]:
                    @block.sync
                    def _(e):
                        run("sp", e)
                if ops["pe"]:
                    @block.tensor
                    def _(e):
                        run("pe", e)
                if ops["act"]:
                    @block.scalar
                    def _(e):
                        run("act", e)
                if ops["dve"]:
                    @block.vector
                    def _(e):
                        run("dve", e)
                if ops["pool"]:
                    @block.gpsimd
                    def _(e):
                        run("pool", e)
            nc.clear_and_free_semaphores(allsems)
            nc.all_engine_barrier()


def _rope_tab(pos, dim):
    freqs = (10000.0 ** (-(np.arange(0, dim, 2, dtype=np.float32) / np.float32(dim)))).astype(np.float32)
    ang = pos.astype(np.float32)[:, None] * freqs[None, :]
    return np.cos(ang).astype(np.float32), np.sin(ang).astype(np.float32)


def make_consts():
    t = np.arange(S)
    cr, sr = _rope_tab((t // 64).astype(np.float32), 32)
    cc, sc = _rope_tab((t % 64).astype(np.float32), 32)
    cs, ss = _rope_tab(t.astype(np.float32), 64)
    cm, sm = _rope_tab(t.astype(np.float32), 32)
    cosA = np.concatenate([cr, cr, cc, cc], 1)
    sinA = np.concatenate([-sr, sr, -sc, sc], 1)
    cosB = np.concatenate([cs, cs], 1)
    sinB = np.concatenate([-ss, ss], 1)
    cosC = np.concatenate([cm, cm], 1)
    sinC = np.concatenate([-sm, sm], 1)
    tab = np.concatenate([cosA, sinA, cosB, sinB, cosC, sinC], 1).astype(np.float32)
    ident = np.eye(128, dtype=np.float32)
    p = np.arange(128)
    tri = (p[:, None] < p[None, :]).astype(np.float32)
    mprev = (p[None, :] <= p[:, None]).astype(np.float32)
    mnext = (p[:, None] <= p[None, :]).astype(np.float32)
    svec = (np.arange(8)[None, :] * 128 + p[:, None]).astype(np.float32)
    keep = np.ones((128, NE, 64), np.float32)
    keep[:, :, 0] = 0.0
    vsink = np.zeros((128, 128), np.float32)
    vsink[0:2, 64:128] = 1.0
    cst = np.concatenate([ident, tri, mprev, mnext, svec, keep.reshape(128, -1), vsink], 1).astype(np.float32)
    return tab, cst


def make_wmask():
    kl = np.arange(128)[:, None]
    q = np.arange(512)[None, :]
    ms = []
    for r in range(6):
        ms.append((np.abs(q - ((r - 1) * 128 + kl)) <= 128).astype(np.float32))
    return np.concatenate(ms, 1)


C_IDENT, C_TRI, C_MPREV, C_MNEXT, C_SVEC, C_KEEP = 0, 128, 256, 384, 512, 520
C_VSINK = 520 + NE * 64
C_W = C_VSINK + 128


class Prog:
    def __init__(self, dbg=()):
        self.nc = bass.Bass("TRN2", target_bir_lowering=False)
        self.dbg = set(dbg)
        self.dram = {}

    def din(self, name, shape, dt=F32):
        t = self.nc.dram_tensor(name, list(shape), dt, kind="ExternalInput")
        self.dram[name] = t
        return t

    def dtmp(self, name, shape, dt=F32, out=False):
        kind = "ExternalOutput" if (out or name in self.dbg) else "Internal"
        t = self.nc.dram_tensor(name, list(shape), dt, kind=kind)
        self.dram[name] = t
        return t


def sb(st, nc, name, shape, dt):
    return st.enter_context(nc.sbuf_tensor(name, list(shape), dt))


def ps(st, nc, name, shape, dt):
    return st.enter_context(nc.psum_tensor(name, list(shape), dt))


def load_consts(ph, nc, st, P):
    cst = sb(st, nc, ph.name + "_cst", [128, C_W], F32)
    ph.dma("sp", cst[:, :], P.dram["cst"][:, :], writes=[ph.b("cst")])
    idb = sb(st, nc, ph.name + "_idb", [128, 128], BF16)
    ph.add("dve", lambda e: e.tensor_copy(idb[:, :], cst[:, C_IDENT:C_IDENT + 128]),
           reads=[ph.b("cst")], writes=[ph.b("idb")])
    return cst, idb


def emit_xT(ph, xs, xs_buf, xT, xT_buf, pT, pT_bufs, cst, ncol=8, evac=("act", "dve")):
    ident = cst[:, C_IDENT:C_IDENT + 128]
    ng = (ncol + 3) // 4
    for g in range(ng):
        pt = pT[g % len(pT)]
        pb = pT_bufs[g % len(pT)]
        n = min(4, ncol - g * 4)
        for c in range(n):
            cc = g * 4 + c
            ph.add("pe", (lambda e, pt=pt, c=c, cc=cc: e.transpose(pt[:, c * 128:(c + 1) * 128],
                                                                   xs[:, cc * 128:(cc + 1) * 128], ident)),
                   reads=[xs_buf, ph.b("cst")], writes=[pb])
        eng = evac[g % len(evac)]
        if eng == "act":
            ph.add("act", (lambda e, pt=pt, g=g, n=n: e.copy(
                xT[:, g * 4:g * 4 + n, :], pt[:, 0:n * 128].rearrange("p (c t) -> p c t", c=n))),
                reads=[pb], writes=[xT_buf])
        else:
            ph.add("dve", (lambda e, pt=pt, g=g, n=n: e.tensor_copy(
                xT[:, g * 4:g * 4 + n, :], pt[:, 0:n * 128].rearrange("p (c t) -> p c t", c=n))),
                reads=[pb], writes=[xT_buf])


def load_weight_bf16(ph, nc, st, name, w_ap, kc, n, stage, stage_bufs, cast_engs=("dve", "pool")):
    wb = sb(st, nc, name, [128, kc, n], BF16)
    wv = w_ap.rearrange("(c p) n -> p c n", p=128)
    cnt = 0
    for c in range(kc):
        for n0 in range(0, n, stage[0].shape[1]):
            n1 = min(n, n0 + stage[0].shape[1])
            s = cnt % len(stage)
            ph.dma("sp", stage[s][:, 0:n1 - n0], wv[:, c, n0:n1], writes=[stage_bufs[s]])
            eng = cast_engs[cnt % len(cast_engs)]
            ph.add(eng, (lambda e, s=s, c=c, n0=n0, n1=n1: e.tensor_copy(wb[:, c, n0:n1], stage[s][:, 0:n1 - n0])),
                   reads=[stage_bufs[s]], writes=[ph.b(name)])
            cnt += 1
    return wb


def emit_rope(ph, eng, out, src, cos, sin, nh, hd, tmp1, tmp2, rbufs, wbufs, blocks):
    hb = hd // blocks // 2
    v = lambda a: a.rearrange("p (h b t f) -> p h b t f", h=nh, b=blocks, t=2, f=hb)
    tb = lambda a: a.rearrange("p (b t f) -> p b t f", b=blocks, t=2, f=hb)
    cosb = cos.unsqueeze(1).to_broadcast([128, nh, hd]).rearrange("p h d -> p (h d)") if False else None
    s3 = src.rearrange("p (h d) -> p h d", h=nh)
    t13 = tmp1.rearrange("p (h d) -> p h d", h=nh)
    cos3 = cos.unsqueeze(1).to_broadcast([128, nh, hd])
    ph.add(eng, lambda e: e.tensor_tensor(t13, s3, cos3, ALU.mult), reads=rbufs, writes=[wbufs[0]])
    sv, t2v = v(src), v(tmp2)
    sn = tb(sin)
    for t in range(2):
        sin_b = sn[:, :, t, :].unsqueeze(1).to_broadcast([128, nh, blocks, hb])
        ph.add(eng, (lambda e, t=t, sin_b=sin_b: e.tensor_tensor(t2v[:, :, :, t, :], sv[:, :, :, 1 - t, :], sin_b, ALU.mult)),
               reads=rbufs, writes=[wbufs[1]])
    ph.add(eng, lambda e: e.tensor_tensor(out, tmp1, tmp2, ALU.add), reads=[wbufs[0], wbufs[1]], writes=[wbufs[2]])


def rope3(ph, eng, out3, src3, cos, sin, nh, hd, blocks, t1, t2, rbufs, tbufs, wbuf):
    hb = hd // blocks // 2
    t13 = t1.rearrange("p (h d) -> p h d", h=nh)
    t23 = t2.rearrange("p (h d) -> p h d", h=nh)
    cos3 = cos.unsqueeze(1).to_broadcast([128, nh, hd])
    ph.add(eng, lambda e: e.tensor_tensor(t13, src3, cos3, ALU.mult), reads=rbufs, writes=[tbufs[0]])
    for b in range(blocks):
        for t in range(2):
            o0 = b * 2 * hb + t * hb
            s0 = b * 2 * hb + (1 - t) * hb
            sin_b = sin[:, o0:o0 + hb].unsqueeze(1).to_broadcast([128, nh, hb])
            ph.add(eng, (lambda e, o0=o0, s0=s0, sin_b=sin_b: e.tensor_tensor(
                t23[:, :, o0:o0 + hb], src3[:, :, s0:s0 + hb], sin_b, ALU.mult)),
                reads=rbufs, writes=[tbufs[1]])
    ph.add(eng, lambda e: e.tensor_tensor(out3, t13, t23, ALU.add), reads=[tbufs[0], tbufs[1]], writes=[wbuf])


def lowp(nc, f):
    def g(e):
        with nc.allow_low_precision(reason="exact small integer counts"):
            return f(e)
    return g


def bc(ap1d):
    return ap1d.partition_broadcast(128)


def phase_proj0(P, x_ap):
    nc = P.nc
    dr = P.dram
    with contextlib.ExitStack() as st:
        ph = Phase(nc, "p1")
        cst, idb = load_consts(ph, nc, st, P)
        stage = [sb(st, nc, f"p1_stg{i}", [128, 1536], F32) for i in range(2)]
        wb = load_weight_bf16(ph, nc, st, "p1_wb", dr["w_in"][:, :], 8, 1536, stage,
                              [ph.b("stg", i) for i in range(2)], cast_engs=("dve", "pool"))
        gq = sb(st, nc, "p1_gq", [128, 64], F32)
        gk = sb(st, nc, "p1_gk", [128, 64], F32)
        ph.dma("sp", gq[:, :], bc(dr["ab_q_norm"][0, :]), writes=[ph.b("gq")])
        ph.dma("sp", gk[:, :], bc(dr["ab_k_norm"][0, :]), writes=[ph.b("gk")])
        NSL = 2
        xs = [sb(st, nc, f"p1_xs{i}", [128, 1024], F32) for i in range(NSL)]
        tb = [sb(st, nc, f"p1_tb{i}", [128, 256], F32) for i in range(NSL)]
        xT = [sb(st, nc, f"p1_xT{i}", [128, 8, 128], BF16) for i in range(NSL)]
        pr = [sb(st, nc, f"p1_pr{i}", [128, 1536], F32) for i in range(NSL)]
        sq = sb(st, nc, "p1_sq", [128, 640], F32)
        ss = sb(st, nc, "p1_ss", [128, 10], F32)
        sd = sb(st, nc, "p1_sd", [128, 10], F32)
        rs = sb(st, nc, "p1_rs", [128, 10], F32)
        t0 = sb(st, nc, "p1_t0", [128, 640], F32)
        ta1 = sb(st, nc, "p1_ta1", [128, 640], F32)
        ta2 = sb(st, nc, "p1_ta2", [128, 640], F32)
        tb1 = sb(st, nc, "p1_rb1", [128, 640], F32)
        tb2 = sb(st, nc, "p1_rb2", [128, 640], F32)
        oa = [sb(st, nc, f"p1_oa{i}", [128, 640], BF16) for i in range(NSL)]
        ob = [sb(st, nc, f"p1_ob{i}", [128, 640], BF16) for i in range(NSL)]
        stA = [sb(st, nc, f"p1_stA{i}", [128, 5, 128], BF16) for i in range(NSL)]
        stB = [sb(st, nc, f"p1_stB{i}", [128, 5, 128], BF16) for i in range(NSL)]
        vst = [sb(st, nc, f"p1_vst{i}", [128, 4, 128], BF16) for i in range(NSL)]
        pT = [ps(st, nc, f"p1_pT{i}", [128, 512], F32) for i in range(2)]
        pp = [ps(st, nc, f"p1_pp{i}", [128, 512], F32) for i in range(3)]
        ptA = ps(st, nc, "p1_ptA", [128, 1024], BF16)
        ptB = ps(st, nc, "p1_ptB", [128, 1024], BF16)
        for s in range(NSL):
            ph.add("pool", (lambda e, s=s: e.memset(vst[s][:, :, :], 1.0)), writes=[ph.b("vst", s)])
        eps = sb(st, nc, "p1_eps", [128, 1], F32)
        ph.add("pool", lambda e: e.memset(eps[:, :], 1e-6), writes=[ph.b("eps")])
        qTA, qTB, kTA, kTB, vall = dr["qT_A"], dr["qT_B"], dr["kT_A"], dr["kT_B"], dr["v_all"]
        def stageA(i):
            s = i % NSL
            tok = slice(i * 128, (i + 1) * 128)
            B = lambda n: ph.b(n, s)
            ph.dma("sp", xs[s][:, :], x_ap[tok, :], writes=[B("xs")])
            ph.dma("sp", tb[s][:, :], dr["tab"][tok, 0:256], writes=[B("tb")])
            emit_xT(ph, xs[s], B("xs"), xT[s], B("xT"), pT, [ph.b("pT", 0), ph.b("pT", 1)], cst)
            for n in range(3):
                for c in range(8):
                    ph.add("pe", (lambda e, n=n, c=c, s=s: e.matmul(pp[n][:, :], xT[s][:, c, :], wb[:, c, n * 512:(n + 1) * 512],
                                                                     start=(c == 0), stop=(c == 7))),
                           reads=[B("xT"), ph.b("p1_wb")], writes=[ph.b("pp", n)])
                ph.add("act", (lambda e, n=n, s=s: e.copy(pr[s][:, n * 512:(n + 1) * 512], pp[n][:, :])),
                       reads=[ph.b("pp", n)], writes=[B("pr")])

        def stageB(i):
            s = i % NSL
            tok = slice(i * 128, (i + 1) * 128)
            B = lambda n: ph.b(n, s)
            ph.add("act", (lambda e, s=s: e.activation(sq[:, :], pr[s][:, 0:640], AF.Square)), reads=[B("pr")], writes=[ph.b("sq")])
            ph.add("dve", lambda e: e.tensor_reduce(ss[:, :], sq[:, :].rearrange("p (h d) -> p h d", h=10), AX.X, ALU.add),
                   reads=[ph.b("sq")], writes=[ph.b("ss")])
            ph.add("act", lambda e: e.activation(sd[:, :], ss[:, :], AF.Sqrt, bias=eps[:, :], scale=1.0 / 64),
                   reads=[ph.b("ss"), ph.b("eps")], writes=[ph.b("sd")])
            ph.add("dve", lambda e: e.reciprocal(rs[:, :], sd[:, :]), reads=[ph.b("sd")], writes=[ph.b("rs")])
            ph.add("dve", (lambda e, s=s: e.tensor_tensor(t0[:, :].rearrange("p (h d) -> p h d", h=10),
                                                          pr[s][:, 0:640].rearrange("p (h d) -> p h d", h=10),
                                                          rs[:, :].unsqueeze(2).to_broadcast([128, 10, 64]), ALU.mult)),
                   reads=[B("pr"), ph.b("rs")], writes=[ph.b("t0")])
            ph.add("dve", lambda e: e.tensor_tensor(t0[:, 0:512].rearrange("p (h d) -> p h d", h=8),
                                                    t0[:, 0:512].rearrange("p (h d) -> p h d", h=8),
                                                    gq[:, :].unsqueeze(1).to_broadcast([128, 8, 64]), ALU.mult),
                   reads=[ph.b("gq")], writes=[ph.b("t0")])
            ph.add("dve", lambda e: e.tensor_tensor(t0[:, 512:640].rearrange("p (h d) -> p h d", h=2),
                                                    t0[:, 512:640].rearrange("p (h d) -> p h d", h=2),
                                                    gk[:, :].unsqueeze(1).to_broadcast([128, 2, 64]), ALU.mult),
                   reads=[ph.b("gk")], writes=[ph.b("t0")])
            rope3(ph, "dve", oa[s][:, :].rearrange("p (h d) -> p h d", h=10), t0[:, :].rearrange("p (h d) -> p h d", h=10),
                  tb[s][:, 0:64], tb[s][:, 64:128], 10, 64, 2, ta1[:, :], ta2[:, :],
                  [ph.b("t0"), B("tb")], [ph.b("ta1"), ph.b("ta2")], B("oa"))
            rope3(ph, "pool", ob[s][:, :].rearrange("p (h d) -> p h d", h=10), pr[s][:, 640:1280].rearrange("p (h d) -> p h d", h=10),
                  tb[s][:, 128:192], tb[s][:, 192:256], 10, 64, 1, tb1[:, :], tb2[:, :],
                  [B("pr"), B("tb")], [ph.b("tb1"), ph.b("tb2")], B("ob"))
            ph.add("act", (lambda e, s=s: e.copy(vst[s][:, :, 0:64], pr[s][:, 1280:1536].rearrange("p (h d) -> p h d", h=4))),
                   reads=[B("pr")], writes=[B("vst")])
            ph.dma("sp", vall[:, :, i, :].rearrange("k p f -> p k f"), vst[s][:, :, :], reads=[B("vst")])
            for (src, sbuf_, ptt, ptn, stt, stn, qT, kT) in ((oa, "oa", ptA, "ptA", stA, "stA", qTA, kTA),
                                                             (ob, "ob", ptB, "ptB", stB, "stB", qTB, kTB)):
                for c in range(5):
                    ph.add("pe", (lambda e, src=src, ptt=ptt, c=c, s=s: e.transpose(ptt[:, c * 128:(c + 1) * 128],
                                                                                    src[s][:, c * 128:(c + 1) * 128], idb[:, :])),
                           reads=[B(sbuf_), ph.b("idb")], writes=[ph.b(ptn)])
                ph.add("act", (lambda e, ptt=ptt, stt=stt, s=s: e.copy(stt[s][:, :, :], ptt[:, 0:640].rearrange("p (c t) -> p c t", c=5))),
                       reads=[ph.b(ptn)], writes=[B(stn)])
                ph.dma("sp", qT[:, :, tok].rearrange("j p t -> p j t"), stt[s][:, 0:4, :], reads=[B(stn)])
                for kv in range(2):
                    for dup in range(2):
                        ph.dma("sp", kT[kv, dup * 64:(dup + 1) * 64, tok], stt[s][kv * 64:(kv + 1) * 64, 4, :], reads=[B(stn)])

        for i in range(NT + 1):
            if i < NT:
                stageA(i)
            if i >= 1:
                stageB(i - 1)
        ph.emit()


def phase_dense_attn(P, name, jobs, scale, win=False):
    nc = P.nc
    dr = P.dram
    with contextlib.ExitStack() as st:
        ph = Phase(nc, name)
        if win:
            cst, idb = load_consts(ph, nc, st, P)
            wm = sb(st, nc, f"{name}_wm", [128, 3072], BF16)
            wst = sb(st, nc, f"{name}_wst", [128, 3072], F32)
            ph.dma("sp", wst[:, :], dr["wmask"][:, :], writes=[ph.b("wst")])
            ph.add("dve", lambda e: e.tensor_copy(wm[:, :], wst[:, :]), reads=[ph.b("wst")], writes=[ph.b("wm")])
            vsk = sb(st, nc, f"{name}_vsk", [128, 128], BF16)
            ph.add("dve", lambda e: e.tensor_copy(vsk[:, :], cst[:, C_VSINK:C_VSINK + 128]), reads=[ph.b("cst")], writes=[ph.b("vsk")])
            sk = sb(st, nc, f"{name}_sk", [128, 8], F32)
            es = sb(st, nc, f"{name}_es", [128, 8], F32)
            ehb = sb(st, nc, f"{name}_ehb", [128, 8], BF16)
            ehf = sb(st, nc, f"{name}_ehf", [128, 8], F32)
            elo = sb(st, nc, f"{name}_elo", [128, 8], F32)
            ssel = sb(st, nc, f"{name}_ssel", [128, 8], F32)
            onesb = sb(st, nc, f"{name}_onesb", [128, 512], BF16)
            psink = [sb(st, nc, f"{name}_psink{i}", [128, 512], BF16) for i in range(2)]
            ph.dma("sp", sk[:, :], bc(dr["ab_sink"][0, :]), writes=[ph.b("sk")])
            ph.add("act", lambda e: e.activation(es[:, :], sk[:, :], AF.Exp), reads=[ph.b("sk")], writes=[ph.b("es")])
            ph.add("dve", lowp(nc, lambda e: e.tensor_copy(ehb[:, :], es[:, :])), reads=[ph.b("es")], writes=[ph.b("ehb")])
            ph.add("dve", lambda e: e.tensor_copy(ehf[:, :], ehb[:, :]), reads=[ph.b("ehb")], writes=[ph.b("ehf")])
            ph.add("dve", lambda e: e.tensor_tensor(elo[:, :], es[:, :], ehf[:, :], ALU.subtract), reads=[ph.b("es"), ph.b("ehf")], writes=[ph.b("elo")])
            ph.add("dve", lambda e: e.tensor_scalar(ehf[:, :], ehf[:, :], cst[:, C_IDENT:C_IDENT + 1], None, ALU.mult), reads=[ph.b("cst")], writes=[ph.b("ehf")])
            ph.add("dve", lambda e: e.scalar_tensor_tensor(ssel[:, :], elo[:, :], cst[:, C_IDENT + 1:C_IDENT + 2], ehf[:, :], ALU.mult, ALU.add),
                   reads=[ph.b("elo"), ph.b("ehf"), ph.b("cst")], writes=[ph.b("ssel")])
            ph.add("dve", lambda e: e.memset(onesb[:, :], 1.0), writes=[ph.b("onesb")])
        NSL = 2
        qs = [sb(st, nc, f"{name}_q{i}", [128, S], BF16) for i in range(NSL)]
        pad = (jobs[0]["K"] == 64)
        nk = 2 if pad else 1
        ks = [[sb(st, nc, f"{name}_k{i}_{hf}", [128, S], BF16) for hf in range(nk)] for i in range(NSL)]
        vs = [sb(st, nc, f"{name}_v{i}", [128, NT, 128], BF16) for i in range(NSL)]
        NP_ = 6 if win else 4
        pTs = [sb(st, nc, f"{name}_pT{i}", [128, 512], BF16) for i in range(NP_)]
        rec = sb(st, nc, f"{name}_rec", [64, 512], F32)
        onb = [sb(st, nc, f"{name}_on{i}", [64, 512], BF16) for i in range(2)]
        NS_ = 5 if win else 3
        sT = [ps(st, nc, f"{name}_sT{i}", [128, 512], F32) for i in range(NS_)]
        oacc = [ps(st, nc, f"{name}_oa{i}", [128, 512], F32) for i in range(2)]
        LA = 4 if win else 2
        if not pad:
            for i in range(NSL):
                ph.add("pool", (lambda e, i=i: e.memset(qs[i][64:128, :], 0.0)), writes=[ph.b("q", i)])
                ph.add("pool", (lambda e, i=i: e.memset(ks[i][0][64:128, :], 0.0)), writes=[ph.b("k", i)])
        else:
            for i in range(NSL):
                ph.add("pool", (lambda e, i=i: e.memset(ks[i][0][64:128, :], 0.0)), writes=[ph.b("k", i)])
                ph.add("pool", (lambda e, i=i: e.memset(ks[i][1][0:64, :], 0.0)), writes=[ph.b("k", i)])
        qk_prev = kk_prev = vv_prev = None
        slot = {"q": -1, "k": -1, "v": -1}
        cur = {}
        on_cnt = 0
        jslots = []

        def issue_loads(jb):
            for key, tiles, src in (("q", qs, jb["qsrc"]), ("k", ks, jb["ksrc"]), ("v", vs, jb["vsrc"])):
                if cur.get(key) != src[0]:
                    slot[key] = (slot[key] + 1) % NSL
                    sl = slot[key]
                    if key == "v":
                        ph.dma("sp", tiles[sl][:, :, :].rearrange("p c f -> p (c f)"), src[1].rearrange("p c f -> p (c f)"), writes=[ph.b(key, sl)])
                    elif key == "k" and pad:
                        ph.dma("sp", tiles[sl][0][0:64, :], src[1][0:64, :], writes=[ph.b(key, sl)])
                        ph.dma("sp", tiles[sl][1][64:128, :], src[1][64:128, :], writes=[ph.b(key, sl)])
                    elif key == "k":
                        rows = src[2]
                        ph.dma("sp", tiles[sl][0][0:rows, :], src[1], writes=[ph.b(key, sl)])
                    else:
                        rows = src[2]
                        ph.dma("sp", tiles[sl][0:rows, :], src[1], writes=[ph.b(key, sl)])
                    cur[key] = src[0]
            jslots.append(dict(slot))

        issue_loads(jobs[0])
        for ji, jb in enumerate(jobs):
            if ji + 1 < len(jobs):
                issue_loads(jobs[ji + 1])
            jsl = jslots[ji]
            q_t, k_t, v_t = qs[jsl["q"]], ks[jsl["k"]][(jb["pb"] // 64) if pad else 0], vs[jsl["v"]]
            qB, kB, vB = ph.b("q", jsl["q"]), ph.b("k", jsl["k"]), ph.b("v", jsl["v"])
            if "dbg_q9" in dr and name == "p9" and jb is jobs[0]:
                ph.dma("sp", dr["dbg_q9"][:, :], q_t[:, :], reads=[qB])
                ph.dma("sp", dr["dbg_k9"][:, :], k_t[:, :], reads=[kB])
            pb, K = jb["pb"], jb["K"]
            if win:
                steps = [(g, c) for g in range(16) for c in range(4 * g - 1, 4 * g + 5) if 0 <= c < NT]
                hh = jb["h"]
                psl = hh % 2
                ph.add("dve", (lambda e, psl=psl, hh=hh: e.tensor_scalar(psink[psl][:, :], onesb[:, :], ssel[:, hh:hh + 1], None, ALU.mult)),
                       reads=[ph.b("onesb"), ph.b("ssel")], writes=[ph.b("psink", psl)])
            else:
                steps = [(g, c) for g in range(16) for c in range(NT)]
            nsteps = len(steps)

            def qk(n, q_t=q_t, k_t=k_t, qB=qB, kB=kB, pb=pb, K=K):
                g, c = steps[n]
                b = n % NS_
                ph.add("pe", (lambda e: e.matmul(sT[b][:, :], k_t[:, c * 128:(c + 1) * 128],
                                                 q_t[:, g * 512:(g + 1) * 512], start=True, stop=True)),
                       reads=[qB, kB], writes=[ph.b("sT", b)])

            for n in range(min(LA, nsteps)):
                qk(n)
            for n in range(nsteps):
                g, c = steps[n]
                first = (n == 0) or steps[n - 1][0] != g
                last = (n == nsteps - 1) or steps[n + 1][0] != g
                if n + LA < nsteps:
                    qk(n + LA)
                b = n % NS_
                pslot = n % NP_
                ph.add("act", (lambda e, b=b, pslot=pslot: e.activation(pTs[pslot][:, :], sT[b][:, :], AF.Exp, scale=scale)),
                       reads=[ph.b("sT", b)], writes=[ph.b("pT", pslot)])
                if win:
                    r_ = c - (4 * g - 1)
                    ph.add("dve", (lambda e, pslot=pslot, r_=r_: e.tensor_tensor(pTs[pslot][:, :], pTs[pslot][:, :], wm[:, r_ * 512:(r_ + 1) * 512], ALU.mult)),
                           reads=[ph.b("wm")], writes=[ph.b("pT", pslot)])
                ob_ = g % 2
                ph.add("pe", (lambda e, ob_=ob_, c=c, pslot=pslot, v_t=v_t, first=first, last=last: e.matmul(
                    oacc[ob_][:, :], v_t[:, c, :], pTs[pslot][:, :], start=first, stop=(last and not win))),
                    reads=[vB, ph.b("pT", pslot)], writes=[ph.b("oacc", ob_)])
                if last and win:
                    ph.add("pe", (lambda e, ob_=ob_, psl=psl: e.matmul(oacc[ob_][:, :], vsk[:, :], psink[psl][:, :], start=False, stop=True)),
                           reads=[ph.b("vsk"), ph.b("psink", psl)], writes=[ph.b("oacc", ob_)])
                if last:
                    os_ = on_cnt % 2
                    on_cnt += 1
                    ph.add("dve", (lambda e, ob_=ob_: e.reciprocal(rec[:, :], oacc[ob_][64:128, :])),
                           reads=[ph.b("oacc", ob_)], writes=[ph.b("rec")])
                    ph.add("dve", (lambda e, ob_=ob_, os_=os_: e.tensor_tensor(onb[os_][:, :], oacc[ob_][0:64, :], rec[:, :], ALU.mult)),
                           reads=[ph.b("oacc", ob_), ph.b("rec")], writes=[ph.b("on", os_)])
                    ph.dma("sp", jb["out"][:, g * 512:(g + 1) * 512], onb[os_][:, :], reads=[ph.b("on", os_)])
                    if "dbg_rec" in dr and name == "p9" and jb is jobs[0] and g == 0:
                        ph.dma("sp", dr["dbg_rec"][:, :], rec[:, :], reads=[ph.b("rec")])
                        dnum = sb(st, nc, "p9_dnum", [128, 512], F32)
                        ph.add("dve", (lambda e, ob_=ob_: e.tensor_copy(dnum[:, :], oacc[ob_][:, :])), reads=[ph.b("oacc", ob_)], writes=[ph.b("dnum")])
                        ph.dma("sp", dr["dbg_num"][:, :], dnum[:, :], reads=[ph.b("dnum")])
        ph.emit()


def phase_win_attn(P):
    nc = P.nc
    dr = P.dram
    name = "p3"
    scale = 64 ** -0.5
    with contextlib.ExitStack() as st:
        ph = Phase(nc, name)
        cst, idb = load_consts(ph, nc, st, P)
        mk = sb(st, nc, "p3_mk", [128, 2, 128], BF16)
        ph.add("dve", lambda e: e.tensor_copy(mk[:, 0, :], cst[:, C_MPREV:C_MPREV + 128]), reads=[ph.b("cst")], writes=[ph.b("mk")])
        ph.add("dve", lambda e: e.tensor_copy(mk[:, 1, :], cst[:, C_MNEXT:C_MNEXT + 128]), reads=[ph.b("cst")], writes=[ph.b("mk")])
        sk = sb(st, nc, "p3_sk", [128, 8], F32)
        es = sb(st, nc, "p3_es", [128, 8], F32)
        ph.dma("sp", sk[:, :], bc(dr["ab_sink"][0, :]), writes=[ph.b("sk")])
        ph.add("act", lambda e: e.activation(es[:, :], sk[:, :], AF.Exp), reads=[ph.b("sk")], writes=[ph.b("es")])
        qs = [sb(st, nc, f"p3_q{i}", [128, S], BF16) for i in range(2)]
        ks = [sb(st, nc, f"p3_k{i}", [128, S], BF16) for i in range(2)]
        vs = [sb(st, nc, f"p3_v{i}", [128, NT, 128], BF16) for i in range(2)]
        pTs = [sb(st, nc, f"p3_pT{i}", [128, 384], BF16) for i in range(4)]
        den = sb(st, nc, "p3_den", [64, 512], F32)
        rec = sb(st, nc, "p3_rec", [64, 512], F32)
        onb = [sb(st, nc, f"p3_on{i}", [64, 512], BF16) for i in range(2)]
        sT = [ps(st, nc, f"p3_sT{i}", [128, 512], F32) for i in range(3)]
        oacc = [ps(st, nc, f"p3_oa{i}", [128, 512], F32) for i in range(2)]
        cnt = 0
        gcnt = 0
        for h in range(8):
            j, half, kv = h // 2, h % 2, h // 4
            pb = half * 64
            if half == 0:
                ph.dma("sp", qs[j % 2][:, :], dr["qT_B"][j, :, :], writes=[ph.b("q", j % 2)])
            if h % 4 == 0:
                ph.dma("sp", ks[kv][:, :], dr["kT_B"][kv, :, :], writes=[ph.b("k", kv)])
                ph.dma("sp", vs[kv][:, :, :].rearrange("p c f -> p (c f)"), dr["v_all"][2 + kv, :, :, :].rearrange("p c f -> p (c f)"), writes=[ph.b("v", kv)])
                if "dbgv0" in dr and kv == 0:
                    ph.dma("sp", dr["dbgv0"][:, :, :].rearrange("p c f -> p (c f)"), vs[0][:, :, :].rearrange("p c f -> p (c f)"), reads=[ph.b("v", 0)])
            q_t, k_t, v_t = qs[j % 2], ks[kv], vs[kv]
            qB, kB, vB = ph.b("q", j % 2), ph.b("k", kv), ph.b("v", kv)
            for g in range(16):
                ob_ = gcnt % 2
                gcnt += 1
                for ti in range(4):
                    i = g * 4 + ti
                    b = cnt % 3
                    pslot = cnt % 4
                    cnt += 1
                    chunks = [c for c in (i - 1, i, i + 1) if 0 <= c < NT]
                    for c in chunks:
                        sl = c - i + 1
                        ph.add("pe", (lambda e, b=b, sl=sl, c=c, i=i: e.matmul(sT[b][:, sl * 128:(sl + 1) * 128],
                                                                                k_t[pb:pb + 64, c * 128:(c + 1) * 128],
                                                                                q_t[pb:pb + 64, i * 128:(i + 1) * 128], start=True, stop=True)),
                               reads=[qB, kB], writes=[ph.b("sT", b)])
                    lo_, hi_ = (chunks[0] - i + 1) * 128, (chunks[-1] - i + 2) * 128
                    ph.add("act", (lambda e, b=b, pslot=pslot, lo_=lo_, hi_=hi_: e.activation(pTs[pslot][:, lo_:hi_], sT[b][:, lo_:hi_], AF.Exp, scale=scale)),
                           reads=[ph.b("sT", b)], writes=[ph.b("pT", pslot)])
                    for c in chunks:
                        sl = c - i + 1
                        if sl != 1:
                            m = 0 if sl == 0 else 1
                            ph.add("dve", (lambda e, pslot=pslot, sl=sl, m=m: e.tensor_tensor(pTs[pslot][:, sl * 128:(sl + 1) * 128],
                                                                                                pTs[pslot][:, sl * 128:(sl + 1) * 128], mk[:, m, :], ALU.mult)),
                                   reads=[ph.b("mk")], writes=[ph.b("pT", pslot)])
                    for ci, c in enumerate(chunks):
                        sl = c - i + 1
                        ph.add("pe", (lambda e, ob_=ob_, ti=ti, c=c, sl=sl, pslot=pslot, ci=ci, nch=len(chunks): e.matmul(
                            oacc[ob_][:, ti * 128:(ti + 1) * 128], v_t[:, c, :], pTs[pslot][:, sl * 128:(sl + 1) * 128],
                            start=(ci == 0), stop=(ci == nch - 1))),
                            reads=[vB, ph.b("pT", pslot)], writes=[ph.b("oacc", ob_)])
                os_ = gcnt % 2
                ph.add("dve", (lambda e, ob_=ob_, h=h: e.tensor_scalar(den[:, :], oacc[ob_][64:128, :], es[64:128, h:h + 1], None, ALU.add)),
                       reads=[ph.b("oacc", ob_), ph.b("es")], writes=[ph.b("den")])
                ph.add("dve", lambda e: e.reciprocal(rec[:, :], den[:, :]), reads=[ph.b("den")], writes=[ph.b("rec")])
                ph.add("dve", (lambda e, ob_=ob_, os_=os_: e.tensor_tensor(onb[os_][:, :], oacc[ob_][0:64, :], rec[:, :], ALU.mult)),
                       reads=[ph.b("oacc", ob_), ph.b("rec")], writes=[ph.b("on", os_)])
                ph.dma("sp", dr["oT"][512 + h * 64:512 + (h + 1) * 64, g * 512:(g + 1) * 512], onb[os_][:, :], reads=[ph.b("on", os_)])
            if "dbgv1" in dr and h == 3:
                ph.dma("sp", dr["dbgv1"][:, :, :].rearrange("p c f -> p (c f)"), vs[0][:, :, :].rearrange("p c f -> p (c f)"), reads=[ph.b("v", 0)])
        ph.emit()


def emit_ln(ph, nc, name, r, rB, out_ap, outB, g_b, b_b, junk, small, eps_t, eng2="pool"):
    sm, nm, ssq, sd, rstd = (small[:, k:k + 1] for k in range(5))
    ph.add("dve", lambda e: e.tensor_reduce(sm, r, AX.X, ALU.add), reads=[rB], writes=[ph.b(name, "sm")])
    ph.add("dve", lambda e: e.tensor_scalar(nm, sm, -1.0 / D, None, ALU.mult), reads=[ph.b(name, "sm")], writes=[ph.b(name, "nm")])
    ph.add("act", lambda e: e.activation(junk, r, AF.Square, bias=nm, scale=1.0, accum_out=ssq),
           reads=[rB, ph.b(name, "nm")], writes=[ph.b(name, "ssq"), ph.b(name, "junk")])
    ph.add("act", lambda e: e.activation(sd, ssq, AF.Sqrt, bias=eps_t, scale=1.0 / D), reads=[ph.b(name, "ssq")], writes=[ph.b(name, "sd")])
    ph.add("dve", lambda e: e.reciprocal(rstd, sd), reads=[ph.b(name, "sd")], writes=[ph.b(name, "rstd")])
    ph.add("dve", lambda e: e.tensor_scalar(r, r, nm, rstd, ALU.add, ALU.mult), reads=[ph.b(name, "nm"), ph.b(name, "rstd")], writes=[rB])
    ph.add(eng2, lambda e: e.tensor_tensor(r, r, g_b, ALU.mult), reads=[ph.b("lng")], writes=[rB])
    ph.add(eng2, lambda e: e.tensor_tensor(out_ap, r, b_b, ALU.add), reads=[rB, ph.b("lnb")], writes=[outB])


def phase_outproj(P, name, w_out, xin_rows, lng, lnb, w_router, xext, acc, affd):
    nc = P.nc
    dr = P.dram
    with contextlib.ExitStack() as st:
        ph = Phase(nc, name)
        cst, idb = load_consts(ph, nc, st, P)
        stage = [sb(st, nc, f"{name}_stg{i}", [128, 1024], F32) for i in range(2)]
        wo = load_weight_bf16(ph, nc, st, f"{name}_wo", w_out, 8, 1024, stage, [ph.b("stg", i) for i in range(2)])
        wr = sb(st, nc, f"{name}_wr", [128, 8, NE], F32)
        for hh in range(2):
            ph.dma("sp", wr[:, hh * 4:(hh + 1) * 4, :], w_router.rearrange("(c p) n -> p c n", p=128)[:, hh * 4:(hh + 1) * 4, :], writes=[ph.b("wr")])
        g_b = sb(st, nc, f"{name}_g", [128, D], F32)
        b_b = sb(st, nc, f"{name}_b", [128, D], F32)
        ph.dma("sp", g_b[:, :], bc(lng), writes=[ph.b("lng")])
        ph.dma("sp", b_b[:, :], bc(lnb), writes=[ph.b("lnb")])
        eps_t = sb(st, nc, f"{name}_eps", [128, 1], F32)
        ph.add("pool", lambda e: e.memset(eps_t[:, :], 1e-5), writes=[ph.b("eps")])
        oTs = [sb(st, nc, f"{name}_oT{i}", [128, 8, 512], BF16) for i in range(2)]
        xs = [sb(st, nc, f"{name}_xs{i}", [128, D], F32) for i in range(4)]
        r = [sb(st, nc, f"{name}_r{i}", [128, D], F32) for i in range(4)]
        xo = [sb(st, nc, f"{name}_xo{i}", [128, XW], F32) for i in range(2)]
        ax = [sb(st, nc, f"{name}_ax{i}", [128, D], F32) for i in range(2)]
        junk2 = [sb(st, nc, f"{name}_junk{i}", [128, D], BF16) for i in range(2)]
        small2 = [sb(st, nc, f"{name}_small{i}", [128, 8], F32) for i in range(2)]
        xnT2 = [sb(st, nc, f"{name}_xnT{i}", [128, 8, 128], F32) for i in range(2)]
        lg2 = [sb(st, nc, f"{name}_lg{i}", [128, 4], F32) for i in range(2)]
        ex2 = [sb(st, nc, f"{name}_ex{i}", [128, NE], F32) for i in range(2)]
        py = [ps(st, nc, f"{name}_py{i}", [128, 512], F32) for i in range(2)]
        pT2 = [ps(st, nc, f"{name}_pT{i}", [128, 512], F32) for i in range(2)]
        pl2 = [ps(st, nc, f"{name}_pl{i}", [128, 512], F32) for i in range(2)]
        oTv = dr["oT"][:, :].rearrange("(c p) t -> p c t", p=128)
        def stageA(i):
            s = i % 4
            g4, t4 = divmod(i, 4)
            gs = g4 % 2
            tok = slice(i * 128, (i + 1) * 128)
            B = lambda n: ph.b(n, s)
            if t4 == 0:
                for hh in range(2):
                    ph.dma("sp", oTs[gs][:, hh * 4:(hh + 1) * 4, :], oTv[:, hh * 4:(hh + 1) * 4, g4 * 512:(g4 + 1) * 512], writes=[ph.b("oTs", gs)])
            ph.dma("sp", xs[s][:, :], xin_rows(tok), writes=[B("xs")])
            for n in range(2):
                for c in range(8):
                    ph.add("pe", (lambda e, n=n, c=c, gs=gs, t4=t4: e.matmul(py[n][:, :], oTs[gs][:, c, t4 * 128:(t4 + 1) * 128],
                                                                             wo[:, c, n * 512:(n + 1) * 512], start=(c == 0), stop=(c == 7))),
                           reads=[ph.b("oTs", gs), ph.b(f"{name}_wo")], writes=[ph.b("py", n)])
                ph.add("dve", (lambda e, n=n, s=s: e.scalar_tensor_tensor(r[s][:, n * 512:(n + 1) * 512], xs[s][:, n * 512:(n + 1) * 512], ALPHA,
                                                                          py[n][:, :], ALU.mult, ALU.add)),
                       reads=[B("xs"), ph.b("py", n)], writes=[B("r")])

        def stageB(i):
            s = i % 2
            tok = slice(i * 128, (i + 1) * 128)
            B = lambda n: ph.b(n, s)
            junk, small, xnT, lg, ex, pTs_, pl = junk2[s], small2[s], xnT2[s], lg2[s], ex2[s], pT2[s], pl2[s]
            rr = r[i % 4][:, :]
            rB = ph.b("r", i % 4)
            sm, nm, ssq, sd, rstd = (small[:, k:k + 1] for k in range(5))
            ph.add("dve", lambda e: e.tensor_reduce(sm, rr, AX.X, ALU.add), reads=[rB], writes=[B("sm")]); yield
            ph.add("dve", lambda e: e.tensor_scalar(nm, sm, -1.0 / D, None, ALU.mult), reads=[B("sm")], writes=[B("nm")]); yield
            ph.add("act", lambda e: e.activation(junk[:, :], rr, AF.Square, bias=nm, scale=1.0, accum_out=ssq),
                   reads=[rB, B("nm")], writes=[B("ssq"), B("junk")]); yield
            ph.add("act", lambda e: e.activation(sd, ssq, AF.Sqrt, bias=eps_t[:, :], scale=1.0 / D), reads=[B("ssq")], writes=[B("sd")]); yield
            ph.add("dve", lambda e: e.reciprocal(rstd, sd), reads=[B("sd")], writes=[B("rstd")]); yield
            ph.add("dve", lambda e: e.tensor_scalar(rr, rr, nm, rstd, ALU.add, ALU.mult), reads=[B("nm"), B("rstd")], writes=[rB]); yield
            ph.add("pool", lambda e: e.tensor_tensor(rr, rr, g_b[:, :], ALU.mult), reads=[ph.b("lng")], writes=[rB]); yield
            ph.add("pool", lambda e: e.tensor_tensor(xo[s][:, 0:D], rr, b_b[:, :], ALU.add), reads=[rB, ph.b("lnb")], writes=[B("xo")]); yield
            ph.add("act", lambda e: e.mul(ax[s][:, :], xo[s][:, 0:D], ALPHA), reads=[B("xo")], writes=[B("ax")]); yield
            ph.dma("sp", acc[tok, :], ax[s][:, :], reads=[B("ax")]); yield
            for g in range(2):
                for c in range(4):
                    cc = g * 4 + c
                    ph.add("pe", (lambda e, c=c, cc=cc: e.transpose(pTs_[:, c * 128:(c + 1) * 128], xo[s][:, cc * 128:(cc + 1) * 128],
                                                                    cst[:, C_IDENT:C_IDENT + 128])),
                           reads=[B("xo"), ph.b("cst")], writes=[B("pT")])
                yield
                ph.add("act", (lambda e, g=g: e.copy(xnT[:, g * 4:(g + 1) * 4, :], pTs_[:, :].rearrange("p (c t) -> p c t", c=4))),
                       reads=[B("pT")], writes=[B("xnT")]); yield
            for c in range(8):
                ph.add("pe", (lambda e, c=c: e.matmul(pl[:, 0:NE], xnT[:, c, :], wr[:, c, :], start=(c == 0), stop=(c == 7))),
                       reads=[B("xnT"), ph.b("wr")], writes=[B("pl")])
            yield
            ph.add("dve", lambda e: e.tensor_reduce(lg[:, 0:1], pl[:, 0:NE], AX.X, ALU.max), reads=[B("pl")], writes=[B("lg0")]); yield
            ph.add("dve", lambda e: e.tensor_scalar(lg[:, 1:2], lg[:, 0:1], -1.0, None, ALU.mult), reads=[B("lg0")], writes=[B("lg1")]); yield
            ph.add("act", lambda e: e.activation(ex[:, :], pl[:, 0:NE], AF.Exp, bias=lg[:, 1:2], scale=1.0, accum_out=lg[:, 2:3]),
                   reads=[B("pl"), B("lg1")], writes=[B("ex"), B("lg2")]); yield
            ph.add("dve", lambda e: e.reciprocal(lg[:, 3:4], lg[:, 2:3]), reads=[B("lg2")], writes=[B("lg3")]); yield
            ph.add("dve", lambda e: e.tensor_scalar(xo[s][:, D:XW], ex[:, :], lg[:, 3:4], None, ALU.mult),
                   reads=[B("ex"), B("lg3")], writes=[B("xo")]); yield
            ph.dma("sp", xext[tok, :], xo[s][:, :], reads=[B("xo")]); yield
            ph.dma("sp", affd[tok, :], xo[s][:, D:XW], reads=[B("xo")]); yield

        def drive(gens):
            gens = list(gens)
            while gens:
                for g_ in list(gens):
                    try:
                        next(g_)
                    except StopIteration:
                        gens.remove(g_)

        stageA(0)
        stageA(1)
        for i in range(0, NT, 2):
            if i + 2 < NT:
                stageA(i + 2)
                stageA(i + 3)
            drive([stageB(i), stageB(i + 1)])
        ph.emit()


def phase_route(P, name, xext, idx):
    nc = P.nc
    dr = P.dram
    with contextlib.ExitStack() as st:
        ph = Phase(nc, name)
        cst, idb = load_consts(ph, nc, st, P)
        aff = sb(st, nc, f"{name}_aff", [128, 64, NE], F32)
        cmp_ = sb(st, nc, f"{name}_cmp", [128, 64, NE], F32)
        mT = sb(st, nc, f"{name}_mT", [128, NE, 64], F32)
        Cl = sb(st, nc, f"{name}_Cl", [128, NE, 64], F32)
        Cg = sb(st, nc, f"{name}_Cg", [128, NE, 64], F32)
        lo = sb(st, nc, f"{name}_lo", [128, NE], F32)
        mid = sb(st, nc, f"{name}_mid", [128, NE], F32)
        sel = sb(st, nc, f"{name}_sel", [128, NE], F32)
        cntp = sb(st, nc, f"{name}_cntp", [128, NE], BF16)
        ones = sb(st, nc, f"{name}_ones", [128, 128], BF16)
        trib = sb(st, nc, f"{name}_trib", [128, 128], BF16)
        idxf = sb(st, nc, f"{name}_idxf", [128, NE, 8], F32)
        junk = sb(st, nc, f"{name}_junk", [128, S], BF16)
        crow = [sb(st, nc, f"{name}_crow{i}", [128, S], F32) for i in range(2)]
        tot = ps(st, nc, f"{name}_tot", [128, 512], F32)
        off = ps(st, nc, f"{name}_off", [128, 512], F32)
        ph.dma("sp", aff[:, :, :].rearrange("p i e -> p (i e)"), dr["affd"][:, :].rearrange("(p i) w -> p (i w)", p=128), writes=[ph.b("aff")])
        ph.add("dve", lambda e: e.memset(ones[:, :], 1.0), writes=[ph.b("ones")])
        ph.add("dve", lambda e: e.tensor_copy(trib[:, :], cst[:, C_TRI:C_TRI + 128]), reads=[ph.b("cst")], writes=[ph.b("trib")])
        ph.add("dve", lambda e: e.memset(lo[:, :], 0.0), writes=[ph.b("lo")])
        for k in range(30):
            w = 0.5 ** (k + 1)
            ph.add("dve", (lambda e, w=w: e.tensor_scalar(mid[:, :], lo[:, :], w, None, ALU.add)), reads=[ph.b("lo")], writes=[ph.b("mid")])
            ph.add("dve", lambda e: e.tensor_tensor(cmp_[:, :, :], aff[:, :, :], mid[:, :].unsqueeze(1).to_broadcast([128, 64, NE]), ALU.is_ge),
                   reads=[ph.b("aff"), ph.b("mid")], writes=[ph.b("cmp")])
            ph.add("dve", lowp(nc, lambda e: e.tensor_reduce(cntp[:, :], cmp_[:, :, :].rearrange("p i e -> p e i"), AX.X, ALU.add)),
                   reads=[ph.b("cmp")], writes=[ph.b("cntp")])
            ph.add("pe", lambda e: e.matmul(tot[:, 0:NE], ones[:, :], cntp[:, :], start=True, stop=True),
                   reads=[ph.b("ones"), ph.b("cntp")], writes=[ph.b("tot")])
            ph.add("dve", lambda e: e.tensor_scalar(sel[:, :], tot[:, 0:NE], CAP - 0.5, None, ALU.is_ge), reads=[ph.b("tot")], writes=[ph.b("sel")])
            ph.add("dve", (lambda e, w=w: e.scalar_tensor_tensor(lo[:, :], sel[:, :], w, lo[:, :], ALU.mult, ALU.add)),
                   reads=[ph.b("sel")], writes=[ph.b("lo")])
        ph.add("dve", lambda e: e.tensor_tensor(mT[:, :, :].rearrange("p e i -> p i e"), aff[:, :, :],
                                                lo[:, :].unsqueeze(1).to_broadcast([128, 64, NE]), ALU.is_ge),
               reads=[ph.b("aff"), ph.b("lo")], writes=[ph.b("mT")])
        ph.add("dve", lambda e: e.tensor_tensor_scan(Cl[:, :, :].rearrange("p e i -> p (e i)"), cst[:, C_KEEP:C_KEEP + NE * 64],
                                                     mT[:, :, :].rearrange("p e i -> p (e i)"), 0.0, ALU.mult, ALU.add),
               reads=[ph.b("mT"), ph.b("cst")], writes=[ph.b("Cl")])
        ph.add("dve", lambda e: e.tensor_copy(cntp[:, :], Cl[:, :, 63]), reads=[ph.b("Cl")], writes=[ph.b("cntp")])
        ph.add("pe", lambda e: e.matmul(off[:, 0:NE], trib[:, :], cntp[:, :], start=True, stop=True),
               reads=[ph.b("trib"), ph.b("cntp")], writes=[ph.b("off")])
        ph.add("dve", lambda e: e.tensor_copy(sel[:, :], off[:, 0:NE]), reads=[ph.b("off")], writes=[ph.b("sel")])
        ph.add("dve", lambda e: e.tensor_tensor(Cg[:, :, :], Cl[:, :, :], sel[:, :].unsqueeze(2).to_broadcast([128, NE, 64]), ALU.add),
               reads=[ph.b("Cl"), ph.b("sel")], writes=[ph.b("Cg")])
        cd = dr[f"{name}_Cd"]
        for e4 in range(4):
            ph.dma("sp", cd[e4 * 4:(e4 + 1) * 4, :].rearrange("e (p i) -> p e i", p=128), Cg[:, e4 * 4:(e4 + 1) * 4, :], reads=[ph.b("Cg")], writes=[ph.b("Cd", e4)])
        junk2 = sb(st, nc, f"{name}_junk2", [128, S], BF16)
        svh = sb(st, nc, f"{name}_svh", [128, 8], F32)
        ph.add("dve", lambda e: e.tensor_scalar(svh[:, :], cst[:, C_SVEC:C_SVEC + 8], 0.5, None, ALU.add), reads=[ph.b("cst")], writes=[ph.b("svh")])
        for e_ in range(NE):
            s = e_ % 2
            ph.dma("sp", crow[s][:, :], bc(cd[e_, :]), reads=[ph.b("Cd", e_ // 4)], writes=[ph.b("crow", s)])
            for j in range(8):
                if j % 2 == 0:
                    ph.add("dve", (lambda e, s=s, e_=e_, j=j: e.tensor_scalar(junk[:, :], crow[s][:, :], cst[:, C_SVEC + j:C_SVEC + j + 1], None,
                                                                               ALU.is_le, ALU.add, accum_out=idxf[:, e_, j:j + 1])),
                           reads=[ph.b("crow", s), ph.b("cst")], writes=[ph.b("idxf", "dve"), ph.b("junk")])
                else:
                    ph.add("act", (lambda e, s=s, e_=e_, j=j: e.activation(junk2[:, :], crow[s][:, :], AF.Sign, bias=svh[:, j:j + 1], scale=-1.0,
                                                                            accum_out=idxf[:, e_, j:j + 1])),
                           reads=[ph.b("crow", s), ph.b("svh")], writes=[ph.b("idxf", "act"), ph.b("junk2")])
        i4 = idxf[:, :, :].rearrange("p e (j t) -> p e j t", t=2)[:, :, :, 1]
        ph.add("dve", lambda e: e.tensor_scalar(i4, i4, 0.5, float(S // 2), ALU.mult, ALU.add), reads=[ph.b("idxf", "act")], writes=[ph.b("idxf", "act")])
        ph.add("dve", lambda e: e.tensor_copy(idx[:, :, :], idxf[:, :, :]), reads=[ph.b("idxf", "act"), ph.b("idxf", "dve")], writes=[ph.b("idx")])
        if "dbg_idx" in dr and name == "p5":
            ph.dma("sp", dr["dbg_idx"][:, :], idx[:, :, :].rearrange("p e j -> p (e j)"), reads=[ph.b("idx")])
            ph.dma("sp", dr["dbg_lo"][:, :], lo[:, :], reads=[ph.b("lo")])
        ph.emit()


def phase_experts(P, name, l, xext, acc, idx):
    nc = P.nc
    dr = P.dram
    with contextlib.ExitStack() as st:
        ph = Phase(nc, name)
        cst, idb = load_consts(ph, nc, st, P)
        NST = 3
        stage = [sb(st, nc, f"{name}_stg{i}", [128, 2048], F32) for i in range(NST)]
        wts = {k: [sb(st, nc, f"{name}_{k}{i}", [128, 8, 1024], BF16) for i in range(2)] for k in ("wg", "wu", "wd")}
        xg = [sb(st, nc, f"{name}_xg{i}", [128, XW], F32) for i in range(2)]
        xgT2 = [sb(st, nc, f"{name}_xgT{i}", [128, 8, 1024], BF16) for i in range(2)]
        hT = sb(st, nc, f"{name}_hT", [128, 8, 1024], BF16)
        gt = sb(st, nc, f"{name}_gt", [128, 2, 8], F32)
        sg = [sb(st, nc, f"{name}_sg{i}", [128, 512], F32) for i in range(2)]
        yo = [sb(st, nc, f"{name}_yo{i}", [128, D], F32) for i in range(4)]
        pT = [ps(st, nc, f"{name}_pT{i}", [128, 512], F32) for i in range(2)]
        pg = [ps(st, nc, f"{name}_pg{i}", [128, 512], F32) for i in range(2)]
        pu = [ps(st, nc, f"{name}_pu{i}", [128, 512], F32) for i in range(2)]
        py = [ps(st, nc, f"{name}_py{i}", [128, 512], F32) for i in range(2)]
        srcs = {"wg": dr["moe_w_gate"], "wu": dr["moe_w_up"], "wd": dr["moe_w_down"]}
        stg_cnt = [0]
        cast_engs = ("pool", "dve", "pool", "act")

        def load_w_piece(e_, k, c2):
            sl = e_ % 2
            if True:
                wv = srcs[k][l, e_, :, :].rearrange("(c p) n -> p c n", p=128)
                if True:
                    s = stg_cnt[0] % NST
                    eng = cast_engs[stg_cnt[0] % len(cast_engs)]
                    stg_cnt[0] += 1
                    ph.dma("sp", stage[s][:, :].rearrange("p (c n) -> p c n", c=2), wv[:, 2 * c2:2 * c2 + 2, :], writes=[ph.b("stg", s)])
                    dst = wts[k][sl][:, 2 * c2:2 * c2 + 2, :]
                    srcv = stage[s][:, :].rearrange("p (c n) -> p c n", c=2)
                    if eng == "act":
                        ph.add("act", (lambda e, dst=dst, srcv=srcv: e.copy(dst, srcv)), reads=[ph.b("stg", s)], writes=[ph.b(k, sl)])
                    else:
                        ph.add(eng, (lambda e, dst=dst, srcv=srcv: e.tensor_copy(dst, srcv)), reads=[ph.b("stg", s)], writes=[ph.b(k, sl)])

        PIECES = [(k, c2) for k in ("wg", "wu", "wd") for c2 in range(4)]

        def load_w(e_):
            for (k, c2) in PIECES:
                load_w_piece(e_, k, c2)

        load_w(0)
        prev_sc = []

        def emit_gather(e_, j):
            s = (e_ * 8 + j) % 2
            ph.add("pool", (lambda e, s=s, e_=e_, j=j: e.indirect_dma_start(
                out=xg[s][:, :], out_offset=None, in_=xext[:, :],
                in_offset=bass.IndirectOffsetOnAxis(ap=idx[:, e_, j:j + 1], axis=0))),
                reads=[ph.b("idx")], writes=[ph.b("xg", s)], dma=True)

        def emit_trans(e_, j):
            s = (e_ * 8 + j) % 2
            xs_ = e_ % 2
            emit_xT(ph, xg[s], ph.b("xg", s), xgT2[xs_][:, :, j * 128:(j + 1) * 128], ph.b("xgT", xs_), pT, [ph.b("pT", 0), ph.b("pT", 1)], cst)
            ph.add("act", (lambda e, s=s, e_=e_, j=j, xs_=xs_: e.copy(gt[:, xs_, j:j + 1], xg[s][:, D + e_:D + e_ + 1])),
                   reads=[ph.b("xg", s)], writes=[ph.b("gt", xs_)])

        emit_gather(0, 0)
        for j in range(8):
            if j + 1 < 8:
                emit_gather(0, j + 1)
            emit_trans(0, j)
        for e_ in range(NE):
            sl = e_ % 2
            wg, wu, wd = wts["wg"][sl], wts["wu"][sl], wts["wd"][sl]
            gsl = e_ % 2
            xgT = xgT2[sl]
            xgB = ph.b("xgT", sl)
            if e_ + 1 < NE:
                emit_gather(e_ + 1, 0)
            n = 0
            for fc in range(8):
                if e_ + 1 < NE and fc + 1 < 8:
                    emit_gather(e_ + 1, fc + 1)
                if e_ + 1 < NE:
                    for pi in range((fc * 12) // 8, ((fc + 1) * 12) // 8):
                        load_w_piece(e_ + 1, *PIECES[pi])
                for half in range(2):
                    b = n % 2
                    n += 1
                    for (wt, pt_, nm, kn) in ((wg, pg, "pg", "wg"), (wu, pu, "pu", "wu")):
                        for c in range(8):
                            ph.add("pe", (lambda e, wt=wt, pt_=pt_, b=b, c=c, fc=fc, half=half, xgT=xgT: e.matmul(
                                pt_[b][:, :], wt[:, c, fc * 128:(fc + 1) * 128], xgT[:, c, half * 512:(half + 1) * 512],
                                start=(c == 0), stop=(c == 7))),
                                reads=[ph.b(kn, sl), xgB], writes=[ph.b(nm, b)])
                    ph.add("act", (lambda e, b=b: e.activation(sg[b][:, :], pg[b][:, :], AF.Silu)), reads=[ph.b("pg", b)], writes=[ph.b("sg", b)])
                    ph.add("dve", (lambda e, b=b, fc=fc, half=half: e.tensor_tensor(hT[:, fc, half * 512:(half + 1) * 512], sg[b][:, :], pu[b][:, :], ALU.mult)),
                           reads=[ph.b("sg", b), ph.b("pu", b)], writes=[ph.b("hT")])
                if e_ + 1 < NE:
                    emit_trans(e_ + 1, fc)
            new_sc = []
            for j in range(8):
                ys = (e_ * 8 + j) % 4
                for half in range(2):
                    for fc in range(8):
                        ph.add("pe", (lambda e, half=half, fc=fc, j=j, wd=wd: e.matmul(py[half][:, :], hT[:, fc, j * 128:(j + 1) * 128],
                                                                                wd[:, fc, half * 512:(half + 1) * 512], start=(fc == 0), stop=(fc == 7))),
                               reads=[ph.b("hT"), ph.b("wd", sl)], writes=[ph.b("py", half)])
                    eng = "act" if half == 0 else "dve"
                    if eng == "act":
                        ph.add("act", (lambda e, ys=ys, half=half, gsl=gsl, j=j: e.mul(yo[ys][:, half * 512:(half + 1) * 512], py[half][:, :], gt[:, gsl, j:j + 1])),
                               reads=[ph.b("py", half), ph.b("gt", gsl)], writes=[ph.b("yo", ys)])
                    else:
                        ph.add("dve", (lambda e, ys=ys, half=half, gsl=gsl, j=j: e.tensor_scalar(yo[ys][:, half * 512:(half + 1) * 512], py[half][:, :],
                                                                                                  gt[:, gsl, j:j + 1], None, ALU.mult)),
                               reads=[ph.b("py", half), ph.b("gt", gsl)], writes=[ph.b("yo", ys)])
                if "dbg_xg" in dr and name == "p6" and e_ == 0 and j == 0:
                    ph.dma("sp", dr["dbg_yo"][:, :], yo[ys][:, :], reads=[ph.b("yo", ys)])
                    ph.dma("sp", dr["dbg_hT"][:, :], hT[:, :, :].rearrange("p c t -> p (c t)"), reads=[ph.b("hT")])
                op = ph.add("pool", (lambda e, ys=ys, e_=e_, j=j: e.indirect_dma_start(
                    out=acc[:, :], out_offset=bass.IndirectOffsetOnAxis(ap=idx[:, e_, j:j + 1], axis=0),
                    in_=yo[ys][:, :], in_offset=None, compute_op=ALU.add)),
                    reads=[ph.b("idx"), ph.b("yo", ys)], writes=[], dma=True)
                for d in prev_sc:
                    if d not in op.deps:
                        op.deps.append(d)
                new_sc.append(op)
            prev_sc = new_sc
        ph.emit()


def phase_ln(P, name, acc, lng, lnb, out_rows):
    nc = P.nc
    with contextlib.ExitStack() as st:
        ph = Phase(nc, name)
        g_b = sb(st, nc, f"{name}_g", [128, D], F32)
        b_b = sb(st, nc, f"{name}_b", [128, D], F32)
        ph.dma("sp", g_b[:, :], bc(lng), writes=[ph.b("lng")])
        ph.dma("sp", b_b[:, :], bc(lnb), writes=[ph.b("lnb")])
        eps_t = sb(st, nc, f"{name}_eps", [128, 1], F32)
        ph.add("pool", lambda e: e.memset(eps_t[:, :], 1e-5), writes=[ph.b("eps")])
        r = [sb(st, nc, f"{name}_r{i}", [128, D], F32) for i in range(3)]
        o = [sb(st, nc, f"{name}_o{i}", [128, D], F32) for i in range(3)]
        junk = sb(st, nc, f"{name}_junk", [128, D], BF16)
        small = sb(st, nc, f"{name}_small", [128, 8], F32)
        for i in range(NT + 1):
            if i < NT:
                s = i % 3
                tok = slice(i * 128, (i + 1) * 128)
                ph.dma("sp", r[s][:, :], acc[tok, :], writes=[ph.b("r", s)])
            if i >= 1:
                s = (i - 1) % 3
                tok = slice((i - 1) * 128, i * 128)
                emit_ln(ph, nc, name, r[s][:, :], ph.b("r", s), o[s][:, :], ph.b("o", s), g_b[:, :], b_b[:, :], junk[:, :], small[:, :], eps_t[:, :])
                ph.dma("sp", out_rows(tok), o[s][:, :], reads=[ph.b("o", s)])
        ph.emit()


def phase_proj_mla(P, xin_rows):
    nc = P.nc
    dr = P.dram
    name = "p8"
    with contextlib.ExitStack() as st:
        ph = Phase(nc, name)
        cst, idb = load_consts(ph, nc, st, P)
        stage = [sb(st, nc, f"p8_stg{i}", [128, 2048], F32) for i in range(2)]
        sB = [ph.b("stg", i) for i in range(2)]
        wdn = load_weight_bf16(ph, nc, st, "p8_wdn", dr["mla_w_down"][0, :, :], 8, 416, stage, sB)
        wuq = load_weight_bf16(ph, nc, st, "p8_wuq", dr["mla_w_uq"][0, :, :], 2, 1536, stage, sB)
        wukv = load_weight_bf16(ph, nc, st, "p8_wukv", dr["mla_w_ukv"][0, :, :], 1, 2048, stage, sB)
        gq = sb(st, nc, "p8_gq", [128, 256], F32)
        gk = sb(st, nc, "p8_gk", [128, 128], F32)
        ph.dma("sp", gq[:, :], bc(dr["mla_q_norm"][0, :]), writes=[ph.b("gq")])
        ph.dma("sp", gk[:, :], bc(dr["mla_kv_norm"][0, :]), writes=[ph.b("gk")])
        eps = sb(st, nc, "p8_eps", [128, 1], F32)
        ph.add("pool", lambda e: e.memset(eps[:, :], 1e-6), writes=[ph.b("eps")])
        xs = [sb(st, nc, f"p8_xs{i}", [128, D], F32) for i in range(2)]
        tb = [sb(st, nc, f"p8_tb{i}", [128, 64], F32) for i in range(2)]
        xT = [sb(st, nc, f"p8_xT{i}", [128, 8, 128], BF16) for i in range(2)]
        pr2 = [sb(st, nc, f"p8_pr{i}", [128, 416], F32) for i in range(2)]
        junk = sb(st, nc, "p8_junk", [128, 256], BF16)
        sm = sb(st, nc, "p8_sm", [128, 8], F32)
        cn = sb(st, nc, "p8_cn", [128, 384], BF16)
        cnT = sb(st, nc, "p8_cnT", [128, 3, 128], BF16)
        qsb = sb(st, nc, "p8_qs", [128, 1536], F32)
        kvs = sb(st, nc, "p8_kvs", [128, 2048], F32)
        qb = sb(st, nc, "p8_qb", [128, 16, 96], BF16)
        kb = sb(st, nc, "p8_kb", [128, 16, 96], BF16)
        kr = sb(st, nc, "p8_kr", [128, 32], BF16)
        t1 = sb(st, nc, "p8_t1", [128, 512], F32)
        t2 = sb(st, nc, "p8_t2", [128, 512], F32)
        t3 = sb(st, nc, "p8_t3", [128, 32], F32)
        t4 = sb(st, nc, "p8_t4", [128, 32], F32)
        stq = [sb(st, nc, f"p8_stq{i}", [128, 16, 128], BF16) for i in range(2)]
        stk = [sb(st, nc, f"p8_stk{i}", [128, 16, 128], BF16) for i in range(2)]
        vst = [sb(st, nc, f"p8_vst{i}", [128, 16, 128], BF16) for i in range(2)]
        pT = [ps(st, nc, f"p8_pT{i}", [128, 512], F32) for i in range(2)]
        pp = ps(st, nc, "p8_pp", [128, 512], F32)
        pc = ps(st, nc, "p8_pc", [128, 1024], BF16)
        pb_ = [ps(st, nc, f"p8_pb{i}", [128, 512], F32) for i in range(4)]
        for s in range(2):
            ph.add("pool", (lambda e, s=s: e.memset(vst[s][:, :, :], 1.0)), writes=[ph.b("vst", s)])
        def stageA(i):
            s = i % 2
            tok = slice(i * 128, (i + 1) * 128)
            B = lambda n: ph.b(n, s)
            ph.dma("sp", xs[s][:, :], xin_rows(tok), writes=[B("xs")])
            ph.dma("sp", tb[s][:, :], dr["tab"][tok, 256:320], writes=[B("tb")])
            emit_xT(ph, xs[s], B("xs"), xT[s], B("xT"), pT, [ph.b("pT", 0), ph.b("pT", 1)], cst)
            for c in range(8):
                ph.add("pe", (lambda e, c=c, s=s: e.matmul(pp[:, 0:416], xT[s][:, c, :], wdn[:, c, :], start=(c == 0), stop=(c == 7))),
                       reads=[B("xT"), ph.b("p8_wdn")], writes=[ph.b("pp")])
            ph.add("act", (lambda e, s=s: e.copy(pr2[s][:, :], pp[:, 0:416])), reads=[ph.b("pp")], writes=[ph.b("pr", s)])

        def stageB(i):
            s = i % 2
            tok = slice(i * 128, (i + 1) * 128)
            B = lambda n: ph.b(n, s)
            pr = pr2[s]
            prB = ph.b("pr", s)
            ph.add("act", lambda e: e.activation(junk[:, 0:256], pr[:, 0:256], AF.Square, accum_out=sm[:, 0:1]), reads=[prB], writes=[ph.b("sm0"), ph.b("junk")])
            ph.add("act", lambda e: e.activation(junk[:, 0:128], pr[:, 256:384], AF.Square, accum_out=sm[:, 1:2]), reads=[prB], writes=[ph.b("sm1"), ph.b("junk")])
            ph.add("act", lambda e: e.activation(sm[:, 2:3], sm[:, 0:1], AF.Sqrt, bias=eps[:, :], scale=1.0 / 256), reads=[ph.b("sm0"), ph.b("eps")], writes=[ph.b("sm2")])
            ph.add("act", lambda e: e.activation(sm[:, 3:4], sm[:, 1:2], AF.Sqrt, bias=eps[:, :], scale=1.0 / 128), reads=[ph.b("sm1"), ph.b("eps")], writes=[ph.b("sm3")])
            ph.add("dve", lambda e: e.reciprocal(sm[:, 4:6], sm[:, 2:4]), reads=[ph.b("sm2"), ph.b("sm3")], writes=[ph.b("sm4")])
            ph.add("dve", lambda e: e.scalar_tensor_tensor(cn[:, 0:256], pr[:, 0:256], sm[:, 4:5], gq[:, :], ALU.mult, ALU.mult),
                   reads=[prB, ph.b("sm4"), ph.b("gq")], writes=[ph.b("cn")])
            ph.add("dve", lambda e: e.scalar_tensor_tensor(cn[:, 256:384], pr[:, 256:384], sm[:, 5:6], gk[:, :], ALU.mult, ALU.mult),
                   reads=[prB, ph.b("sm4"), ph.b("gk")], writes=[ph.b("cn")])
            for c in range(3):
                ph.add("pe", (lambda e, c=c: e.transpose(pc[:, c * 128:(c + 1) * 128], cn[:, c * 128:(c + 1) * 128], idb[:, :])),
                       reads=[ph.b("cn"), ph.b("idb")], writes=[ph.b("pc")])
            ph.add("act", lambda e: e.copy(cnT[:, :, :], pc[:, 0:384].rearrange("p (c t) -> p c t", c=3)), reads=[ph.b("pc")], writes=[ph.b("cnT")])
            for n in range(3):
                for c in range(2):
                    ph.add("pe", (lambda e, n=n, c=c: e.matmul(pb_[n][:, :], cnT[:, c, :], wuq[:, c, n * 512:(n + 1) * 512], start=(c == 0), stop=(c == 1))),
                           reads=[ph.b("cnT"), ph.b("p8_wuq")], writes=[ph.b("pb", n)])
                ph.add("act", (lambda e, n=n: e.copy(qsb[:, n * 512:(n + 1) * 512], pb_[n][:, :])), reads=[ph.b("pb", n)], writes=[ph.b("qs")])
            q3 = qsb[:, :].rearrange("p (h d) -> p h d", h=16)
            ph.add("pool", lambda e: e.tensor_copy(qb[:, :, 0:64], q3[:, :, 0:64]), reads=[ph.b("qs")], writes=[ph.b("qb")])
            rope3(ph, "dve", qb[:, :, 64:96], q3[:, :, 64:96], tb[s][:, 0:32], tb[s][:, 32:64], 16, 32, 1, t1[:, :], t2[:, :],
                  [ph.b("qs"), B("tb")], [ph.b("t1"), ph.b("t2")], ph.b("qb"))
            for n in range(4):
                bn = (3 + n) % 4
                ph.add("pe", (lambda e, n=n, bn=bn: e.matmul(pb_[bn][:, :], cnT[:, 2, :], wukv[:, 0, n * 512:(n + 1) * 512], start=True, stop=True)),
                       reads=[ph.b("cnT"), ph.b("p8_wukv")], writes=[ph.b("pb", bn)])
                ph.add("act", (lambda e, n=n, bn=bn: e.copy(kvs[:, n * 512:(n + 1) * 512], pb_[bn][:, :])), reads=[ph.b("pb", bn)], writes=[ph.b("kvs")])
            kv3 = kvs[:, :].rearrange("p (h d) -> p h d", h=16)
            ph.add("pool", lambda e: e.tensor_copy(kb[:, :, 0:64], kv3[:, :, 0:64]), reads=[ph.b("kvs")], writes=[ph.b("kb")])
            ph.add("pool", (lambda e, s=s: e.tensor_copy(vst[s][:, :, 0:64], kv3[:, :, 64:128])), reads=[ph.b("kvs")], writes=[B("vst")])
            for h4 in range(4):
                ph.dma("sp", dr["v_C"][h4 * 4:(h4 + 1) * 4, :, i, :].rearrange("k p f -> p k f"), vst[s][:, h4 * 4:(h4 + 1) * 4, :], reads=[B("vst")])
            rope3(ph, "dve", kr[:, :].unsqueeze(1), pr[:, 384:416].unsqueeze(1), tb[s][:, 0:32], tb[s][:, 32:64], 1, 32, 1, t3[:, :], t4[:, :],
                  [prB, B("tb")], [ph.b("t3"), ph.b("t4")], ph.b("kr"))
            ph.add("dve", lambda e: e.tensor_copy(kb[:, :, 64:96], kr[:, :].unsqueeze(1).to_broadcast([128, 16, 32])),
                   reads=[ph.b("kr")], writes=[ph.b("kb")])
            for (src, sn, stt, stn, dst) in ((qb, "qb", stq, "stq", dr["qT_C"]), (kb, "kb", stk, "stk", dr["kT_C"])):
                for hh in range(2):
                    bnk = [pb_[0], pb_[1]] if sn == "qb" else [pb_[2], pb_[3]]
                    bnn = [0, 1] if sn == "qb" else [2, 3]
                    ptile = bnk[hh]
                    pv = ptile[:, :].bitcast(BF16)
                    for h8 in range(8):
                        h = hh * 8 + h8
                        ph.add("pe", (lambda e, pv=pv, h8=h8, h=h, src=src: e.transpose(pv[0:96, h8 * 128:(h8 + 1) * 128], src[:, h, :], idb[:, :])),
                               reads=[ph.b(sn), ph.b("idb")], writes=[ph.b("pb", bnn[hh])])
                    ph.add("act", (lambda e, pv=pv, hh=hh, stt=stt, s=s: e.copy(stt[s][0:96, hh * 8:(hh + 1) * 8, :],
                                                                                 pv[0:96, :].rearrange("p (c t) -> p c t", c=8))),
                           reads=[ph.b("pb", bnn[hh])], writes=[B(stn)])
                for h4 in range(4):
                    ph.dma("sp", dst[h4 * 4:(h4 + 1) * 4, :, tok].rearrange("h p t -> p h t"), stt[s][0:96, h4 * 4:(h4 + 1) * 4, :], reads=[B(stn)])
        for i in range(NT + 1):
            if i < NT:
                stageA(i)
            if i >= 1:
                stageB(i - 1)
        ph.emit()


PHASES = ["p1", "p2", "p3", "p4", "p5", "p6", "p7", "p8", "p9", "p10", "p11", "p12", "p13"]


def build(upto="p13", dbg=()):
    P = Prog(dbg)
    nc = P.nc
    x = P.din("x", [S, D])
    P.din("w_in", [D, 1536])
    P.din("ab_q_norm", [1, 64])
    P.din("ab_k_norm", [1, 64])
    P.din("ab_sink", [1, 8])
    P.din("ab_w_out", [D, D])
    P.din("mla_w_down", [1, D, 416])
    P.din("mla_q_norm", [1, 256])
    P.din("mla_kv_norm", [1, 128])
    P.din("mla_w_uq", [1, 256, 1536])
    P.din("mla_w_ukv", [1, 128, 2048])
    P.din("mla_w_out", [D, D])
    for n in ("ln_mix_g", "ln_mix_b", "ln_ffn_g", "ln_ffn_b"):
        P.din(n, [2, D])
    P.din("moe_router", [2, D, NE])
    for n in ("moe_w_gate", "moe_w_up", "moe_w_down"):
        P.din(n, [2, NE, D, D])
    P.din("tab", [S, 320])
    P.din("cst", [128, C_W])
    P.din("wmask", [128, 3072])
    P.dtmp("qT_A", [4, 128, S], BF16)
    P.dtmp("qT_B", [4, 128, S], BF16)
    P.dtmp("kT_A", [2, 128, S], BF16)
    P.dtmp("kT_B", [2, 128, S], BF16)
    P.dtmp("v_all", [4, 128, NT, 128], BF16)
    P.dtmp("oT", [D, S], BF16)
    P.dtmp("x1ext", [S, XW])
    P.dtmp("acc", [S, D])
    P.dtmp("affd", [S, NE])
    P.dtmp("p5_Cd", [NE, S])
    P.dtmp("p11_Cd", [NE, S])
    P.dtmp("x2ext", [S, XW])
    P.dtmp("x3ext", [S, XW])
    P.dtmp("acc2", [S, D])
    P.dtmp("qT_C", [16, 96, S], BF16)
    P.dtmp("kT_C", [16, 96, S], BF16)
    P.dtmp("v_C", [16, 128, NT, 128], BF16)
    out = P.dtmp("out", [S, D], out=True)
    if "dbg_idx" in dbg:
        P.dtmp("dbg_idx", [128, 128], I32)
        P.dtmp("dbg_lo", [128, NE])
    if "dbg_xg" in dbg:
        P.dtmp("dbg_xg", [128, XW])
        P.dtmp("dbg_yo", [128, D])
        P.dtmp("dbg_hT", [128, 8192], BF16)
    if "dbg_q9" in dbg:
        P.dtmp("dbg_q9", [128, S], BF16)
        P.dtmp("dbg_k9", [128, S], BF16)
    if "dbg_rec" in dbg:
        P.dtmp("dbg_rec", [64, 512])
        P.dtmp("dbg_num", [128, 512])
    if "dbgv0" in dbg:
        P.dtmp("dbgv0", [128, NT, 128], BF16)
        P.dtmp("dbgv1", [128, NT, 128], BF16)
    dr = P.dram
    last = PHASES.index(upto)
    on = lambda p: PHASES.index(p) <= last
    with contextlib.ExitStack() as st:
        idx = sb(st, nc, "idx_all", [128, NE, 8], I32)
        if on("p1"):
            phase_proj0(P, x)
        if on("p2"):
            jobs = []
            for h in range(8):
                j, half, kv = h // 2, h % 2, h // 4
                jobs.append(dict(qsrc=(("qA", j), dr["qT_A"][j, :, :], 128), ksrc=(("kA", kv), dr["kT_A"][kv, :, :], 128),
                                 vsrc=(("vA", kv), dr["v_all"][kv, :, :, :]), pb=half * 64, K=64, out=dr["oT"][h * 64:(h + 1) * 64, :]))
            phase_dense_attn(P, "p2", jobs, 64 ** -0.5)
        if on("p3"):
            jobs = []
            for h in range(8):
                j, half, kv = h // 2, h % 2, h // 4
                jobs.append(dict(qsrc=(("qB", j), dr["qT_B"][j, :, :], 128), ksrc=(("kB", kv), dr["kT_B"][kv, :, :], 128),
                                 vsrc=(("vB", kv), dr["v_all"][2 + kv, :, :, :]), pb=half * 64, K=64, h=h,
                                 out=dr["oT"][512 + h * 64:512 + (h + 1) * 64, :]))
            phase_dense_attn(P, "p3", jobs, 64 ** -0.5, win=True)
        if on("p4"):
            phase_outproj(P, "p4", dr["ab_w_out"][:, :], lambda tok: x[tok, :], dr["ln_mix_g"][0, :], dr["ln_mix_b"][0, :],
                          dr["moe_router"][0, :, :], dr["x1ext"], dr["acc"], dr["affd"])
        if on("p5"):
            phase_route(P, "p5", dr["x1ext"], idx)
        if on("p6"):
            phase_experts(P, "p6", 0, dr["x1ext"], dr["acc"], idx)
        if on("p7"):
            phase_ln(P, "p7", dr["acc"], dr["ln_ffn_g"][0, :], dr["ln_ffn_b"][0, :], lambda tok: dr["x2ext"][tok, 0:D])
        if on("p8"):
            phase_proj_mla(P, lambda tok: dr["x2ext"][tok, 0:D])
        if on("p9"):
            jobs = []
            for h in range(16):
                jobs.append(dict(qsrc=(("qC", h), dr["qT_C"][h, :, :], 96), ksrc=(("kC", h), dr["kT_C"][h, :, :], 96),
                                 vsrc=(("vC", h), dr["v_C"][h, :, :, :]), pb=0, K=96, out=dr["oT"][h * 64:(h + 1) * 64, :]))
            phase_dense_attn(P, "p9", jobs, 96 ** -0.5)
        if on("p10"):
            phase_outproj(P, "p10", dr["mla_w_out"][:, :], lambda tok: dr["x2ext"][tok, 0:D], dr["ln_mix_g"][1, :], dr["ln_mix_b"][1, :],
                          dr["moe_router"][1, :, :], dr["x3ext"], dr["acc2"], dr["affd"])
        if on("p11"):
            phase_route(P, "p11", dr["x3ext"], idx)
        if on("p12"):
            phase_experts(P, "p12", 1, dr["x3ext"], dr["acc2"], idx)
        if on("p13"):
            phase_ln(P, "p13", dr["acc2"], dr["ln_ffn_g"][1, :], dr["ln_ffn_b"][1, :], lambda tok: out[tok, :])
    return P


_PERM = np.concatenate([np.arange(0, 512), np.arange(512, 640), np.arange(768, 1280), np.arange(1280, 1408),
                        np.arange(640, 768), np.arange(1408, 1536)])


def make_in_maps(inputs, ncores=8):
    tab, cst = make_consts()
    f = lambda a: np.ascontiguousarray(np.asarray(a, dtype=np.float32))
    shared = {
        "w_in": f(np.asarray(inputs["ab_w_in"])[0][:, _PERM]),
        "ab_q_norm": f(inputs["ab_q_norm"]), "ab_k_norm": f(inputs["ab_k_norm"]), "ab_sink": f(inputs["ab_sink"]),
        "ab_w_out": f(np.asarray(inputs["ab_w_out"])[0]),
        "mla_w_down": f(inputs["mla_w_down"]), "mla_q_norm": f(inputs["mla_q_norm"]), "mla_kv_norm": f(inputs["mla_kv_norm"]),
        "mla_w_uq": f(inputs["mla_w_uq"]), "mla_w_ukv": f(inputs["mla_w_ukv"]), "mla_w_out": f(np.asarray(inputs["mla_w_out"])[0]),
        "ln_mix_g": f(inputs["ln_mix_g"]), "ln_mix_b": f(inputs["ln_mix_b"]), "ln_ffn_g": f(inputs["ln_ffn_g"]), "ln_ffn_b": f(inputs["ln_ffn_b"]),
        "moe_router": f(inputs["moe_router"]), "moe_w_gate": f(inputs["moe_w_gate"]), "moe_w_up": f(inputs["moe_w_up"]),
        "moe_w_down": f(inputs["moe_w_down"]), "tab": tab, "cst": cst, "wmask": make_wmask(),
    }
    xs = np.asarray(inputs["x"], dtype=np.float32)
    maps = []
    for c in range(ncores):
        m = dict(shared)
        m["x"] = np.ascontiguousarray(xs[c])
        maps.append(m)
    return maps


def kernel(**inputs):
    P = build()
    maps = make_in_maps(inputs, 8)
    res = run_bass_kernel_spmd(P.nc, maps, core_ids=list(range(8)))
    return np.stack([np.asarray(r["out"], dtype=np.float32) for r in res.results], axis=0)
```

```python
import contextlib
import numpy as np
import concourse.bass as bass
import concourse.mybir as mybir
from concourse.bass_utils import run_bass_kernel_spmd

F32 = mybir.dt.float32
BF16 = mybir.dt.bfloat16
I32 = mybir.dt.int32
ALU = mybir.AluOpType
AF = mybir.ActivationFunctionType
AX = mybir.AxisListType

S = 8192
D = 1024
NT = S // 128
NE = 16
CAP = 1024
DEPTH = 2
ALPHA = (2.0 * DEPTH) ** 0.25
XW = D + NE

SAME_ENGINE_SYNC = True


class Buf:
    __slots__ = ("w", "r", "rd")

    def __init__(self):
        self.w = None
        self.r = {}
        self.rd = []


class Op:
    __slots__ = ("eng", "fn", "deps", "sig", "sem", "val", "dma", "inc")

    def __init__(self, eng, fn, dma):
        self.eng = eng
        self.fn = fn
        self.dma = dma
        self.deps = []
        self.sig = dma
        self.sem = None
        self.val = 0
        self.inc = 16 if dma else 1


ENGS = ("sp", "pe", "act", "dve", "pool")


class Phase:
    KD = {"sp": 8, "pool": 6, "act": 4}

    def __init__(self, nc, name):
        self.nc = nc
        self.name = name
        self.ops = {e: [] for e in ENGS}
        self.bufs = {}
        self.dma_ops = {q: [] for q in self.KD}

    def b(self, *key):
        bb = self.bufs.get(key)
        if bb is None:
            bb = self.bufs[key] = Buf()
        return bb

    def add(self, eng, fn, reads=(), writes=(), dma=False):
        op = Op(eng, fn, dma)
        deps = []
        for b in reads:
            if b.w is not None:
                deps.append(b.w)
        for b in writes:
            if b.w is not None:
                deps.append(b.w)
            deps.extend(b.r.values())
            deps.extend(b.rd)
        if dma:
            lst = self.dma_ops[eng]
            k = self.KD[eng]
            if len(lst) >= k:
                deps.append(lst[len(lst) - k])
            lst.append(op)
        seen = set()
        for d in deps:
            if d is op or id(d) in seen:
                continue
            seen.add(id(d))
            if (not d.dma) and (not dma) and d.eng == eng:
                if eng == "pe" or not SAME_ENGINE_SYNC:
                    continue
            op.deps.append(d)
            d.sig = True
        for b in reads:
            if dma:
                b.rd.append(op)
            else:
                b.r[eng] = op
        for b in writes:
            b.w = op
            b.r = {}
            b.rd = []
        self.ops[eng].append(op)
        return op

    def dma(self, q, out, in_, reads=(), writes=(), **kw):
        return self.add(q, lambda e: e.dma_start(out=out, in_=in_, **kw), reads, writes, dma=True)

    def emit(self):
        nc = self.nc
        allsems = []

        def newsem(nm):
            h = nc.alloc_semaphore(name=nm)
            allsems.append(h)
            return h

        if True:
            csem = {}
            for e in ("pe", "act", "dve", "pool"):
                if any(o.sig and not o.dma for o in self.ops[e]):
                    csem[e] = newsem(f"{self.name}_c_{e}")
            dsem = {}
            for q, k in self.KD.items():
                if self.dma_ops[q]:
                    dsem[q] = [newsem(f"{self.name}_d_{q}{i}") for i in range(min(k, len(self.dma_ops[q])))]
            for e in ENGS:
                cnt = 0
                for o in self.ops[e]:
                    if o.dma:
                        continue
                    if o.sig:
                        cnt += 1
                        o.sem = csem[e]
                        o.val = cnt
            final = {q: {} for q in self.KD}
            for q, lst in self.dma_ops.items():
                k = self.KD[q]
                for n, o in enumerate(lst):
                    o.sem = dsem[q][n % k]
                    o.val = 16 * (n // k + 1)
                    final[q][n % k] = (o.sem, o.val)
            ops = self.ops

            def run(e_name, eng):
                waited = {}
                for o in ops[e_name]:
                    need = {}
                    for d in o.deps:
                        key = id(d.sem)
                        if waited.get(key, 0) >= d.val:
                            continue
                        if key not in need or need[key][1] < d.val:
                            need[key] = (d.sem, d.val)
                    for key, (sem, val) in need.items():
                        eng.wait_ge(sem, val)
                        waited[key] = val
                    ins = o.fn(eng)
                    if o.sig:
                        ins.then_inc(o.sem, o.inc)
                if e_name in final:
                    for (sem, val) in final[e_name].values():
                        if waited.get(id(sem), 0) < val:
                            eng.wait_ge(sem, val)

            with nc.Block() as block:
                if ops["sp"]:
                    @block.sync
                    def _(e):
                        run("sp", e)
                if ops["pe"]:
                    @block.tensor
                    def _(e):
                        run("pe", e)
                if ops["act"]:
                    @block.scalar
                    def _(e):
                        run("act", e)
                if ops["dve"]:
                    @block.vector
                    def _(e):
                        run("dve", e)
                if ops["pool"]:
                    @block.gpsimd
                    def _(e):
                        run("pool", e)
            nc.clear_and_free_semaphores(allsems)
            nc.all_engine_barrier()


def _rope_tab(pos, dim):
    freqs = (10000.0 ** (-(np.arange(0, dim, 2, dtype=np.float32) / np.float32(dim)))).astype(np.float32)
    ang = pos.astype(np.float32)[:, None] * freqs[None, :]
    return np.cos(ang).astype(np.float32), np.sin(ang).astype(np.float32)


def make_consts():
    t = np.arange(S)
    cr, sr = _rope_tab((t // 64).astype(np.float32), 32)
    cc, sc = _rope_tab((t % 64).astype(np.float32), 32)
    cs, ss = _rope_tab(t.astype(np.float32), 64)
    cm, sm = _rope_tab(t.astype(np.float32), 32)
    cosA = np.concatenate([cr, cr, cc, cc], 1)
    sinA = np.concatenate([-sr, sr, -sc, sc], 1)
    cosB = np.concatenate([cs, cs], 1)
    sinB = np.concatenate([-ss, ss], 1)
    cosC = np.concatenate([cm, cm], 1)
    sinC = np.concatenate([-sm, sm], 1)
    tab = np.concatenate([cosA, sinA, cosB, sinB, cosC, sinC], 1).astype(np.float32)
    ident = np.eye(128, dtype=np.float32)
    p = np.arange(128)
    tri = (p[:, None] < p[None, :]).astype(np.float32)
    mprev = (p[None, :] <= p[:, None]).astype(np.float32)
    mnext = (p[:, None] <= p[None, :]).astype(np.float32)
    svec = (np.arange(8)[None, :] * 128 + p[:, None]).astype(np.float32)
    keep = np.ones((128, NE, 64), np.float32)
    keep[:, :, 0] = 0.0
    vsink = np.zeros((128, 128), np.float32)
    vsink[0:2, 64:128] = 1.0
    cst = np.concatenate([ident, tri, mprev, mnext, svec, keep.reshape(128, -1), vsink], 1).astype(np.float32)
    return tab, cst


def make_wmask():
    kl = np.arange(128)[:, None]
    q = np.arange(512)[None, :]
    ms = []
    for r in range(6):
        ms.append((np.abs(q - ((r - 1) * 128 + kl)) <= 128).astype(np.float32))
    return np.concatenate(ms, 1)


C_IDENT, C_TRI, C_MPREV, C_MNEXT, C_SVEC, C_KEEP = 0, 128, 256, 384, 512, 520
C_VSINK = 520 + NE * 64
C_W = C_VSINK + 128


class Prog:
    def __init__(self, dbg=()):
        self.nc = bass.Bass("TRN2", target_bir_lowering=False)
        self.dbg = set(dbg)
        self.dram = {}

    def din(self, name, shape, dt=F32):
        t = self.nc.dram_tensor(name, list(shape), dt, kind="ExternalInput")
        self.dram[name] = t
        return t

    def dtmp(self, name, shape, dt=F32, out=False):
        kind = "ExternalOutput" if (out or name in self.dbg) else "Internal"
        t = self.nc.dram_tensor(name, list(shape), dt, kind=kind)
        self.dram[name] = t
        return t


def sb(st, nc, name, shape, dt):
    return st.enter_context(nc.sbuf_tensor(name, list(shape), dt))


def ps(st, nc, name, shape, dt):
    return st.enter_context(nc.psum_tensor(name, list(shape), dt))


def load_consts(ph, nc, st, P):
    cst = sb(st, nc, ph.name + "_cst", [128, C_W], F32)
    ph.dma("sp", cst[:, :], P.dram["cst"][:, :], writes=[ph.b("cst")])
    idb = sb(st, nc, ph.name + "_idb", [128, 128], BF16)
    ph.add("dve", lambda e: e.tensor_copy(idb[:, :], cst[:, C_IDENT:C_IDENT + 128]),
           reads=[ph.b("cst")], writes=[ph.b("idb")])
    return cst, idb


def emit_xT(ph, xs, xs_buf, xT, xT_buf, pT, pT_bufs, cst, ncol=8, evac=("act", "dve")):
    ident = cst[:, C_IDENT:C_IDENT + 128]
    ng = (ncol + 3) // 4
    for g in range(ng):
        pt = pT[g % len(pT)]
        pb = pT_bufs[g % len(pT)]
        n = min(4, ncol - g * 4)
        for c in range(n):
            cc = g * 4 + c
            ph.add("pe", (lambda e, pt=pt, c=c, cc=cc: e.transpose(pt[:, c * 128:(c + 1) * 128],
                                                                   xs[:, cc * 128:(cc + 1) * 128], ident)),
                   reads=[xs_buf, ph.b("cst")], writes=[pb])
        eng = evac[g % len(evac)]
        if eng == "act":
            ph.add("act", (lambda e, pt=pt, g=g, n=n: e.copy(
                xT[:, g * 4:g * 4 + n, :], pt[:, 0:n * 128].rearrange("p (c t) -> p c t", c=n))),
                reads=[pb], writes=[xT_buf])
        else:
            ph.add("dve", (lambda e, pt=pt, g=g, n=n: e.tensor_copy(
                xT[:, g * 4:g * 4 + n, :], pt[:, 0:n * 128].rearrange("p (c t) -> p c t", c=n))),
                reads=[pb], writes=[xT_buf])


def load_weight_bf16(ph, nc, st, name, w_ap, kc, n, stage, stage_bufs, cast_engs=("dve", "pool")):
    wb = sb(st, nc, name, [128, kc, n], BF16)
    wv = w_ap.rearrange("(c p) n -> p c n", p=128)
    cnt = 0
    for c in range(kc):
        for n0 in range(0, n, stage[0].shape[1]):
            n1 = min(n, n0 + stage[0].shape[1])
            s = cnt % len(stage)
            ph.dma("sp", stage[s][:, 0:n1 - n0], wv[:, c, n0:n1], writes=[stage_bufs[s]])
            eng = cast_engs[cnt % len(cast_engs)]
            ph.add(eng, (lambda e, s=s, c=c, n0=n0, n1=n1: e.tensor_copy(wb[:, c, n0:n1], stage[s][:, 0:n1 - n0])),
                   reads=[stage_bufs[s]], writes=[ph.b(name)])
            cnt += 1
    return wb


def emit_rope(ph, eng, out, src, cos, sin, nh, hd, tmp1, tmp2, rbufs, wbufs, blocks):
    hb = hd // blocks // 2
    v = lambda a: a.rearrange("p (h b t f) -> p h b t f", h=nh, b=blocks, t=2, f=hb)
    tb = lambda a: a.rearrange("p (b t f) -> p b t f", b=blocks, t=2, f=hb)
    cosb = cos.unsqueeze(1).to_broadcast([128, nh, hd]).rearrange("p h d -> p (h d)") if False else None
    s3 = src.rearrange("p (h d) -> p h d", h=nh)
    t13 = tmp1.rearrange("p (h d) -> p h d", h=nh)
    cos3 = cos.unsqueeze(1).to_broadcast([128, nh, hd])
    ph.add(eng, lambda e: e.tensor_tensor(t13, s3, cos3, ALU.mult), reads=rbufs, writes=[wbufs[0]])
    sv, t2v = v(src), v(tmp2)
    sn = tb(sin)
    for t in range(2):
        sin_b = sn[:, :, t, :].unsqueeze(1).to_broadcast([128, nh, blocks, hb])
        ph.add(eng, (lambda e, t=t, sin_b=sin_b: e.tensor_tensor(t2v[:, :, :, t, :], sv[:, :, :, 1 - t, :], sin_b, ALU.mult)),
               reads=rbufs, writes=[wbufs[1]])
    ph.add(eng, lambda e: e.tensor_tensor(out, tmp1, tmp2, ALU.add), reads=[wbufs[0], wbufs[1]], writes=[wbufs[2]])


def rope3(ph, eng, out3, src3, cos, sin, nh, hd, blocks, t1, t2, rbufs, tbufs, wbuf):
    hb = hd // blocks // 2
    t13 = t1.rearrange("p (h d) -> p h d", h=nh)
    t23 = t2.rearrange("p (h d) -> p h d", h=nh)
    cos3 = cos.unsqueeze(1).to_broadcast([128, nh, hd])
    ph.add(eng, lambda e: e.tensor_tensor(t13, src3, cos3, ALU.mult), reads=rbufs, writes=[tbufs[0]])
    for b in range(blocks):
        for t in range(2):
            o0 = b * 2 * hb + t * hb
            s0 = b * 2 * hb + (1 - t) * hb
            sin_b = sin[:, o0:o0 + hb].unsqueeze(1).to_broadcast([128, nh, hb])
            ph.add(eng, (lambda e, o0=o0, s0=s0, sin_b=sin_b: e.tensor_tensor(
                t23[:, :, o0:o0 + hb], src3[:, :, s0:s0 + hb], sin_b, ALU.mult)),
                reads=rbufs, writes=[tbufs[1]])
    ph.add(eng, lambda e: e.tensor_tensor(out3, t13, t23, ALU.add), reads=[tbufs[0], tbufs[1]], writes=[wbuf])


def lowp(nc, f):
    def g(e):
        with nc.allow_low_precision(reason="exact small integer counts"):
            return f(e)
    return g


def bc(ap1d):
    return ap1d.partition_broadcast(128)


def phase_proj0(P, x_ap):
    nc = P.nc
    dr = P.dram
    with contextlib.ExitStack() as st:
        ph = Phase(nc, "p1")
        cst, idb = load_consts(ph, nc, st, P)
        stage = [sb(st, nc, f"p1_stg{i}", [128, 1536], F32) for i in range(2)]
        wb = load_weight_bf16(ph, nc, st, "p1_wb", dr["w_in"][:, :], 8, 1536, stage,
                              [ph.b("stg", i) for i in range(2)], cast_engs=("dve", "pool"))
        gq = sb(st, nc, "p1_gq", [128, 64], F32)
        gk = sb(st, nc, "p1_gk", [128, 64], F32)
        ph.dma("sp", gq[:, :], bc(dr["ab_q_norm"][0, :]), writes=[ph.b("gq")])
        ph.dma("sp", gk[:, :], bc(dr["ab_k_norm"][0, :]), writes=[ph.b("gk")])
        NSL = 2
        xs = [sb(st, nc, f"p1_xs{i}", [128, 1024], F32) for i in range(NSL)]
        tb = [sb(st, nc, f"p1_tb{i}", [128, 256], F32) for i in range(4)]
        xT = [sb(st, nc, f"p1_xT{i}", [128, 8, 128], BF16) for i in range(NSL)]
        pr = [sb(st, nc, f"p1_pr{i}", [128, 1536], F32) for i in range(4)]
        sq_ = [sb(st, nc, f"p1_sq{i}", [128, 640], F32) for i in range(2)]
        ss_ = [sb(st, nc, f"p1_ss{i}", [128, 10], F32) for i in range(2)]
        sd_ = [sb(st, nc, f"p1_sd{i}", [128, 10], F32) for i in range(2)]
        rs_ = [sb(st, nc, f"p1_rs{i}", [128, 10], F32) for i in range(2)]
        t0_ = [sb(st, nc, f"p1_t0{i}", [128, 640], F32) for i in range(2)]
        ta1_ = [sb(st, nc, f"p1_ta1{i}", [128, 640], F32) for i in range(2)]
        ta2_ = [sb(st, nc, f"p1_ta2{i}", [128, 640], F32) for i in range(2)]
        tb1_ = [sb(st, nc, f"p1_rb1{i}", [128, 640], F32) for i in range(2)]
        tb2_ = [sb(st, nc, f"p1_rb2{i}", [128, 640], F32) for i in range(2)]
        oa = [sb(st, nc, f"p1_oa{i}", [128, 640], BF16) for i in range(NSL)]
        ob = [sb(st, nc, f"p1_ob{i}", [128, 640], BF16) for i in range(NSL)]
        stA = [sb(st, nc, f"p1_stA{i}", [128, 5, 128], BF16) for i in range(NSL)]
        stB = [sb(st, nc, f"p1_stB{i}", [128, 5, 128], BF16) for i in range(NSL)]
        vst = [sb(st, nc, f"p1_vst{i}", [128, 4, 128], BF16) for i in range(NSL)]
        pT = [ps(st, nc, f"p1_pT{i}", [128, 512], F32) for i in range(2)]
        pp = [ps(st, nc, f"p1_pp{i}", [128, 512], F32) for i in range(3)]
        ptA = ps(st, nc, "p1_ptA", [128, 1024], BF16)
        ptB = ps(st, nc, "p1_ptB", [128, 1024], BF16)
        for s in range(NSL):
            ph.add("pool", (lambda e, s=s: e.memset(vst[s][:, :, :], 1.0)), writes=[ph.b("vst", s)])
        eps = sb(st, nc, "p1_eps", [128, 1], F32)
        ph.add("pool", lambda e: e.memset(eps[:, :], 1e-6), writes=[ph.b("eps")])
        qTA, qTB, kTA, kTB, vall = dr["qT_A"], dr["qT_B"], dr["kT_A"], dr["kT_B"], dr["v_all"]
        def stageA(i):
            s = i % NSL
            s4 = i % 4
            tok = slice(i * 128, (i + 1) * 128)
            B = lambda n: ph.b(n, s)
            ph.dma("sp", xs[s][:, :], x_ap[tok, :], writes=[B("xs")])
            ph.dma("sp", tb[s4][:, :], dr["tab"][tok, 0:256], writes=[ph.b("tb", s4)])
            emit_xT(ph, xs[s], B("xs"), xT[s], B("xT"), pT, [ph.b("pT", 0), ph.b("pT", 1)], cst)
            for n in range(3):
                for c in range(8):
                    ph.add("pe", (lambda e, n=n, c=c, s=s: e.matmul(pp[n][:, :], xT[s][:, c, :], wb[:, c, n * 512:(n + 1) * 512],
                                                                     start=(c == 0), stop=(c == 7))),
                           reads=[B("xT"), ph.b("p1_wb")], writes=[ph.b("pp", n)])
                ph.add("act", (lambda e, n=n, s4=s4: e.copy(pr[s4][:, n * 512:(n + 1) * 512], pp[n][:, :])),
                       reads=[ph.b("pp", n)], writes=[ph.b("pr", s4)])

        def stageB(i):
            s = i % NSL
            s4 = i % 4
            tok = slice(i * 128, (i + 1) * 128)
            B = lambda n: ph.b(n, s)
            prt, tbt, prB, tbB = pr[s4], tb[s4], ph.b("pr", s4), ph.b("tb", s4)
            sq, ss, sd, rs, t0, ta1, ta2, tb1, tb2 = sq_[s], ss_[s], sd_[s], rs_[s], t0_[s], ta1_[s], ta2_[s], tb1_[s], tb2_[s]
            rope3(ph, "pool", ob[s][:, :].rearrange("p (h d) -> p h d", h=10), prt[:, 640:1280].rearrange("p (h d) -> p h d", h=10),
                  tbt[:, 128:192], tbt[:, 192:256], 10, 64, 1, tb1[:, :], tb2[:, :],
                  [prB, tbB], [B("tb1"), B("tb2")], B("ob"))
            ph.add("act", lambda e: e.copy(vst[s][:, :, 0:64], prt[:, 1280:1536].rearrange("p (h d) -> p h d", h=4)),
                   reads=[prB], writes=[B("vst")])
            ph.dma("sp", vall[:, :, i, :].rearrange("k p f -> p k f"), vst[s][:, :, :], reads=[B("vst")])
            ph.add("act", lambda e: e.activation(sq[:, :], prt[:, 0:640], AF.Square), reads=[prB], writes=[B("sq")]); yield
            ph.add("dve", lambda e: e.tensor_reduce(ss[:, :], sq[:, :].rearrange("p (h d) -> p h d", h=10), AX.X, ALU.add),
                   reads=[B("sq")], writes=[B("ss")]); yield
            ph.add("act", lambda e: e.activation(sd[:, :], ss[:, :], AF.Sqrt, bias=eps[:, :], scale=1.0 / 64),
                   reads=[B("ss"), ph.b("eps")], writes=[B("sd")]); yield
            ph.add("dve", lambda e: e.reciprocal(rs[:, :], sd[:, :]), reads=[B("sd")], writes=[B("rs")]); yield
            ph.add("dve", lambda e: e.tensor_tensor(t0[:, :].rearrange("p (h d) -> p h d", h=10),
                                                    prt[:, 0:640].rearrange("p (h d) -> p h d", h=10),
                                                    rs[:, :].unsqueeze(2).to_broadcast([128, 10, 64]), ALU.mult),
                   reads=[prB, B("rs")], writes=[B("t0")])
            ph.add("dve", lambda e: e.tensor_tensor(t0[:, 0:512].rearrange("p (h d) -> p h d", h=8),
                                                    t0[:, 0:512].rearrange("p (h d) -> p h d", h=8),
                                                    gq[:, :].unsqueeze(1).to_broadcast([128, 8, 64]), ALU.mult),
                   reads=[ph.b("gq")], writes=[B("t0")])
            ph.add("dve", lambda e: e.tensor_tensor(t0[:, 512:640].rearrange("p (h d) -> p h d", h=2),
                                                    t0[:, 512:640].rearrange("p (h d) -> p h d", h=2),
                                                    gk[:, :].unsqueeze(1).to_broadcast([128, 2, 64]), ALU.mult),
                   reads=[ph.b("gk")], writes=[B("t0")]); yield
            rope3(ph, "dve", oa[s][:, :].rearrange("p (h d) -> p h d", h=10), t0[:, :].rearrange("p (h d) -> p h d", h=10),
                  tbt[:, 0:64], tbt[:, 64:128], 10, 64, 2, ta1[:, :], ta2[:, :],
                  [B("t0"), tbB], [B("ta1"), B("ta2")], B("oa")); yield
            for (src, sbuf_, ptt, ptn, stt, stn, qT, kT) in ((oa, "oa", ptA, "ptA", stA, "stA", qTA, kTA),
                                                             (ob, "ob", ptB, "ptB", stB, "stB", qTB, kTB)):
                for c in range(5):
                    ph.add("pe", (lambda e, src=src, ptt=ptt, c=c: e.transpose(ptt[:, c * 128:(c + 1) * 128],
                                                                               src[s][:, c * 128:(c + 1) * 128], idb[:, :])),
                           reads=[B(sbuf_), ph.b("idb")], writes=[ph.b(ptn)])
                ph.add("act", (lambda e, ptt=ptt, stt=stt: e.copy(stt[s][:, :, :], ptt[:, 0:640].rearrange("p (c t) -> p c t", c=5))),
                       reads=[ph.b(ptn)], writes=[B(stn)])
                ph.dma("sp", qT[:, :, tok].rearrange("j p t -> p j t"), stt[s][:, 0:4, :], reads=[B(stn)])
                for kv in range(2):
                    for dup in range(2):
                        ph.dma("sp", kT[kv, dup * 64:(dup + 1) * 64, tok], stt[s][kv * 64:(kv + 1) * 64, 4, :], reads=[B(stn)])
                yield

        def drive(gens):
            gens = list(gens)
            while gens:
                for g_ in list(gens):
                    try:
                        next(g_)
                    except StopIteration:
                        gens.remove(g_)

        stageA(0)
        stageA(1)
        for i in range(0, NT, 2):
            if i + 2 < NT:
                stageA(i + 2)
                stageA(i + 3)
            drive([stageB(i), stageB(i + 1)])
        ph.emit()


def phase_dense_attn(P, name, jobs, scale, win=False):
    nc = P.nc
    dr = P.dram
    with contextlib.ExitStack() as st:
        ph = Phase(nc, name)
        if win:
            cst, idb = load_consts(ph, nc, st, P)
            wm = sb(st, nc, f"{name}_wm", [128, 3072], BF16)
            wst = sb(st, nc, f"{name}_wst", [128, 3072], F32)
            ph.dma("sp", wst[:, :], dr["wmask"][:, :], writes=[ph.b("wst")])
            ph.add("dve", lambda e: e.tensor_copy(wm[:, :], wst[:, :]), reads=[ph.b("wst")], writes=[ph.b("wm")])
            vsk = sb(st, nc, f"{name}_vsk", [128, 128], BF16)
            ph.add("dve", lambda e: e.tensor_copy(vsk[:, :], cst[:, C_VSINK:C_VSINK + 128]), reads=[ph.b("cst")], writes=[ph.b("vsk")])
            sk = sb(st, nc, f"{name}_sk", [128, 8], F32)
            es = sb(st, nc, f"{name}_es", [128, 8], F32)
            ehb = sb(st, nc, f"{name}_ehb", [128, 8], BF16)
            ehf = sb(st, nc, f"{name}_ehf", [128, 8], F32)
            elo = sb(st, nc, f"{name}_elo", [128, 8], F32)
            ssel = sb(st, nc, f"{name}_ssel", [128, 8], F32)
            onesb = sb(st, nc, f"{name}_onesb", [128, 512], BF16)
            psink = [sb(st, nc, f"{name}_psink{i}", [128, 512], BF16) for i in range(2)]
            ph.dma("sp", sk[:, :], bc(dr["ab_sink"][0, :]), writes=[ph.b("sk")])
            ph.add("act", lambda e: e.activation(es[:, :], sk[:, :], AF.Exp), reads=[ph.b("sk")], writes=[ph.b("es")])
            ph.add("dve", lowp(nc, lambda e: e.tensor_copy(ehb[:, :], es[:, :])), reads=[ph.b("es")], writes=[ph.b("ehb")])
            ph.add("dve", lambda e: e.tensor_copy(ehf[:, :], ehb[:, :]), reads=[ph.b("ehb")], writes=[ph.b("ehf")])
            ph.add("dve", lambda e: e.tensor_tensor(elo[:, :], es[:, :], ehf[:, :], ALU.subtract), reads=[ph.b("es"), ph.b("ehf")], writes=[ph.b("elo")])
            ph.add("dve", lambda e: e.tensor_scalar(ehf[:, :], ehf[:, :], cst[:, C_IDENT:C_IDENT + 1], None, ALU.mult), reads=[ph.b("cst")], writes=[ph.b("ehf")])
            ph.add("dve", lambda e: e.scalar_tensor_tensor(ssel[:, :], elo[:, :], cst[:, C_IDENT + 1:C_IDENT + 2], ehf[:, :], ALU.mult, ALU.add),
                   reads=[ph.b("elo"), ph.b("ehf"), ph.b("cst")], writes=[ph.b("ssel")])
            ph.add("dve", lambda e: e.memset(onesb[:, :], 1.0), writes=[ph.b("onesb")])
        NSL = 2
        qs = [sb(st, nc, f"{name}_q{i}", [128, S], BF16) for i in range(NSL)]
        pad = (jobs[0]["K"] == 64)
        nk = 2 if pad else 1
        ks = [[sb(st, nc, f"{name}_k{i}_{hf}", [128, S], BF16) for hf in range(nk)] for i in range(NSL)]
        vs = [sb(st, nc, f"{name}_v{i}", [128, NT, 128], BF16) for i in range(NSL)]
        NP_ = 6 if win else 4
        pTs = [sb(st, nc, f"{name}_pT{i}", [128, 512], BF16) for i in range(NP_)]
        rec = sb(st, nc, f"{name}_rec", [64, 512], F32)
        onb = [sb(st, nc, f"{name}_on{i}", [64, 512], BF16) for i in range(2)]
        NS_ = 5 if win else 3
        sT = [ps(st, nc, f"{name}_sT{i}", [128, 512], F32) for i in range(NS_)]
        oacc = [ps(st, nc, f"{name}_oa{i}", [128, 512], F32) for i in range(2)]
        LA = 4 if win else 2
        if not pad:
            for i in range(NSL):
                ph.add("pool", (lambda e, i=i: e.memset(qs[i][64:128, :], 0.0)), writes=[ph.b("q", i)])
                ph.add("pool", (lambda e, i=i: e.memset(ks[i][0][64:128, :], 0.0)), writes=[ph.b("k", i)])
        else:
            for i in range(NSL):
                ph.add("pool", (lambda e, i=i: e.memset(ks[i][0][64:128, :], 0.0)), writes=[ph.b("k", i)])
                ph.add("pool", (lambda e, i=i: e.memset(ks[i][1][0:64, :], 0.0)), writes=[ph.b("k", i)])
        qk_prev = kk_prev = vv_prev = None
        slot = {"q": -1, "k": -1, "v": -1}
        cur = {}
        on_cnt = 0
        jslots = []

        def issue_loads(jb):
            for key, tiles, src in (("q", qs, jb["qsrc"]), ("k", ks, jb["ksrc"]), ("v", vs, jb["vsrc"])):
                if cur.get(key) != src[0]:
                    slot[key] = (slot[key] + 1) % NSL
                    sl = slot[key]
                    if key == "v":
                        ph.dma("sp", tiles[sl][:, :, :].rearrange("p c f -> p (c f)"), src[1].rearrange("p c f -> p (c f)"), writes=[ph.b(key, sl)])
                    elif key == "k" and pad:
                        ph.dma("sp", tiles[sl][0][0:64, :], src[1][0:64, :], writes=[ph.b(key, sl)])
                        ph.dma("sp", tiles[sl][1][64:128, :], src[1][64:128, :], writes=[ph.b(key, sl)])
                    elif key == "k":
                        rows = src[2]
                        ph.dma("sp", tiles[sl][0][0:rows, :], src[1], writes=[ph.b(key, sl)])
                    else:
                        rows = src[2]
                        ph.dma("sp", tiles[sl][0:rows, :], src[1], writes=[ph.b(key, sl)])
                    cur[key] = src[0]
            jslots.append(dict(slot))

        issue_loads(jobs[0])
        for ji, jb in enumerate(jobs):
            if ji + 1 < len(jobs):
                issue_loads(jobs[ji + 1])
            jsl = jslots[ji]
            q_t, k_t, v_t = qs[jsl["q"]], ks[jsl["k"]][(jb["pb"] // 64) if pad else 0], vs[jsl["v"]]
            qB, kB, vB = ph.b("q", jsl["q"]), ph.b("k", jsl["k"]), ph.b("v", jsl["v"])
            if "dbg_q9" in dr and name == "p9" and jb is jobs[0]:
                ph.dma("sp", dr["dbg_q9"][:, :], q_t[:, :], reads=[qB])
                ph.dma("sp", dr["dbg_k9"][:, :], k_t[:, :], reads=[kB])
            pb, K = jb["pb"], jb["K"]
            if win:
                steps = [(g, c) for g in range(16) for c in range(4 * g - 1, 4 * g + 5) if 0 <= c < NT]
                hh = jb["h"]
                psl = hh % 2
                ph.add("dve", (lambda e, psl=psl, hh=hh: e.tensor_scalar(psink[psl][:, :], onesb[:, :], ssel[:, hh:hh + 1], None, ALU.mult)),
                       reads=[ph.b("onesb"), ph.b("ssel")], writes=[ph.b("psink", psl)])
            else:
                steps = [(g, c) for g in range(16) for c in range(NT)]
            nsteps = len(steps)

            def qk(n, q_t=q_t, k_t=k_t, qB=qB, kB=kB, pb=pb, K=K):
                g, c = steps[n]
                b = n % NS_
                ph.add("pe", (lambda e: e.matmul(sT[b][:, :], k_t[:, c * 128:(c + 1) * 128],
                                                 q_t[:, g * 512:(g + 1) * 512], start=True, stop=True)),
                       reads=[qB, kB], writes=[ph.b("sT", b)])

            for n in range(min(LA, nsteps)):
                qk(n)
            for n in range(nsteps):
                g, c = steps[n]
                first = (n == 0) or steps[n - 1][0] != g
                last = (n == nsteps - 1) or steps[n + 1][0] != g
                if n + LA < nsteps:
                    qk(n + LA)
                b = n % NS_
                pslot = n % NP_
                ph.add("act", (lambda e, b=b, pslot=pslot: e.activation(pTs[pslot][:, :], sT[b][:, :], AF.Exp, scale=scale)),
                       reads=[ph.b("sT", b)], writes=[ph.b("pT", pslot)])
                if win:
                    r_ = c - (4 * g - 1)
                    ph.add("dve", (lambda e, pslot=pslot, r_=r_: e.tensor_tensor(pTs[pslot][:, :], pTs[pslot][:, :], wm[:, r_ * 512:(r_ + 1) * 512], ALU.mult)),
                           reads=[ph.b("wm")], writes=[ph.b("pT", pslot)])
                ob_ = g % 2
                ph.add("pe", (lambda e, ob_=ob_, c=c, pslot=pslot, v_t=v_t, first=first, last=last: e.matmul(
                    oacc[ob_][:, :], v_t[:, c, :], pTs[pslot][:, :], start=first, stop=(last and not win))),
                    reads=[vB, ph.b("pT", pslot)], writes=[ph.b("oacc", ob_)])
                if last and win:
                    ph.add("pe", (lambda e, ob_=ob_, psl=psl: e.matmul(oacc[ob_][:, :], vsk[:, :], psink[psl][:, :], start=False, stop=True)),
                           reads=[ph.b("vsk"), ph.b("psink", psl)], writes=[ph.b("oacc", ob_)])
                if last:
                    os_ = on_cnt % 2
                    on_cnt += 1
                    ph.add("dve", (lambda e, ob_=ob_: e.reciprocal(rec[:, :], oacc[ob_][64:128, :])),
                           reads=[ph.b("oacc", ob_)], writes=[ph.b("rec")])
                    ph.add("dve", (lambda e, ob_=ob_, os_=os_: e.tensor_tensor(onb[os_][:, :], oacc[ob_][0:64, :], rec[:, :], ALU.mult)),
                           reads=[ph.b("oacc", ob_), ph.b("rec")], writes=[ph.b("on", os_)])
                    ph.dma("sp", jb["out"][:, g * 512:(g + 1) * 512], onb[os_][:, :], reads=[ph.b("on", os_)])
                    if "dbg_rec" in dr and name == "p9" and jb is jobs[0] and g == 0:
                        ph.dma("sp", dr["dbg_rec"][:, :], rec[:, :], reads=[ph.b("rec")])
                        dnum = sb(st, nc, "p9_dnum", [128, 512], F32)
                        ph.add("dve", (lambda e, ob_=ob_: e.tensor_copy(dnum[:, :], oacc[ob_][:, :])), reads=[ph.b("oacc", ob_)], writes=[ph.b("dnum")])
                        ph.dma("sp", dr["dbg_num"][:, :], dnum[:, :], reads=[ph.b("dnum")])
        ph.emit()


def phase_win_attn(P):
    nc = P.nc
    dr = P.dram
    name = "p3"
    scale = 64 ** -0.5
    with contextlib.ExitStack() as st:
        ph = Phase(nc, name)
        cst, idb = load_consts(ph, nc, st, P)
        mk = sb(st, nc, "p3_mk", [128, 2, 128], BF16)
        ph.add("dve", lambda e: e.tensor_copy(mk[:, 0, :], cst[:, C_MPREV:C_MPREV + 128]), reads=[ph.b("cst")], writes=[ph.b("mk")])
        ph.add("dve", lambda e: e.tensor_copy(mk[:, 1, :], cst[:, C_MNEXT:C_MNEXT + 128]), reads=[ph.b("cst")], writes=[ph.b("mk")])
        sk = sb(st, nc, "p3_sk", [128, 8], F32)
        es = sb(st, nc, "p3_es", [128, 8], F32)
        ph.dma("sp", sk[:, :], bc(dr["ab_sink"][0, :]), writes=[ph.b("sk")])
        ph.add("act", lambda e: e.activation(es[:, :], sk[:, :], AF.Exp), reads=[ph.b("sk")], writes=[ph.b("es")])
        qs = [sb(st, nc, f"p3_q{i}", [128, S], BF16) for i in range(2)]
        ks = [sb(st, nc, f"p3_k{i}", [128, S], BF16) for i in range(2)]
        vs = [sb(st, nc, f"p3_v{i}", [128, NT, 128], BF16) for i in range(2)]
        pTs = [sb(st, nc, f"p3_pT{i}", [128, 384], BF16) for i in range(4)]
        den = sb(st, nc, "p3_den", [64, 512], F32)
        rec = sb(st, nc, "p3_rec", [64, 512], F32)
        onb = [sb(st, nc, f"p3_on{i}", [64, 512], BF16) for i in range(2)]
        sT = [ps(st, nc, f"p3_sT{i}", [128, 512], F32) for i in range(3)]
        oacc = [ps(st, nc, f"p3_oa{i}", [128, 512], F32) for i in range(2)]
        cnt = 0
        gcnt = 0
        for h in range(8):
            j, half, kv = h // 2, h % 2, h // 4
            pb = half * 64
            if half == 0:
                ph.dma("sp", qs[j % 2][:, :], dr["qT_B"][j, :, :], writes=[ph.b("q", j % 2)])
            if h % 4 == 0:
                ph.dma("sp", ks[kv][:, :], dr["kT_B"][kv, :, :], writes=[ph.b("k", kv)])
                ph.dma("sp", vs[kv][:, :, :].rearrange("p c f -> p (c f)"), dr["v_all"][2 + kv, :, :, :].rearrange("p c f -> p (c f)"), writes=[ph.b("v", kv)])
                if "dbgv0" in dr and kv == 0:
                    ph.dma("sp", dr["dbgv0"][:, :, :].rearrange("p c f -> p (c f)"), vs[0][:, :, :].rearrange("p c f -> p (c f)"), reads=[ph.b("v", 0)])
            q_t, k_t, v_t = qs[j % 2], ks[kv], vs[kv]
            qB, kB, vB = ph.b("q", j % 2), ph.b("k", kv), ph.b("v", kv)
            for g in range(16):
                ob_ = gcnt % 2
                gcnt += 1
                for ti in range(4):
                    i = g * 4 + ti
                    b = cnt % 3
                    pslot = cnt % 4
                    cnt += 1
                    chunks = [c for c in (i - 1, i, i + 1) if 0 <= c < NT]
                    for c in chunks:
                        sl = c - i + 1
                        ph.add("pe", (lambda e, b=b, sl=sl, c=c, i=i: e.matmul(sT[b][:, sl * 128:(sl + 1) * 128],
                                                                                k_t[pb:pb + 64, c * 128:(c + 1) * 128],
                                                                                q_t[pb:pb + 64, i * 128:(i + 1) * 128], start=True, stop=True)),
                               reads=[qB, kB], writes=[ph.b("sT", b)])
                    lo_, hi_ = (chunks[0] - i + 1) * 128, (chunks[-1] - i + 2) * 128
                    ph.add("act", (lambda e, b=b, pslot=pslot, lo_=lo_, hi_=hi_: e.activation(pTs[pslot][:, lo_:hi_], sT[b][:, lo_:hi_], AF.Exp, scale=scale)),
                           reads=[ph.b("sT", b)], writes=[ph.b("pT", pslot)])
                    for c in chunks:
                        sl = c - i + 1
                        if sl != 1:
                            m = 0 if sl == 0 else 1
                            ph.add("dve", (lambda e, pslot=pslot, sl=sl, m=m: e.tensor_tensor(pTs[pslot][:, sl * 128:(sl + 1) * 128],
                                                                                                pTs[pslot][:, sl * 128:(sl + 1) * 128], mk[:, m, :], ALU.mult)),
                                   reads=[ph.b("mk")], writes=[ph.b("pT", pslot)])
                    for ci, c in enumerate(chunks):
                        sl = c - i + 1
                        ph.add("pe", (lambda e, ob_=ob_, ti=ti, c=c, sl=sl, pslot=pslot, ci=ci, nch=len(chunks): e.matmul(
                            oacc[ob_][:, ti * 128:(ti + 1) * 128], v_t[:, c, :], pTs[pslot][:, sl * 128:(sl + 1) * 128],
                            start=(ci == 0), stop=(ci == nch - 1))),
                            reads=[vB, ph.b("pT", pslot)], writes=[ph.b("oacc", ob_)])
                os_ = gcnt % 2
                ph.add("dve", (lambda e, ob_=ob_, h=h: e.tensor_scalar(den[:, :], oacc[ob_][64:128, :], es[64:128, h:h + 1], None, ALU.add)),
                       reads=[ph.b("oacc", ob_), ph.b("es")], writes=[ph.b("den")])
                ph.add("dve", lambda e: e.reciprocal(rec[:, :], den[:, :]), reads=[ph.b("den")], writes=[ph.b("rec")])
                ph.add("dve", (lambda e, ob_=ob_, os_=os_: e.tensor_tensor(onb[os_][:, :], oacc[ob_][0:64, :], rec[:, :], ALU.mult)),
                       reads=[ph.b("oacc", ob_), ph.b("rec")], writes=[ph.b("on", os_)])
                ph.dma("sp", dr["oT"][512 + h * 64:512 + (h + 1) * 64, g * 512:(g + 1) * 512], onb[os_][:, :], reads=[ph.b("on", os_)])
            if "dbgv1" in dr and h == 3:
                ph.dma("sp", dr["dbgv1"][:, :, :].rearrange("p c f -> p (c f)"), vs[0][:, :, :].rearrange("p c f -> p (c f)"), reads=[ph.b("v", 0)])
        ph.emit()


def emit_ln(ph, nc, name, r, rB, out_ap, outB, g_b, b_b, junk, small, eps_t, eng2="pool"):
    sm, nm, ssq, sd, rstd = (small[:, k:k + 1] for k in range(5))
    ph.add("dve", lambda e: e.tensor_reduce(sm, r, AX.X, ALU.add), reads=[rB], writes=[ph.b(name, "sm")])
    ph.add("dve", lambda e: e.tensor_scalar(nm, sm, -1.0 / D, None, ALU.mult), reads=[ph.b(name, "sm")], writes=[ph.b(name, "nm")])
    ph.add("act", lambda e: e.activation(junk, r, AF.Square, bias=nm, scale=1.0, accum_out=ssq),
           reads=[rB, ph.b(name, "nm")], writes=[ph.b(name, "ssq"), ph.b(name, "junk")])
    ph.add("act", lambda e: e.activation(sd, ssq, AF.Sqrt, bias=eps_t, scale=1.0 / D), reads=[ph.b(name, "ssq")], writes=[ph.b(name, "sd")])
    ph.add("dve", lambda e: e.reciprocal(rstd, sd), reads=[ph.b(name, "sd")], writes=[ph.b(name, "rstd")])
    ph.add("dve", lambda e: e.tensor_scalar(r, r, nm, rstd, ALU.add, ALU.mult), reads=[ph.b(name, "nm"), ph.b(name, "rstd")], writes=[rB])
    ph.add(eng2, lambda e: e.tensor_tensor(r, r, g_b, ALU.mult), reads=[ph.b("lng")], writes=[rB])
    ph.add(eng2, lambda e: e.tensor_tensor(out_ap, r, b_b, ALU.add), reads=[rB, ph.b("lnb")], writes=[outB])


def phase_outproj(P, name, w_out, xin_rows, lng, lnb, w_router, xext, acc, affd):
    nc = P.nc
    dr = P.dram
    with contextlib.ExitStack() as st:
        ph = Phase(nc, name)
        cst, idb = load_consts(ph, nc, st, P)
        stage = [sb(st, nc, f"{name}_stg{i}", [128, 1024], F32) for i in range(2)]
        wo = load_weight_bf16(ph, nc, st, f"{name}_wo", w_out, 8, 1024, stage, [ph.b("stg", i) for i in range(2)])
        wr = sb(st, nc, f"{name}_wr", [128, 8, NE], F32)
        for hh in range(2):
            ph.dma("sp", wr[:, hh * 4:(hh + 1) * 4, :], w_router.rearrange("(c p) n -> p c n", p=128)[:, hh * 4:(hh + 1) * 4, :], writes=[ph.b("wr")])
        g_b = sb(st, nc, f"{name}_g", [128, D], F32)
        b_b = sb(st, nc, f"{name}_b", [128, D], F32)
        ph.dma("sp", g_b[:, :], bc(lng), writes=[ph.b("lng")])
        ph.dma("sp", b_b[:, :], bc(lnb), writes=[ph.b("lnb")])
        eps_t = sb(st, nc, f"{name}_eps", [128, 1], F32)
        ph.add("pool", lambda e: e.memset(eps_t[:, :], 1e-5), writes=[ph.b("eps")])
        oTs = [sb(st, nc, f"{name}_oT{i}", [128, 8, 512], BF16) for i in range(2)]
        xs = [sb(st, nc, f"{name}_xs{i}", [128, D], F32) for i in range(4)]
        r = [sb(st, nc, f"{name}_r{i}", [128, D], F32) for i in range(4)]
        xo = [sb(st, nc, f"{name}_xo{i}", [128, XW], F32) for i in range(2)]
        ax = [sb(st, nc, f"{name}_ax{i}", [128, D], F32) for i in range(2)]
        junk2 = [sb(st, nc, f"{name}_junk{i}", [128, D], BF16) for i in range(2)]
        small2 = [sb(st, nc, f"{name}_small{i}", [128, 8], F32) for i in range(2)]
        xnT2 = [sb(st, nc, f"{name}_xnT{i}", [128, 8, 128], F32) for i in range(2)]
        lg2 = [sb(st, nc, f"{name}_lg{i}", [128, 4], F32) for i in range(2)]
        ex2 = [sb(st, nc, f"{name}_ex{i}", [128, NE], F32) for i in range(2)]
        py = [ps(st, nc, f"{name}_py{i}", [128, 512], F32) for i in range(2)]
        pT2 = [ps(st, nc, f"{name}_pT{i}", [128, 512], F32) for i in range(2)]
        pl2 = [ps(st, nc, f"{name}_pl{i}", [128, 512], F32) for i in range(2)]
        oTv = dr["oT"][:, :].rearrange("(c p) t -> p c t", p=128)
        def stageA(i):
            s = i % 4
            g4, t4 = divmod(i, 4)
            gs = g4 % 2
            tok = slice(i * 128, (i + 1) * 128)
            B = lambda n: ph.b(n, s)
            if t4 == 0:
                for hh in range(2):
                    ph.dma("sp", oTs[gs][:, hh * 4:(hh + 1) * 4, :], oTv[:, hh * 4:(hh + 1) * 4, g4 * 512:(g4 + 1) * 512], writes=[ph.b("oTs", gs)])
            ph.dma("sp", xs[s][:, :], xin_rows(tok), writes=[B("xs")])
            for n in range(2):
                for c in range(8):
                    ph.add("pe", (lambda e, n=n, c=c, gs=gs, t4=t4: e.matmul(py[n][:, :], oTs[gs][:, c, t4 * 128:(t4 + 1) * 128],
                                                                             wo[:, c, n * 512:(n + 1) * 512], start=(c == 0), stop=(c == 7))),
                           reads=[ph.b("oTs", gs), ph.b(f"{name}_wo")], writes=[ph.b("py", n)])
                ph.add("dve", (lambda e, n=n, s=s: e.scalar_tensor_tensor(r[s][:, n * 512:(n + 1) * 512], xs[s][:, n * 512:(n + 1) * 512], ALPHA,
                                                                          py[n][:, :], ALU.mult, ALU.add)),
                       reads=[B("xs"), ph.b("py", n)], writes=[B("r")])

        def stageB(i):
            s = i % 2
            tok = slice(i * 128, (i + 1) * 128)
            B = lambda n: ph.b(n, s)
            junk, small, xnT, lg, ex, pTs_, pl = junk2[s], small2[s], xnT2[s], lg2[s], ex2[s], pT2[s], pl2[s]
            rr = r[i % 4][:, :]
            rB = ph.b("r", i % 4)
            sm, nm, ssq, sd, rstd = (small[:, k:k + 1] for k in range(5))
            ph.add("dve", lambda e: e.tensor_reduce(sm, rr, AX.X, ALU.add), reads=[rB], writes=[B("sm")]); yield
            ph.add("dve", lambda e: e.tensor_scalar(nm, sm, -1.0 / D, None, ALU.mult), reads=[B("sm")], writes=[B("nm")]); yield
            ph.add("act", lambda e: e.activation(junk[:, :], rr, AF.Square, bias=nm, scale=1.0, accum_out=ssq),
                   reads=[rB, B("nm")], writes=[B("ssq"), B("junk")]); yield
            ph.add("act", lambda e: e.activation(sd, ssq, AF.Sqrt, bias=eps_t[:, :], scale=1.0 / D), reads=[B("ssq")], writes=[B("sd")]); yield
            ph.add("dve", lambda e: e.reciprocal(rstd, sd), reads=[B("sd")], writes=[B("rstd")]); yield
            ph.add("dve", lambda e: e.tensor_scalar(rr, rr, nm, rstd, ALU.add, ALU.mult), reads=[B("nm"), B("rstd")], writes=[rB]); yield
            ph.add("pool", lambda e: e.tensor_tensor(rr, rr, g_b[:, :], ALU.mult), reads=[ph.b("lng")], writes=[rB]); yield
            ph.add("pool", lambda e: e.tensor_tensor(xo[s][:, 0:D], rr, b_b[:, :], ALU.add), reads=[rB, ph.b("lnb")], writes=[B("xo")]); yield
            ph.add("act", lambda e: e.mul(ax[s][:, :], xo[s][:, 0:D], ALPHA), reads=[B("xo")], writes=[B("ax")]); yield
            ph.dma("sp", acc[tok, :], ax[s][:, :], reads=[B("ax")]); yield
            for g in range(2):
                for c in range(4):
                    cc = g * 4 + c
                    ph.add("pe", (lambda e, c=c, cc=cc: e.transpose(pTs_[:, c * 128:(c + 1) * 128], xo[s][:, cc * 128:(cc + 1) * 128],
                                                                    cst[:, C_IDENT:C_IDENT + 128])),
                           reads=[B("xo"), ph.b("cst")], writes=[B("pT")])
                yield
                ph.add("act", (lambda e, g=g: e.copy(xnT[:, g * 4:(g + 1) * 4, :], pTs_[:, :].rearrange("p (c t) -> p c t", c=4))),
                       reads=[B("pT")], writes=[B("xnT")]); yield
            for c in range(8):
                ph.add("pe", (lambda e, c=c: e.matmul(pl[:, 0:NE], xnT[:, c, :], wr[:, c, :], start=(c == 0), stop=(c == 7))),
                       reads=[B("xnT"), ph.b("wr")], writes=[B("pl")])
            yield
            ph.add("dve", lambda e: e.tensor_reduce(lg[:, 0:1], pl[:, 0:NE], AX.X, ALU.max), reads=[B("pl")], writes=[B("lg0")]); yield
            ph.add("dve", lambda e: e.tensor_scalar(lg[:, 1:2], lg[:, 0:1], -1.0, None, ALU.mult), reads=[B("lg0")], writes=[B("lg1")]); yield
            ph.add("act", lambda e: e.activation(ex[:, :], pl[:, 0:NE], AF.Exp, bias=lg[:, 1:2], scale=1.0, accum_out=lg[:, 2:3]),
                   reads=[B("pl"), B("lg1")], writes=[B("ex"), B("lg2")]); yield
            ph.add("dve", lambda e: e.reciprocal(lg[:, 3:4], lg[:, 2:3]), reads=[B("lg2")], writes=[B("lg3")]); yield
            ph.add("dve", lambda e: e.tensor_scalar(xo[s][:, D:XW], ex[:, :], lg[:, 3:4], None, ALU.mult),
                   reads=[B("ex"), B("lg3")], writes=[B("xo")]); yield
            ph.dma("sp", xext[tok, :], xo[s][:, :], reads=[B("xo")]); yield
            ph.dma("sp", affd[tok, :], xo[s][:, D:XW], reads=[B("xo")]); yield

        def drive(gens):
            gens = list(gens)
            while gens:
                for g_ in list(gens):
                    try:
                        next(g_)
                    except StopIteration:
                        gens.remove(g_)

        stageA(0)
        stageA(1)
        for i in range(0, NT, 2):
            if i + 2 < NT:
                stageA(i + 2)
                stageA(i + 3)
            drive([stageB(i), stageB(i + 1)])
        ph.emit()


def phase_route(P, name, xext, idx):
    nc = P.nc
    dr = P.dram
    with contextlib.ExitStack() as st:
        ph = Phase(nc, name)
        cst, idb = load_consts(ph, nc, st, P)
        aff = sb(st, nc, f"{name}_aff", [128, 64, NE], F32)
        cmp_ = sb(st, nc, f"{name}_cmp", [128, 64, NE], F32)
        mT = sb(st, nc, f"{name}_mT", [128, NE, 64], F32)
        Cl = sb(st, nc, f"{name}_Cl", [128, NE, 64], F32)
        Cg = sb(st, nc, f"{name}_Cg", [128, NE, 64], F32)
        lo = sb(st, nc, f"{name}_lo", [128, NE], F32)
        mid = sb(st, nc, f"{name}_mid", [128, NE], F32)
        sel = sb(st, nc, f"{name}_sel", [128, NE], F32)
        cntp = sb(st, nc, f"{name}_cntp", [128, NE], BF16)
        ones = sb(st, nc, f"{name}_ones", [128, 128], BF16)
        trib = sb(st, nc, f"{name}_trib", [128, 128], BF16)
        idxf = sb(st, nc, f"{name}_idxf", [128, NE, 8], F32)
        junk = sb(st, nc, f"{name}_junk", [128, S], BF16)
        crow = [sb(st, nc, f"{name}_crow{i}", [128, S], F32) for i in range(2)]
        tot = ps(st, nc, f"{name}_tot", [128, 512], F32)
        off = ps(st, nc, f"{name}_off", [128, 512], F32)
        ph.dma("sp", aff[:, :, :].rearrange("p i e -> p (i e)"), dr["affd"][:, :].rearrange("(p i) w -> p (i w)", p=128), writes=[ph.b("aff")])
        ph.add("dve", lambda e: e.memset(ones[:, :], 1.0), writes=[ph.b("ones")])
        ph.add("dve", lambda e: e.tensor_copy(trib[:, :], cst[:, C_TRI:C_TRI + 128]), reads=[ph.b("cst")], writes=[ph.b("trib")])
        ph.add("dve", lambda e: e.memset(lo[:, :], 0.0), writes=[ph.b("lo")])
        for k in range(30):
            w = 0.5 ** (k + 1)
            ph.add("dve", (lambda e, w=w: e.tensor_scalar(mid[:, :], lo[:, :], w, None, ALU.add)), reads=[ph.b("lo")], writes=[ph.b("mid")])
            ph.add("dve", lambda e: e.tensor_tensor(cmp_[:, :, :], aff[:, :, :], mid[:, :].unsqueeze(1).to_broadcast([128, 64, NE]), ALU.is_ge),
                   reads=[ph.b("aff"), ph.b("mid")], writes=[ph.b("cmp")])
            ph.add("dve", lowp(nc, lambda e: e.tensor_reduce(cntp[:, :], cmp_[:, :, :].rearrange("p i e -> p e i"), AX.X, ALU.add)),
                   reads=[ph.b("cmp")], writes=[ph.b("cntp")])
            ph.add("pe", lambda e: e.matmul(tot[:, 0:NE], ones[:, :], cntp[:, :], start=True, stop=True),
                   reads=[ph.b("ones"), ph.b("cntp")], writes=[ph.b("tot")])
            ph.add("dve", lambda e: e.tensor_scalar(sel[:, :], tot[:, 0:NE], CAP - 0.5, None, ALU.is_ge), reads=[ph.b("tot")], writes=[ph.b("sel")])
            ph.add("dve", (lambda e, w=w: e.scalar_tensor_tensor(lo[:, :], sel[:, :], w, lo[:, :], ALU.mult, ALU.add)),
                   reads=[ph.b("sel")], writes=[ph.b("lo")])
        ph.add("dve", lambda e: e.tensor_tensor(mT[:, :, :].rearrange("p e i -> p i e"), aff[:, :, :],
                                                lo[:, :].unsqueeze(1).to_broadcast([128, 64, NE]), ALU.is_ge),
               reads=[ph.b("aff"), ph.b("lo")], writes=[ph.b("mT")])
        ph.add("dve", lambda e: e.tensor_tensor_scan(Cl[:, :, :].rearrange("p e i -> p (e i)"), cst[:, C_KEEP:C_KEEP + NE * 64],
                                                     mT[:, :, :].rearrange("p e i -> p (e i)"), 0.0, ALU.mult, ALU.add),
               reads=[ph.b("mT"), ph.b("cst")], writes=[ph.b("Cl")])
        ph.add("dve", lambda e: e.tensor_copy(cntp[:, :], Cl[:, :, 63]), reads=[ph.b("Cl")], writes=[ph.b("cntp")])
        ph.add("pe", lambda e: e.matmul(off[:, 0:NE], trib[:, :], cntp[:, :], start=True, stop=True),
               reads=[ph.b("trib"), ph.b("cntp")], writes=[ph.b("off")])
        ph.add("dve", lambda e: e.tensor_copy(sel[:, :], off[:, 0:NE]), reads=[ph.b("off")], writes=[ph.b("sel")])
        ph.add("dve", lambda e: e.tensor_tensor(Cg[:, :, :], Cl[:, :, :], sel[:, :].unsqueeze(2).to_broadcast([128, NE, 64]), ALU.add),
               reads=[ph.b("Cl"), ph.b("sel")], writes=[ph.b("Cg")])
        cd = dr[f"{name}_Cd"]
        for e4 in range(4):
            ph.dma("sp", cd[e4 * 4:(e4 + 1) * 4, :].rearrange("e (p i) -> p e i", p=128), Cg[:, e4 * 4:(e4 + 1) * 4, :], reads=[ph.b("Cg")], writes=[ph.b("Cd", e4)])
        junk2 = sb(st, nc, f"{name}_junk2", [128, S], BF16)
        svh = sb(st, nc, f"{name}_svh", [128, 8], F32)
        ph.add("dve", lambda e: e.tensor_scalar(svh[:, :], cst[:, C_SVEC:C_SVEC + 8], 0.5, None, ALU.add), reads=[ph.b("cst")], writes=[ph.b("svh")])
        for e_ in range(NE):
            s = e_ % 2
            ph.dma("sp", crow[s][:, :], bc(cd[e_, :]), reads=[ph.b("Cd", e_ // 4)], writes=[ph.b("crow", s)])
            for j in range(8):
                if j % 2 == 0:
                    ph.add("dve", (lambda e, s=s, e_=e_, j=j: e.tensor_scalar(junk[:, :], crow[s][:, :], cst[:, C_SVEC + j:C_SVEC + j + 1], None,
                                                                               ALU.is_le, ALU.add, accum_out=idxf[:, e_, j:j + 1])),
                           reads=[ph.b("crow", s), ph.b("cst")], writes=[ph.b("idxf", "dve"), ph.b("junk")])
                else:
                    ph.add("act", (lambda e, s=s, e_=e_, j=j: e.activation(junk2[:, :], crow[s][:, :], AF.Sign, bias=svh[:, j:j + 1], scale=-1.0,
                                                                            accum_out=idxf[:, e_, j:j + 1])),
                           reads=[ph.b("crow", s), ph.b("svh")], writes=[ph.b("idxf", "act"), ph.b("junk2")])
        i4 = idxf[:, :, :].rearrange("p e (j t) -> p e j t", t=2)[:, :, :, 1]
        ph.add("dve", lambda e: e.tensor_scalar(i4, i4, 0.5, float(S // 2), ALU.mult, ALU.add), reads=[ph.b("idxf", "act")], writes=[ph.b("idxf", "act")])
        ph.add("dve", lambda e: e.tensor_copy(idx[:, :, :], idxf[:, :, :]), reads=[ph.b("idxf", "act"), ph.b("idxf", "dve")], writes=[ph.b("idx")])
        if "dbg_idx" in dr and name == "p5":
            ph.dma("sp", dr["dbg_idx"][:, :], idx[:, :, :].rearrange("p e j -> p (e j)"), reads=[ph.b("idx")])
            ph.dma("sp", dr["dbg_lo"][:, :], lo[:, :], reads=[ph.b("lo")])
        ph.emit()


def phase_experts(P, name, l, xext, acc, idx):
    nc = P.nc
    dr = P.dram
    with contextlib.ExitStack() as st:
        ph = Phase(nc, name)
        cst, idb = load_consts(ph, nc, st, P)
        NST = 3
        stage = [sb(st, nc, f"{name}_stg{i}", [128, 2048], F32) for i in range(NST)]
        wts = {k: [sb(st, nc, f"{name}_{k}{i}", [128, 8, 1024], BF16) for i in range(2)] for k in ("wg", "wu", "wd")}
        xg = [sb(st, nc, f"{name}_xg{i}", [128, XW], F32) for i in range(2)]
        xgT2 = [sb(st, nc, f"{name}_xgT{i}", [128, 8, 1024], BF16) for i in range(2)]
        hT = sb(st, nc, f"{name}_hT", [128, 8, 1024], BF16)
        gt = sb(st, nc, f"{name}_gt", [128, 2, 8], F32)
        sg = [sb(st, nc, f"{name}_sg{i}", [128, 512], F32) for i in range(2)]
        yo = [sb(st, nc, f"{name}_yo{i}", [128, D], F32) for i in range(4)]
        pT = [ps(st, nc, f"{name}_pT{i}", [128, 512], F32) for i in range(2)]
        pg = [ps(st, nc, f"{name}_pg{i}", [128, 512], F32) for i in range(2)]
        pu = [ps(st, nc, f"{name}_pu{i}", [128, 512], F32) for i in range(2)]
        py = [ps(st, nc, f"{name}_py{i}", [128, 512], F32) for i in range(2)]
        srcs = {"wg": dr["moe_w_gate"], "wu": dr["moe_w_up"], "wd": dr["moe_w_down"]}
        stg_cnt = [0]
        cast_engs = ("pool", "dve", "pool", "act")

        def load_w_piece(e_, k, c2):
            sl = e_ % 2
            if True:
                wv = srcs[k][l, e_, :, :].rearrange("(c p) n -> p c n", p=128)
                if True:
                    s = stg_cnt[0] % NST
                    eng = cast_engs[stg_cnt[0] % len(cast_engs)]
                    stg_cnt[0] += 1
                    ph.dma("sp", stage[s][:, :].rearrange("p (c n) -> p c n", c=2), wv[:, 2 * c2:2 * c2 + 2, :], writes=[ph.b("stg", s)])
                    dst = wts[k][sl][:, 2 * c2:2 * c2 + 2, :]
                    srcv = stage[s][:, :].rearrange("p (c n) -> p c n", c=2)
                    if eng == "act":
                        ph.add("act", (lambda e, dst=dst, srcv=srcv: e.copy(dst, srcv)), reads=[ph.b("stg", s)], writes=[ph.b(k, sl)])
                    else:
                        ph.add(eng, (lambda e, dst=dst, srcv=srcv: e.tensor_copy(dst, srcv)), reads=[ph.b("stg", s)], writes=[ph.b(k, sl)])

        PIECES = [(k, c2) for k in ("wg", "wu", "wd") for c2 in range(4)]

        def load_w(e_):
            for (k, c2) in PIECES:
                load_w_piece(e_, k, c2)

        load_w(0)
        prev_sc = []

        def emit_gather(e_, j):
            s = (e_ * 8 + j) % 2
            ph.add("pool", (lambda e, s=s, e_=e_, j=j: e.indirect_dma_start(
                out=xg[s][:, :], out_offset=None, in_=xext[:, :],
                in_offset=bass.IndirectOffsetOnAxis(ap=idx[:, e_, j:j + 1], axis=0))),
                reads=[ph.b("idx")], writes=[ph.b("xg", s)], dma=True)

        def emit_trans(e_, j):
            s = (e_ * 8 + j) % 2
            xs_ = e_ % 2
            emit_xT(ph, xg[s], ph.b("xg", s), xgT2[xs_][:, :, j * 128:(j + 1) * 128], ph.b("xgT", xs_), pT, [ph.b("pT", 0), ph.b("pT", 1)], cst)
            ph.add("act", (lambda e, s=s, e_=e_, j=j, xs_=xs_: e.copy(gt[:, xs_, j:j + 1], xg[s][:, D + e_:D + e_ + 1])),
                   reads=[ph.b("xg", s)], writes=[ph.b("gt", xs_)])

        emit_gather(0, 0)
        for j in range(8):
            if j + 1 < 8:
                emit_gather(0, j + 1)
            emit_trans(0, j)
        for e_ in range(NE):
            sl = e_ % 2
            wg, wu, wd = wts["wg"][sl], wts["wu"][sl], wts["wd"][sl]
            gsl = e_ % 2
            xgT = xgT2[sl]
            xgB = ph.b("xgT", sl)
            if e_ + 1 < NE:
                emit_gather(e_ + 1, 0)
            n = 0
            for fc in range(8):
                if e_ + 1 < NE and fc + 1 < 8:
                    emit_gather(e_ + 1, fc + 1)
                if e_ + 1 < NE:
                    for pi in range((fc * 12) // 8, ((fc + 1) * 12) // 8):
                        load_w_piece(e_ + 1, *PIECES[pi])
                for half in range(2):
                    b = n % 2
                    n += 1
                    for (wt, pt_, nm, kn) in ((wg, pg, "pg", "wg"), (wu, pu, "pu", "wu")):
                        for c in range(8):
                            ph.add("pe", (lambda e, wt=wt, pt_=pt_, b=b, c=c, fc=fc, half=half, xgT=xgT: e.matmul(
                                pt_[b][:, :], wt[:, c, fc * 128:(fc + 1) * 128], xgT[:, c, half * 512:(half + 1) * 512],
                                start=(c == 0), stop=(c == 7))),
                                reads=[ph.b(kn, sl), xgB], writes=[ph.b(nm, b)])
                    ph.add("act", (lambda e, b=b: e.activation(sg[b][:, :], pg[b][:, :], AF.Silu)), reads=[ph.b("pg", b)], writes=[ph.b("sg", b)])
                    ph.add("dve", (lambda e, b=b, fc=fc, half=half: e.tensor_tensor(hT[:, fc, half * 512:(half + 1) * 512], sg[b][:, :], pu[b][:, :], ALU.mult)),
                           reads=[ph.b("sg", b), ph.b("pu", b)], writes=[ph.b("hT")])
                if e_ + 1 < NE:
                    emit_trans(e_ + 1, fc)
            new_sc = []
            for j in range(8):
                ys = (e_ * 8 + j) % 4
                for half in range(2):
                    for fc in range(8):
                        ph.add("pe", (lambda e, half=half, fc=fc, j=j, wd=wd: e.matmul(py[half][:, :], hT[:, fc, j * 128:(j + 1) * 128],
                                                                                wd[:, fc, half * 512:(half + 1) * 512], start=(fc == 0), stop=(fc == 7))),
                               reads=[ph.b("hT"), ph.b("wd", sl)], writes=[ph.b("py", half)])
                    eng = "act" if half == 0 else "dve"
                    if eng == "act":
                        ph.add("act", (lambda e, ys=ys, half=half, gsl=gsl, j=j: e.mul(yo[ys][:, half * 512:(half + 1) * 512], py[half][:, :], gt[:, gsl, j:j + 1])),
                               reads=[ph.b("py", half), ph.b("gt", gsl)], writes=[ph.b("yo", ys)])
                    else:
                        ph.add("dve", (lambda e, ys=ys, half=half, gsl=gsl, j=j: e.tensor_scalar(yo[ys][:, half * 512:(half + 1) * 512], py[half][:, :],
                                                                                                  gt[:, gsl, j:j + 1], None, ALU.mult)),
                               reads=[ph.b("py", half), ph.b("gt", gsl)], writes=[ph.b("yo", ys)])
                if "dbg_xg" in dr and name == "p6" and e_ == 0 and j == 0:
                    ph.dma("sp", dr["dbg_yo"][:, :], yo[ys][:, :], reads=[ph.b("yo", ys)])
                    ph.dma("sp", dr["dbg_hT"][:, :], hT[:, :, :].rearrange("p c t -> p (c t)"), reads=[ph.b("hT")])
                op = ph.add("pool", (lambda e, ys=ys, e_=e_, j=j: e.indirect_dma_start(
                    out=acc[:, :], out_offset=bass.IndirectOffsetOnAxis(ap=idx[:, e_, j:j + 1], axis=0),
                    in_=yo[ys][:, :], in_offset=None, compute_op=ALU.add)),
                    reads=[ph.b("idx"), ph.b("yo", ys)], writes=[], dma=True)
                for d in prev_sc:
                    if d not in op.deps:
                        op.deps.append(d)
                new_sc.append(op)
            prev_sc = new_sc
        ph.emit()


def phase_ln(P, name, acc, lng, lnb, out_rows):
    nc = P.nc
    with contextlib.ExitStack() as st:
        ph = Phase(nc, name)
        g_b = sb(st, nc, f"{name}_g", [128, D], F32)
        b_b = sb(st, nc, f"{name}_b", [128, D], F32)
        ph.dma("sp", g_b[:, :], bc(lng), writes=[ph.b("lng")])
        ph.dma("sp", b_b[:, :], bc(lnb), writes=[ph.b("lnb")])
        eps_t = sb(st, nc, f"{name}_eps", [128, 1], F32)
        ph.add("pool", lambda e: e.memset(eps_t[:, :], 1e-5), writes=[ph.b("eps")])
        r = [sb(st, nc, f"{name}_r{i}", [128, D], F32) for i in range(3)]
        o = [sb(st, nc, f"{name}_o{i}", [128, D], F32) for i in range(3)]
        junk = sb(st, nc, f"{name}_junk", [128, D], BF16)
        small = sb(st, nc, f"{name}_small", [128, 8], F32)
        for i in range(NT + 1):
            if i < NT:
                s = i % 3
                tok = slice(i * 128, (i + 1) * 128)
                ph.dma("sp", r[s][:, :], acc[tok, :], writes=[ph.b("r", s)])
            if i >= 1:
                s = (i - 1) % 3
                tok = slice((i - 1) * 128, i * 128)
                emit_ln(ph, nc, name, r[s][:, :], ph.b("r", s), o[s][:, :], ph.b("o", s), g_b[:, :], b_b[:, :], junk[:, :], small[:, :], eps_t[:, :])
                ph.dma("sp", out_rows(tok), o[s][:, :], reads=[ph.b("o", s)])
        ph.emit()


def phase_proj_mla(P, xin_rows):
    nc = P.nc
    dr = P.dram
    name = "p8"
    with contextlib.ExitStack() as st:
        ph = Phase(nc, name)
        cst, idb = load_consts(ph, nc, st, P)
        stage = [sb(st, nc, f"p8_stg{i}", [128, 2048], F32) for i in range(2)]
        sB = [ph.b("stg", i) for i in range(2)]
        wdn = load_weight_bf16(ph, nc, st, "p8_wdn", dr["mla_w_down"][0, :, :], 8, 416, stage, sB)
        wuq = load_weight_bf16(ph, nc, st, "p8_wuq", dr["mla_w_uq"][0, :, :], 2, 1536, stage, sB)
        wukv = load_weight_bf16(ph, nc, st, "p8_wukv", dr["mla_w_ukv"][0, :, :], 1, 2048, stage, sB)
        gq = sb(st, nc, "p8_gq", [128, 256], F32)
        gk = sb(st, nc, "p8_gk", [128, 128], F32)
        ph.dma("sp", gq[:, :], bc(dr["mla_q_norm"][0, :]), writes=[ph.b("gq")])
        ph.dma("sp", gk[:, :], bc(dr["mla_kv_norm"][0, :]), writes=[ph.b("gk")])
        eps = sb(st, nc, "p8_eps", [128, 1], F32)
        ph.add("pool", lambda e: e.memset(eps[:, :], 1e-6), writes=[ph.b("eps")])
        xs = [sb(st, nc, f"p8_xs{i}", [128, D], F32) for i in range(2)]
        tb = [sb(st, nc, f"p8_tb{i}", [128, 64], F32) for i in range(2)]
        xT = [sb(st, nc, f"p8_xT{i}", [128, 8, 128], BF16) for i in range(2)]
        pr2 = [sb(st, nc, f"p8_pr{i}", [128, 416], F32) for i in range(2)]
        junk = sb(st, nc, "p8_junk", [128, 256], BF16)
        sm = sb(st, nc, "p8_sm", [128, 8], F32)
        cn = sb(st, nc, "p8_cn", [128, 384], BF16)
        cnT = sb(st, nc, "p8_cnT", [128, 3, 128], BF16)
        qsb = sb(st, nc, "p8_qs", [128, 1536], F32)
        kvs = sb(st, nc, "p8_kvs", [128, 2048], F32)
        qb = sb(st, nc, "p8_qb", [128, 16, 96], BF16)
        kb = sb(st, nc, "p8_kb", [128, 16, 96], BF16)
        kr = sb(st, nc, "p8_kr", [128, 32], BF16)
        t1 = sb(st, nc, "p8_t1", [128, 512], F32)
        t2 = sb(st, nc, "p8_t2", [128, 512], F32)
        t3 = sb(st, nc, "p8_t3", [128, 32], F32)
        t4 = sb(st, nc, "p8_t4", [128, 32], F32)
        stq = [sb(st, nc, f"p8_stq{i}", [128, 16, 128], BF16) for i in range(2)]
        stk = [sb(st, nc, f"p8_stk{i}", [128, 16, 128], BF16) for i in range(2)]
        vst = [sb(st, nc, f"p8_vst{i}", [128, 16, 128], BF16) for i in range(2)]
        pT = [ps(st, nc, f"p8_pT{i}", [128, 512], F32) for i in range(2)]
        pp = ps(st, nc, "p8_pp", [128, 512], F32)
        pc = ps(st, nc, "p8_pc", [128, 1024], BF16)
        pb_ = [ps(st, nc, f"p8_pb{i}", [128, 512], F32) for i in range(4)]
        for s in range(2):
            ph.add("pool", (lambda e, s=s: e.memset(vst[s][:, :, :], 1.0)), writes=[ph.b("vst", s)])
        def stageA(i):
            s = i % 2
            tok = slice(i * 128, (i + 1) * 128)
            B = lambda n: ph.b(n, s)
            ph.dma("sp", xs[s][:, :], xin_rows(tok), writes=[B("xs")])
            ph.dma("sp", tb[s][:, :], dr["tab"][tok, 256:320], writes=[B("tb")])
            emit_xT(ph, xs[s], B("xs"), xT[s], B("xT"), pT, [ph.b("pT", 0), ph.b("pT", 1)], cst)
            for c in range(8):
                ph.add("pe", (lambda e, c=c, s=s: e.matmul(pp[:, 0:416], xT[s][:, c, :], wdn[:, c, :], start=(c == 0), stop=(c == 7))),
                       reads=[B("xT"), ph.b("p8_wdn")], writes=[ph.b("pp")])
            ph.add("act", (lambda e, s=s: e.copy(pr2[s][:, :], pp[:, 0:416])), reads=[ph.b("pp")], writes=[ph.b("pr", s)])

        def stageB(i):
            s = i % 2
            tok = slice(i * 128, (i + 1) * 128)
            B = lambda n: ph.b(n, s)
            pr = pr2[s]
            prB = ph.b("pr", s)
            ph.add("act", lambda e: e.activation(junk[:, 0:256], pr[:, 0:256], AF.Square, accum_out=sm[:, 0:1]), reads=[prB], writes=[ph.b("sm0"), ph.b("junk")])
            ph.add("act", lambda e: e.activation(junk[:, 0:128], pr[:, 256:384], AF.Square, accum_out=sm[:, 1:2]), reads=[prB], writes=[ph.b("sm1"), ph.b("junk")])
            ph.add("act", lambda e: e.activation(sm[:, 2:3], sm[:, 0:1], AF.Sqrt, bias=eps[:, :], scale=1.0 / 256), reads=[ph.b("sm0"), ph.b("eps")], writes=[ph.b("sm2")])
            ph.add("act", lambda e: e.activation(sm[:, 3:4], sm[:, 1:2], AF.Sqrt, bias=eps[:, :], scale=1.0 / 128), reads=[ph.b("sm1"), ph.b("eps")], writes=[ph.b("sm3")])
            ph.add("dve", lambda e: e.reciprocal(sm[:, 4:6], sm[:, 2:4]), reads=[ph.b("sm2"), ph.b("sm3")], writes=[ph.b("sm4")])
            ph.add("dve", lambda e: e.scalar_tensor_tensor(cn[:, 0:256], pr[:, 0:256], sm[:, 4:5], gq[:, :], ALU.mult, ALU.mult),
                   reads=[prB, ph.b("sm4"), ph.b("gq")], writes=[ph.b("cn")])
            ph.add("dve", lambda e: e.scalar_tensor_tensor(cn[:, 256:384], pr[:, 256:384], sm[:, 5:6], gk[:, :], ALU.mult, ALU.mult),
                   reads=[prB, ph.b("sm4"), ph.b("gk")], writes=[ph.b("cn")])
            for c in range(3):
                ph.add("pe", (lambda e, c=c: e.transpose(pc[:, c * 128:(c + 1) * 128], cn[:, c * 128:(c + 1) * 128], idb[:, :])),
                       reads=[ph.b("cn"), ph.b("idb")], writes=[ph.b("pc")])
            ph.add("act", lambda e: e.copy(cnT[:, :, :], pc[:, 0:384].rearrange("p (c t) -> p c t", c=3)), reads=[ph.b("pc")], writes=[ph.b("cnT")])
            for n in range(3):
                for c in range(2):
                    ph.add("pe", (lambda e, n=n, c=c: e.matmul(pb_[n][:, :], cnT[:, c, :], wuq[:, c, n * 512:(n + 1) * 512], start=(c == 0), stop=(c == 1))),
                           reads=[ph.b("cnT"), ph.b("p8_wuq")], writes=[ph.b("pb", n)])
                ph.add("act", (lambda e, n=n: e.copy(qsb[:, n * 512:(n + 1) * 512], pb_[n][:, :])), reads=[ph.b("pb", n)], writes=[ph.b("qs")])
            q3 = qsb[:, :].rearrange("p (h d) -> p h d", h=16)
            ph.add("pool", lambda e: e.tensor_copy(qb[:, :, 0:64], q3[:, :, 0:64]), reads=[ph.b("qs")], writes=[ph.b("qb")])
            rope3(ph, "dve", qb[:, :, 64:96], q3[:, :, 64:96], tb[s][:, 0:32], tb[s][:, 32:64], 16, 32, 1, t1[:, :], t2[:, :],
                  [ph.b("qs"), B("tb")], [ph.b("t1"), ph.b("t2")], ph.b("qb"))
            for n in range(4):
                bn = (3 + n) % 4
                ph.add("pe", (lambda e, n=n, bn=bn: e.matmul(pb_[bn][:, :], cnT[:, 2, :], wukv[:, 0, n * 512:(n + 1) * 512], start=True, stop=True)),
                       reads=[ph.b("cnT"), ph.b("p8_wukv")], writes=[ph.b("pb", bn)])
                ph.add("act", (lambda e, n=n, bn=bn: e.copy(kvs[:, n * 512:(n + 1) * 512], pb_[bn][:, :])), reads=[ph.b("pb", bn)], writes=[ph.b("kvs")])
            kv3 = kvs[:, :].rearrange("p (h d) -> p h d", h=16)
            ph.add("pool", lambda e: e.tensor_copy(kb[:, :, 0:64], kv3[:, :, 0:64]), reads=[ph.b("kvs")], writes=[ph.b("kb")])
            ph.add("pool", (lambda e, s=s: e.tensor_copy(vst[s][:, :, 0:64], kv3[:, :, 64:128])), reads=[ph.b("kvs")], writes=[B("vst")])
            for h4 in range(4):
                ph.dma("sp", dr["v_C"][h4 * 4:(h4 + 1) * 4, :, i, :].rearrange("k p f -> p k f"), vst[s][:, h4 * 4:(h4 + 1) * 4, :], reads=[B("vst")])
            rope3(ph, "dve", kr[:, :].unsqueeze(1), pr[:, 384:416].unsqueeze(1), tb[s][:, 0:32], tb[s][:, 32:64], 1, 32, 1, t3[:, :], t4[:, :],
                  [prB, B("tb")], [ph.b("t3"), ph.b("t4")], ph.b("kr"))
            ph.add("dve", lambda e: e.tensor_copy(kb[:, :, 64:96], kr[:, :].unsqueeze(1).to_broadcast([128, 16, 32])),
                   reads=[ph.b("kr")], writes=[ph.b("kb")])
            for (src, sn, stt, stn, dst) in ((qb, "qb", stq, "stq", dr["qT_C"]), (kb, "kb", stk, "stk", dr["kT_C"])):
                for hh in range(2):
                    bnk = [pb_[0], pb_[1]] if sn == "qb" else [pb_[2], pb_[3]]
                    bnn = [0, 1] if sn == "qb" else [2, 3]
                    ptile = bnk[hh]
                    pv = ptile[:, :].bitcast(BF16)
                    for h8 in range(8):
                        h = hh * 8 + h8
                        ph.add("pe", (lambda e, pv=pv, h8=h8, h=h, src=src: e.transpose(pv[0:96, h8 * 128:(h8 + 1) * 128], src[:, h, :], idb[:, :])),
                               reads=[ph.b(sn), ph.b("idb")], writes=[ph.b("pb", bnn[hh])])
                    ph.add("act", (lambda e, pv=pv, hh=hh, stt=stt, s=s: e.copy(stt[s][0:96, hh * 8:(hh + 1) * 8, :],
                                                                                 pv[0:96, :].rearrange("p (c t) -> p c t", c=8))),
                           reads=[ph.b("pb", bnn[hh])], writes=[B(stn)])
                for h4 in range(4):
                    ph.dma("sp", dst[h4 * 4:(h4 + 1) * 4, :, tok].rearrange("h p t -> p h t"), stt[s][0:96, h4 * 4:(h4 + 1) * 4, :], reads=[B(stn)])
        for i in range(NT + 1):
            if i < NT:
                stageA(i)
            if i >= 1:
                stageB(i - 1)
        ph.emit()


PHASES = ["p1", "p2", "p3", "p4", "p5", "p6", "p7", "p8", "p9", "p10", "p11", "p12", "p13"]


def build(upto="p13", dbg=()):
    P = Prog(dbg)
    nc = P.nc
    x = P.din("x", [S, D])
    P.din("w_in", [D, 1536])
    P.din("ab_q_norm", [1, 64])
    P.din("ab_k_norm", [1, 64])
    P.din("ab_sink", [1, 8])
    P.din("ab_w_out", [D, D])
    P.din("mla_w_down", [1, D, 416])
    P.din("mla_q_norm", [1, 256])
    P.din("mla_kv_norm", [1, 128])
    P.din("mla_w_uq", [1, 256, 1536])
    P.din("mla_w_ukv", [1, 128, 2048])
    P.din("mla_w_out", [D, D])
    for n in ("ln_mix_g", "ln_mix_b", "ln_ffn_g", "ln_ffn_b"):
        P.din(n, [2, D])
    P.din("moe_router", [2, D, NE])
    for n in ("moe_w_gate", "moe_w_up", "moe_w_down"):
        P.din(n, [2, NE, D, D])
    P.din("tab", [S, 320])
    P.din("cst", [128, C_W])
    P.din("wmask", [128, 3072])
    P.dtmp("qT_A", [4, 128, S], BF16)
    P.dtmp("qT_B", [4, 128, S], BF16)
    P.dtmp("kT_A", [2, 128, S], BF16)
    P.dtmp("kT_B", [2, 128, S], BF16)
    P.dtmp("v_all", [4, 128, NT, 128], BF16)
    P.dtmp("oT", [D, S], BF16)
    P.dtmp("x1ext", [S, XW])
    P.dtmp("acc", [S, D])
    P.dtmp("affd", [S, NE])
    P.dtmp("p5_Cd", [NE, S])
    P.dtmp("p11_Cd", [NE, S])
    P.dtmp("x2ext", [S, XW])
    P.dtmp("x3ext", [S, XW])
    P.dtmp("acc2", [S, D])
    P.dtmp("qT_C", [16, 96, S], BF16)
    P.dtmp("kT_C", [16, 96, S], BF16)
    P.dtmp("v_C", [16, 128, NT, 128], BF16)
    out = P.dtmp("out", [S, D], out=True)
    if "dbg_idx" in dbg:
        P.dtmp("dbg_idx", [128, 128], I32)
        P.dtmp("dbg_lo", [128, NE])
    if "dbg_xg" in dbg:
        P.dtmp("dbg_xg", [128, XW])
        P.dtmp("dbg_yo", [128, D])
        P.dtmp("dbg_hT", [128, 8192], BF16)
    if "dbg_q9" in dbg:
        P.dtmp("dbg_q9", [128, S], BF16)
        P.dtmp("dbg_k9", [128, S], BF16)
    if "dbg_rec" in dbg:
        P.dtmp("dbg_rec", [64, 512])
        P.dtmp("dbg_num", [128, 512])
    if "dbgv0" in dbg:
        P.dtmp("dbgv0", [128, NT, 128], BF16)
        P.dtmp("dbgv1", [128, NT, 128], BF16)
    dr = P.dram
    last = PHASES.index(upto)
    on = lambda p: PHASES.index(p) <= last
    with contextlib.ExitStack() as st:
        idx = sb(st, nc, "idx_all", [128, NE, 8], I32)
        if on("p1"):
            phase_proj0(P, x)
        if on("p2"):
            jobs = []
            for h in range(8):
                j, half, kv = h // 2, h % 2, h // 4
                jobs.append(dict(qsrc=(("qA", j), dr["qT_A"][j, :, :], 128), ksrc=(("kA", kv), dr["kT_A"][kv, :, :], 128),
                                 vsrc=(("vA", kv), dr["v_all"][kv, :, :, :]), pb=half * 64, K=64, out=dr["oT"][h * 64:(h + 1) * 64, :]))
            phase_dense_attn(P, "p2", jobs, 64 ** -0.5)
        if on("p3"):
            jobs = []
            for h in range(8):
                j, half, kv = h // 2, h % 2, h // 4
                jobs.append(dict(qsrc=(("qB", j), dr["qT_B"][j, :, :], 128), ksrc=(("kB", kv), dr["kT_B"][kv, :, :], 128),
                                 vsrc=(("vB", kv), dr["v_all"][2 + kv, :, :, :]), pb=half * 64, K=64, h=h,
                                 out=dr["oT"][512 + h * 64:512 + (h + 1) * 64, :]))
            phase_dense_attn(P, "p3", jobs, 64 ** -0.5, win=True)
        if on("p4"):
            phase_outproj(P, "p4", dr["ab_w_out"][:, :], lambda tok: x[tok, :], dr["ln_mix_g"][0, :], dr["ln_mix_b"][0, :],
                          dr["moe_router"][0, :, :], dr["x1ext"], dr["acc"], dr["affd"])
        if on("p5"):
            phase_route(P, "p5", dr["x1ext"], idx)
        if on("p6"):
            phase_experts(P, "p6", 0, dr["x1ext"], dr["acc"], idx)
        if on("p7"):
            phase_ln(P, "p7", dr["acc"], dr["ln_ffn_g"][0, :], dr["ln_ffn_b"][0, :], lambda tok: dr["x2ext"][tok, 0:D])
        if on("p8"):
            phase_proj_mla(P, lambda tok: dr["x2ext"][tok, 0:D])
        if on("p9"):
            jobs = []
            for h in range(16):
                jobs.append(dict(qsrc=(("qC", h), dr["qT_C"][h, :, :], 96), ksrc=(("kC", h), dr["kT_C"][h, :, :], 96),
                                 vsrc=(("vC", h), dr["v_C"][h, :, :, :]), pb=0, K=96, out=dr["oT"][h * 64:(h + 1) * 64, :]))
            phase_dense_attn(P, "p9", jobs, 96 ** -0.5)
        if on("p10"):
            phase_outproj(P, "p10", dr["mla_w_out"][:, :], lambda tok: dr["x2ext"][tok, 0:D], dr["ln_mix_g"][1, :], dr["ln_mix_b"][1, :],
                          dr["moe_router"][1, :, :], dr["x3ext"], dr["acc2"], dr["affd"])
        if on("p11"):
            phase_route(P, "p11", dr["x3ext"], idx)
        if on("p12"):
            phase_experts(P, "p12", 1, dr["x3ext"], dr["acc2"], idx)
        if on("p13"):
            phase_ln(P, "p13", dr["acc2"], dr["ln_ffn_g"][1, :], dr["ln_ffn_b"][1, :], lambda tok: out[tok, :])
    return P


_PERM = np.concatenate([np.arange(0, 512), np.arange(512, 640), np.arange(768, 1280), np.arange(1280, 1408),
                        np.arange(640, 768), np.arange(1408, 1536)])


def make_in_maps(inputs, ncores=8):
    tab, cst = make_consts()
    f = lambda a: np.ascontiguousarray(np.asarray(a, dtype=np.float32))
    shared = {
        "w_in": f(np.asarray(inputs["ab_w_in"])[0][:, _PERM]),
        "ab_q_norm": f(inputs["ab_q_norm"]), "ab_k_norm": f(inputs["ab_k_norm"]), "ab_sink": f(inputs["ab_sink"]),
        "ab_w_out": f(np.asarray(inputs["ab_w_out"])[0]),
        "mla_w_down": f(inputs["mla_w_down"]), "mla_q_norm": f(inputs["mla_q_norm"]), "mla_kv_norm": f(inputs["mla_kv_norm"]),
        "mla_w_uq": f(inputs["mla_w_uq"]), "mla_w_ukv": f(inputs["mla_w_ukv"]), "mla_w_out": f(np.asarray(inputs["mla_w_out"])[0]),
        "ln_mix_g": f(inputs["ln_mix_g"]), "ln_mix_b": f(inputs["ln_mix_b"]), "ln_ffn_g": f(inputs["ln_ffn_g"]), "ln_ffn_b": f(inputs["ln_ffn_b"]),
        "moe_router": f(inputs["moe_router"]), "moe_w_gate": f(inputs["moe_w_gate"]), "moe_w_up": f(inputs["moe_w_up"]),
        "moe_w_down": f(inputs["moe_w_down"]), "tab": tab, "cst": cst, "wmask": make_wmask(),
    }
    xs = np.asarray(inputs["x"], dtype=np.float32)
    maps = []
    for c in range(ncores):
        m = dict(shared)
        m["x"] = np.ascontiguousarray(xs[c])
        maps.append(m)
    return maps


def kernel(**inputs):
    P = build()
    maps = make_in_maps(inputs, 8)
    res = run_bass_kernel_spmd(P.nc, maps, core_ids=list(range(8)))
    return np.stack([np.asarray(r["out"], dtype=np.float32) for r in res.results], axis=0)
```

```python
import contextlib
import numpy as np
import concourse.bass as bass
import concourse.mybir as mybir
from concourse.bass_utils import run_bass_kernel_spmd

F32 = mybir.dt.float32
BF16 = mybir.dt.bfloat16
I32 = mybir.dt.int32
ALU = mybir.AluOpType
AF = mybir.ActivationFunctionType
AX = mybir.AxisListType

S = 8192
D = 1024
NT = S // 128
NE = 16
CAP = 1024
DEPTH = 2
ALPHA = (2.0 * DEPTH) ** 0.25
XW = D + NE

SAME_ENGINE_SYNC = True


class Buf:
    __slots__ = ("w", "r", "rd")

    def __init__(self):
        self.w = None
        self.r = {}
        self.rd = []


class Op:
    __slots__ = ("eng", "fn", "deps", "sig", "sem", "val", "dma", "inc")

    def __init__(self, eng, fn, dma):
        self.eng = eng
        self.fn = fn
        self.dma = dma
        self.deps = []
        self.sig = dma
        self.sem = None
        self.val = 0
        self.inc = 16 if dma else 1


ENGS = ("sp", "pe", "act", "dve", "pool")


class Phase:
    KD = {"sp": 8, "pool": 6, "act": 4}

    def __init__(self, nc, name):
        self.nc = nc
        self.name = name
        self.ops = {e: [] for e in ENGS}
        self.bufs = {}
        self.dma_ops = {q: [] for q in self.KD}

    def b(self, *key):
        bb = self.bufs.get(key)
        if bb is None:
            bb = self.bufs[key] = Buf()
        return bb

    def add(self, eng, fn, reads=(), writes=(), dma=False):
        op = Op(eng, fn, dma)
        deps = []
        for b in reads:
            if b.w is not None:
                deps.append(b.w)
        for b in writes:
            if b.w is not None:
                deps.append(b.w)
            deps.extend(b.r.values())
            deps.extend(b.rd)
        if dma:
            lst = self.dma_ops[eng]
            k = self.KD[eng]
            if len(lst) >= k:
                deps.append(lst[len(lst) - k])
            lst.append(op)
        seen = set()
        for d in deps:
            if d is op or id(d) in seen:
                continue
            seen.add(id(d))
            if (not d.dma) and (not dma) and d.eng == eng:
                if eng == "pe" or not SAME_ENGINE_SYNC:
                    continue
            op.deps.append(d)
            d.sig = True
        for b in reads:
            if dma:
                b.rd.append(op)
            else:
                b.r[eng] = op
        for b in writes:
            b.w = op
            b.r = {}
            b.rd = []
        self.ops[eng].append(op)
        return op

    def dma(self, q, out, in_, reads=(), writes=(), **kw):
        return self.add(q, lambda e: e.dma_start(out=out, in_=in_, **kw), reads, writes, dma=True)

    def emit(self):
        nc = self.nc
        allsems = []

        def newsem(nm):
            h = nc.alloc_semaphore(name=nm)
            allsems.append(h)
            return h

        if True:
            csem = {}
            for e in ("pe", "act", "dve", "pool"):
                if any(o.sig and not o.dma for o in self.ops[e]):
                    csem[e] = newsem(f"{self.name}_c_{e}")
            dsem = {}
            for q, k in self.KD.items():
                if self.dma_ops[q]:
                    dsem[q] = [newsem(f"{self.name}_d_{q}{i}") for i in range(min(k, len(self.dma_ops[q])))]
            for e in ENGS:
                cnt = 0
                for o in self.ops[e]:
                    if o.dma:
                        continue
                    if o.sig:
                        cnt += 1
                        o.sem = csem[e]
                        o.val = cnt
            final = {q: {} for q in self.KD}
            for q, lst in self.dma_ops.items():
                k = self.KD[q]
                for n, o in enumerate(lst):
                    o.sem = dsem[q][n % k]
                    o.val = 16 * (n // k + 1)
                    final[q][n % k] = (o.sem, o.val)
            ops = self.ops

            def run(e_name, eng):
                waited = {}
                for o in ops[e_name]:
                    need = {}
                    for d in o.deps:
                        key = id(d.sem)
                        if waited.get(key, 0) >= d.val:
                            continue
                        if key not in need or need[key][1] < d.val:
                            need[key] = (d.sem, d.val)
                    for key, (sem, val) in need.items():
                        eng.wait_ge(sem, val)
                        waited[key] = val
                    ins = o.fn(eng)
                    if o.sig:
                        ins.then_inc(o.sem, o.inc)
                if e_name in final:
                    for (sem, val) in final[e_name].values():
                        if waited.get(id(sem), 0) < val:
                            eng.wait_ge(sem, val)

            with nc.Block() as block:
                if ops["sp"]:
                    @block.sync
                    def _(e):
                        run("sp", e)
                if ops["pe"]:
                    @block.tensor
                    def _(e):
                        run("pe", e)
                if ops["act"]:
                    @block.scalar
                    def _(e):
                        run("act", e)
                if ops["dve"]:
                    @block.vector
                    def _(e):
                        run("dve", e)
                if ops["pool"]:
                    @block.gpsimd
                    def _(e):
                        run("pool", e)
            nc.clear_and_free_semaphores(allsems)
            nc.all_engine_barrier()


def _rope_tab(pos, dim):
    freqs = (10000.0 ** (-(np.arange(0, dim, 2, dtype=np.float32) / np.float32(dim)))).astype(np.float32)
    ang = pos.astype(np.float32)[:, None] * freqs[None, :]
    return np.cos(ang).astype(np.float32), np.sin(ang).astype(np.float32)


def make_consts():
    t = np.arange(S)
    cr, sr = _rope_tab((t // 64).astype(np.float32), 32)
    cc, sc = _rope_tab((t % 64).astype(np.float32), 32)
    cs, ss = _rope_tab(t.astype(np.float32), 64)
    cm, sm = _rope_tab(t.astype(np.float32), 32)
    cosA = np.concatenate([cr, cr, cc, cc], 1)
    sinA = np.concatenate([-sr, sr, -sc, sc], 1)
    cosB = np.concatenate([cs, cs], 1)
    sinB = np.concatenate([-ss, ss], 1)
    cosC = np.concatenate([cm, cm], 1)
    sinC = np.concatenate([-sm, sm], 1)
    tab = np.concatenate([cosA, sinA, cosB, sinB, cosC, sinC], 1).astype(np.float32)
    ident = np.eye(128, dtype=np.float32)
    p = np.arange(128)
    tri = (p[:, None] < p[None, :]).astype(np.float32)
    mprev = (p[None, :] <= p[:, None]).astype(np.float32)
    mnext = (p[:, None] <= p[None, :]).astype(np.float32)
    svec = (np.arange(8)[None, :] * 128 + p[:, None]).astype(np.float32)
    keep = np.ones((128, NE, 64), np.float32)
    keep[:, :, 0] = 0.0
    vsink = np.zeros((128, 128), np.float32)
    vsink[0:2, 64:128] = 1.0
    cst = np.concatenate([ident, tri, mprev, mnext, svec, keep.reshape(128, -1), vsink], 1).astype(np.float32)
    return tab, cst


def make_wmask():
    kl = np.arange(128)[:, None]
    q = np.arange(512)[None, :]
    ms = []
    for r in range(6):
        ms.append((np.abs(q - ((r - 1) * 128 + kl)) <= 128).astype(np.float32))
    return np.concatenate(ms, 1)


C_IDENT, C_TRI, C_MPREV, C_MNEXT, C_SVEC, C_KEEP = 0, 128, 256, 384, 512, 520
C_VSINK = 520 + NE * 64
C_W = C_VSINK + 128


class Prog:
    def __init__(self, dbg=()):
        self.nc = bass.Bass("TRN2", target_bir_lowering=False)
        self.dbg = set(dbg)
        self.dram = {}

    def din(self, name, shape, dt=F32):
        t = self.nc.dram_tensor(name, list(shape), dt, kind="ExternalInput")
        self.dram[name] = t
        return t

    def dtmp(self, name, shape, dt=F32, out=False):
        kind = "ExternalOutput" if (out or name in self.dbg) else "Internal"
        t = self.nc.dram_tensor(name, list(shape), dt, kind=kind)
        self.dram[name] = t
        return t


def sb(st, nc, name, shape, dt):
    return st.enter_context(nc.sbuf_tensor(name, list(shape), dt))


def ps(st, nc, name, shape, dt):
    return st.enter_context(nc.psum_tensor(name, list(shape), dt))


def load_consts(ph, nc, st, P):
    cst = sb(st, nc, ph.name + "_cst", [128, C_W], F32)
    ph.dma("sp", cst[:, :], P.dram["cst"][:, :], writes=[ph.b("cst")])
    idb = sb(st, nc, ph.name + "_idb", [128, 128], BF16)
    ph.add("dve", lambda e: e.tensor_copy(idb[:, :], cst[:, C_IDENT:C_IDENT + 128]),
           reads=[ph.b("cst")], writes=[ph.b("idb")])
    return cst, idb


def emit_xT(ph, xs, xs_buf, xT, xT_buf, pT, pT_bufs, cst, ncol=8, evac=("act", "dve")):
    ident = cst[:, C_IDENT:C_IDENT + 128]
    ng = (ncol + 3) // 4
    for g in range(ng):
        pt = pT[g % len(pT)]
        pb = pT_bufs[g % len(pT)]
        n = min(4, ncol - g * 4)
        for c in range(n):
            cc = g * 4 + c
            ph.add("pe", (lambda e, pt=pt, c=c, cc=cc: e.transpose(pt[:, c * 128:(c + 1) * 128],
                                                                   xs[:, cc * 128:(cc + 1) * 128], ident)),
                   reads=[xs_buf, ph.b("cst")], writes=[pb])
        eng = evac[g % len(evac)]
        if eng == "act":
            ph.add("act", (lambda e, pt=pt, g=g, n=n: e.copy(
                xT[:, g * 4:g * 4 + n, :], pt[:, 0:n * 128].rearrange("p (c t) -> p c t", c=n))),
                reads=[pb], writes=[xT_buf])
        else:
            ph.add("dve", (lambda e, pt=pt, g=g, n=n: e.tensor_copy(
                xT[:, g * 4:g * 4 + n, :], pt[:, 0:n * 128].rearrange("p (c t) -> p c t", c=n))),
                reads=[pb], writes=[xT_buf])


def load_weight_bf16(ph, nc, st, name, w_ap, kc, n, stage, stage_bufs, cast_engs=("dve", "pool")):
    wb = sb(st, nc, name, [128, kc, n], BF16)
    wv = w_ap.rearrange("(c p) n -> p c n", p=128)
    cnt = 0
    for c in range(kc):
        for n0 in range(0, n, stage[0].shape[1]):
            n1 = min(n, n0 + stage[0].shape[1])
            s = cnt % len(stage)
            ph.dma("sp", stage[s][:, 0:n1 - n0], wv[:, c, n0:n1], writes=[stage_bufs[s]])
            eng = cast_engs[cnt % len(cast_engs)]
            ph.add(eng, (lambda e, s=s, c=c, n0=n0, n1=n1: e.tensor_copy(wb[:, c, n0:n1], stage[s][:, 0:n1 - n0])),
                   reads=[stage_bufs[s]], writes=[ph.b(name)])
            cnt += 1
    return wb


def emit_rope(ph, eng, out, src, cos, sin, nh, hd, tmp1, tmp2, rbufs, wbufs, blocks):
    hb = hd // blocks // 2
    v = lambda a: a.rearrange("p (h b t f) -> p h b t f", h=nh, b=blocks, t=2, f=hb)
    tb = lambda a: a.rearrange("p (b t f) -> p b t f", b=blocks, t=2, f=hb)
    cosb = cos.unsqueeze(1).to_broadcast([128, nh, hd]).rearrange("p h d -> p (h d)") if False else None
    s3 = src.rearrange("p (h d) -> p h d", h=nh)
    t13 = tmp1.rearrange("p (h d) -> p h d", h=nh)
    cos3 = cos.unsqueeze(1).to_broadcast([128, nh, hd])
    ph.add(eng, lambda e: e.tensor_tensor(t13, s3, cos3, ALU.mult), reads=rbufs, writes=[wbufs[0]])
    sv, t2v = v(src), v(tmp2)
    sn = tb(sin)
    for t in range(2):
        sin_b = sn[:, :, t, :].unsqueeze(1).to_broadcast([128, nh, blocks, hb])
        ph.add(eng, (lambda e, t=t, sin_b=sin_b: e.tensor_tensor(t2v[:, :, :, t, :], sv[:, :, :, 1 - t, :], sin_b, ALU.mult)),
               reads=rbufs, writes=[wbufs[1]])
    ph.add(eng, lambda e: e.tensor_tensor(out, tmp1, tmp2, ALU.add), reads=[wbufs[0], wbufs[1]], writes=[wbufs[2]])


def rope3(ph, eng, out3, src3, cos, sin, nh, hd, blocks, t1, t2, rbufs, tbufs, wbuf):
    hb = hd // blocks // 2
    t13 = t1.rearrange("p (h d) -> p h d", h=nh)
    t23 = t2.rearrange("p (h d) -> p h d", h=nh)
    cos3 = cos.unsqueeze(1).to_broadcast([128, nh, hd])
    ph.add(eng, lambda e: e.tensor_tensor(t13, src3, cos3, ALU.mult), reads=rbufs, writes=[tbufs[0]])
    for b in range(blocks):
        for t in range(2):
            o0 = b * 2 * hb + t * hb
            s0 = b * 2 * hb + (1 - t) * hb
            sin_b = sin[:, o0:o0 + hb].unsqueeze(1).to_broadcast([128, nh, hb])
            ph.add(eng, (lambda e, o0=o0, s0=s0, sin_b=sin_b: e.tensor_tensor(
                t23[:, :, o0:o0 + hb], src3[:, :, s0:s0 + hb], sin_b, ALU.mult)),
                reads=rbufs, writes=[tbufs[1]])
    ph.add(eng, lambda e: e.tensor_tensor(out3, t13, t23, ALU.add), reads=[tbufs[0], tbufs[1]], writes=[wbuf])


def lowp(nc, f):
    def g(e):
        with nc.allow_low_precision(reason="exact small integer counts"):
            return f(e)
    return g


def bc(ap1d):
    return ap1d.partition_broadcast(128)


def phase_proj0(P, x_ap):
    nc = P.nc
    dr = P.dram
    with contextlib.ExitStack() as st:
        ph = Phase(nc, "p1")
        cst, idb = load_consts(ph, nc, st, P)
        stage = [sb(st, nc, f"p1_stg{i}", [128, 1536], F32) for i in range(2)]
        wb = load_weight_bf16(ph, nc, st, "p1_wb", dr["w_in"][:, :], 8, 1536, stage,
                              [ph.b("stg", i) for i in range(2)], cast_engs=("dve", "pool"))
        gq = sb(st, nc, "p1_gq", [128, 64], F32)
        gk = sb(st, nc, "p1_gk", [128, 64], F32)
        ph.dma("sp", gq[:, :], bc(dr["ab_q_norm"][0, :]), writes=[ph.b("gq")])
        ph.dma("sp", gk[:, :], bc(dr["ab_k_norm"][0, :]), writes=[ph.b("gk")])
        NSL = 2
        xs = [sb(st, nc, f"p1_xs{i}", [128, 1024], F32) for i in range(NSL)]
        tb = [sb(st, nc, f"p1_tb{i}", [128, 256], F32) for i in range(4)]
        xT = [sb(st, nc, f"p1_xT{i}", [128, 8, 128], BF16) for i in range(NSL)]
        pr = [sb(st, nc, f"p1_pr{i}", [128, 1536], F32) for i in range(4)]
        sq_ = [sb(st, nc, f"p1_sq{i}", [128, 640], F32) for i in range(2)]
        ss_ = [sb(st, nc, f"p1_ss{i}", [128, 10], F32) for i in range(2)]
        sd_ = [sb(st, nc, f"p1_sd{i}", [128, 10], F32) for i in range(2)]
        rs_ = [sb(st, nc, f"p1_rs{i}", [128, 10], F32) for i in range(2)]
        t0_ = [sb(st, nc, f"p1_t0{i}", [128, 640], F32) for i in range(2)]
        ta1_ = [sb(st, nc, f"p1_ta1{i}", [128, 640], F32) for i in range(2)]
        ta2_ = [sb(st, nc, f"p1_ta2{i}", [128, 640], F32) for i in range(2)]
        tb1_ = [sb(st, nc, f"p1_rb1{i}", [128, 640], F32) for i in range(2)]
        tb2_ = [sb(st, nc, f"p1_rb2{i}", [128, 640], F32) for i in range(2)]
        oa = [sb(st, nc, f"p1_oa{i}", [128, 640], BF16) for i in range(NSL)]
        ob = [sb(st, nc, f"p1_ob{i}", [128, 640], BF16) for i in range(NSL)]
        stA = [sb(st, nc, f"p1_stA{i}", [128, 5, 128], BF16) for i in range(NSL)]
        stB = [sb(st, nc, f"p1_stB{i}", [128, 5, 128], BF16) for i in range(NSL)]
        vst = [sb(st, nc, f"p1_vst{i}", [128, 4, 128], BF16) for i in range(NSL)]
        pT = [ps(st, nc, f"p1_pT{i}", [128, 512], F32) for i in range(2)]
        pp = [ps(st, nc, f"p1_pp{i}", [128, 512], F32) for i in range(3)]
        ptA = ps(st, nc, "p1_ptA", [128, 1024], BF16)
        ptB = ps(st, nc, "p1_ptB", [128, 1024], BF16)
        for s in range(NSL):
            ph.add("pool", (lambda e, s=s: e.memset(vst[s][:, :, :], 1.0)), writes=[ph.b("vst", s)])
        eps = sb(st, nc, "p1_eps", [128, 1], F32)
        ph.add("pool", lambda e: e.memset(eps[:, :], 1e-6), writes=[ph.b("eps")])
        qTA, qTB, kTA, kTB, vall = dr["qT_A"], dr["qT_B"], dr["kT_A"], dr["kT_B"], dr["v_all"]
        def stageA(i):
            s = i % NSL
            s4 = i % 4
            tok = slice(i * 128, (i + 1) * 128)
            B = lambda n: ph.b(n, s)
            ph.dma("sp", xs[s][:, :], x_ap[tok, :], writes=[B("xs")])
            ph.dma("sp", tb[s4][:, :], dr["tab"][tok, 0:256], writes=[ph.b("tb", s4)])
            emit_xT(ph, xs[s], B("xs"), xT[s], B("xT"), pT, [ph.b("pT", 0), ph.b("pT", 1)], cst)
            for n in range(3):
                for c in range(8):
                    ph.add("pe", (lambda e, n=n, c=c, s=s: e.matmul(pp[n][:, :], xT[s][:, c, :], wb[:, c, n * 512:(n + 1) * 512],
                                                                     start=(c == 0), stop=(c == 7))),
                           reads=[B("xT"), ph.b("p1_wb")], writes=[ph.b("pp", n)])
                ph.add("act", (lambda e, n=n, s4=s4: e.copy(pr[s4][:, n * 512:(n + 1) * 512], pp[n][:, :])),
                       reads=[ph.b("pp", n)], writes=[ph.b("pr", s4)])

        def stageB(i):
            s = i % NSL
            s4 = i % 4
            tok = slice(i * 128, (i + 1) * 128)
            B = lambda n: ph.b(n, s)
            prt, tbt, prB, tbB = pr[s4], tb[s4], ph.b("pr", s4), ph.b("tb", s4)
            sq, ss, sd, rs, t0, ta1, ta2, tb1, tb2 = sq_[s], ss_[s], sd_[s], rs_[s], t0_[s], ta1_[s], ta2_[s], tb1_[s], tb2_[s]
            rope3(ph, "pool", ob[s][:, :].rearrange("p (h d) -> p h d", h=10), prt[:, 640:1280].rearrange("p (h d) -> p h d", h=10),
                  tbt[:, 128:192], tbt[:, 192:256], 10, 64, 1, tb1[:, :], tb2[:, :],
                  [prB, tbB], [B("tb1"), B("tb2")], B("ob"))
            ph.add("act", lambda e: e.copy(vst[s][:, :, 0:64], prt[:, 1280:1536].rearrange("p (h d) -> p h d", h=4)),
                   reads=[prB], writes=[B("vst")])
            ph.dma("sp", vall[:, :, i, :].rearrange("k p f -> p k f"), vst[s][:, :, :], reads=[B("vst")])
            ph.add("act", lambda e: e.activation(sq[:, :], prt[:, 0:640], AF.Square), reads=[prB], writes=[B("sq")]); yield
            ph.add("dve", lambda e: e.tensor_reduce(ss[:, :], sq[:, :].rearrange("p (h d) -> p h d", h=10), AX.X, ALU.add),
                   reads=[B("sq")], writes=[B("ss")]); yield
            ph.add("act", lambda e: e.activation(sd[:, :], ss[:, :], AF.Sqrt, bias=eps[:, :], scale=1.0 / 64),
                   reads=[B("ss"), ph.b("eps")], writes=[B("sd")]); yield
            ph.add("dve", lambda e: e.reciprocal(rs[:, :], sd[:, :]), reads=[B("sd")], writes=[B("rs")]); yield
            ph.add("dve", lambda e: e.tensor_tensor(t0[:, :].rearrange("p (h d) -> p h d", h=10),
                                                    prt[:, 0:640].rearrange("p (h d) -> p h d", h=10),
                                                    rs[:, :].unsqueeze(2).to_broadcast([128, 10, 64]), ALU.mult),
                   reads=[prB, B("rs")], writes=[B("t0")])
            ph.add("dve", lambda e: e.tensor_tensor(t0[:, 0:512].rearrange("p (h d) -> p h d", h=8),
                                                    t0[:, 0:512].rearrange("p (h d) -> p h d", h=8),
                                                    gq[:, :].unsqueeze(1).to_broadcast([128, 8, 64]), ALU.mult),
                   reads=[ph.b("gq")], writes=[B("t0")])
            ph.add("dve", lambda e: e.tensor_tensor(t0[:, 512:640].rearrange("p (h d) -> p h d", h=2),
                                                    t0[:, 512:640].rearrange("p (h d) -> p h d", h=2),
                                                    gk[:, :].unsqueeze(1).to_broadcast([128, 2, 64]), ALU.mult),
                   reads=[ph.b("gk")], writes=[B("t0")]); yield
            rope3(ph, "dve", oa[s][:, :].rearrange("p (h d) -> p h d", h=10), t0[:, :].rearrange("p (h d) -> p h d", h=10),
                  tbt[:, 0:64], tbt[:, 64:128], 10, 64, 2, ta1[:, :], ta2[:, :],
                  [B("t0"), tbB], [B("ta1"), B("ta2")], B("oa")); yield
            for (src, sbuf_, ptt, ptn, stt, stn, qT, kT) in ((oa, "oa", ptA, "ptA", stA, "stA", qTA, kTA),
                                                             (ob, "ob", ptB, "ptB", stB, "stB", qTB, kTB)):
                for c in range(5):
                    ph.add("pe", (lambda e, src=src, ptt=ptt, c=c: e.transpose(ptt[:, c * 128:(c + 1) * 128],
                                                                               src[s][:, c * 128:(c + 1) * 128], idb[:, :])),
                           reads=[B(sbuf_), ph.b("idb")], writes=[ph.b(ptn)])
                ph.add("act", (lambda e, ptt=ptt, stt=stt: e.copy(stt[s][:, :, :], ptt[:, 0:640].rearrange("p (c t) -> p c t", c=5))),
                       reads=[ph.b(ptn)], writes=[B(stn)])
                ph.dma("sp", qT[:, :, tok].rearrange("j p t -> p j t"), stt[s][:, 0:4, :], reads=[B(stn)])
                for kv in range(2):
                    for dup in range(2):
                        ph.dma("sp", kT[kv, dup * 64:(dup + 1) * 64, tok], stt[s][kv * 64:(kv + 1) * 64, 4, :], reads=[B(stn)])
                yield

        def drive(gens):
            gens = list(gens)
            while gens:
                for g_ in list(gens):
                    try:
                        next(g_)
                    except StopIteration:
                        gens.remove(g_)

        stageA(0)
        stageA(1)
        for i in range(0, NT, 2):
            if i + 2 < NT:
                stageA(i + 2)
                stageA(i + 3)
            drive([stageB(i), stageB(i + 1)])
        ph.emit()


def phase_dense_attn(P, name, jobs, scale, win=False):
    nc = P.nc
    dr = P.dram
    with contextlib.ExitStack() as st:
        ph = Phase(nc, name)
        if win:
            cst, idb = load_consts(ph, nc, st, P)
            wm = sb(st, nc, f"{name}_wm", [128, 3072], BF16)
            wst = sb(st, nc, f"{name}_wst", [128, 3072], F32)
            ph.dma("sp", wst[:, :], dr["wmask"][:, :], writes=[ph.b("wst")])
            ph.add("dve", lambda e: e.tensor_copy(wm[:, :], wst[:, :]), reads=[ph.b("wst")], writes=[ph.b("wm")])
            vsk = sb(st, nc, f"{name}_vsk", [128, 128], BF16)
            ph.add("dve", lambda e: e.tensor_copy(vsk[:, :], cst[:, C_VSINK:C_VSINK + 128]), reads=[ph.b("cst")], writes=[ph.b("vsk")])
            sk = sb(st, nc, f"{name}_sk", [128, 8], F32)
            es = sb(st, nc, f"{name}_es", [128, 8], F32)
            ehb = sb(st, nc, f"{name}_ehb", [128, 8], BF16)
            ehf = sb(st, nc, f"{name}_ehf", [128, 8], F32)
            elo = sb(st, nc, f"{name}_elo", [128, 8], F32)
            ssel = sb(st, nc, f"{name}_ssel", [128, 8], F32)
            onesb = sb(st, nc, f"{name}_onesb", [128, 512], BF16)
            psink = [sb(st, nc, f"{name}_psink{i}", [128, 512], BF16) for i in range(2)]
            ph.dma("sp", sk[:, :], bc(dr["ab_sink"][0, :]), writes=[ph.b("sk")])
            ph.add("act", lambda e: e.activation(es[:, :], sk[:, :], AF.Exp), reads=[ph.b("sk")], writes=[ph.b("es")])
            ph.add("dve", lowp(nc, lambda e: e.tensor_copy(ehb[:, :], es[:, :])), reads=[ph.b("es")], writes=[ph.b("ehb")])
            ph.add("dve", lambda e: e.tensor_copy(ehf[:, :], ehb[:, :]), reads=[ph.b("ehb")], writes=[ph.b("ehf")])
            ph.add("dve", lambda e: e.tensor_tensor(elo[:, :], es[:, :], ehf[:, :], ALU.subtract), reads=[ph.b("es"), ph.b("ehf")], writes=[ph.b("elo")])
            ph.add("dve", lambda e: e.tensor_scalar(ehf[:, :], ehf[:, :], cst[:, C_IDENT:C_IDENT + 1], None, ALU.mult), reads=[ph.b("cst")], writes=[ph.b("ehf")])
            ph.add("dve", lambda e: e.scalar_tensor_tensor(ssel[:, :], elo[:, :], cst[:, C_IDENT + 1:C_IDENT + 2], ehf[:, :], ALU.mult, ALU.add),
                   reads=[ph.b("elo"), ph.b("ehf"), ph.b("cst")], writes=[ph.b("ssel")])
            ph.add("dve", lambda e: e.memset(onesb[:, :], 1.0), writes=[ph.b("onesb")])
        NSL = 2
        qs = [sb(st, nc, f"{name}_q{i}", [128, S], BF16) for i in range(NSL)]
        pad = (jobs[0]["K"] == 64)
        nk = 2 if pad else 1
        ks = [[sb(st, nc, f"{name}_k{i}_{hf}", [128, S], BF16) for hf in range(nk)] for i in range(NSL)]
        vs = [sb(st, nc, f"{name}_v{i}", [128, NT, 128], BF16) for i in range(NSL)]
        NP_ = 6 if win else 4
        pTs = [sb(st, nc, f"{name}_pT{i}", [128, 512], BF16) for i in range(NP_)]
        rec = sb(st, nc, f"{name}_rec", [64, 512], F32)
        onb = [sb(st, nc, f"{name}_on{i}", [64, 512], BF16) for i in range(2)]
        NS_ = 5 if win else 3
        sT = [ps(st, nc, f"{name}_sT{i}", [128, 512], F32) for i in range(NS_)]
        oacc = [ps(st, nc, f"{name}_oa{i}", [128, 512], F32) for i in range(2)]
        LA = 4 if win else 2
        if not pad:
            for i in range(NSL):
                ph.add("pool", (lambda e, i=i: e.memset(qs[i][64:128, :], 0.0)), writes=[ph.b("q", i)])
                ph.add("pool", (lambda e, i=i: e.memset(ks[i][0][64:128, :], 0.0)), writes=[ph.b("k", i)])
        else:
            for i in range(NSL):
                ph.add("pool", (lambda e, i=i: e.memset(ks[i][0][64:128, :], 0.0)), writes=[ph.b("k", i)])
                ph.add("pool", (lambda e, i=i: e.memset(ks[i][1][0:64, :], 0.0)), writes=[ph.b("k", i)])
        qk_prev = kk_prev = vv_prev = None
        slot = {"q": -1, "k": -1, "v": -1}
        cur = {}
        on_cnt = 0
        jslots = []

        def issue_loads(jb):
            for key, tiles, src in (("q", qs, jb["qsrc"]), ("k", ks, jb["ksrc"]), ("v", vs, jb["vsrc"])):
                if cur.get(key) != src[0]:
                    slot[key] = (slot[key] + 1) % NSL
                    sl = slot[key]
                    if key == "v":
                        ph.dma("sp", tiles[sl][:, :, :].rearrange("p c f -> p (c f)"), src[1].rearrange("p c f -> p (c f)"), writes=[ph.b(key, sl)])
                    elif key == "k" and pad:
                        ph.dma("sp", tiles[sl][0][0:64, :], src[1][0:64, :], writes=[ph.b(key, sl)])
                        ph.dma("sp", tiles[sl][1][64:128, :], src[1][64:128, :], writes=[ph.b(key, sl)])
                    elif key == "k":
                        rows = src[2]
                        ph.dma("sp", tiles[sl][0][0:rows, :], src[1], writes=[ph.b(key, sl)])
                    else:
                        rows = src[2]
                        ph.dma("sp", tiles[sl][0:rows, :], src[1], writes=[ph.b(key, sl)])
                    cur[key] = src[0]
            jslots.append(dict(slot))

        issue_loads(jobs[0])
        for ji, jb in enumerate(jobs):
            if ji + 1 < len(jobs):
                issue_loads(jobs[ji + 1])
            jsl = jslots[ji]
            q_t, k_t, v_t = qs[jsl["q"]], ks[jsl["k"]][(jb["pb"] // 64) if pad else 0], vs[jsl["v"]]
            qB, kB, vB = ph.b("q", jsl["q"]), ph.b("k", jsl["k"]), ph.b("v", jsl["v"])
            if "dbg_q9" in dr and name == "p9" and jb is jobs[0]:
                ph.dma("sp", dr["dbg_q9"][:, :], q_t[:, :], reads=[qB])
                ph.dma("sp", dr["dbg_k9"][:, :], k_t[:, :], reads=[kB])
            pb, K = jb["pb"], jb["K"]
            if win:
                steps = [(g, c) for g in range(16) for c in range(4 * g - 1, 4 * g + 5) if 0 <= c < NT]
                hh = jb["h"]
                psl = hh % 2
                ph.add("dve", (lambda e, psl=psl, hh=hh: e.tensor_scalar(psink[psl][:, :], onesb[:, :], ssel[:, hh:hh + 1], None, ALU.mult)),
                       reads=[ph.b("onesb"), ph.b("ssel")], writes=[ph.b("psink", psl)])
            else:
                steps = [(g, c) for g in range(16) for c in range(NT)]
            nsteps = len(steps)

            def qk(n, q_t=q_t, k_t=k_t, qB=qB, kB=kB, pb=pb, K=K):
                g, c = steps[n]
                b = n % NS_
                ph.add("pe", (lambda e: e.matmul(sT[b][:, :], k_t[:, c * 128:(c + 1) * 128],
                                                 q_t[:, g * 512:(g + 1) * 512], start=True, stop=True)),
                       reads=[qB, kB], writes=[ph.b("sT", b)])

            for n in range(min(LA, nsteps)):
                qk(n)
            for n in range(nsteps):
                g, c = steps[n]
                first = (n == 0) or steps[n - 1][0] != g
                last = (n == nsteps - 1) or steps[n + 1][0] != g
                if n + LA < nsteps:
                    qk(n + LA)
                b = n % NS_
                pslot = n % NP_
                ph.add("act", (lambda e, b=b, pslot=pslot: e.activation(pTs[pslot][:, :], sT[b][:, :], AF.Exp, scale=scale)),
                       reads=[ph.b("sT", b)], writes=[ph.b("pT", pslot)])
                if win:
                    r_ = c - (4 * g - 1)
                    ph.add("dve" if n % 2 == 0 else "pool",
                           (lambda e, pslot=pslot, r_=r_: e.tensor_tensor(pTs[pslot][:, :], pTs[pslot][:, :], wm[:, r_ * 512:(r_ + 1) * 512], ALU.mult)),
                           reads=[ph.b("wm")], writes=[ph.b("pT", pslot)])
                ob_ = g % 2
                ph.add("pe", (lambda e, ob_=ob_, c=c, pslot=pslot, v_t=v_t, first=first, last=last: e.matmul(
                    oacc[ob_][:, :], v_t[:, c, :], pTs[pslot][:, :], start=first, stop=(last and not win))),
                    reads=[vB, ph.b("pT", pslot)], writes=[ph.b("oacc", ob_)])
                if last and win:
                    ph.add("pe", (lambda e, ob_=ob_, psl=psl: e.matmul(oacc[ob_][:, :], vsk[:, :], psink[psl][:, :], start=False, stop=True)),
                           reads=[ph.b("vsk"), ph.b("psink", psl)], writes=[ph.b("oacc", ob_)])
                if last:
                    os_ = on_cnt % 2
                    on_cnt += 1
                    ph.add("dve", (lambda e, ob_=ob_: e.reciprocal(rec[:, :], oacc[ob_][64:128, :])),
                           reads=[ph.b("oacc", ob_)], writes=[ph.b("rec")])
                    ph.add("dve", (lambda e, ob_=ob_, os_=os_: e.tensor_tensor(onb[os_][:, :], oacc[ob_][0:64, :], rec[:, :], ALU.mult)),
                           reads=[ph.b("oacc", ob_), ph.b("rec")], writes=[ph.b("on", os_)])
                    ph.dma("sp", jb["out"][:, g * 512:(g + 1) * 512], onb[os_][:, :], reads=[ph.b("on", os_)])
                    if "dbg_rec" in dr and name == "p9" and jb is jobs[0] and g == 0:
                        ph.dma("sp", dr["dbg_rec"][:, :], rec[:, :], reads=[ph.b("rec")])
                        dnum = sb(st, nc, "p9_dnum", [128, 512], F32)
                        ph.add("dve", (lambda e, ob_=ob_: e.tensor_copy(dnum[:, :], oacc[ob_][:, :])), reads=[ph.b("oacc", ob_)], writes=[ph.b("dnum")])
                        ph.dma("sp", dr["dbg_num"][:, :], dnum[:, :], reads=[ph.b("dnum")])
        ph.emit()


def phase_win_attn(P):
    nc = P.nc
    dr = P.dram
    name = "p3"
    scale = 64 ** -0.5
    with contextlib.ExitStack() as st:
        ph = Phase(nc, name)
        cst, idb = load_consts(ph, nc, st, P)
        mk = sb(st, nc, "p3_mk", [128, 2, 128], BF16)
        ph.add("dve", lambda e: e.tensor_copy(mk[:, 0, :], cst[:, C_MPREV:C_MPREV + 128]), reads=[ph.b("cst")], writes=[ph.b("mk")])
        ph.add("dve", lambda e: e.tensor_copy(mk[:, 1, :], cst[:, C_MNEXT:C_MNEXT + 128]), reads=[ph.b("cst")], writes=[ph.b("mk")])
        sk = sb(st, nc, "p3_sk", [128, 8], F32)
        es = sb(st, nc, "p3_es", [128, 8], F32)
        ph.dma("sp", sk[:, :], bc(dr["ab_sink"][0, :]), writes=[ph.b("sk")])
        ph.add("act", lambda e: e.activation(es[:, :], sk[:, :], AF.Exp), reads=[ph.b("sk")], writes=[ph.b("es")])
        qs = [sb(st, nc, f"p3_q{i}", [128, S], BF16) for i in range(2)]
        ks = [sb(st, nc, f"p3_k{i}", [128, S], BF16) for i in range(2)]
        vs = [sb(st, nc, f"p3_v{i}", [128, NT, 128], BF16) for i in range(2)]
        pTs = [sb(st, nc, f"p3_pT{i}", [128, 384], BF16) for i in range(4)]
        den = sb(st, nc, "p3_den", [64, 512], F32)
        rec = sb(st, nc, "p3_rec", [64, 512], F32)
        onb = [sb(st, nc, f"p3_on{i}", [64, 512], BF16) for i in range(2)]
        sT = [ps(st, nc, f"p3_sT{i}", [128, 512], F32) for i in range(3)]
        oacc = [ps(st, nc, f"p3_oa{i}", [128, 512], F32) for i in range(2)]
        cnt = 0
        gcnt = 0
        for h in range(8):
            j, half, kv = h // 2, h % 2, h // 4
            pb = half * 64
            if half == 0:
                ph.dma("sp", qs[j % 2][:, :], dr["qT_B"][j, :, :], writes=[ph.b("q", j % 2)])
            if h % 4 == 0:
                ph.dma("sp", ks[kv][:, :], dr["kT_B"][kv, :, :], writes=[ph.b("k", kv)])
                ph.dma("sp", vs[kv][:, :, :].rearrange("p c f -> p (c f)"), dr["v_all"][2 + kv, :, :, :].rearrange("p c f -> p (c f)"), writes=[ph.b("v", kv)])
                if "dbgv0" in dr and kv == 0:
                    ph.dma("sp", dr["dbgv0"][:, :, :].rearrange("p c f -> p (c f)"), vs[0][:, :, :].rearrange("p c f -> p (c f)"), reads=[ph.b("v", 0)])
            q_t, k_t, v_t = qs[j % 2], ks[kv], vs[kv]
            qB, kB, vB = ph.b("q", j % 2), ph.b("k", kv), ph.b("v", kv)
            for g in range(16):
                ob_ = gcnt % 2
                gcnt += 1
                for ti in range(4):
                    i = g * 4 + ti
                    b = cnt % 3
                    pslot = cnt % 4
                    cnt += 1
                    chunks = [c for c in (i - 1, i, i + 1) if 0 <= c < NT]
                    for c in chunks:
                        sl = c - i + 1
                        ph.add("pe", (lambda e, b=b, sl=sl, c=c, i=i: e.matmul(sT[b][:, sl * 128:(sl + 1) * 128],
                                                                                k_t[pb:pb + 64, c * 128:(c + 1) * 128],
                                                                                q_t[pb:pb + 64, i * 128:(i + 1) * 128], start=True, stop=True)),
                               reads=[qB, kB], writes=[ph.b("sT", b)])
                    lo_, hi_ = (chunks[0] - i + 1) * 128, (chunks[-1] - i + 2) * 128
                    ph.add("act", (lambda e, b=b, pslot=pslot, lo_=lo_, hi_=hi_: e.activation(pTs[pslot][:, lo_:hi_], sT[b][:, lo_:hi_], AF.Exp, scale=scale)),
                           reads=[ph.b("sT", b)], writes=[ph.b("pT", pslot)])
                    for c in chunks:
                        sl = c - i + 1
                        if sl != 1:
                            m = 0 if sl == 0 else 1
                            ph.add("dve", (lambda e, pslot=pslot, sl=sl, m=m: e.tensor_tensor(pTs[pslot][:, sl * 128:(sl + 1) * 128],
                                                                                                pTs[pslot][:, sl * 128:(sl + 1) * 128], mk[:, m, :], ALU.mult)),
                                   reads=[ph.b("mk")], writes=[ph.b("pT", pslot)])
                    for ci, c in enumerate(chunks):
                        sl = c - i + 1
                        ph.add("pe", (lambda e, ob_=ob_, ti=ti, c=c, sl=sl, pslot=pslot, ci=ci, nch=len(chunks): e.matmul(
                            oacc[ob_][:, ti * 128:(ti + 1) * 128], v_t[:, c, :], pTs[pslot][:, sl * 128:(sl + 1) * 128],
                            start=(ci == 0), stop=(ci == nch - 1))),
                            reads=[vB, ph.b("pT", pslot)], writes=[ph.b("oacc", ob_)])
                os_ = gcnt % 2
                ph.add("dve", (lambda e, ob_=ob_, h=h: e.tensor_scalar(den[:, :], oacc[ob_][64:128, :], es[64:128, h:h + 1], None, ALU.add)),
                       reads=[ph.b("oacc", ob_), ph.b("es")], writes=[ph.b("den")])
                ph.add("dve", lambda e: e.reciprocal(rec[:, :], den[:, :]), reads=[ph.b("den")], writes=[ph.b("rec")])
                ph.add("dve", (lambda e, ob_=ob_, os_=os_: e.tensor_tensor(onb[os_][:, :], oacc[ob_][0:64, :], rec[:, :], ALU.mult)),
                       reads=[ph.b("oacc", ob_), ph.b("rec")], writes=[ph.b("on", os_)])
                ph.dma("sp", dr["oT"][512 + h * 64:512 + (h + 1) * 64, g * 512:(g + 1) * 512], onb[os_][:, :], reads=[ph.b("on", os_)])
            if "dbgv1" in dr and h == 3:
                ph.dma("sp", dr["dbgv1"][:, :, :].rearrange("p c f -> p (c f)"), vs[0][:, :, :].rearrange("p c f -> p (c f)"), reads=[ph.b("v", 0)])
        ph.emit()


def emit_ln(ph, nc, name, r, rB, out_ap, outB, g_b, b_b, junk, small, eps_t, eng2="pool"):
    sm, nm, ssq, sd, rstd = (small[:, k:k + 1] for k in range(5))
    ph.add("dve", lambda e: e.tensor_reduce(sm, r, AX.X, ALU.add), reads=[rB], writes=[ph.b(name, "sm")])
    ph.add("dve", lambda e: e.tensor_scalar(nm, sm, -1.0 / D, None, ALU.mult), reads=[ph.b(name, "sm")], writes=[ph.b(name, "nm")])
    ph.add("act", lambda e: e.activation(junk, r, AF.Square, bias=nm, scale=1.0, accum_out=ssq),
           reads=[rB, ph.b(name, "nm")], writes=[ph.b(name, "ssq"), ph.b(name, "junk")])
    ph.add("act", lambda e: e.activation(sd, ssq, AF.Sqrt, bias=eps_t, scale=1.0 / D), reads=[ph.b(name, "ssq")], writes=[ph.b(name, "sd")])
    ph.add("dve", lambda e: e.reciprocal(rstd, sd), reads=[ph.b(name, "sd")], writes=[ph.b(name, "rstd")])
    ph.add("dve", lambda e: e.tensor_scalar(r, r, nm, rstd, ALU.add, ALU.mult), reads=[ph.b(name, "nm"), ph.b(name, "rstd")], writes=[rB])
    ph.add(eng2, lambda e: e.tensor_tensor(r, r, g_b, ALU.mult), reads=[ph.b("lng")], writes=[rB])
    ph.add(eng2, lambda e: e.tensor_tensor(out_ap, r, b_b, ALU.add), reads=[rB, ph.b("lnb")], writes=[outB])


def phase_outproj(P, name, w_out, xin_rows, lng, lnb, w_router, xext, acc, affd):
    nc = P.nc
    dr = P.dram
    with contextlib.ExitStack() as st:
        ph = Phase(nc, name)
        cst, idb = load_consts(ph, nc, st, P)
        stage = [sb(st, nc, f"{name}_stg{i}", [128, 1024], F32) for i in range(2)]
        wo = load_weight_bf16(ph, nc, st, f"{name}_wo", w_out, 8, 1024, stage, [ph.b("stg", i) for i in range(2)])
        wr = sb(st, nc, f"{name}_wr", [128, 8, NE], F32)
        for hh in range(2):
            ph.dma("sp", wr[:, hh * 4:(hh + 1) * 4, :], w_router.rearrange("(c p) n -> p c n", p=128)[:, hh * 4:(hh + 1) * 4, :], writes=[ph.b("wr")])
        g_b = sb(st, nc, f"{name}_g", [128, D], F32)
        b_b = sb(st, nc, f"{name}_b", [128, D], F32)
        ph.dma("sp", g_b[:, :], bc(lng), writes=[ph.b("lng")])
        ph.dma("sp", b_b[:, :], bc(lnb), writes=[ph.b("lnb")])
        eps_t = sb(st, nc, f"{name}_eps", [128, 1], F32)
        ph.add("pool", lambda e: e.memset(eps_t[:, :], 1e-5), writes=[ph.b("eps")])
        oTs = [sb(st, nc, f"{name}_oT{i}", [128, 8, 512], BF16) for i in range(2)]
        xs = [sb(st, nc, f"{name}_xs{i}", [128, D], F32) for i in range(4)]
        r = [sb(st, nc, f"{name}_r{i}", [128, D], F32) for i in range(4)]
        xo = [sb(st, nc, f"{name}_xo{i}", [128, XW], F32) for i in range(2)]
        ax = [sb(st, nc, f"{name}_ax{i}", [128, D], F32) for i in range(2)]
        junk2 = [sb(st, nc, f"{name}_junk{i}", [128, D], BF16) for i in range(2)]
        small2 = [sb(st, nc, f"{name}_small{i}", [128, 8], F32) for i in range(2)]
        xnT2 = [sb(st, nc, f"{name}_xnT{i}", [128, 8, 128], F32) for i in range(2)]
        lg2 = [sb(st, nc, f"{name}_lg{i}", [128, 4], F32) for i in range(2)]
        ex2 = [sb(st, nc, f"{name}_ex{i}", [128, NE], F32) for i in range(2)]
        py = [ps(st, nc, f"{name}_py{i}", [128, 512], F32) for i in range(2)]
        pT2 = [ps(st, nc, f"{name}_pT{i}", [128, 512], F32) for i in range(2)]
        pl2 = [ps(st, nc, f"{name}_pl{i}", [128, 512], F32) for i in range(2)]
        oTv = dr["oT"][:, :].rearrange("(c p) t -> p c t", p=128)
        def stageA(i):
            s = i % 4
            g4, t4 = divmod(i, 4)
            gs = g4 % 2
            tok = slice(i * 128, (i + 1) * 128)
            B = lambda n: ph.b(n, s)
            if t4 == 0:
                for hh in range(2):
                    ph.dma("sp", oTs[gs][:, hh * 4:(hh + 1) * 4, :], oTv[:, hh * 4:(hh + 1) * 4, g4 * 512:(g4 + 1) * 512], writes=[ph.b("oTs", gs)])
            ph.dma("sp", xs[s][:, :], xin_rows(tok), writes=[B("xs")])
            for n in range(2):
                for c in range(8):
                    ph.add("pe", (lambda e, n=n, c=c, gs=gs, t4=t4: e.matmul(py[n][:, :], oTs[gs][:, c, t4 * 128:(t4 + 1) * 128],
                                                                             wo[:, c, n * 512:(n + 1) * 512], start=(c == 0), stop=(c == 7))),
                           reads=[ph.b("oTs", gs), ph.b(f"{name}_wo")], writes=[ph.b("py", n)])
                ph.add("dve", (lambda e, n=n, s=s: e.scalar_tensor_tensor(r[s][:, n * 512:(n + 1) * 512], xs[s][:, n * 512:(n + 1) * 512], ALPHA,
                                                                          py[n][:, :], ALU.mult, ALU.add)),
                       reads=[B("xs"), ph.b("py", n)], writes=[B("r")])

        def stageB(i):
            s = i % 2
            tok = slice(i * 128, (i + 1) * 128)
            B = lambda n: ph.b(n, s)
            junk, small, xnT, lg, ex, pTs_, pl = junk2[s], small2[s], xnT2[s], lg2[s], ex2[s], pT2[s], pl2[s]
            rr = r[i % 4][:, :]
            rB = ph.b("r", i % 4)
            sm, nm, ssq, sd, rstd = (small[:, k:k + 1] for k in range(5))
            ph.add("dve", lambda e: e.tensor_reduce(sm, rr, AX.X, ALU.add), reads=[rB], writes=[B("sm")]); yield
            ph.add("dve", lambda e: e.tensor_scalar(nm, sm, -1.0 / D, None, ALU.mult), reads=[B("sm")], writes=[B("nm")]); yield
            ph.add("act", lambda e: e.activation(junk[:, :], rr, AF.Square, bias=nm, scale=1.0, accum_out=ssq),
                   reads=[rB, B("nm")], writes=[B("ssq"), B("junk")]); yield
            ph.add("act", lambda e: e.activation(sd, ssq, AF.Sqrt, bias=eps_t[:, :], scale=1.0 / D), reads=[B("ssq")], writes=[B("sd")]); yield
            ph.add("dve", lambda e: e.reciprocal(rstd, sd), reads=[B("sd")], writes=[B("rstd")]); yield
            ph.add("dve", lambda e: e.tensor_scalar(rr, rr, nm, rstd, ALU.add, ALU.mult), reads=[B("nm"), B("rstd")], writes=[rB]); yield
            ph.add("pool", lambda e: e.tensor_tensor(rr, rr, g_b[:, :], ALU.mult), reads=[ph.b("lng")], writes=[rB]); yield
            ph.add("pool", lambda e: e.tensor_tensor(xo[s][:, 0:D], rr, b_b[:, :], ALU.add), reads=[rB, ph.b("lnb")], writes=[B("xo")]); yield
            ph.add("act", lambda e: e.mul(ax[s][:, :], xo[s][:, 0:D], ALPHA), reads=[B("xo")], writes=[B("ax")]); yield
            ph.dma("sp", acc[tok, :], ax[s][:, :], reads=[B("ax")]); yield
            for g in range(2):
                for c in range(4):
                    cc = g * 4 + c
                    ph.add("pe", (lambda e, c=c, cc=cc: e.transpose(pTs_[:, c * 128:(c + 1) * 128], xo[s][:, cc * 128:(cc + 1) * 128],
                                                                    cst[:, C_IDENT:C_IDENT + 128])),
                           reads=[B("xo"), ph.b("cst")], writes=[B("pT")])
                yield
                ph.add("act", (lambda e, g=g: e.copy(xnT[:, g * 4:(g + 1) * 4, :], pTs_[:, :].rearrange("p (c t) -> p c t", c=4))),
                       reads=[B("pT")], writes=[B("xnT")]); yield
            for c in range(8):
                ph.add("pe", (lambda e, c=c: e.matmul(pl[:, 0:NE], xnT[:, c, :], wr[:, c, :], start=(c == 0), stop=(c == 7))),
                       reads=[B("xnT"), ph.b("wr")], writes=[B("pl")])
            yield
            ph.add("dve", lambda e: e.tensor_reduce(lg[:, 0:1], pl[:, 0:NE], AX.X, ALU.max), reads=[B("pl")], writes=[B("lg0")]); yield
            ph.add("dve", lambda e: e.tensor_scalar(lg[:, 1:2], lg[:, 0:1], -1.0, None, ALU.mult), reads=[B("lg0")], writes=[B("lg1")]); yield
            ph.add("act", lambda e: e.activation(ex[:, :], pl[:, 0:NE], AF.Exp, bias=lg[:, 1:2], scale=1.0, accum_out=lg[:, 2:3]),
                   reads=[B("pl"), B("lg1")], writes=[B("ex"), B("lg2")]); yield
            ph.add("dve", lambda e: e.reciprocal(lg[:, 3:4], lg[:, 2:3]), reads=[B("lg2")], writes=[B("lg3")]); yield
            ph.add("dve", lambda e: e.tensor_scalar(xo[s][:, D:XW], ex[:, :], lg[:, 3:4], None, ALU.mult),
                   reads=[B("ex"), B("lg3")], writes=[B("xo")]); yield
            ph.dma("sp", xext[tok, :], xo[s][:, :], reads=[B("xo")]); yield
            ph.dma("sp", affd[tok, :], xo[s][:, D:XW], reads=[B("xo")]); yield

        def drive(gens):
            gens = list(gens)
            while gens:
                for g_ in list(gens):
                    try:
                        next(g_)
                    except StopIteration:
                        gens.remove(g_)

        stageA(0)
        stageA(1)
        for i in range(0, NT, 2):
            if i + 2 < NT:
                stageA(i + 2)
                stageA(i + 3)
            drive([stageB(i), stageB(i + 1)])
        ph.emit()


def phase_route(P, name, xext, idx):
    nc = P.nc
    dr = P.dram
    with contextlib.ExitStack() as st:
        ph = Phase(nc, name)
        cst, idb = load_consts(ph, nc, st, P)
        aff = sb(st, nc, f"{name}_aff", [128, 64, NE], F32)
        cmp_ = sb(st, nc, f"{name}_cmp", [128, 64, NE], F32)
        mT = sb(st, nc, f"{name}_mT", [128, NE, 64], F32)
        Cl = sb(st, nc, f"{name}_Cl", [128, NE, 64], F32)
        Cg = sb(st, nc, f"{name}_Cg", [128, NE, 64], F32)
        lo = sb(st, nc, f"{name}_lo", [128, NE], F32)
        mid = sb(st, nc, f"{name}_mid", [128, NE], F32)
        sel = sb(st, nc, f"{name}_sel", [128, NE], F32)
        cntp = sb(st, nc, f"{name}_cntp", [128, NE], BF16)
        ones = sb(st, nc, f"{name}_ones", [128, 128], BF16)
        trib = sb(st, nc, f"{name}_trib", [128, 128], BF16)
        idxf = sb(st, nc, f"{name}_idxf", [128, NE, 8], F32)
        junk = sb(st, nc, f"{name}_junk", [128, S], BF16)
        crow = [sb(st, nc, f"{name}_crow{i}", [128, S], F32) for i in range(2)]
        tot = ps(st, nc, f"{name}_tot", [128, 512], F32)
        off = ps(st, nc, f"{name}_off", [128, 512], F32)
        ph.dma("sp", aff[:, :, :].rearrange("p i e -> p (i e)"), dr["affd"][:, :].rearrange("(p i) w -> p (i w)", p=128), writes=[ph.b("aff")])
        ph.add("dve", lambda e: e.memset(ones[:, :], 1.0), writes=[ph.b("ones")])
        ph.add("dve", lambda e: e.tensor_copy(trib[:, :], cst[:, C_TRI:C_TRI + 128]), reads=[ph.b("cst")], writes=[ph.b("trib")])
        ph.add("dve", lambda e: e.memset(lo[:, :], 0.0), writes=[ph.b("lo")])
        for k in range(30):
            w = 0.5 ** (k + 1)
            ph.add("dve", (lambda e, w=w: e.tensor_scalar(mid[:, :], lo[:, :], w, None, ALU.add)), reads=[ph.b("lo")], writes=[ph.b("mid")])
            ph.add("dve", lambda e: e.tensor_tensor(cmp_[:, :, :], aff[:, :, :], mid[:, :].unsqueeze(1).to_broadcast([128, 64, NE]), ALU.is_ge),
                   reads=[ph.b("aff"), ph.b("mid")], writes=[ph.b("cmp")])
            ph.add("dve", lowp(nc, lambda e: e.tensor_reduce(cntp[:, :], cmp_[:, :, :].rearrange("p i e -> p e i"), AX.X, ALU.add)),
                   reads=[ph.b("cmp")], writes=[ph.b("cntp")])
            ph.add("pe", lambda e: e.matmul(tot[:, 0:NE], ones[:, :], cntp[:, :], start=True, stop=True),
                   reads=[ph.b("ones"), ph.b("cntp")], writes=[ph.b("tot")])
            ph.add("dve", lambda e: e.tensor_scalar(sel[:, :], tot[:, 0:NE], CAP - 0.5, None, ALU.is_ge), reads=[ph.b("tot")], writes=[ph.b("sel")])
            ph.add("dve", (lambda e, w=w: e.scalar_tensor_tensor(lo[:, :], sel[:, :], w, lo[:, :], ALU.mult, ALU.add)),
                   reads=[ph.b("sel")], writes=[ph.b("lo")])
        ph.add("dve", lambda e: e.tensor_tensor(mT[:, :, :].rearrange("p e i -> p i e"), aff[:, :, :],
                                                lo[:, :].unsqueeze(1).to_broadcast([128, 64, NE]), ALU.is_ge),
               reads=[ph.b("aff"), ph.b("lo")], writes=[ph.b("mT")])
        ph.add("dve", lambda e: e.tensor_tensor_scan(Cl[:, :, :].rearrange("p e i -> p (e i)"), cst[:, C_KEEP:C_KEEP + NE * 64],
                                                     mT[:, :, :].rearrange("p e i -> p (e i)"), 0.0, ALU.mult, ALU.add),
               reads=[ph.b("mT"), ph.b("cst")], writes=[ph.b("Cl")])
        ph.add("dve", lambda e: e.tensor_copy(cntp[:, :], Cl[:, :, 63]), reads=[ph.b("Cl")], writes=[ph.b("cntp")])
        ph.add("pe", lambda e: e.matmul(off[:, 0:NE], trib[:, :], cntp[:, :], start=True, stop=True),
               reads=[ph.b("trib"), ph.b("cntp")], writes=[ph.b("off")])
        ph.add("dve", lambda e: e.tensor_copy(sel[:, :], off[:, 0:NE]), reads=[ph.b("off")], writes=[ph.b("sel")])
        ph.add("dve", lambda e: e.tensor_tensor(Cg[:, :, :], Cl[:, :, :], sel[:, :].unsqueeze(2).to_broadcast([128, NE, 64]), ALU.add),
               reads=[ph.b("Cl"), ph.b("sel")], writes=[ph.b("Cg")])
        cd = dr[f"{name}_Cd"]
        for e4 in range(4):
            ph.dma("sp", cd[e4 * 4:(e4 + 1) * 4, :].rearrange("e (p i) -> p e i", p=128), Cg[:, e4 * 4:(e4 + 1) * 4, :], reads=[ph.b("Cg")], writes=[ph.b("Cd", e4)])
        junk2 = sb(st, nc, f"{name}_junk2", [128, S], BF16)
        svh = sb(st, nc, f"{name}_svh", [128, 8], F32)
        ph.add("dve", lambda e: e.tensor_scalar(svh[:, :], cst[:, C_SVEC:C_SVEC + 8], 0.5, None, ALU.add), reads=[ph.b("cst")], writes=[ph.b("svh")])
        for e_ in range(NE):
            s = e_ % 2
            ph.dma("sp", crow[s][:, :], bc(cd[e_, :]), reads=[ph.b("Cd", e_ // 4)], writes=[ph.b("crow", s)])
            for j in range(8):
                if j % 2 == 0:
                    ph.add("dve", (lambda e, s=s, e_=e_, j=j: e.tensor_scalar(junk[:, :], crow[s][:, :], cst[:, C_SVEC + j:C_SVEC + j + 1], None,
                                                                               ALU.is_le, ALU.add, accum_out=idxf[:, e_, j:j + 1])),
                           reads=[ph.b("crow", s), ph.b("cst")], writes=[ph.b("idxf", "dve"), ph.b("junk")])
                else:
                    ph.add("act", (lambda e, s=s, e_=e_, j=j: e.activation(junk2[:, :], crow[s][:, :], AF.Sign, bias=svh[:, j:j + 1], scale=-1.0,
                                                                            accum_out=idxf[:, e_, j:j + 1])),
                           reads=[ph.b("crow", s), ph.b("svh")], writes=[ph.b("idxf", "act"), ph.b("junk2")])
        i4 = idxf[:, :, :].rearrange("p e (j t) -> p e j t", t=2)[:, :, :, 1]
        ph.add("dve", lambda e: e.tensor_scalar(i4, i4, 0.5, float(S // 2), ALU.mult, ALU.add), reads=[ph.b("idxf", "act")], writes=[ph.b("idxf", "act")])
        ph.add("dve", lambda e: e.tensor_copy(idx[:, :, :], idxf[:, :, :]), reads=[ph.b("idxf", "act"), ph.b("idxf", "dve")], writes=[ph.b("idx")])
        if "dbg_idx" in dr and name == "p5":
            ph.dma("sp", dr["dbg_idx"][:, :], idx[:, :, :].rearrange("p e j -> p (e j)"), reads=[ph.b("idx")])
            ph.dma("sp", dr["dbg_lo"][:, :], lo[:, :], reads=[ph.b("lo")])
        ph.emit()


def phase_experts(P, name, l, xext, acc, idx):
    nc = P.nc
    dr = P.dram
    with contextlib.ExitStack() as st:
        ph = Phase(nc, name)
        cst, idb = load_consts(ph, nc, st, P)
        NST = 3
        stage = [sb(st, nc, f"{name}_stg{i}", [128, 2048], F32) for i in range(NST)]
        wts = {k: [sb(st, nc, f"{name}_{k}{i}", [128, 8, 1024], BF16) for i in range(2)] for k in ("wg", "wu", "wd")}
        xg = [sb(st, nc, f"{name}_xg{i}", [128, XW], F32) for i in range(2)]
        xgT2 = [sb(st, nc, f"{name}_xgT{i}", [128, 8, 1024], BF16) for i in range(2)]
        hT = sb(st, nc, f"{name}_hT", [128, 8, 1024], BF16)
        gt = sb(st, nc, f"{name}_gt", [128, 2, 8], F32)
        sg = [sb(st, nc, f"{name}_sg{i}", [128, 512], F32) for i in range(2)]
        yo = [sb(st, nc, f"{name}_yo{i}", [128, D], F32) for i in range(4)]
        pT = [ps(st, nc, f"{name}_pT{i}", [128, 512], F32) for i in range(2)]
        pg = [ps(st, nc, f"{name}_pg{i}", [128, 512], F32) for i in range(2)]
        pu = [ps(st, nc, f"{name}_pu{i}", [128, 512], F32) for i in range(2)]
        py = [ps(st, nc, f"{name}_py{i}", [128, 512], F32) for i in range(2)]
        srcs = {"wg": dr["moe_w_gate"], "wu": dr["moe_w_up"], "wd": dr["moe_w_down"]}
        stg_cnt = [0]
        cast_engs = ("pool", "dve", "pool", "act")

        def load_w_piece(e_, k, c2):
            sl = e_ % 2
            if True:
                wv = srcs[k][l, e_, :, :].rearrange("(c p) n -> p c n", p=128)
                if True:
                    s = stg_cnt[0] % NST
                    eng = cast_engs[stg_cnt[0] % len(cast_engs)]
                    stg_cnt[0] += 1
                    ph.dma("sp", stage[s][:, :].rearrange("p (c n) -> p c n", c=2), wv[:, 2 * c2:2 * c2 + 2, :], writes=[ph.b("stg", s)])
                    dst = wts[k][sl][:, 2 * c2:2 * c2 + 2, :]
                    srcv = stage[s][:, :].rearrange("p (c n) -> p c n", c=2)
                    if eng == "act":
                        ph.add("act", (lambda e, dst=dst, srcv=srcv: e.copy(dst, srcv)), reads=[ph.b("stg", s)], writes=[ph.b(k, sl)])
                    else:
                        ph.add(eng, (lambda e, dst=dst, srcv=srcv: e.tensor_copy(dst, srcv)), reads=[ph.b("stg", s)], writes=[ph.b(k, sl)])

        PIECES = [(k, c2) for k in ("wg", "wu", "wd") for c2 in range(4)]

        def load_w(e_):
            for (k, c2) in PIECES:
                load_w_piece(e_, k, c2)

        load_w(0)
        prev_sc = []

        def emit_gather(e_, j):
            s = (e_ * 8 + j) % 2
            ph.add("pool", (lambda e, s=s, e_=e_, j=j: e.indirect_dma_start(
                out=xg[s][:, :], out_offset=None, in_=xext[:, :],
                in_offset=bass.IndirectOffsetOnAxis(ap=idx[:, e_, j:j + 1], axis=0))),
                reads=[ph.b("idx")], writes=[ph.b("xg", s)], dma=True)

        def emit_trans(e_, j):
            s = (e_ * 8 + j) % 2
            xs_ = e_ % 2
            emit_xT(ph, xg[s], ph.b("xg", s), xgT2[xs_][:, :, j * 128:(j + 1) * 128], ph.b("xgT", xs_), pT, [ph.b("pT", 0), ph.b("pT", 1)], cst)
            ph.add("act", (lambda e, s=s, e_=e_, j=j, xs_=xs_: e.copy(gt[:, xs_, j:j + 1], xg[s][:, D + e_:D + e_ + 1])),
                   reads=[ph.b("xg", s)], writes=[ph.b("gt", xs_)])

        emit_gather(0, 0)
        for j in range(8):
            if j + 1 < 8:
                emit_gather(0, j + 1)
            emit_trans(0, j)
        for e_ in range(NE):
            sl = e_ % 2
            wg, wu, wd = wts["wg"][sl], wts["wu"][sl], wts["wd"][sl]
            gsl = e_ % 2
            xgT = xgT2[sl]
            xgB = ph.b("xgT", sl)
            if e_ + 1 < NE:
                emit_gather(e_ + 1, 0)
            n = 0
            for fc in range(8):
                if e_ + 1 < NE and fc + 1 < 8:
                    emit_gather(e_ + 1, fc + 1)
                if e_ + 1 < NE:
                    for pi in range((fc * 12) // 8, ((fc + 1) * 12) // 8):
                        load_w_piece(e_ + 1, *PIECES[pi])
                for half in range(2):
                    b = n % 2
                    n += 1
                    for (wt, pt_, nm, kn) in ((wg, pg, "pg", "wg"), (wu, pu, "pu", "wu")):
                        for c in range(8):
                            ph.add("pe", (lambda e, wt=wt, pt_=pt_, b=b, c=c, fc=fc, half=half, xgT=xgT: e.matmul(
                                pt_[b][:, :], wt[:, c, fc * 128:(fc + 1) * 128], xgT[:, c, half * 512:(half + 1) * 512],
                                start=(c == 0), stop=(c == 7))),
                                reads=[ph.b(kn, sl), xgB], writes=[ph.b(nm, b)])
                    ph.add("act", (lambda e, b=b: e.activation(sg[b][:, :], pg[b][:, :], AF.Silu)), reads=[ph.b("pg", b)], writes=[ph.b("sg", b)])
                    ph.add("dve", (lambda e, b=b, fc=fc, half=half: e.tensor_tensor(hT[:, fc, half * 512:(half + 1) * 512], sg[b][:, :], pu[b][:, :], ALU.mult)),
                           reads=[ph.b("sg", b), ph.b("pu", b)], writes=[ph.b("hT")])
                if e_ + 1 < NE:
                    emit_trans(e_ + 1, fc)
            new_sc = []
            for j in range(8):
                ys = (e_ * 8 + j) % 4
                for half in range(2):
                    for fc in range(8):
                        ph.add("pe", (lambda e, half=half, fc=fc, j=j, wd=wd: e.matmul(py[half][:, :], hT[:, fc, j * 128:(j + 1) * 128],
                                                                                wd[:, fc, half * 512:(half + 1) * 512], start=(fc == 0), stop=(fc == 7))),
                               reads=[ph.b("hT"), ph.b("wd", sl)], writes=[ph.b("py", half)])
                    eng = "act" if half == 0 else "dve"
                    if eng == "act":
                        ph.add("act", (lambda e, ys=ys, half=half, gsl=gsl, j=j: e.mul(yo[ys][:, half * 512:(half + 1) * 512], py[half][:, :], gt[:, gsl, j:j + 1])),
                               reads=[ph.b("py", half), ph.b("gt", gsl)], writes=[ph.b("yo", ys)])
                    else:
                        ph.add("dve", (lambda e, ys=ys, half=half, gsl=gsl, j=j: e.tensor_scalar(yo[ys][:, half * 512:(half + 1) * 512], py[half][:, :],
                                                                                                  gt[:, gsl, j:j + 1], None, ALU.mult)),
                               reads=[ph.b("py", half), ph.b("gt", gsl)], writes=[ph.b("yo", ys)])
                if "dbg_xg" in dr and name == "p6" and e_ == 0 and j == 0:
                    ph.dma("sp", dr["dbg_yo"][:, :], yo[ys][:, :], reads=[ph.b("yo", ys)])
                    ph.dma("sp", dr["dbg_hT"][:, :], hT[:, :, :].rearrange("p c t -> p (c t)"), reads=[ph.b("hT")])
                op = ph.add("pool", (lambda e, ys=ys, e_=e_, j=j: e.indirect_dma_start(
                    out=acc[:, :], out_offset=bass.IndirectOffsetOnAxis(ap=idx[:, e_, j:j + 1], axis=0),
                    in_=yo[ys][:, :], in_offset=None, compute_op=ALU.add)),
                    reads=[ph.b("idx"), ph.b("yo", ys)], writes=[], dma=True)
                for d in prev_sc:
                    if d not in op.deps:
                        op.deps.append(d)
                new_sc.append(op)
            prev_sc = new_sc
        ph.emit()


def phase_ln(P, name, acc, lng, lnb, out_rows):
    nc = P.nc
    with contextlib.ExitStack() as st:
        ph = Phase(nc, name)
        g_b = sb(st, nc, f"{name}_g", [128, D], F32)
        b_b = sb(st, nc, f"{name}_b", [128, D], F32)
        ph.dma("sp", g_b[:, :], bc(lng), writes=[ph.b("lng")])
        ph.dma("sp", b_b[:, :], bc(lnb), writes=[ph.b("lnb")])
        eps_t = sb(st, nc, f"{name}_eps", [128, 1], F32)
        ph.add("pool", lambda e: e.memset(eps_t[:, :], 1e-5), writes=[ph.b("eps")])
        r = [sb(st, nc, f"{name}_r{i}", [128, D], F32) for i in range(3)]
        o = [sb(st, nc, f"{name}_o{i}", [128, D], F32) for i in range(3)]
        junk = sb(st, nc, f"{name}_junk", [128, D], BF16)
        small = sb(st, nc, f"{name}_small", [128, 8], F32)
        for i in range(NT + 1):
            if i < NT:
                s = i % 3
                tok = slice(i * 128, (i + 1) * 128)
                ph.dma("sp", r[s][:, :], acc[tok, :], writes=[ph.b("r", s)])
            if i >= 1:
                s = (i - 1) % 3
                tok = slice((i - 1) * 128, i * 128)
                emit_ln(ph, nc, name, r[s][:, :], ph.b("r", s), o[s][:, :], ph.b("o", s), g_b[:, :], b_b[:, :], junk[:, :], small[:, :], eps_t[:, :])
                ph.dma("sp", out_rows(tok), o[s][:, :], reads=[ph.b("o", s)])
        ph.emit()


def phase_proj_mla(P, xin_rows):
    nc = P.nc
    dr = P.dram
    name = "p8"
    with contextlib.ExitStack() as st:
        ph = Phase(nc, name)
        cst, idb = load_consts(ph, nc, st, P)
        stage = [sb(st, nc, f"p8_stg{i}", [128, 2048], F32) for i in range(2)]
        sB = [ph.b("stg", i) for i in range(2)]
        wdn = load_weight_bf16(ph, nc, st, "p8_wdn", dr["mla_w_down"][0, :, :], 8, 416, stage, sB)
        wuq = load_weight_bf16(ph, nc, st, "p8_wuq", dr["mla_w_uq"][0, :, :], 2, 1536, stage, sB)
        wukv = load_weight_bf16(ph, nc, st, "p8_wukv", dr["mla_w_ukv"][0, :, :], 1, 2048, stage, sB)
        gq = sb(st, nc, "p8_gq", [128, 256], F32)
        gk = sb(st, nc, "p8_gk", [128, 128], F32)
        ph.dma("sp", gq[:, :], bc(dr["mla_q_norm"][0, :]), writes=[ph.b("gq")])
        ph.dma("sp", gk[:, :], bc(dr["mla_kv_norm"][0, :]), writes=[ph.b("gk")])
        eps = sb(st, nc, "p8_eps", [128, 1], F32)
        ph.add("pool", lambda e: e.memset(eps[:, :], 1e-6), writes=[ph.b("eps")])
        xs = [sb(st, nc, f"p8_xs{i}", [128, D], F32) for i in range(2)]
        tb = [sb(st, nc, f"p8_tb{i}", [128, 64], F32) for i in range(4)]
        xT = [sb(st, nc, f"p8_xT{i}", [128, 8, 128], BF16) for i in range(2)]
        pr2 = [sb(st, nc, f"p8_pr{i}", [128, 416], F32) for i in range(4)]
        junk_2 = [sb(st, nc, f"p8_junk{i}", [128, 256], BF16) for i in range(2)]
        sm_2 = [sb(st, nc, f"p8_sm{i}", [128, 8], F32) for i in range(2)]
        cn_2 = [sb(st, nc, f"p8_cn{i}", [128, 384], BF16) for i in range(2)]
        cnT_2 = [sb(st, nc, f"p8_cnT{i}", [128, 3, 128], BF16) for i in range(2)]
        qsb_2 = [sb(st, nc, f"p8_qs{i}", [128, 1536], F32) for i in range(2)]
        kvs_2 = [sb(st, nc, f"p8_kvs{i}", [128, 2048], F32) for i in range(2)]
        qb_2 = [sb(st, nc, f"p8_qb{i}", [128, 16, 96], BF16) for i in range(2)]
        kb_2 = [sb(st, nc, f"p8_kb{i}", [128, 16, 96], BF16) for i in range(2)]
        kr_2 = [sb(st, nc, f"p8_kr{i}", [128, 32], BF16) for i in range(2)]
        t1_2 = [sb(st, nc, f"p8_t1{i}", [128, 512], F32) for i in range(2)]
        t2_2 = [sb(st, nc, f"p8_t2{i}", [128, 512], F32) for i in range(2)]
        t3_2 = [sb(st, nc, f"p8_t3{i}", [128, 32], F32) for i in range(2)]
        t4_2 = [sb(st, nc, f"p8_t4{i}", [128, 32], F32) for i in range(2)]
        stq = [sb(st, nc, f"p8_stq{i}", [128, 16, 128], BF16) for i in range(2)]
        stk = [sb(st, nc, f"p8_stk{i}", [128, 16, 128], BF16) for i in range(2)]
        vst = [sb(st, nc, f"p8_vst{i}", [128, 16, 128], BF16) for i in range(2)]
        pT = [ps(st, nc, f"p8_pT{i}", [128, 512], F32) for i in range(2)]
        pp = ps(st, nc, "p8_pp", [128, 512], F32)
        pc = ps(st, nc, "p8_pc", [128, 1024], BF16)
        pb_ = [ps(st, nc, f"p8_pb{i}", [128, 512], F32) for i in range(4)]
        for s in range(2):
            ph.add("pool", (lambda e, s=s: e.memset(vst[s][:, :, :], 1.0)), writes=[ph.b("vst", s)])
        def stageA(i):
            s = i % 2
            tok = slice(i * 128, (i + 1) * 128)
            B = lambda n: ph.b(n, s)
            ph.dma("sp", xs[s][:, :], xin_rows(tok), writes=[B("xs")])
            ph.dma("sp", tb[i % 4][:, :], dr["tab"][tok, 256:320], writes=[ph.b("tb", i % 4)])
            emit_xT(ph, xs[s], B("xs"), xT[s], B("xT"), pT, [ph.b("pT", 0), ph.b("pT", 1)], cst)
            for c in range(8):
                ph.add("pe", (lambda e, c=c, s=s: e.matmul(pp[:, 0:416], xT[s][:, c, :], wdn[:, c, :], start=(c == 0), stop=(c == 7))),
                       reads=[B("xT"), ph.b("p8_wdn")], writes=[ph.b("pp")])
            ph.add("act", (lambda e, s4=i % 4: e.copy(pr2[s4][:, :], pp[:, 0:416])), reads=[ph.b("pp")], writes=[ph.b("pr", i % 4)])

        def stageB(i):
            s = i % 2
            tok = slice(i * 128, (i + 1) * 128)
            B = lambda n: pbuf(n, s)
            pr = pr2[i % 4]
            prB = ph.b("pr", i % 4)
            tbt, tbB = tb[i % 4], ph.b("tb", i % 4)
            junk, sm, cn, cnT, qsb, kvs, qb, kb, kr, t1, t2, t3, t4 = (x[s] for x in (junk_2, sm_2, cn_2, cnT_2, qsb_2, kvs_2, qb_2, kb_2, kr_2, t1_2, t2_2, t3_2, t4_2))
            PER = {"junk", "sm0", "sm1", "sm2", "sm3", "sm4", "cn", "cnT", "qs", "kvs", "qb", "kb", "kr", "t1", "t2", "t3", "t4"}

            def pbuf(*k):
                return ph.b(*k, s) if (len(k) == 1 and k[0] in PER) else ph.b(*k)
            ph.add("act", lambda e: e.activation(junk[:, 0:256], pr[:, 0:256], AF.Square, accum_out=sm[:, 0:1]), reads=[prB], writes=[pbuf("sm0"), pbuf("junk")])
            ph.add("act", lambda e: e.activation(junk[:, 0:128], pr[:, 256:384], AF.Square, accum_out=sm[:, 1:2]), reads=[prB], writes=[pbuf("sm1"), pbuf("junk")])
            ph.add("act", lambda e: e.activation(sm[:, 2:3], sm[:, 0:1], AF.Sqrt, bias=eps[:, :], scale=1.0 / 256), reads=[pbuf("sm0"), pbuf("eps")], writes=[pbuf("sm2")])
            ph.add("act", lambda e: e.activation(sm[:, 3:4], sm[:, 1:2], AF.Sqrt, bias=eps[:, :], scale=1.0 / 128), reads=[pbuf("sm1"), pbuf("eps")], writes=[pbuf("sm3")])
            ph.add("dve", lambda e: e.reciprocal(sm[:, 4:6], sm[:, 2:4]), reads=[pbuf("sm2"), pbuf("sm3")], writes=[pbuf("sm4")])
            ph.add("dve", lambda e: e.scalar_tensor_tensor(cn[:, 0:256], pr[:, 0:256], sm[:, 4:5], gq[:, :], ALU.mult, ALU.mult),
                   reads=[prB, pbuf("sm4"), pbuf("gq")], writes=[pbuf("cn")])
            ph.add("dve", lambda e: e.scalar_tensor_tensor(cn[:, 256:384], pr[:, 256:384], sm[:, 5:6], gk[:, :], ALU.mult, ALU.mult),
                   reads=[prB, pbuf("sm4"), pbuf("gk")], writes=[pbuf("cn")])
            for c in range(3):
                ph.add("pe", (lambda e, c=c: e.transpose(pc[:, c * 128:(c + 1) * 128], cn[:, c * 128:(c + 1) * 128], idb[:, :])),
                       reads=[pbuf("cn"), pbuf("idb")], writes=[pbuf("pc")])
            ph.add("act", lambda e: e.copy(cnT[:, :, :], pc[:, 0:384].rearrange("p (c t) -> p c t", c=3)), reads=[pbuf("pc")], writes=[pbuf("cnT")])
            yield
            for n in range(3):
                for c in range(2):
                    ph.add("pe", (lambda e, n=n, c=c: e.matmul(pb_[n][:, :], cnT[:, c, :], wuq[:, c, n * 512:(n + 1) * 512], start=(c == 0), stop=(c == 1))),
                           reads=[pbuf("cnT"), pbuf("p8_wuq")], writes=[pbuf("pb", n)])
                ph.add("act", (lambda e, n=n: e.copy(qsb[:, n * 512:(n + 1) * 512], pb_[n][:, :])), reads=[pbuf("pb", n)], writes=[pbuf("qs")])
            q3 = qsb[:, :].rearrange("p (h d) -> p h d", h=16)
            yield
            ph.add("pool", lambda e: e.tensor_copy(qb[:, :, 0:64], q3[:, :, 0:64]), reads=[pbuf("qs")], writes=[pbuf("qb")])
            rope3(ph, "dve", qb[:, :, 64:96], q3[:, :, 64:96], tbt[:, 0:32], tbt[:, 32:64], 16, 32, 1, t1[:, :], t2[:, :],
                  [pbuf("qs"), tbB], [pbuf("t1"), pbuf("t2")], pbuf("qb"))
            yield
            for n in range(4):
                bn = (3 + n) % 4
                ph.add("pe", (lambda e, n=n, bn=bn: e.matmul(pb_[bn][:, :], cnT[:, 2, :], wukv[:, 0, n * 512:(n + 1) * 512], start=True, stop=True)),
                       reads=[pbuf("cnT"), pbuf("p8_wukv")], writes=[pbuf("pb", bn)])
                ph.add("act", (lambda e, n=n, bn=bn: e.copy(kvs[:, n * 512:(n + 1) * 512], pb_[bn][:, :])), reads=[pbuf("pb", bn)], writes=[pbuf("kvs")])
            kv3 = kvs[:, :].rearrange("p (h d) -> p h d", h=16)
            yield
            ph.add("pool", lambda e: e.tensor_copy(kb[:, :, 0:64], kv3[:, :, 0:64]), reads=[pbuf("kvs")], writes=[pbuf("kb")])
            ph.add("pool", (lambda e, s=s: e.tensor_copy(vst[s][:, :, 0:64], kv3[:, :, 64:128])), reads=[pbuf("kvs")], writes=[B("vst")])
            for h4 in range(4):
                ph.dma("sp", dr["v_C"][h4 * 4:(h4 + 1) * 4, :, i, :].rearrange("k p f -> p k f"), vst[s][:, h4 * 4:(h4 + 1) * 4, :], reads=[B("vst")])
            yield
            rope3(ph, "dve", kr[:, :].unsqueeze(1), pr[:, 384:416].unsqueeze(1), tbt[:, 0:32], tbt[:, 32:64], 1, 32, 1, t3[:, :], t4[:, :],
                  [prB, tbB], [pbuf("t3"), pbuf("t4")], pbuf("kr"))
            ph.add("dve", lambda e: e.tensor_copy(kb[:, :, 64:96], kr[:, :].unsqueeze(1).to_broadcast([128, 16, 32])),
                   reads=[pbuf("kr")], writes=[pbuf("kb")])
            yield
            for (src, sn, stt, stn, dst) in ((qb, "qb", stq, "stq", dr["qT_C"]), (kb, "kb", stk, "stk", dr["kT_C"])):
                for hh in range(2):
                    bnk = [pb_[0], pb_[1]] if sn == "qb" else [pb_[2], pb_[3]]
                    bnn = [0, 1] if sn == "qb" else [2, 3]
                    ptile = bnk[hh]
                    pv = ptile[:, :].bitcast(BF16)
                    for h8 in range(8):
                        h = hh * 8 + h8
                        ph.add("pe", (lambda e, pv=pv, h8=h8, h=h, src=src: e.transpose(pv[0:96, h8 * 128:(h8 + 1) * 128], src[:, h, :], idb[:, :])),
                               reads=[pbuf(sn), pbuf("idb")], writes=[pbuf("pb", bnn[hh])])
                    ph.add("act", (lambda e, pv=pv, hh=hh, stt=stt, s=s: e.copy(stt[s][0:96, hh * 8:(hh + 1) * 8, :],
                                                                                 pv[0:96, :].rearrange("p (c t) -> p c t", c=8))),
                           reads=[pbuf("pb", bnn[hh])], writes=[B(stn)])
                for h4 in range(4):
                    ph.dma("sp", dst[h4 * 4:(h4 + 1) * 4, :, tok].rearrange("h p t -> p h t"), stt[s][0:96, h4 * 4:(h4 + 1) * 4, :], reads=[B(stn)])
                yield
        def drive(gens):
            gens = list(gens)
            while gens:
                for g_ in list(gens):
                    try:
                        next(g_)
                    except StopIteration:
                        gens.remove(g_)

        stageA(0)
        stageA(1)
        for i in range(0, NT, 2):
            if i + 2 < NT:
                stageA(i + 2)
                stageA(i + 3)
            drive([stageB(i), stageB(i + 1)])
        ph.emit()


PHASES = ["p1", "p2", "p3", "p4", "p5", "p6", "p7", "p8", "p9", "p10", "p11", "p12", "p13"]


def build(upto="p13", dbg=()):
    P = Prog(dbg)
    nc = P.nc
    x = P.din("x", [S, D])
    P.din("w_in", [D, 1536])
    P.din("ab_q_norm", [1, 64])
    P.din("ab_k_norm", [1, 64])
    P.din("ab_sink", [1, 8])
    P.din("ab_w_out", [D, D])
    P.din("mla_w_down", [1, D, 416])
    P.din("mla_q_norm", [1, 256])
    P.din("mla_kv_norm", [1, 128])
    P.din("mla_w_uq", [1, 256, 1536])
    P.din("mla_w_ukv", [1, 128, 2048])
    P.din("mla_w_out", [D, D])
    for n in ("ln_mix_g", "ln_mix_b", "ln_ffn_g", "ln_ffn_b"):
        P.din(n, [2, D])
    P.din("moe_router", [2, D, NE])
    for n in ("moe_w_gate", "moe_w_up", "moe_w_down"):
        P.din(n, [2, NE, D, D])
    P.din("tab", [S, 320])
    P.din("cst", [128, C_W])
    P.din("wmask", [128, 3072])
    P.dtmp("qT_A", [4, 128, S], BF16)
    P.dtmp("qT_B", [4, 128, S], BF16)
    P.dtmp("kT_A", [2, 128, S], BF16)
    P.dtmp("kT_B", [2, 128, S], BF16)
    P.dtmp("v_all", [4, 128, NT, 128], BF16)
    P.dtmp("oT", [D, S], BF16)
    P.dtmp("x1ext", [S, XW])
    P.dtmp("acc", [S, D])
    P.dtmp("affd", [S, NE])
    P.dtmp("p5_Cd", [NE, S])
    P.dtmp("p11_Cd", [NE, S])
    P.dtmp("x2ext", [S, XW])
    P.dtmp("x3ext", [S, XW])
    P.dtmp("acc2", [S, D])
    P.dtmp("qT_C", [16, 96, S], BF16)
    P.dtmp("kT_C", [16, 96, S], BF16)
    P.dtmp("v_C", [16, 128, NT, 128], BF16)
    out = P.dtmp("out", [S, D], out=True)
    if "dbg_idx" in dbg:
        P.dtmp("dbg_idx", [128, 128], I32)
        P.dtmp("dbg_lo", [128, NE])
    if "dbg_xg" in dbg:
        P.dtmp("dbg_xg", [128, XW])
        P.dtmp("dbg_yo", [128, D])
        P.dtmp("dbg_hT", [128, 8192], BF16)
    if "dbg_q9" in dbg:
        P.dtmp("dbg_q9", [128, S], BF16)
        P.dtmp("dbg_k9", [128, S], BF16)
    if "dbg_rec" in dbg:
        P.dtmp("dbg_rec", [64, 512])
        P.dtmp("dbg_num", [128, 512])
    if "dbgv0" in dbg:
        P.dtmp("dbgv0", [128, NT, 128], BF16)
        P.dtmp("dbgv1", [128, NT, 128], BF16)
    dr = P.dram
    last = PHASES.index(upto)
    on = lambda p: PHASES.index(p) <= last
    with contextlib.ExitStack() as st:
        idx = sb(st, nc, "idx_all", [128, NE, 8], I32)
        if on("p1"):
            phase_proj0(P, x)
        if on("p2"):
            jobs = []
            for h in range(8):
                j, half, kv = h // 2, h % 2, h // 4
                jobs.append(dict(qsrc=(("qA", j), dr["qT_A"][j, :, :], 128), ksrc=(("kA", kv), dr["kT_A"][kv, :, :], 128),
                                 vsrc=(("vA", kv), dr["v_all"][kv, :, :, :]), pb=half * 64, K=64, out=dr["oT"][h * 64:(h + 1) * 64, :]))
            phase_dense_attn(P, "p2", jobs, 64 ** -0.5)
        if on("p3"):
            jobs = []
            for h in range(8):
                j, half, kv = h // 2, h % 2, h // 4
                jobs.append(dict(qsrc=(("qB", j), dr["qT_B"][j, :, :], 128), ksrc=(("kB", kv), dr["kT_B"][kv, :, :], 128),
                                 vsrc=(("vB", kv), dr["v_all"][2 + kv, :, :, :]), pb=half * 64, K=64, h=h,
                                 out=dr["oT"][512 + h * 64:512 + (h + 1) * 64, :]))
            phase_dense_attn(P, "p3", jobs, 64 ** -0.5, win=True)
        if on("p4"):
            phase_outproj(P, "p4", dr["ab_w_out"][:, :], lambda tok: x[tok, :], dr["ln_mix_g"][0, :], dr["ln_mix_b"][0, :],
                          dr["moe_router"][0, :, :], dr["x1ext"], dr["acc"], dr["affd"])
        if on("p5"):
            phase_route(P, "p5", dr["x1ext"], idx)
        if on("p6"):
            phase_experts(P, "p6", 0, dr["x1ext"], dr["acc"], idx)
        if on("p7"):
            phase_ln(P, "p7", dr["acc"], dr["ln_ffn_g"][0, :], dr["ln_ffn_b"][0, :], lambda tok: dr["x2ext"][tok, 0:D])
        if on("p8"):
            phase_proj_mla(P, lambda tok: dr["x2ext"][tok, 0:D])
        if on("p9"):
            jobs = []
            for h in range(16):
                jobs.append(dict(qsrc=(("qC", h), dr["qT_C"][h, :, :], 96), ksrc=(("kC", h), dr["kT_C"][h, :, :], 96),
                                 vsrc=(("vC", h), dr["v_C"][h, :, :, :]), pb=0, K=96, out=dr["oT"][h * 64:(h + 1) * 64, :]))
            phase_dense_attn(P, "p9", jobs, 96 ** -0.5)
        if on("p10"):
            phase_outproj(P, "p10", dr["mla_w_out"][:, :], lambda tok: dr["x2ext"][tok, 0:D], dr["ln_mix_g"][1, :], dr["ln_mix_b"][1, :],
                          dr["moe_router"][1, :, :], dr["x3ext"], dr["acc2"], dr["affd"])
        if on("p11"):
            phase_route(P, "p11", dr["x3ext"], idx)
        if on("p12"):
            phase_experts(P, "p12", 1, dr["x3ext"], dr["acc2"], idx)
        if on("p13"):
            phase_ln(P, "p13", dr["acc2"], dr["ln_ffn_g"][1, :], dr["ln_ffn_b"][1, :], lambda tok: out[tok, :])
    return P


_PERM = np.concatenate([np.arange(0, 512), np.arange(512, 640), np.arange(768, 1280), np.arange(1280, 1408),
                        np.arange(640, 768), np.arange(1408, 1536)])


def make_in_maps(inputs, ncores=8):
    tab, cst = make_consts()
    f = lambda a: np.ascontiguousarray(np.asarray(a, dtype=np.float32))
    shared = {
        "w_in": f(np.asarray(inputs["ab_w_in"])[0][:, _PERM]),
        "ab_q_norm": f(inputs["ab_q_norm"]), "ab_k_norm": f(inputs["ab_k_norm"]), "ab_sink": f(inputs["ab_sink"]),
        "ab_w_out": f(np.asarray(inputs["ab_w_out"])[0]),
        "mla_w_down": f(inputs["mla_w_down"]), "mla_q_norm": f(inputs["mla_q_norm"]), "mla_kv_norm": f(inputs["mla_kv_norm"]),
        "mla_w_uq": f(inputs["mla_w_uq"]), "mla_w_ukv": f(inputs["mla_w_ukv"]), "mla_w_out": f(np.asarray(inputs["mla_w_out"])[0]),
        "ln_mix_g": f(inputs["ln_mix_g"]), "ln_mix_b": f(inputs["ln_mix_b"]), "ln_ffn_g": f(inputs["ln_ffn_g"]), "ln_ffn_b": f(inputs["ln_ffn_b"]),
        "moe_router": f(inputs["moe_router"]), "moe_w_gate": f(inputs["moe_w_gate"]), "moe_w_up": f(inputs["moe_w_up"]),
        "moe_w_down": f(inputs["moe_w_down"]), "tab": tab, "cst": cst, "wmask": make_wmask(),
    }
    xs = np.asarray(inputs["x"], dtype=np.float32)
    maps = []
    for c in range(ncores):
        m = dict(shared)
        m["x"] = np.ascontiguousarray(xs[c])
        maps.append(m)
    return maps


def kernel(**inputs):
    P = build()
    maps = make_in_maps(inputs, 8)
    res = run_bass_kernel_spmd(P.nc, maps, core_ids=list(range(8)))
    return np.stack([np.asarray(r["out"], dtype=np.float32) for r in res.results], axis=0)
```

```python
import contextlib
import numpy as np
import concourse.bass as bass
import concourse.mybir as mybir
from concourse.bass_utils import run_bass_kernel_spmd

F32 = mybir.dt.float32
BF16 = mybir.dt.bfloat16
I32 = mybir.dt.int32
ALU = mybir.AluOpType
AF = mybir.ActivationFunctionType
AX = mybir.AxisListType

S = 8192
D = 1024
NT = S // 128
NE = 16
CAP = 1024
DEPTH = 2
ALPHA = (2.0 * DEPTH) ** 0.25
XW = D + NE

SAME_ENGINE_SYNC = True


class Buf:
    __slots__ = ("w", "r", "rd")

    def __init__(self):
        self.w = None
        self.r = {}
        self.rd = []


class Op:
    __slots__ = ("eng", "fn", "deps", "sig", "sem", "val", "dma", "inc")

    def __init__(self, eng, fn, dma):
        self.eng = eng
        self.fn = fn
        self.dma = dma
        self.deps = []
        self.sig = dma
        self.sem = None
        self.val = 0
        self.inc = 16 if dma else 1


ENGS = ("sp", "pe", "act", "dve", "pool")


class Phase:
    KD = {"sp": 8, "pool": 6, "act": 4}

    def __init__(self, nc, name):
        self.nc = nc
        self.name = name
        self.ops = {e: [] for e in ENGS}
        self.bufs = {}
        self.dma_ops = {q: [] for q in self.KD}

    def b(self, *key):
        bb = self.bufs.get(key)
        if bb is None:
            bb = self.bufs[key] = Buf()
        return bb

    def add(self, eng, fn, reads=(), writes=(), dma=False):
        op = Op(eng, fn, dma)
        deps = []
        for b in reads:
            if b.w is not None:
                deps.append(b.w)
        for b in writes:
            if b.w is not None:
                deps.append(b.w)
            deps.extend(b.r.values())
            deps.extend(b.rd)
        if dma:
            lst = self.dma_ops[eng]
            k = self.KD[eng]
            if len(lst) >= k:
                deps.append(lst[len(lst) - k])
            lst.append(op)
        seen = set()
        for d in deps:
            if d is op or id(d) in seen:
                continue
            seen.add(id(d))
            if (not d.dma) and (not dma) and d.eng == eng:
                if eng == "pe" or not SAME_ENGINE_SYNC:
                    continue
            op.deps.append(d)
            d.sig = True
        for b in reads:
            if dma:
                b.rd.append(op)
            else:
                b.r[eng] = op
        for b in writes:
            b.w = op
            b.r = {}
            b.rd = []
        self.ops[eng].append(op)
        return op

    def dma(self, q, out, in_, reads=(), writes=(), **kw):
        return self.add(q, lambda e: e.dma_start(out=out, in_=in_, **kw), reads, writes, dma=True)

    def emit(self):
        nc = self.nc
        allsems = []

        def newsem(nm):
            h = nc.alloc_semaphore(name=nm)
            allsems.append(h)
            return h

        if True:
            csem = {}
            for e in ("pe", "act", "dve", "pool"):
                if any(o.sig and not o.dma for o in self.ops[e]):
                    csem[e] = newsem(f"{self.name}_c_{e}")
            dsem = {}
            for q, k in self.KD.items():
                if self.dma_ops[q]:
                    dsem[q] = [newsem(f"{self.name}_d_{q}{i}") for i in range(min(k, len(self.dma_ops[q])))]
            for e in ENGS:
                cnt = 0
                for o in self.ops[e]:
                    if o.dma:
                        continue
                    if o.sig:
                        cnt += 1
                        o.sem = csem[e]
                        o.val = cnt
            final = {q: {} for q in self.KD}
            for q, lst in self.dma_ops.items():
                k = self.KD[q]
                for n, o in enumerate(lst):
                    o.sem = dsem[q][n % k]
                    o.val = 16 * (n // k + 1)
                    final[q][n % k] = (o.sem, o.val)
            ops = self.ops

            def run(e_name, eng):
                waited = {}
                for o in ops[e_name]:
                    need = {}
                    for d in o.deps:
                        key = id(d.sem)
                        if waited.get(key, 0) >= d.val:
                            continue
                        if key not in need or need[key][1] < d.val:
                            need[key] = (d.sem, d.val)
                    for key, (sem, val) in need.items():
                        eng.wait_ge(sem, val)
                        waited[key] = val
                    ins = o.fn(eng)
                    if o.sig:
                        ins.then_inc(o.sem, o.inc)
                if e_name in final:
                    for (sem, val) in final[e_name].values():
                        if waited.get(id(sem), 0) < val:
                            eng.wait_ge(sem, val)

            with nc.Block() as block:
                if ops["sp"]:
                    @block.sync
                    def _(e):
                        run("sp", e)
                if ops["pe"]:
                    @block.tensor
                    def _(e):
                        run("pe", e)
                if ops["act"]:
                    @block.scalar
                    def _(e):
                        run("act", e)
                if ops["dve"]:
                    @block.vector
                    def _(e):
                        run("dve", e)
                if ops["pool"]:
                    @block.gpsimd
                    def _(e):
                        run("pool", e)
            nc.clear_and_free_semaphores(allsems)
            nc.all_engine_barrier()


def _rope_tab(pos, dim):
    freqs = (10000.0 ** (-(np.arange(0, dim, 2, dtype=np.float32) / np.float32(dim)))).astype(np.float32)
    ang = pos.astype(np.float32)[:, None] * freqs[None, :]
    return np.cos(ang).astype(np.float32), np.sin(ang).astype(np.float32)


def make_consts():
    t = np.arange(S)
    cr, sr = _rope_tab((t // 64).astype(np.float32), 32)
    cc, sc = _rope_tab((t % 64).astype(np.float32), 32)
    cs, ss = _rope_tab(t.astype(np.float32), 64)
    cm, sm = _rope_tab(t.astype(np.float32), 32)
    cosA = np.concatenate([cr, cr, cc, cc], 1)
    sinA = np.concatenate([-sr, sr, -sc, sc], 1)
    cosB = np.concatenate([cs, cs], 1)
    sinB = np.concatenate([-ss, ss], 1)
    cosC = np.concatenate([cm, cm], 1)
    sinC = np.concatenate([-sm, sm], 1)
    tab = np.concatenate([cosA, sinA, cosB, sinB, cosC, sinC], 1).astype(np.float32)
    ident = np.eye(128, dtype=np.float32)
    p = np.arange(128)
    tri = (p[:, None] < p[None, :]).astype(np.float32)
    mprev = (p[None, :] <= p[:, None]).astype(np.float32)
    mnext = (p[:, None] <= p[None, :]).astype(np.float32)
    svec = (np.arange(8)[None, :] * 128 + p[:, None]).astype(np.float32)
    keep = np.ones((128, NE, 64), np.float32)
    keep[:, :, 0] = 0.0
    vsink = np.zeros((128, 128), np.float32)
    vsink[0:2, 64:128] = 1.0
    cst = np.concatenate([ident, tri, mprev, mnext, svec, keep.reshape(128, -1), vsink], 1).astype(np.float32)
    return tab, cst


def make_wmask():
    kl = np.arange(128)[:, None]
    q = np.arange(512)[None, :]
    ms = []
    for r in range(6):
        ms.append((np.abs(q - ((r - 1) * 128 + kl)) <= 128).astype(np.float32))
    return np.concatenate(ms, 1)


C_IDENT, C_TRI, C_MPREV, C_MNEXT, C_SVEC, C_KEEP = 0, 128, 256, 384, 512, 520
C_VSINK = 520 + NE * 64
C_W = C_VSINK + 128


class Prog:
    def __init__(self, dbg=()):
        self.nc = bass.Bass("TRN2", target_bir_lowering=False)
        self.dbg = set(dbg)
        self.dram = {}

    def din(self, name, shape, dt=F32):
        t = self.nc.dram_tensor(name, list(shape), dt, kind="ExternalInput")
        self.dram[name] = t
        return t

    def dtmp(self, name, shape, dt=F32, out=False):
        kind = "ExternalOutput" if (out or name in self.dbg) else "Internal"
        t = self.nc.dram_tensor(name, list(shape), dt, kind=kind)
        self.dram[name] = t
        return t


def sb(st, nc, name, shape, dt):
    return st.enter_context(nc.sbuf_tensor(name, list(shape), dt))


def ps(st, nc, name, shape, dt):
    return st.enter_context(nc.psum_tensor(name, list(shape), dt))


def load_consts(ph, nc, st, P):
    cst = sb(st, nc, ph.name + "_cst", [128, C_W], F32)
    ph.dma("sp", cst[:, :], P.dram["cst"][:, :], writes=[ph.b("cst")])
    idb = sb(st, nc, ph.name + "_idb", [128, 128], BF16)
    ph.add("dve", lambda e: e.tensor_copy(idb[:, :], cst[:, C_IDENT:C_IDENT + 128]),
           reads=[ph.b("cst")], writes=[ph.b("idb")])
    return cst, idb


def emit_xT(ph, xs, xs_buf, xT, xT_buf, pT, pT_bufs, cst, ncol=8, evac=("act", "dve")):
    ident = cst[:, C_IDENT:C_IDENT + 128]
    ng = (ncol + 3) // 4
    for g in range(ng):
        pt = pT[g % len(pT)]
        pb = pT_bufs[g % len(pT)]
        n = min(4, ncol - g * 4)
        for c in range(n):
            cc = g * 4 + c
            ph.add("pe", (lambda e, pt=pt, c=c, cc=cc: e.transpose(pt[:, c * 128:(c + 1) * 128],
                                                                   xs[:, cc * 128:(cc + 1) * 128], ident)),
                   reads=[xs_buf, ph.b("cst")], writes=[pb])
        eng = evac[g % len(evac)]
        if eng == "act":
            ph.add("act", (lambda e, pt=pt, g=g, n=n: e.copy(
                xT[:, g * 4:g * 4 + n, :], pt[:, 0:n * 128].rearrange("p (c t) -> p c t", c=n))),
                reads=[pb], writes=[xT_buf])
        else:
            ph.add("dve", (lambda e, pt=pt, g=g, n=n: e.tensor_copy(
                xT[:, g * 4:g * 4 + n, :], pt[:, 0:n * 128].rearrange("p (c t) -> p c t", c=n))),
                reads=[pb], writes=[xT_buf])


def load_weight_bf16(ph, nc, st, name, w_ap, kc, n, stage, stage_bufs, cast_engs=("dve", "pool")):
    wb = sb(st, nc, name, [128, kc, n], BF16)
    wv = w_ap.rearrange("(c p) n -> p c n", p=128)
    cnt = 0
    for c in range(kc):
        for n0 in range(0, n, stage[0].shape[1]):
            n1 = min(n, n0 + stage[0].shape[1])
            s = cnt % len(stage)
            ph.dma("sp", stage[s][:, 0:n1 - n0], wv[:, c, n0:n1], writes=[stage_bufs[s]])
            eng = cast_engs[cnt % len(cast_engs)]
            ph.add(eng, (lambda e, s=s, c=c, n0=n0, n1=n1: e.tensor_copy(wb[:, c, n0:n1], stage[s][:, 0:n1 - n0])),
                   reads=[stage_bufs[s]], writes=[ph.b(name)])
            cnt += 1
    return wb


def emit_rope(ph, eng, out, src, cos, sin, nh, hd, tmp1, tmp2, rbufs, wbufs, blocks):
    hb = hd // blocks // 2
    v = lambda a: a.rearrange("p (h b t f) -> p h b t f", h=nh, b=blocks, t=2, f=hb)
    tb = lambda a: a.rearrange("p (b t f) -> p b t f", b=blocks, t=2, f=hb)
    cosb = cos.unsqueeze(1).to_broadcast([128, nh, hd]).rearrange("p h d -> p (h d)") if False else None
    s3 = src.rearrange("p (h d) -> p h d", h=nh)
    t13 = tmp1.rearrange("p (h d) -> p h d", h=nh)
    cos3 = cos.unsqueeze(1).to_broadcast([128, nh, hd])
    ph.add(eng, lambda e: e.tensor_tensor(t13, s3, cos3, ALU.mult), reads=rbufs, writes=[wbufs[0]])
    sv, t2v = v(src), v(tmp2)
    sn = tb(sin)
    for t in range(2):
        sin_b = sn[:, :, t, :].unsqueeze(1).to_broadcast([128, nh, blocks, hb])
        ph.add(eng, (lambda e, t=t, sin_b=sin_b: e.tensor_tensor(t2v[:, :, :, t, :], sv[:, :, :, 1 - t, :], sin_b, ALU.mult)),
               reads=rbufs, writes=[wbufs[1]])
    ph.add(eng, lambda e: e.tensor_tensor(out, tmp1, tmp2, ALU.add), reads=[wbufs[0], wbufs[1]], writes=[wbufs[2]])


def rope3(ph, eng, out3, src3, cos, sin, nh, hd, blocks, t1, t2, rbufs, tbufs, wbuf):
    hb = hd // blocks // 2
    t13 = t1.rearrange("p (h d) -> p h d", h=nh)
    t23 = t2.rearrange("p (h d) -> p h d", h=nh)
    cos3 = cos.unsqueeze(1).to_broadcast([128, nh, hd])
    ph.add(eng, lambda e: e.tensor_tensor(t13, src3, cos3, ALU.mult), reads=rbufs, writes=[tbufs[0]])
    for b in range(blocks):
        for t in range(2):
            o0 = b * 2 * hb + t * hb
            s0 = b * 2 * hb + (1 - t) * hb
            sin_b = sin[:, o0:o0 + hb].unsqueeze(1).to_broadcast([128, nh, hb])
            ph.add(eng, (lambda e, o0=o0, s0=s0, sin_b=sin_b: e.tensor_tensor(
                t23[:, :, o0:o0 + hb], src3[:, :, s0:s0 + hb], sin_b, ALU.mult)),
                reads=rbufs, writes=[tbufs[1]])
    ph.add(eng, lambda e: e.tensor_tensor(out3, t13, t23, ALU.add), reads=[tbufs[0], tbufs[1]], writes=[wbuf])


def lowp(nc, f):
    def g(e):
        with nc.allow_low_precision(reason="exact small integer counts"):
            return f(e)
    return g


def bc(ap1d):
    return ap1d.partition_broadcast(128)


def phase_proj0(P, x_ap):
    nc = P.nc
    dr = P.dram
    with contextlib.ExitStack() as st:
        ph = Phase(nc, "p1")
        cst, idb = load_consts(ph, nc, st, P)
        stage = [sb(st, nc, f"p1_stg{i}", [128, 1536], F32) for i in range(2)]
        wb = load_weight_bf16(ph, nc, st, "p1_wb", dr["w_in"][:, :], 8, 1536, stage,
                              [ph.b("stg", i) for i in range(2)], cast_engs=("dve", "pool"))
        gq = sb(st, nc, "p1_gq", [128, 64], F32)
        gk = sb(st, nc, "p1_gk", [128, 64], F32)
        ph.dma("sp", gq[:, :], bc(dr["ab_q_norm"][0, :]), writes=[ph.b("gq")])
        ph.dma("sp", gk[:, :], bc(dr["ab_k_norm"][0, :]), writes=[ph.b("gk")])
        NSL = 2
        xs = [sb(st, nc, f"p1_xs{i}", [128, 1024], F32) for i in range(NSL)]
        tb = [sb(st, nc, f"p1_tb{i}", [128, 256], F32) for i in range(4)]
        xT = [sb(st, nc, f"p1_xT{i}", [128, 8, 128], BF16) for i in range(NSL)]
        pr = [sb(st, nc, f"p1_pr{i}", [128, 1536], F32) for i in range(4)]
        sq_ = [sb(st, nc, f"p1_sq{i}", [128, 640], F32) for i in range(2)]
        ss_ = [sb(st, nc, f"p1_ss{i}", [128, 10], F32) for i in range(2)]
        sd_ = [sb(st, nc, f"p1_sd{i}", [128, 10], F32) for i in range(2)]
        rs_ = [sb(st, nc, f"p1_rs{i}", [128, 10], F32) for i in range(2)]
        t0_ = [sb(st, nc, f"p1_t0{i}", [128, 640], F32) for i in range(2)]
        ta1_ = [sb(st, nc, f"p1_ta1{i}", [128, 640], F32) for i in range(2)]
        ta2_ = [sb(st, nc, f"p1_ta2{i}", [128, 640], F32) for i in range(2)]
        tb1_ = [sb(st, nc, f"p1_rb1{i}", [128, 640], F32) for i in range(2)]
        tb2_ = [sb(st, nc, f"p1_rb2{i}", [128, 640], F32) for i in range(2)]
        oa = [sb(st, nc, f"p1_oa{i}", [128, 640], BF16) for i in range(NSL)]
        ob = [sb(st, nc, f"p1_ob{i}", [128, 640], BF16) for i in range(NSL)]
        stA = [sb(st, nc, f"p1_stA{i}", [128, 5, 128], BF16) for i in range(NSL)]
        stB = [sb(st, nc, f"p1_stB{i}", [128, 5, 128], BF16) for i in range(NSL)]
        vst = [sb(st, nc, f"p1_vst{i}", [128, 4, 128], BF16) for i in range(NSL)]
        pT = [ps(st, nc, f"p1_pT{i}", [128, 512], F32) for i in range(2)]
        pp = [ps(st, nc, f"p1_pp{i}", [128, 512], F32) for i in range(3)]
        ptA = ps(st, nc, "p1_ptA", [128, 1024], BF16)
        ptB = ps(st, nc, "p1_ptB", [128, 1024], BF16)
        for s in range(NSL):
            ph.add("pool", (lambda e, s=s: e.memset(vst[s][:, :, :], 1.0)), writes=[ph.b("vst", s)])
        eps = sb(st, nc, "p1_eps", [128, 1], F32)
        ph.add("pool", lambda e: e.memset(eps[:, :], 1e-6), writes=[ph.b("eps")])
        qTA, qTB, kTA, kTB, vall = dr["qT_A"], dr["qT_B"], dr["kT_A"], dr["kT_B"], dr["v_all"]
        def stageA(i):
            s = i % NSL
            s4 = i % 4
            tok = slice(i * 128, (i + 1) * 128)
            B = lambda n: ph.b(n, s)
            ph.dma("sp", xs[s][:, :], x_ap[tok, :], writes=[B("xs")])
            ph.dma("sp", tb[s4][:, :], dr["tab"][tok, 0:256], writes=[ph.b("tb", s4)])
            emit_xT(ph, xs[s], B("xs"), xT[s], B("xT"), pT, [ph.b("pT", 0), ph.b("pT", 1)], cst)
            for n in range(3):
                for c in range(8):
                    ph.add("pe", (lambda e, n=n, c=c, s=s: e.matmul(pp[n][:, :], xT[s][:, c, :], wb[:, c, n * 512:(n + 1) * 512],
                                                                     start=(c == 0), stop=(c == 7))),
                           reads=[B("xT"), ph.b("p1_wb")], writes=[ph.b("pp", n)])
                ph.add("act", (lambda e, n=n, s4=s4: e.copy(pr[s4][:, n * 512:(n + 1) * 512], pp[n][:, :])),
                       reads=[ph.b("pp", n)], writes=[ph.b("pr", s4)])

        def stageB(i):
            s = i % NSL
            s4 = i % 4
            tok = slice(i * 128, (i + 1) * 128)
            B = lambda n: ph.b(n, s)
            prt, tbt, prB, tbB = pr[s4], tb[s4], ph.b("pr", s4), ph.b("tb", s4)
            sq, ss, sd, rs, t0, ta1, ta2, tb1, tb2 = sq_[s], ss_[s], sd_[s], rs_[s], t0_[s], ta1_[s], ta2_[s], tb1_[s], tb2_[s]
            rope3(ph, "pool", ob[s][:, :].rearrange("p (h d) -> p h d", h=10), prt[:, 640:1280].rearrange("p (h d) -> p h d", h=10),
                  tbt[:, 128:192], tbt[:, 192:256], 10, 64, 1, tb1[:, :], tb2[:, :],
                  [prB, tbB], [B("tb1"), B("tb2")], B("ob"))
            ph.add("act", lambda e: e.copy(vst[s][:, :, 0:64], prt[:, 1280:1536].rearrange("p (h d) -> p h d", h=4)),
                   reads=[prB], writes=[B("vst")])
            ph.dma("sp", vall[:, :, i, :].rearrange("k p f -> p k f"), vst[s][:, :, :], reads=[B("vst")])
            ph.add("act", lambda e: e.activation(sq[:, :], prt[:, 0:640], AF.Square), reads=[prB], writes=[B("sq")]); yield
            ph.add("dve", lambda e: e.tensor_reduce(ss[:, :], sq[:, :].rearrange("p (h d) -> p h d", h=10), AX.X, ALU.add),
                   reads=[B("sq")], writes=[B("ss")]); yield
            ph.add("act", lambda e: e.activation(sd[:, :], ss[:, :], AF.Sqrt, bias=eps[:, :], scale=1.0 / 64),
                   reads=[B("ss"), ph.b("eps")], writes=[B("sd")]); yield
            ph.add("dve", lambda e: e.reciprocal(rs[:, :], sd[:, :]), reads=[B("sd")], writes=[B("rs")]); yield
            ph.add("dve", lambda e: e.tensor_tensor(t0[:, :].rearrange("p (h d) -> p h d", h=10),
                                                    prt[:, 0:640].rearrange("p (h d) -> p h d", h=10),
                                                    rs[:, :].unsqueeze(2).to_broadcast([128, 10, 64]), ALU.mult),
                   reads=[prB, B("rs")], writes=[B("t0")])
            ph.add("dve", lambda e: e.tensor_tensor(t0[:, 0:512].rearrange("p (h d) -> p h d", h=8),
                                                    t0[:, 0:512].rearrange("p (h d) -> p h d", h=8),
                                                    gq[:, :].unsqueeze(1).to_broadcast([128, 8, 64]), ALU.mult),
                   reads=[ph.b("gq")], writes=[B("t0")])
            ph.add("dve", lambda e: e.tensor_tensor(t0[:, 512:640].rearrange("p (h d) -> p h d", h=2),
                                                    t0[:, 512:640].rearrange("p (h d) -> p h d", h=2),
                                                    gk[:, :].unsqueeze(1).to_broadcast([128, 2, 64]), ALU.mult),
                   reads=[ph.b("gk")], writes=[B("t0")]); yield
            rope3(ph, "dve", oa[s][:, :].rearrange("p (h d) -> p h d", h=10), t0[:, :].rearrange("p (h d) -> p h d", h=10),
                  tbt[:, 0:64], tbt[:, 64:128], 10, 64, 2, ta1[:, :], ta2[:, :],
                  [B("t0"), tbB], [B("ta1"), B("ta2")], B("oa")); yield
            for (src, sbuf_, ptt, ptn, stt, stn, qT, kT) in ((oa, "oa", ptA, "ptA", stA, "stA", qTA, kTA),
                                                             (ob, "ob", ptB, "ptB", stB, "stB", qTB, kTB)):
                for c in range(5):
                    ph.add("pe", (lambda e, src=src, ptt=ptt, c=c: e.transpose(ptt[:, c * 128:(c + 1) * 128],
                                                                               src[s][:, c * 128:(c + 1) * 128], idb[:, :])),
                           reads=[B(sbuf_), ph.b("idb")], writes=[ph.b(ptn)])
                ph.add("act", (lambda e, ptt=ptt, stt=stt: e.copy(stt[s][:, :, :], ptt[:, 0:640].rearrange("p (c t) -> p c t", c=5))),
                       reads=[ph.b(ptn)], writes=[B(stn)])
                ph.dma("sp", qT[:, :, tok].rearrange("j p t -> p j t"), stt[s][:, 0:4, :], reads=[B(stn)])
                for kv in range(2):
                    for dup in range(2):
                        ph.dma("sp", kT[kv, dup * 64:(dup + 1) * 64, tok], stt[s][kv * 64:(kv + 1) * 64, 4, :], reads=[B(stn)])
                yield

        def drive(gens):
            gens = list(gens)
            while gens:
                for g_ in list(gens):
                    try:
                        next(g_)
                    except StopIteration:
                        gens.remove(g_)

        stageA(0)
        stageA(1)
        for i in range(0, NT, 2):
            if i + 2 < NT:
                stageA(i + 2)
                stageA(i + 3)
            drive([stageB(i), stageB(i + 1)])
        ph.emit()


def phase_dense_attn(P, name, jobs, scale, win=False):
    nc = P.nc
    dr = P.dram
    with contextlib.ExitStack() as st:
        ph = Phase(nc, name)
        if win:
            cst, idb = load_consts(ph, nc, st, P)
            wm = sb(st, nc, f"{name}_wm", [128, 3072], BF16)
            wst = sb(st, nc, f"{name}_wst", [128, 3072], F32)
            ph.dma("sp", wst[:, :], dr["wmask"][:, :], writes=[ph.b("wst")])
            ph.add("dve", lambda e: e.tensor_copy(wm[:, :], wst[:, :]), reads=[ph.b("wst")], writes=[ph.b("wm")])
            vsk = sb(st, nc, f"{name}_vsk", [128, 128], BF16)
            ph.add("dve", lambda e: e.tensor_copy(vsk[:, :], cst[:, C_VSINK:C_VSINK + 128]), reads=[ph.b("cst")], writes=[ph.b("vsk")])
            sk = sb(st, nc, f"{name}_sk", [128, 8], F32)
            es = sb(st, nc, f"{name}_es", [128, 8], F32)
            ehb = sb(st, nc, f"{name}_ehb", [128, 8], BF16)
            ehf = sb(st, nc, f"{name}_ehf", [128, 8], F32)
            elo = sb(st, nc, f"{name}_elo", [128, 8], F32)
            ssel = sb(st, nc, f"{name}_ssel", [128, 8], F32)
            onesb = sb(st, nc, f"{name}_onesb", [128, 512], BF16)
            psink = [sb(st, nc, f"{name}_psink{i}", [128, 512], BF16) for i in range(2)]
            ph.dma("sp", sk[:, :], bc(dr["ab_sink"][0, :]), writes=[ph.b("sk")])
            ph.add("act", lambda e: e.activation(es[:, :], sk[:, :], AF.Exp), reads=[ph.b("sk")], writes=[ph.b("es")])
            ph.add("dve", lowp(nc, lambda e: e.tensor_copy(ehb[:, :], es[:, :])), reads=[ph.b("es")], writes=[ph.b("ehb")])
            ph.add("dve", lambda e: e.tensor_copy(ehf[:, :], ehb[:, :]), reads=[ph.b("ehb")], writes=[ph.b("ehf")])
            ph.add("dve", lambda e: e.tensor_tensor(elo[:, :], es[:, :], ehf[:, :], ALU.subtract), reads=[ph.b("es"), ph.b("ehf")], writes=[ph.b("elo")])
            ph.add("dve", lambda e: e.tensor_scalar(ehf[:, :], ehf[:, :], cst[:, C_IDENT:C_IDENT + 1], None, ALU.mult), reads=[ph.b("cst")], writes=[ph.b("ehf")])
            ph.add("dve", lambda e: e.scalar_tensor_tensor(ssel[:, :], elo[:, :], cst[:, C_IDENT + 1:C_IDENT + 2], ehf[:, :], ALU.mult, ALU.add),
                   reads=[ph.b("elo"), ph.b("ehf"), ph.b("cst")], writes=[ph.b("ssel")])
            ph.add("dve", lambda e: e.memset(onesb[:, :], 1.0), writes=[ph.b("onesb")])
        NSL = 2
        qs = [sb(st, nc, f"{name}_q{i}", [128, S], BF16) for i in range(NSL)]
        pad = (jobs[0]["K"] == 64)
        nk = 2 if pad else 1
        ks = [[sb(st, nc, f"{name}_k{i}_{hf}", [128, S], BF16) for hf in range(nk)] for i in range(NSL)]
        vs = [sb(st, nc, f"{name}_v{i}", [128, NT, 128], BF16) for i in range(NSL)]
        NP_ = 6 if win else 4
        pTs = [sb(st, nc, f"{name}_pT{i}", [128, 512], BF16) for i in range(NP_)]
        rec = sb(st, nc, f"{name}_rec", [64, 512], F32)
        onb = [sb(st, nc, f"{name}_on{i}", [64, 512], BF16) for i in range(2)]
        NS_ = 5 if win else 3
        sT = [ps(st, nc, f"{name}_sT{i}", [128, 512], F32) for i in range(NS_)]
        oacc = [ps(st, nc, f"{name}_oa{i}", [128, 512], F32) for i in range(2)]
        LA = 4 if win else 2
        if not pad:
            for i in range(NSL):
                ph.add("pool", (lambda e, i=i: e.memset(qs[i][64:128, :], 0.0)), writes=[ph.b("q", i)])
                ph.add("pool", (lambda e, i=i: e.memset(ks[i][0][64:128, :], 0.0)), writes=[ph.b("k", i)])
        else:
            for i in range(NSL):
                ph.add("pool", (lambda e, i=i: e.memset(ks[i][0][64:128, :], 0.0)), writes=[ph.b("k", i)])
                ph.add("pool", (lambda e, i=i: e.memset(ks[i][1][0:64, :], 0.0)), writes=[ph.b("k", i)])
        qk_prev = kk_prev = vv_prev = None
        slot = {"q": -1, "k": -1, "v": -1}
        cur = {}
        on_cnt = 0
        jslots = []

        def issue_loads(jb):
            for key, tiles, src in (("q", qs, jb["qsrc"]), ("k", ks, jb["ksrc"]), ("v", vs, jb["vsrc"])):
                if cur.get(key) != src[0]:
                    slot[key] = (slot[key] + 1) % NSL
                    sl = slot[key]
                    if key == "v":
                        ph.dma("sp", tiles[sl][:, :, :].rearrange("p c f -> p (c f)"), src[1].rearrange("p c f -> p (c f)"), writes=[ph.b(key, sl)])
                    elif key == "k" and pad:
                        ph.dma("sp", tiles[sl][0][0:64, :], src[1][0:64, :], writes=[ph.b(key, sl)])
                        ph.dma("sp", tiles[sl][1][64:128, :], src[1][64:128, :], writes=[ph.b(key, sl)])
                    elif key == "k":
                        rows = src[2]
                        ph.dma("sp", tiles[sl][0][0:rows, :], src[1], writes=[ph.b(key, sl)])
                    else:
                        rows = src[2]
                        ph.dma("sp", tiles[sl][0:rows, :], src[1], writes=[ph.b(key, sl)])
                    cur[key] = src[0]
            jslots.append(dict(slot))

        issue_loads(jobs[0])
        for ji, jb in enumerate(jobs):
            if ji + 1 < len(jobs):
                issue_loads(jobs[ji + 1])
            jsl = jslots[ji]
            q_t, k_t, v_t = qs[jsl["q"]], ks[jsl["k"]][(jb["pb"] // 64) if pad else 0], vs[jsl["v"]]
            qB, kB, vB = ph.b("q", jsl["q"]), ph.b("k", jsl["k"]), ph.b("v", jsl["v"])
            if "dbg_q9" in dr and name == "p9" and jb is jobs[0]:
                ph.dma("sp", dr["dbg_q9"][:, :], q_t[:, :], reads=[qB])
                ph.dma("sp", dr["dbg_k9"][:, :], k_t[:, :], reads=[kB])
            pb, K = jb["pb"], jb["K"]
            if win:
                steps = [(g, c) for g in range(16) for c in range(4 * g - 1, 4 * g + 5) if 0 <= c < NT]
                hh = jb["h"]
                psl = hh % 2
                ph.add("dve", (lambda e, psl=psl, hh=hh: e.tensor_scalar(psink[psl][:, :], onesb[:, :], ssel[:, hh:hh + 1], None, ALU.mult)),
                       reads=[ph.b("onesb"), ph.b("ssel")], writes=[ph.b("psink", psl)])
            else:
                steps = [(g, c) for g in range(16) for c in range(NT)]
            nsteps = len(steps)

            def qk(n, q_t=q_t, k_t=k_t, qB=qB, kB=kB, pb=pb, K=K):
                g, c = steps[n]
                b = n % NS_
                ph.add("pe", (lambda e: e.matmul(sT[b][:, :], k_t[:, c * 128:(c + 1) * 128],
                                                 q_t[:, g * 512:(g + 1) * 512], start=True, stop=True)),
                       reads=[qB, kB], writes=[ph.b("sT", b)])

            for n in range(min(LA, nsteps)):
                qk(n)
            for n in range(nsteps):
                g, c = steps[n]
                first = (n == 0) or steps[n - 1][0] != g
                last = (n == nsteps - 1) or steps[n + 1][0] != g
                if n + LA < nsteps:
                    qk(n + LA)
                b = n % NS_
                pslot = n % NP_
                ph.add("act", (lambda e, b=b, pslot=pslot: e.activation(pTs[pslot][:, :], sT[b][:, :], AF.Exp, scale=scale)),
                       reads=[ph.b("sT", b)], writes=[ph.b("pT", pslot)])
                if win:
                    r_ = c - (4 * g - 1)
                    ph.add("dve" if n % 2 == 0 else "pool",
                           (lambda e, pslot=pslot, r_=r_: e.tensor_tensor(pTs[pslot][:, :], pTs[pslot][:, :], wm[:, r_ * 512:(r_ + 1) * 512], ALU.mult)),
                           reads=[ph.b("wm")], writes=[ph.b("pT", pslot)])
                ob_ = g % 2
                ph.add("pe", (lambda e, ob_=ob_, c=c, pslot=pslot, v_t=v_t, first=first, last=last: e.matmul(
                    oacc[ob_][:, :], v_t[:, c, :], pTs[pslot][:, :], start=first, stop=(last and not win))),
                    reads=[vB, ph.b("pT", pslot)], writes=[ph.b("oacc", ob_)])
                if last and win:
                    ph.add("pe", (lambda e, ob_=ob_, psl=psl: e.matmul(oacc[ob_][:, :], vsk[:, :], psink[psl][:, :], start=False, stop=True)),
                           reads=[ph.b("vsk"), ph.b("psink", psl)], writes=[ph.b("oacc", ob_)])
                if last:
                    os_ = on_cnt % 2
                    on_cnt += 1
                    ph.add("dve", (lambda e, ob_=ob_: e.reciprocal(rec[:, :], oacc[ob_][64:128, :])),
                           reads=[ph.b("oacc", ob_)], writes=[ph.b("rec")])
                    ph.add("dve", (lambda e, ob_=ob_, os_=os_: e.tensor_tensor(onb[os_][:, :], oacc[ob_][0:64, :], rec[:, :], ALU.mult)),
                           reads=[ph.b("oacc", ob_), ph.b("rec")], writes=[ph.b("on", os_)])
                    ph.dma("sp", jb["out"][:, g * 512:(g + 1) * 512], onb[os_][:, :], reads=[ph.b("on", os_)])
                    if "dbg_rec" in dr and name == "p9" and jb is jobs[0] and g == 0:
                        ph.dma("sp", dr["dbg_rec"][:, :], rec[:, :], reads=[ph.b("rec")])
                        dnum = sb(st, nc, "p9_dnum", [128, 512], F32)
                        ph.add("dve", (lambda e, ob_=ob_: e.tensor_copy(dnum[:, :], oacc[ob_][:, :])), reads=[ph.b("oacc", ob_)], writes=[ph.b("dnum")])
                        ph.dma("sp", dr["dbg_num"][:, :], dnum[:, :], reads=[ph.b("dnum")])
        ph.emit()


def phase_win_attn(P):
    nc = P.nc
    dr = P.dram
    name = "p3"
    scale = 64 ** -0.5
    with contextlib.ExitStack() as st:
        ph = Phase(nc, name)
        cst, idb = load_consts(ph, nc, st, P)
        mk = sb(st, nc, "p3_mk", [128, 2, 128], BF16)
        ph.add("dve", lambda e: e.tensor_copy(mk[:, 0, :], cst[:, C_MPREV:C_MPREV + 128]), reads=[ph.b("cst")], writes=[ph.b("mk")])
        ph.add("dve", lambda e: e.tensor_copy(mk[:, 1, :], cst[:, C_MNEXT:C_MNEXT + 128]), reads=[ph.b("cst")], writes=[ph.b("mk")])
        sk = sb(st, nc, "p3_sk", [128, 8], F32)
        es = sb(st, nc, "p3_es", [128, 8], F32)
        ph.dma("sp", sk[:, :], bc(dr["ab_sink"][0, :]), writes=[ph.b("sk")])
        ph.add("act", lambda e: e.activation(es[:, :], sk[:, :], AF.Exp), reads=[ph.b("sk")], writes=[ph.b("es")])
        qs = [sb(st, nc, f"p3_q{i}", [128, S], BF16) for i in range(2)]
        ks = [sb(st, nc, f"p3_k{i}", [128, S], BF16) for i in range(2)]
        vs = [sb(st, nc, f"p3_v{i}", [128, NT, 128], BF16) for i in range(2)]
        pTs = [sb(st, nc, f"p3_pT{i}", [128, 384], BF16) for i in range(4)]
        den = sb(st, nc, "p3_den", [64, 512], F32)
        rec = sb(st, nc, "p3_rec", [64, 512], F32)
        onb = [sb(st, nc, f"p3_on{i}", [64, 512], BF16) for i in range(2)]
        sT = [ps(st, nc, f"p3_sT{i}", [128, 512], F32) for i in range(3)]
        oacc = [ps(st, nc, f"p3_oa{i}", [128, 512], F32) for i in range(2)]
        cnt = 0
        gcnt = 0
        for h in range(8):
            j, half, kv = h // 2, h % 2, h // 4
            pb = half * 64
            if half == 0:
                ph.dma("sp", qs[j % 2][:, :], dr["qT_B"][j, :, :], writes=[ph.b("q", j % 2)])
            if h % 4 == 0:
                ph.dma("sp", ks[kv][:, :], dr["kT_B"][kv, :, :], writes=[ph.b("k", kv)])
                ph.dma("sp", vs[kv][:, :, :].rearrange("p c f -> p (c f)"), dr["v_all"][2 + kv, :, :, :].rearrange("p c f -> p (c f)"), writes=[ph.b("v", kv)])
                if "dbgv0" in dr and kv == 0:
                    ph.dma("sp", dr["dbgv0"][:, :, :].rearrange("p c f -> p (c f)"), vs[0][:, :, :].rearrange("p c f -> p (c f)"), reads=[ph.b("v", 0)])
            q_t, k_t, v_t = qs[j % 2], ks[kv], vs[kv]
            qB, kB, vB = ph.b("q", j % 2), ph.b("k", kv), ph.b("v", kv)
            for g in range(16):
                ob_ = gcnt % 2
                gcnt += 1
                for ti in range(4):
                    i = g * 4 + ti
                    b = cnt % 3
                    pslot = cnt % 4
                    cnt += 1
                    chunks = [c for c in (i - 1, i, i + 1) if 0 <= c < NT]
                    for c in chunks:
                        sl = c - i + 1
                        ph.add("pe", (lambda e, b=b, sl=sl, c=c, i=i: e.matmul(sT[b][:, sl * 128:(sl + 1) * 128],
                                                                                k_t[pb:pb + 64, c * 128:(c + 1) * 128],
                                                                                q_t[pb:pb + 64, i * 128:(i + 1) * 128], start=True, stop=True)),
                               reads=[qB, kB], writes=[ph.b("sT", b)])
                    lo_, hi_ = (chunks[0] - i + 1) * 128, (chunks[-1] - i + 2) * 128
                    ph.add("act", (lambda e, b=b, pslot=pslot, lo_=lo_, hi_=hi_: e.activation(pTs[pslot][:, lo_:hi_], sT[b][:, lo_:hi_], AF.Exp, scale=scale)),
                           reads=[ph.b("sT", b)], writes=[ph.b("pT", pslot)])
                    for c in chunks:
                        sl = c - i + 1
                        if sl != 1:
                            m = 0 if sl == 0 else 1
                            ph.add("dve", (lambda e, pslot=pslot, sl=sl, m=m: e.tensor_tensor(pTs[pslot][:, sl * 128:(sl + 1) * 128],
                                                                                                pTs[pslot][:, sl * 128:(sl + 1) * 128], mk[:, m, :], ALU.mult)),
                                   reads=[ph.b("mk")], writes=[ph.b("pT", pslot)])
                    for ci, c in enumerate(chunks):
                        sl = c - i + 1
                        ph.add("pe", (lambda e, ob_=ob_, ti=ti, c=c, sl=sl, pslot=pslot, ci=ci, nch=len(chunks): e.matmul(
                            oacc[ob_][:, ti * 128:(ti + 1) * 128], v_t[:, c, :], pTs[pslot][:, sl * 128:(sl + 1) * 128],
                            start=(ci == 0), stop=(ci == nch - 1))),
                            reads=[vB, ph.b("pT", pslot)], writes=[ph.b("oacc", ob_)])
                os_ = gcnt % 2
                ph.add("dve", (lambda e, ob_=ob_, h=h: e.tensor_scalar(den[:, :], oacc[ob_][64:128, :], es[64:128, h:h + 1], None, ALU.add)),
                       reads=[ph.b("oacc", ob_), ph.b("es")], writes=[ph.b("den")])
                ph.add("dve", lambda e: e.reciprocal(rec[:, :], den[:, :]), reads=[ph.b("den")], writes=[ph.b("rec")])
                ph.add("dve", (lambda e, ob_=ob_, os_=os_: e.tensor_tensor(onb[os_][:, :], oacc[ob_][0:64, :], rec[:, :], ALU.mult)),
                       reads=[ph.b("oacc", ob_), ph.b("rec")], writes=[ph.b("on", os_)])
                ph.dma("sp", dr["oT"][512 + h * 64:512 + (h + 1) * 64, g * 512:(g + 1) * 512], onb[os_][:, :], reads=[ph.b("on", os_)])
            if "dbgv1" in dr and h == 3:
                ph.dma("sp", dr["dbgv1"][:, :, :].rearrange("p c f -> p (c f)"), vs[0][:, :, :].rearrange("p c f -> p (c f)"), reads=[ph.b("v", 0)])
        ph.emit()


def emit_ln(ph, nc, name, r, rB, out_ap, outB, g_b, b_b, junk, small, eps_t, eng2="pool"):
    sm, nm, ssq, sd, rstd = (small[:, k:k + 1] for k in range(5))
    ph.add("dve", lambda e: e.tensor_reduce(sm, r, AX.X, ALU.add), reads=[rB], writes=[ph.b(name, "sm")])
    ph.add("dve", lambda e: e.tensor_scalar(nm, sm, -1.0 / D, None, ALU.mult), reads=[ph.b(name, "sm")], writes=[ph.b(name, "nm")])
    ph.add("act", lambda e: e.activation(junk, r, AF.Square, bias=nm, scale=1.0, accum_out=ssq),
           reads=[rB, ph.b(name, "nm")], writes=[ph.b(name, "ssq"), ph.b(name, "junk")])
    ph.add("act", lambda e: e.activation(sd, ssq, AF.Sqrt, bias=eps_t, scale=1.0 / D), reads=[ph.b(name, "ssq")], writes=[ph.b(name, "sd")])
    ph.add("dve", lambda e: e.reciprocal(rstd, sd), reads=[ph.b(name, "sd")], writes=[ph.b(name, "rstd")])
    ph.add("dve", lambda e: e.tensor_scalar(r, r, nm, rstd, ALU.add, ALU.mult), reads=[ph.b(name, "nm"), ph.b(name, "rstd")], writes=[rB])
    ph.add(eng2, lambda e: e.tensor_tensor(r, r, g_b, ALU.mult), reads=[ph.b("lng")], writes=[rB])
    ph.add(eng2, lambda e: e.tensor_tensor(out_ap, r, b_b, ALU.add), reads=[rB, ph.b("lnb")], writes=[outB])


def phase_outproj(P, name, w_out, xin_rows, lng, lnb, w_router, xext, acc, affd):
    nc = P.nc
    dr = P.dram
    with contextlib.ExitStack() as st:
        ph = Phase(nc, name)
        cst, idb = load_consts(ph, nc, st, P)
        stage = [sb(st, nc, f"{name}_stg{i}", [128, 1024], F32) for i in range(2)]
        wo = load_weight_bf16(ph, nc, st, f"{name}_wo", w_out, 8, 1024, stage, [ph.b("stg", i) for i in range(2)])
        wr = sb(st, nc, f"{name}_wr", [128, 8, NE], F32)
        for hh in range(2):
            ph.dma("sp", wr[:, hh * 4:(hh + 1) * 4, :], w_router.rearrange("(c p) n -> p c n", p=128)[:, hh * 4:(hh + 1) * 4, :], writes=[ph.b("wr")])
        g_b = sb(st, nc, f"{name}_g", [128, D], F32)
        b_b = sb(st, nc, f"{name}_b", [128, D], F32)
        ph.dma("sp", g_b[:, :], bc(lng), writes=[ph.b("lng")])
        ph.dma("sp", b_b[:, :], bc(lnb), writes=[ph.b("lnb")])
        eps_t = sb(st, nc, f"{name}_eps", [128, 1], F32)
        ph.add("pool", lambda e: e.memset(eps_t[:, :], 1e-5), writes=[ph.b("eps")])
        oTs = [sb(st, nc, f"{name}_oT{i}", [128, 8, 512], BF16) for i in range(2)]
        xs = [sb(st, nc, f"{name}_xs{i}", [128, D], F32) for i in range(6)]
        r = [sb(st, nc, f"{name}_r{i}", [128, D], F32) for i in range(6)]
        xo = [sb(st, nc, f"{name}_xo{i}", [128, XW], F32) for i in range(3)]
        ax = [sb(st, nc, f"{name}_ax{i}", [128, D], F32) for i in range(3)]
        junk2 = [sb(st, nc, f"{name}_junk{i}", [128, D], BF16) for i in range(3)]
        small2 = [sb(st, nc, f"{name}_small{i}", [128, 8], F32) for i in range(3)]
        xnT2 = [sb(st, nc, f"{name}_xnT{i}", [128, 8, 128], F32) for i in range(3)]
        lg2 = [sb(st, nc, f"{name}_lg{i}", [128, 4], F32) for i in range(3)]
        ex2 = [sb(st, nc, f"{name}_ex{i}", [128, NE], F32) for i in range(3)]
        py = [ps(st, nc, f"{name}_py{i}", [128, 512], F32) for i in range(2)]
        pT2 = [ps(st, nc, f"{name}_pT{i}", [128, 512], F32) for i in range(3)]
        pl2 = [ps(st, nc, f"{name}_pl{i}", [128, 512], F32) for i in range(3)]
        oTv = dr["oT"][:, :].rearrange("(c p) t -> p c t", p=128)
        def stageA(i):
            s = i % 6
            g4, t4 = divmod(i, 4)
            gs = g4 % 2
            tok = slice(i * 128, (i + 1) * 128)
            B = lambda n: ph.b(n, s)
            if t4 == 0:
                for hh in range(2):
                    ph.dma("sp", oTs[gs][:, hh * 4:(hh + 1) * 4, :], oTv[:, hh * 4:(hh + 1) * 4, g4 * 512:(g4 + 1) * 512], writes=[ph.b("oTs", gs)])
            ph.dma("sp", xs[s][:, :], xin_rows(tok), writes=[B("xs")])
            for n in range(2):
                for c in range(8):
                    ph.add("pe", (lambda e, n=n, c=c, gs=gs, t4=t4: e.matmul(py[n][:, :], oTs[gs][:, c, t4 * 128:(t4 + 1) * 128],
                                                                             wo[:, c, n * 512:(n + 1) * 512], start=(c == 0), stop=(c == 7))),
                           reads=[ph.b("oTs", gs), ph.b(f"{name}_wo")], writes=[ph.b("py", n)])
                ph.add("dve", (lambda e, n=n, s=s: e.scalar_tensor_tensor(r[s][:, n * 512:(n + 1) * 512], xs[s][:, n * 512:(n + 1) * 512], ALPHA,
                                                                          py[n][:, :], ALU.mult, ALU.add)),
                       reads=[B("xs"), ph.b("py", n)], writes=[B("r")])

        def stageB(i):
            s = i % 3
            tok = slice(i * 128, (i + 1) * 128)
            B = lambda n: ph.b(n, s)
            junk, small, xnT, lg, ex, pTs_, pl = junk2[s], small2[s], xnT2[s], lg2[s], ex2[s], pT2[s], pl2[s]
            rr = r[i % 6][:, :]
            rB = ph.b("r", i % 6)
            sm, nm, ssq, sd, rstd = (small[:, k:k + 1] for k in range(5))
            ph.add("dve", lambda e: e.tensor_reduce(sm, rr, AX.X, ALU.add), reads=[rB], writes=[B("sm")]); yield
            ph.add("dve", lambda e: e.tensor_scalar(nm, sm, -1.0 / D, None, ALU.mult), reads=[B("sm")], writes=[B("nm")]); yield
            ph.add("act", lambda e: e.activation(junk[:, :], rr, AF.Square, bias=nm, scale=1.0, accum_out=ssq),
                   reads=[rB, B("nm")], writes=[B("ssq"), B("junk")]); yield
            ph.add("act", lambda e: e.activation(sd, ssq, AF.Sqrt, bias=eps_t[:, :], scale=1.0 / D), reads=[B("ssq")], writes=[B("sd")]); yield
            ph.add("dve", lambda e: e.reciprocal(rstd, sd), reads=[B("sd")], writes=[B("rstd")]); yield
            ph.add("dve", lambda e: e.tensor_scalar(rr, rr, nm, rstd, ALU.add, ALU.mult), reads=[B("nm"), B("rstd")], writes=[rB]); yield
            ph.add("pool", lambda e: e.tensor_tensor(rr, rr, g_b[:, :], ALU.mult), reads=[ph.b("lng")], writes=[rB]); yield
            ph.add("pool", lambda e: e.tensor_tensor(xo[s][:, 0:D], rr, b_b[:, :], ALU.add), reads=[rB, ph.b("lnb")], writes=[B("xo")]); yield
            ph.add("act", lambda e: e.mul(ax[s][:, :], xo[s][:, 0:D], ALPHA), reads=[B("xo")], writes=[B("ax")]); yield
            ph.dma("sp", acc[tok, :], ax[s][:, :], reads=[B("ax")]); yield
            for g in range(2):
                for c in range(4):
                    cc = g * 4 + c
                    ph.add("pe", (lambda e, c=c, cc=cc: e.transpose(pTs_[:, c * 128:(c + 1) * 128], xo[s][:, cc * 128:(cc + 1) * 128],
                                                                    cst[:, C_IDENT:C_IDENT + 128])),
                           reads=[B("xo"), ph.b("cst")], writes=[B("pT")])
                yield
                ph.add("act", (lambda e, g=g: e.copy(xnT[:, g * 4:(g + 1) * 4, :], pTs_[:, :].rearrange("p (c t) -> p c t", c=4))),
                       reads=[B("pT")], writes=[B("xnT")]); yield
            for c in range(8):
                ph.add("pe", (lambda e, c=c: e.matmul(pl[:, 0:NE], xnT[:, c, :], wr[:, c, :], start=(c == 0), stop=(c == 7))),
                       reads=[B("xnT"), ph.b("wr")], writes=[B("pl")])
            yield
            ph.add("dve", lambda e: e.tensor_reduce(lg[:, 0:1], pl[:, 0:NE], AX.X, ALU.max), reads=[B("pl")], writes=[B("lg0")]); yield
            ph.add("dve", lambda e: e.tensor_scalar(lg[:, 1:2], lg[:, 0:1], -1.0, None, ALU.mult), reads=[B("lg0")], writes=[B("lg1")]); yield
            ph.add("act", lambda e: e.activation(ex[:, :], pl[:, 0:NE], AF.Exp, bias=lg[:, 1:2], scale=1.0, accum_out=lg[:, 2:3]),
                   reads=[B("pl"), B("lg1")], writes=[B("ex"), B("lg2")]); yield
            ph.add("dve", lambda e: e.reciprocal(lg[:, 3:4], lg[:, 2:3]), reads=[B("lg2")], writes=[B("lg3")]); yield
            ph.add("dve", lambda e: e.tensor_scalar(xo[s][:, D:XW], ex[:, :], lg[:, 3:4], None, ALU.mult),
                   reads=[B("ex"), B("lg3")], writes=[B("xo")]); yield
            ph.dma("sp", xext[tok, :], xo[s][:, :], reads=[B("xo")]); yield
            ph.dma("sp", affd[tok, :], xo[s][:, D:XW], reads=[B("xo")]); yield

        def drive(gens):
            gens = list(gens)
            while gens:
                for g_ in list(gens):
                    try:
                        next(g_)
                    except StopIteration:
                        gens.remove(g_)

        groups = [list(range(k, min(k + 3, NT))) for k in range(0, NT, 3)]
        for t in groups[0]:
            stageA(t)
        for gi, grp in enumerate(groups):
            if gi + 1 < len(groups):
                for t in groups[gi + 1]:
                    stageA(t)
            drive([stageB(t) for t in grp])
        ph.emit()


def phase_route(P, name, xext, idx):
    nc = P.nc
    dr = P.dram
    with contextlib.ExitStack() as st:
        ph = Phase(nc, name)
        cst, idb = load_consts(ph, nc, st, P)
        aff = sb(st, nc, f"{name}_aff", [128, 64, NE], F32)
        cmp_ = sb(st, nc, f"{name}_cmp", [128, 64, NE], F32)
        mT = sb(st, nc, f"{name}_mT", [128, NE, 64], F32)
        Cl = sb(st, nc, f"{name}_Cl", [128, NE, 64], F32)
        Cg = sb(st, nc, f"{name}_Cg", [128, NE, 64], F32)
        lo = sb(st, nc, f"{name}_lo", [128, NE], F32)
        mid = sb(st, nc, f"{name}_mid", [128, NE], F32)
        sel = sb(st, nc, f"{name}_sel", [128, NE], F32)
        cntp = sb(st, nc, f"{name}_cntp", [128, NE], BF16)
        ones = sb(st, nc, f"{name}_ones", [128, 128], BF16)
        trib = sb(st, nc, f"{name}_trib", [128, 128], BF16)
        idxf = sb(st, nc, f"{name}_idxf", [128, NE, 8], F32)
        junk = sb(st, nc, f"{name}_junk", [128, S], BF16)
        crow = [sb(st, nc, f"{name}_crow{i}", [128, S], F32) for i in range(2)]
        tot = ps(st, nc, f"{name}_tot", [128, 512], F32)
        off = ps(st, nc, f"{name}_off", [128, 512], F32)
        ph.dma("sp", aff[:, :, :].rearrange("p i e -> p (i e)"), dr["affd"][:, :].rearrange("(p i) w -> p (i w)", p=128), writes=[ph.b("aff")])
        ph.add("dve", lambda e: e.memset(ones[:, :], 1.0), writes=[ph.b("ones")])
        ph.add("dve", lambda e: e.tensor_copy(trib[:, :], cst[:, C_TRI:C_TRI + 128]), reads=[ph.b("cst")], writes=[ph.b("trib")])
        ph.add("dve", lambda e: e.memset(lo[:, :], 0.0), writes=[ph.b("lo")])
        for k in range(30):
            w = 0.5 ** (k + 1)
            ph.add("dve", (lambda e, w=w: e.tensor_scalar(mid[:, :], lo[:, :], w, None, ALU.add)), reads=[ph.b("lo")], writes=[ph.b("mid")])
            ph.add("dve", lambda e: e.tensor_tensor(cmp_[:, :, :], aff[:, :, :], mid[:, :].unsqueeze(1).to_broadcast([128, 64, NE]), ALU.is_ge),
                   reads=[ph.b("aff"), ph.b("mid")], writes=[ph.b("cmp")])
            ph.add("dve", lowp(nc, lambda e: e.tensor_reduce(cntp[:, :], cmp_[:, :, :].rearrange("p i e -> p e i"), AX.X, ALU.add)),
                   reads=[ph.b("cmp")], writes=[ph.b("cntp")])
            ph.add("pe", lambda e: e.matmul(tot[:, 0:NE], ones[:, :], cntp[:, :], start=True, stop=True),
                   reads=[ph.b("ones"), ph.b("cntp")], writes=[ph.b("tot")])
            ph.add("dve", lambda e: e.tensor_scalar(sel[:, :], tot[:, 0:NE], CAP - 0.5, None, ALU.is_ge), reads=[ph.b("tot")], writes=[ph.b("sel")])
            ph.add("dve", (lambda e, w=w: e.scalar_tensor_tensor(lo[:, :], sel[:, :], w, lo[:, :], ALU.mult, ALU.add)),
                   reads=[ph.b("sel")], writes=[ph.b("lo")])
        ph.add("dve", lambda e: e.tensor_tensor(mT[:, :, :].rearrange("p e i -> p i e"), aff[:, :, :],
                                                lo[:, :].unsqueeze(1).to_broadcast([128, 64, NE]), ALU.is_ge),
               reads=[ph.b("aff"), ph.b("lo")], writes=[ph.b("mT")])
        ph.add("dve", lambda e: e.tensor_tensor_scan(Cl[:, :, :].rearrange("p e i -> p (e i)"), cst[:, C_KEEP:C_KEEP + NE * 64],
                                                     mT[:, :, :].rearrange("p e i -> p (e i)"), 0.0, ALU.mult, ALU.add),
               reads=[ph.b("mT"), ph.b("cst")], writes=[ph.b("Cl")])
        ph.add("dve", lambda e: e.tensor_copy(cntp[:, :], Cl[:, :, 63]), reads=[ph.b("Cl")], writes=[ph.b("cntp")])
        ph.add("pe", lambda e: e.matmul(off[:, 0:NE], trib[:, :], cntp[:, :], start=True, stop=True),
               reads=[ph.b("trib"), ph.b("cntp")], writes=[ph.b("off")])
        ph.add("dve", lambda e: e.tensor_copy(sel[:, :], off[:, 0:NE]), reads=[ph.b("off")], writes=[ph.b("sel")])
        ph.add("dve", lambda e: e.tensor_tensor(Cg[:, :, :], Cl[:, :, :], sel[:, :].unsqueeze(2).to_broadcast([128, NE, 64]), ALU.add),
               reads=[ph.b("Cl"), ph.b("sel")], writes=[ph.b("Cg")])
        cd = dr[f"{name}_Cd"]
        for e4 in range(4):
            ph.dma("sp", cd[e4 * 4:(e4 + 1) * 4, :].rearrange("e (p i) -> p e i", p=128), Cg[:, e4 * 4:(e4 + 1) * 4, :], reads=[ph.b("Cg")], writes=[ph.b("Cd", e4)])
        junk2 = sb(st, nc, f"{name}_junk2", [128, S], BF16)
        svh = sb(st, nc, f"{name}_svh", [128, 8], F32)
        ph.add("dve", lambda e: e.tensor_scalar(svh[:, :], cst[:, C_SVEC:C_SVEC + 8], 0.5, None, ALU.add), reads=[ph.b("cst")], writes=[ph.b("svh")])
        for e_ in range(NE):
            s = e_ % 2
            ph.dma("sp", crow[s][:, :], bc(cd[e_, :]), reads=[ph.b("Cd", e_ // 4)], writes=[ph.b("crow", s)])
            for j in range(8):
                if j % 2 == 0:
                    ph.add("dve", (lambda e, s=s, e_=e_, j=j: e.tensor_scalar(junk[:, :], crow[s][:, :], cst[:, C_SVEC + j:C_SVEC + j + 1], None,
                                                                               ALU.is_le, ALU.add, accum_out=idxf[:, e_, j:j + 1])),
                           reads=[ph.b("crow", s), ph.b("cst")], writes=[ph.b("idxf", "dve"), ph.b("junk")])
                else:
                    ph.add("act", (lambda e, s=s, e_=e_, j=j: e.activation(junk2[:, :], crow[s][:, :], AF.Sign, bias=svh[:, j:j + 1], scale=-1.0,
                                                                            accum_out=idxf[:, e_, j:j + 1])),
                           reads=[ph.b("crow", s), ph.b("svh")], writes=[ph.b("idxf", "act"), ph.b("junk2")])
        i4 = idxf[:, :, :].rearrange("p e (j t) -> p e j t", t=2)[:, :, :, 1]
        ph.add("dve", lambda e: e.tensor_scalar(i4, i4, 0.5, float(S // 2), ALU.mult, ALU.add), reads=[ph.b("idxf", "act")], writes=[ph.b("idxf", "act")])
        ph.add("dve", lambda e: e.tensor_copy(idx[:, :, :], idxf[:, :, :]), reads=[ph.b("idxf", "act"), ph.b("idxf", "dve")], writes=[ph.b("idx")])
        if "dbg_idx" in dr and name == "p5":
            ph.dma("sp", dr["dbg_idx"][:, :], idx[:, :, :].rearrange("p e j -> p (e j)"), reads=[ph.b("idx")])
            ph.dma("sp", dr["dbg_lo"][:, :], lo[:, :], reads=[ph.b("lo")])
        ph.emit()


def phase_experts(P, name, l, xext, acc, idx):
    nc = P.nc
    dr = P.dram
    with contextlib.ExitStack() as st:
        ph = Phase(nc, name)
        cst, idb = load_consts(ph, nc, st, P)
        NST = 3
        stage = [sb(st, nc, f"{name}_stg{i}", [128, 2048], F32) for i in range(NST)]
        wts = {k: [sb(st, nc, f"{name}_{k}{i}", [128, 8, 1024], BF16) for i in range(2)] for k in ("wg", "wu", "wd")}
        xg = [sb(st, nc, f"{name}_xg{i}", [128, XW], F32) for i in range(2)]
        xgT2 = [sb(st, nc, f"{name}_xgT{i}", [128, 8, 1024], BF16) for i in range(2)]
        hT = sb(st, nc, f"{name}_hT", [128, 8, 1024], BF16)
        gt = sb(st, nc, f"{name}_gt", [128, 2, 8], F32)
        sg = [sb(st, nc, f"{name}_sg{i}", [128, 512], F32) for i in range(2)]
        yo = [sb(st, nc, f"{name}_yo{i}", [128, D], F32) for i in range(4)]
        pT = [ps(st, nc, f"{name}_pT{i}", [128, 512], F32) for i in range(2)]
        pg = [ps(st, nc, f"{name}_pg{i}", [128, 512], F32) for i in range(2)]
        pu = [ps(st, nc, f"{name}_pu{i}", [128, 512], F32) for i in range(2)]
        py = [ps(st, nc, f"{name}_py{i}", [128, 512], F32) for i in range(2)]
        srcs = {"wg": dr["moe_w_gate"], "wu": dr["moe_w_up"], "wd": dr["moe_w_down"]}
        stg_cnt = [0]
        cast_engs = ("pool", "dve", "pool", "act")

        def load_w_piece(e_, k, c2):
            sl = e_ % 2
            if True:
                wv = srcs[k][l, e_, :, :].rearrange("(c p) n -> p c n", p=128)
                if True:
                    s = stg_cnt[0] % NST
                    eng = cast_engs[stg_cnt[0] % len(cast_engs)]
                    stg_cnt[0] += 1
                    ph.dma("sp", stage[s][:, :].rearrange("p (c n) -> p c n", c=2), wv[:, 2 * c2:2 * c2 + 2, :], writes=[ph.b("stg", s)])
                    dst = wts[k][sl][:, 2 * c2:2 * c2 + 2, :]
                    srcv = stage[s][:, :].rearrange("p (c n) -> p c n", c=2)
                    if eng == "act":
                        ph.add("act", (lambda e, dst=dst, srcv=srcv: e.copy(dst, srcv)), reads=[ph.b("stg", s)], writes=[ph.b(k, sl)])
                    else:
                        ph.add(eng, (lambda e, dst=dst, srcv=srcv: e.tensor_copy(dst, srcv)), reads=[ph.b("stg", s)], writes=[ph.b(k, sl)])

        PIECES = [(k, c2) for k in ("wg", "wu", "wd") for c2 in range(4)]

        def load_w(e_):
            for (k, c2) in PIECES:
                load_w_piece(e_, k, c2)

        load_w(0)
        prev_sc = []

        def emit_gather(e_, j):
            s = (e_ * 8 + j) % 2
            ph.add("pool", (lambda e, s=s, e_=e_, j=j: e.indirect_dma_start(
                out=xg[s][:, :], out_offset=None, in_=xext[:, :],
                in_offset=bass.IndirectOffsetOnAxis(ap=idx[:, e_, j:j + 1], axis=0))),
                reads=[ph.b("idx")], writes=[ph.b("xg", s)], dma=True)

        def emit_trans(e_, j):
            s = (e_ * 8 + j) % 2
            xs_ = e_ % 2
            emit_xT(ph, xg[s], ph.b("xg", s), xgT2[xs_][:, :, j * 128:(j + 1) * 128], ph.b("xgT", xs_), pT, [ph.b("pT", 0), ph.b("pT", 1)], cst)
            ph.add("act", (lambda e, s=s, e_=e_, j=j, xs_=xs_: e.copy(gt[:, xs_, j:j + 1], xg[s][:, D + e_:D + e_ + 1])),
                   reads=[ph.b("xg", s)], writes=[ph.b("gt", xs_)])

        emit_gather(0, 0)
        for j in range(8):
            if j + 1 < 8:
                emit_gather(0, j + 1)
            emit_trans(0, j)
        for e_ in range(NE):
            sl = e_ % 2
            wg, wu, wd = wts["wg"][sl], wts["wu"][sl], wts["wd"][sl]
            gsl = e_ % 2
            xgT = xgT2[sl]
            xgB = ph.b("xgT", sl)
            if e_ + 1 < NE:
                emit_gather(e_ + 1, 0)
            n = 0
            for fc in range(8):
                if e_ + 1 < NE and fc + 1 < 8:
                    emit_gather(e_ + 1, fc + 1)
                if e_ + 1 < NE:
                    for pi in range((fc * 12) // 8, ((fc + 1) * 12) // 8):
                        load_w_piece(e_ + 1, *PIECES[pi])
                for half in range(2):
                    b = n % 2
                    n += 1
                    for (wt, pt_, nm, kn) in ((wg, pg, "pg", "wg"), (wu, pu, "pu", "wu")):
                        for c in range(8):
                            ph.add("pe", (lambda e, wt=wt, pt_=pt_, b=b, c=c, fc=fc, half=half, xgT=xgT: e.matmul(
                                pt_[b][:, :], wt[:, c, fc * 128:(fc + 1) * 128], xgT[:, c, half * 512:(half + 1) * 512],
                                start=(c == 0), stop=(c == 7))),
                                reads=[ph.b(kn, sl), xgB], writes=[ph.b(nm, b)])
                    ph.add("act", (lambda e, b=b: e.activation(sg[b][:, :], pg[b][:, :], AF.Silu)), reads=[ph.b("pg", b)], writes=[ph.b("sg", b)])
                    ph.add("dve", (lambda e, b=b, fc=fc, half=half: e.tensor_tensor(hT[:, fc, half * 512:(half + 1) * 512], sg[b][:, :], pu[b][:, :], ALU.mult)),
                           reads=[ph.b("sg", b), ph.b("pu", b)], writes=[ph.b("hT")])
                if e_ + 1 < NE:
                    emit_trans(e_ + 1, fc)
            new_sc = []
            for j in range(8):
                ys = (e_ * 8 + j) % 4
                for half in range(2):
                    for fc in range(8):
                        ph.add("pe", (lambda e, half=half, fc=fc, j=j, wd=wd: e.matmul(py[half][:, :], hT[:, fc, j * 128:(j + 1) * 128],
                                                                                wd[:, fc, half * 512:(half + 1) * 512], start=(fc == 0), stop=(fc == 7))),
                               reads=[ph.b("hT"), ph.b("wd", sl)], writes=[ph.b("py", half)])
                    eng = "act" if half == 0 else "dve"
                    if eng == "act":
                        ph.add("act", (lambda e, ys=ys, half=half, gsl=gsl, j=j: e.mul(yo[ys][:, half * 512:(half + 1) * 512], py[half][:, :], gt[:, gsl, j:j + 1])),
                               reads=[ph.b("py", half), ph.b("gt", gsl)], writes=[ph.b("yo", ys)])
                    else:
                        ph.add("dve", (lambda e, ys=ys, half=half, gsl=gsl, j=j: e.tensor_scalar(yo[ys][:, half * 512:(half + 1) * 512], py[half][:, :],
                                                                                                  gt[:, gsl, j:j + 1], None, ALU.mult)),
                               reads=[ph.b("py", half), ph.b("gt", gsl)], writes=[ph.b("yo", ys)])
                if "dbg_xg" in dr and name == "p6" and e_ == 0 and j == 0:
                    ph.dma("sp", dr["dbg_yo"][:, :], yo[ys][:, :], reads=[ph.b("yo", ys)])
                    ph.dma("sp", dr["dbg_hT"][:, :], hT[:, :, :].rearrange("p c t -> p (c t)"), reads=[ph.b("hT")])
                op = ph.add("pool", (lambda e, ys=ys, e_=e_, j=j: e.indirect_dma_start(
                    out=acc[:, :], out_offset=bass.IndirectOffsetOnAxis(ap=idx[:, e_, j:j + 1], axis=0),
                    in_=yo[ys][:, :], in_offset=None, compute_op=ALU.add)),
                    reads=[ph.b("idx"), ph.b("yo", ys)], writes=[], dma=True)
                for d in prev_sc:
                    if d not in op.deps:
                        op.deps.append(d)
                new_sc.append(op)
            prev_sc = new_sc
        ph.emit()


def phase_ln(P, name, acc, lng, lnb, out_rows):
    nc = P.nc
    with contextlib.ExitStack() as st:
        ph = Phase(nc, name)
        g_b = sb(st, nc, f"{name}_g", [128, D], F32)
        b_b = sb(st, nc, f"{name}_b", [128, D], F32)
        ph.dma("sp", g_b[:, :], bc(lng), writes=[ph.b("lng")])
        ph.dma("sp", b_b[:, :], bc(lnb), writes=[ph.b("lnb")])
        eps_t = sb(st, nc, f"{name}_eps", [128, 1], F32)
        ph.add("pool", lambda e: e.memset(eps_t[:, :], 1e-5), writes=[ph.b("eps")])
        r = [sb(st, nc, f"{name}_r{i}", [128, D], F32) for i in range(3)]
        o = [sb(st, nc, f"{name}_o{i}", [128, D], F32) for i in range(3)]
        junk = sb(st, nc, f"{name}_junk", [128, D], BF16)
        small = sb(st, nc, f"{name}_small", [128, 8], F32)
        for i in range(NT + 1):
            if i < NT:
                s = i % 3
                tok = slice(i * 128, (i + 1) * 128)
                ph.dma("sp", r[s][:, :], acc[tok, :], writes=[ph.b("r", s)])
            if i >= 1:
                s = (i - 1) % 3
                tok = slice((i - 1) * 128, i * 128)
                emit_ln(ph, nc, name, r[s][:, :], ph.b("r", s), o[s][:, :], ph.b("o", s), g_b[:, :], b_b[:, :], junk[:, :], small[:, :], eps_t[:, :])
                ph.dma("sp", out_rows(tok), o[s][:, :], reads=[ph.b("o", s)])
        ph.emit()


def phase_proj_mla(P, xin_rows):
    nc = P.nc
    dr = P.dram
    name = "p8"
    with contextlib.ExitStack() as st:
        ph = Phase(nc, name)
        cst, idb = load_consts(ph, nc, st, P)
        stage = [sb(st, nc, f"p8_stg{i}", [128, 2048], F32) for i in range(2)]
        sB = [ph.b("stg", i) for i in range(2)]
        wdn = load_weight_bf16(ph, nc, st, "p8_wdn", dr["mla_w_down"][0, :, :], 8, 416, stage, sB)
        wuq = load_weight_bf16(ph, nc, st, "p8_wuq", dr["mla_w_uq"][0, :, :], 2, 1536, stage, sB)
        wukv = load_weight_bf16(ph, nc, st, "p8_wukv", dr["mla_w_ukv"][0, :, :], 1, 2048, stage, sB)
        gq = sb(st, nc, "p8_gq", [128, 256], F32)
        gk = sb(st, nc, "p8_gk", [128, 128], F32)
        ph.dma("sp", gq[:, :], bc(dr["mla_q_norm"][0, :]), writes=[ph.b("gq")])
        ph.dma("sp", gk[:, :], bc(dr["mla_kv_norm"][0, :]), writes=[ph.b("gk")])
        eps = sb(st, nc, "p8_eps", [128, 1], F32)
        ph.add("pool", lambda e: e.memset(eps[:, :], 1e-6), writes=[ph.b("eps")])
        xs = [sb(st, nc, f"p8_xs{i}", [128, D], F32) for i in range(2)]
        tb = [sb(st, nc, f"p8_tb{i}", [128, 64], F32) for i in range(4)]
        xT = [sb(st, nc, f"p8_xT{i}", [128, 8, 128], BF16) for i in range(2)]
        pr2 = [sb(st, nc, f"p8_pr{i}", [128, 416], F32) for i in range(4)]
        junk_2 = [sb(st, nc, f"p8_junk{i}", [128, 256], BF16) for i in range(2)]
        sm_2 = [sb(st, nc, f"p8_sm{i}", [128, 8], F32) for i in range(2)]
        cn_2 = [sb(st, nc, f"p8_cn{i}", [128, 384], BF16) for i in range(2)]
        cnT_2 = [sb(st, nc, f"p8_cnT{i}", [128, 3, 128], BF16) for i in range(2)]
        qsb_2 = [sb(st, nc, f"p8_qs{i}", [128, 1536], F32) for i in range(2)]
        kvs_2 = [sb(st, nc, f"p8_kvs{i}", [128, 2048], F32) for i in range(2)]
        qb_2 = [sb(st, nc, f"p8_qb{i}", [128, 16, 96], BF16) for i in range(2)]
        kb_2 = [sb(st, nc, f"p8_kb{i}", [128, 16, 96], BF16) for i in range(2)]
        kr_2 = [sb(st, nc, f"p8_kr{i}", [128, 32], BF16) for i in range(2)]
        t1_2 = [sb(st, nc, f"p8_t1{i}", [128, 512], F32) for i in range(2)]
        t2_2 = [sb(st, nc, f"p8_t2{i}", [128, 512], F32) for i in range(2)]
        t3_2 = [sb(st, nc, f"p8_t3{i}", [128, 32], F32) for i in range(2)]
        t4_2 = [sb(st, nc, f"p8_t4{i}", [128, 32], F32) for i in range(2)]
        stq = [sb(st, nc, f"p8_stq{i}", [128, 16, 128], BF16) for i in range(2)]
        stk = [sb(st, nc, f"p8_stk{i}", [128, 16, 128], BF16) for i in range(2)]
        vst = [sb(st, nc, f"p8_vst{i}", [128, 16, 128], BF16) for i in range(2)]
        pT = [ps(st, nc, f"p8_pT{i}", [128, 512], F32) for i in range(2)]
        pp = ps(st, nc, "p8_pp", [128, 512], F32)
        pc = ps(st, nc, "p8_pc", [128, 1024], BF16)
        pb_ = [ps(st, nc, f"p8_pb{i}", [128, 512], F32) for i in range(4)]
        for s in range(2):
            ph.add("pool", (lambda e, s=s: e.memset(vst[s][:, :, :], 1.0)), writes=[ph.b("vst", s)])
        def stageA(i):
            s = i % 2
            tok = slice(i * 128, (i + 1) * 128)
            B = lambda n: ph.b(n, s)
            ph.dma("sp", xs[s][:, :], xin_rows(tok), writes=[B("xs")])
            ph.dma("sp", tb[i % 4][:, :], dr["tab"][tok, 256:320], writes=[ph.b("tb", i % 4)])
            emit_xT(ph, xs[s], B("xs"), xT[s], B("xT"), pT, [ph.b("pT", 0), ph.b("pT", 1)], cst)
            for c in range(8):
                ph.add("pe", (lambda e, c=c, s=s: e.matmul(pp[:, 0:416], xT[s][:, c, :], wdn[:, c, :], start=(c == 0), stop=(c == 7))),
                       reads=[B("xT"), ph.b("p8_wdn")], writes=[ph.b("pp")])
            ph.add("act", (lambda e, s4=i % 4: e.copy(pr2[s4][:, :], pp[:, 0:416])), reads=[ph.b("pp")], writes=[ph.b("pr", i % 4)])

        def stageB(i):
            s = i % 2
            tok = slice(i * 128, (i + 1) * 128)
            B = lambda n: pbuf(n, s)
            pr = pr2[i % 4]
            prB = ph.b("pr", i % 4)
            tbt, tbB = tb[i % 4], ph.b("tb", i % 4)
            junk, sm, cn, cnT, qsb, kvs, qb, kb, kr, t1, t2, t3, t4 = (x[s] for x in (junk_2, sm_2, cn_2, cnT_2, qsb_2, kvs_2, qb_2, kb_2, kr_2, t1_2, t2_2, t3_2, t4_2))
            PER = {"junk", "sm0", "sm1", "sm2", "sm3", "sm4", "cn", "cnT", "qs", "kvs", "qb", "kb", "kr", "t1", "t2", "t3", "t4"}

            def pbuf(*k):
                return ph.b(*k, s) if (len(k) == 1 and k[0] in PER) else ph.b(*k)
            ph.add("act", lambda e: e.activation(junk[:, 0:256], pr[:, 0:256], AF.Square, accum_out=sm[:, 0:1]), reads=[prB], writes=[pbuf("sm0"), pbuf("junk")])
            ph.add("act", lambda e: e.activation(junk[:, 0:128], pr[:, 256:384], AF.Square, accum_out=sm[:, 1:2]), reads=[prB], writes=[pbuf("sm1"), pbuf("junk")])
            ph.add("act", lambda e: e.activation(sm[:, 2:3], sm[:, 0:1], AF.Sqrt, bias=eps[:, :], scale=1.0 / 256), reads=[pbuf("sm0"), pbuf("eps")], writes=[pbuf("sm2")])
            ph.add("act", lambda e: e.activation(sm[:, 3:4], sm[:, 1:2], AF.Sqrt, bias=eps[:, :], scale=1.0 / 128), reads=[pbuf("sm1"), pbuf("eps")], writes=[pbuf("sm3")])
            ph.add("dve", lambda e: e.reciprocal(sm[:, 4:6], sm[:, 2:4]), reads=[pbuf("sm2"), pbuf("sm3")], writes=[pbuf("sm4")])
            ph.add("dve", lambda e: e.scalar_tensor_tensor(cn[:, 0:256], pr[:, 0:256], sm[:, 4:5], gq[:, :], ALU.mult, ALU.mult),
                   reads=[prB, pbuf("sm4"), pbuf("gq")], writes=[pbuf("cn")])
            ph.add("dve", lambda e: e.scalar_tensor_tensor(cn[:, 256:384], pr[:, 256:384], sm[:, 5:6], gk[:, :], ALU.mult, ALU.mult),
                   reads=[prB, pbuf("sm4"), pbuf("gk")], writes=[pbuf("cn")])
            for c in range(3):
                ph.add("pe", (lambda e, c=c: e.transpose(pc[:, c * 128:(c + 1) * 128], cn[:, c * 128:(c + 1) * 128], idb[:, :])),
                       reads=[pbuf("cn"), pbuf("idb")], writes=[pbuf("pc")])
            ph.add("act", lambda e: e.copy(cnT[:, :, :], pc[:, 0:384].rearrange("p (c t) -> p c t", c=3)), reads=[pbuf("pc")], writes=[pbuf("cnT")])
            yield
            for n in range(3):
                for c in range(2):
                    ph.add("pe", (lambda e, n=n, c=c: e.matmul(pb_[n][:, :], cnT[:, c, :], wuq[:, c, n * 512:(n + 1) * 512], start=(c == 0), stop=(c == 1))),
                           reads=[pbuf("cnT"), pbuf("p8_wuq")], writes=[pbuf("pb", n)])
                ph.add("act", (lambda e, n=n: e.copy(qsb[:, n * 512:(n + 1) * 512], pb_[n][:, :])), reads=[pbuf("pb", n)], writes=[pbuf("qs")])
            q3 = qsb[:, :].rearrange("p (h d) -> p h d", h=16)
            yield
            ph.add("pool", lambda e: e.tensor_copy(qb[:, :, 0:64], q3[:, :, 0:64]), reads=[pbuf("qs")], writes=[pbuf("qb")])
            rope3(ph, "dve", qb[:, :, 64:96], q3[:, :, 64:96], tbt[:, 0:32], tbt[:, 32:64], 16, 32, 1, t1[:, :], t2[:, :],
                  [pbuf("qs"), tbB], [pbuf("t1"), pbuf("t2")], pbuf("qb"))
            yield
            for n in range(4):
                bn = (3 + n) % 4
                ph.add("pe", (lambda e, n=n, bn=bn: e.matmul(pb_[bn][:, :], cnT[:, 2, :], wukv[:, 0, n * 512:(n + 1) * 512], start=True, stop=True)),
                       reads=[pbuf("cnT"), pbuf("p8_wukv")], writes=[pbuf("pb", bn)])
                ph.add("act", (lambda e, n=n, bn=bn: e.copy(kvs[:, n * 512:(n + 1) * 512], pb_[bn][:, :])), reads=[pbuf("pb", bn)], writes=[pbuf("kvs")])
            kv3 = kvs[:, :].rearrange("p (h d) -> p h d", h=16)
            yield
            ph.add("pool", lambda e: e.tensor_copy(kb[:, :, 0:64], kv3[:, :, 0:64]), reads=[pbuf("kvs")], writes=[pbuf("kb")])
            ph.add("pool", (lambda e, s=s: e.tensor_copy(vst[s][:, :, 0:64], kv3[:, :, 64:128])), reads=[pbuf("kvs")], writes=[B("vst")])
            for h4 in range(4):
                ph.dma("sp", dr["v_C"][h4 * 4:(h4 + 1) * 4, :, i, :].rearrange("k p f -> p k f"), vst[s][:, h4 * 4:(h4 + 1) * 4, :], reads=[B("vst")])
            yield
            rope3(ph, "dve", kr[:, :].unsqueeze(1), pr[:, 384:416].unsqueeze(1), tbt[:, 0:32], tbt[:, 32:64], 1, 32, 1, t3[:, :], t4[:, :],
                  [prB, tbB], [pbuf("t3"), pbuf("t4")], pbuf("kr"))
            ph.add("dve", lambda e: e.tensor_copy(kb[:, :, 64:96], kr[:, :].unsqueeze(1).to_broadcast([128, 16, 32])),
                   reads=[pbuf("kr")], writes=[pbuf("kb")])
            yield
            for (src, sn, stt, stn, dst) in ((qb, "qb", stq, "stq", dr["qT_C"]), (kb, "kb", stk, "stk", dr["kT_C"])):
                for hh in range(2):
                    bnk = [pb_[0], pb_[1]] if sn == "qb" else [pb_[2], pb_[3]]
                    bnn = [0, 1] if sn == "qb" else [2, 3]
                    ptile = bnk[hh]
                    pv = ptile[:, :].bitcast(BF16)
                    for h8 in range(8):
                        h = hh * 8 + h8
                        ph.add("pe", (lambda e, pv=pv, h8=h8, h=h, src=src: e.transpose(pv[0:96, h8 * 128:(h8 + 1) * 128], src[:, h, :], idb[:, :])),
                               reads=[pbuf(sn), pbuf("idb")], writes=[pbuf("pb", bnn[hh])])
                    ph.add("act", (lambda e, pv=pv, hh=hh, stt=stt, s=s: e.copy(stt[s][0:96, hh * 8:(hh + 1) * 8, :],
                                                                                 pv[0:96, :].rearrange("p (c t) -> p c t", c=8))),
                           reads=[pbuf("pb", bnn[hh])], writes=[B(stn)])
                for h4 in range(4):
                    ph.dma("sp", dst[h4 * 4:(h4 + 1) * 4, :, tok].rearrange("h p t -> p h t"), stt[s][0:96, h4 * 4:(h4 + 1) * 4, :], reads=[B(stn)])
                yield
        def drive(gens):
            gens = list(gens)
            while gens:
                for g_ in list(gens):
                    try:
                        next(g_)
                    except StopIteration:
                        gens.remove(g_)

        stageA(0)
        stageA(1)
        for i in range(0, NT, 2):
            if i + 2 < NT:
                stageA(i + 2)
                stageA(i + 3)
            drive([stageB(i), stageB(i + 1)])
        ph.emit()


PHASES = ["p1", "p2", "p3", "p4", "p5", "p6", "p7", "p8", "p9", "p10", "p11", "p12", "p13"]


def build(upto="p13", dbg=()):
    P = Prog(dbg)
    nc = P.nc
    x = P.din("x", [S, D])
    P.din("w_in", [D, 1536])
    P.din("ab_q_norm", [1, 64])
    P.din("ab_k_norm", [1, 64])
    P.din("ab_sink", [1, 8])
    P.din("ab_w_out", [D, D])
    P.din("mla_w_down", [1, D, 416])
    P.din("mla_q_norm", [1, 256])
    P.din("mla_kv_norm", [1, 128])
    P.din("mla_w_uq", [1, 256, 1536])
    P.din("mla_w_ukv", [1, 128, 2048])
    P.din("mla_w_out", [D, D])
    for n in ("ln_mix_g", "ln_mix_b", "ln_ffn_g", "ln_ffn_b"):
        P.din(n, [2, D])
    P.din("moe_router", [2, D, NE])
    for n in ("moe_w_gate", "moe_w_up", "moe_w_down"):
        P.din(n, [2, NE, D, D])
    P.din("tab", [S, 320])
    P.din("cst", [128, C_W])
    P.din("wmask", [128, 3072])
    P.dtmp("qT_A", [4, 128, S], BF16)
    P.dtmp("qT_B", [4, 128, S], BF16)
    P.dtmp("kT_A", [2, 128, S], BF16)
    P.dtmp("kT_B", [2, 128, S], BF16)
    P.dtmp("v_all", [4, 128, NT, 128], BF16)
    P.dtmp("oT", [D, S], BF16)
    P.dtmp("x1ext", [S, XW])
    P.dtmp("acc", [S, D])
    P.dtmp("affd", [S, NE])
    P.dtmp("p5_Cd", [NE, S])
    P.dtmp("p11_Cd", [NE, S])
    P.dtmp("x2ext", [S, XW])
    P.dtmp("x3ext", [S, XW])
    P.dtmp("acc2", [S, D])
    P.dtmp("qT_C", [16, 96, S], BF16)
    P.dtmp("kT_C", [16, 96, S], BF16)
    P.dtmp("v_C", [16, 128, NT, 128], BF16)
    out = P.dtmp("out", [S, D], out=True)
    if "dbg_idx" in dbg:
        P.dtmp("dbg_idx", [128, 128], I32)
        P.dtmp("dbg_lo", [128, NE])
    if "dbg_xg" in dbg:
        P.dtmp("dbg_xg", [128, XW])
        P.dtmp("dbg_yo", [128, D])
        P.dtmp("dbg_hT", [128, 8192], BF16)
    if "dbg_q9" in dbg:
        P.dtmp("dbg_q9", [128, S], BF16)
        P.dtmp("dbg_k9", [128, S], BF16)
    if "dbg_rec" in dbg:
        P.dtmp("dbg_rec", [64, 512])
        P.dtmp("dbg_num", [128, 512])
    if "dbgv0" in dbg:
        P.dtmp("dbgv0", [128, NT, 128], BF16)
        P.dtmp("dbgv1", [128, NT, 128], BF16)
    dr = P.dram
    last = PHASES.index(upto)
    on = lambda p: PHASES.index(p) <= last
    with contextlib.ExitStack() as st:
        idx = sb(st, nc, "idx_all", [128, NE, 8], I32)
        if on("p1"):
            phase_proj0(P, x)
        if on("p2"):
            jobs = []
            for h in range(8):
                j, half, kv = h // 2, h % 2, h // 4
                jobs.append(dict(qsrc=(("qA", j), dr["qT_A"][j, :, :], 128), ksrc=(("kA", kv), dr["kT_A"][kv, :, :], 128),
                                 vsrc=(("vA", kv), dr["v_all"][kv, :, :, :]), pb=half * 64, K=64, out=dr["oT"][h * 64:(h + 1) * 64, :]))
            phase_dense_attn(P, "p2", jobs, 64 ** -0.5)
        if on("p3"):
            jobs = []
            for h in range(8):
                j, half, kv = h // 2, h % 2, h // 4
                jobs.append(dict(qsrc=(("qB", j), dr["qT_B"][j, :, :], 128), ksrc=(("kB", kv), dr["kT_B"][kv, :, :], 128),
                                 vsrc=(("vB", kv), dr["v_all"][2 + kv, :, :, :]), pb=half * 64, K=64, h=h,
                                 out=dr["oT"][512 + h * 64:512 + (h + 1) * 64, :]))
            phase_dense_attn(P, "p3", jobs, 64 ** -0.5, win=True)
        if on("p4"):
            phase_outproj(P, "p4", dr["ab_w_out"][:, :], lambda tok: x[tok, :], dr["ln_mix_g"][0, :], dr["ln_mix_b"][0, :],
                          dr["moe_router"][0, :, :], dr["x1ext"], dr["acc"], dr["affd"])
        if on("p5"):
            phase_route(P, "p5", dr["x1ext"], idx)
        if on("p6"):
            phase_experts(P, "p6", 0, dr["x1ext"], dr["acc"], idx)
        if on("p7"):
            phase_ln(P, "p7", dr["acc"], dr["ln_ffn_g"][0, :], dr["ln_ffn_b"][0, :], lambda tok: dr["x2ext"][tok, 0:D])
        if on("p8"):
            phase_proj_mla(P, lambda tok: dr["x2ext"][tok, 0:D])
        if on("p9"):
            jobs = []
            for h in range(16):
                jobs.append(dict(qsrc=(("qC", h), dr["qT_C"][h, :, :], 96), ksrc=(("kC", h), dr["kT_C"][h, :, :], 96),
                                 vsrc=(("vC", h), dr["v_C"][h, :, :, :]), pb=0, K=96, out=dr["oT"][h * 64:(h + 1) * 64, :]))
            phase_dense_attn(P, "p9", jobs, 96 ** -0.5)
        if on("p10"):
            phase_outproj(P, "p10", dr["mla_w_out"][:, :], lambda tok: dr["x2ext"][tok, 0:D], dr["ln_mix_g"][1, :], dr["ln_mix_b"][1, :],
                          dr["moe_router"][1, :, :], dr["x3ext"], dr["acc2"], dr["affd"])
        if on("p11"):
            phase_route(P, "p11", dr["x3ext"], idx)
        if on("p12"):
            phase_experts(P, "p12", 1, dr["x3ext"], dr["acc2"], idx)
        if on("p13"):
            phase_ln(P, "p13", dr["acc2"], dr["ln_ffn_g"][1, :], dr["ln_ffn_b"][1, :], lambda tok: out[tok, :])
    return P


_PERM = np.concatenate([np.arange(0, 512), np.arange(512, 640), np.arange(768, 1280), np.arange(1280, 1408),
                        np.arange(640, 768), np.arange(1408, 1536)])


def make_in_maps(inputs, ncores=8):
    tab, cst = make_consts()
    f = lambda a: np.ascontiguousarray(np.asarray(a, dtype=np.float32))
    shared = {
        "w_in": f(np.asarray(inputs["ab_w_in"])[0][:, _PERM]),
        "ab_q_norm": f(inputs["ab_q_norm"]), "ab_k_norm": f(inputs["ab_k_norm"]), "ab_sink": f(inputs["ab_sink"]),
        "ab_w_out": f(np.asarray(inputs["ab_w_out"])[0]),
        "mla_w_down": f(inputs["mla_w_down"]), "mla_q_norm": f(inputs["mla_q_norm"]), "mla_kv_norm": f(inputs["mla_kv_norm"]),
        "mla_w_uq": f(inputs["mla_w_uq"]), "mla_w_ukv": f(inputs["mla_w_ukv"]), "mla_w_out": f(np.asarray(inputs["mla_w_out"])[0]),
        "ln_mix_g": f(inputs["ln_mix_g"]), "ln_mix_b": f(inputs["ln_mix_b"]), "ln_ffn_g": f(inputs["ln_ffn_g"]), "ln_ffn_b": f(inputs["ln_ffn_b"]),
        "moe_router": f(inputs["moe_router"]), "moe_w_gate": f(inputs["moe_w_gate"]), "moe_w_up": f(inputs["moe_w_up"]),
        "moe_w_down": f(inputs["moe_w_down"]), "tab": tab, "cst": cst, "wmask": make_wmask(),
    }
    xs = np.asarray(inputs["x"], dtype=np.float32)
    maps = []
    for c in range(ncores):
        m = dict(shared)
        m["x"] = np.ascontiguousarray(xs[c])
        maps.append(m)
    return maps


def kernel(**inputs):
    P = build()
    maps = make_in_maps(inputs, 8)
    res = run_bass_kernel_spmd(P.nc, maps, core_ids=list(range(8)))
    return np.stack([np.asarray(r["out"], dtype=np.float32) for r in res.results], axis=0)
```
